# Optimizing a Trainium2 kernel written in Bass

```python
import math
import jax, jax.numpy as jnp
from jax import lax
import numpy as np

D_MODEL = 1024
BATCH = 4
SEQ = 4096
DEPTH = 4

BRANCH_WIDTH = D_MODEL // 2
N_BRANCH = 3
S5_WIDTH = BRANCH_WIDTH
S5_GROUP = 16
S5_GROUPS = S5_WIDTH // S5_GROUP
S5_STATE = 64
S5_DT_MIN = 0.001
S5_DT_MAX = 0.1
LRU_WIDTH = BRANCH_WIDTH
LRU_HEADS = 8
LRU_HEAD_DIM = LRU_WIDTH // LRU_HEADS
LRU_C = 8.0
LRU_A_MIN = 0.9
LRU_A_MAX = 0.999
CONV_WIDTH = 4
FOX_HEAD_DIM = 64
FOX_HEADS = BRANCH_WIDTH // FOX_HEAD_DIM
FOX_WIDTH = FOX_HEADS * FOX_HEAD_DIM
FOX_FORGET_BIAS = 3.0
Q_BLOCK = 128
FFN_HIDDEN = ((8 * D_MODEL + 3 * 256 - 1) // (3 * 256)) * 256
ALPHA = (2.0 * DEPTH) ** 0.25
BETA = (8.0 * DEPTH) ** -0.25
LN_EPS = 1e-5
IN_SIZES = (S5_WIDTH, LRU_WIDTH, LRU_WIDTH, FOX_WIDTH, FOX_WIDTH, FOX_WIDTH, FOX_HEADS, N_BRANCH * D_MODEL)
IN_TOTAL = sum(IN_SIZES)

kernel_name = "hybrid_s5_rglru_fox_deepnorm"


def _layer_norm(x, g, b):
    x32 = x.astype(jnp.float32)
    mu = jnp.mean(x32, axis=-1, keepdims=True)
    var = jnp.mean(jnp.square(x32 - mu), axis=-1, keepdims=True)
    y = (x32 - mu) * lax.rsqrt(var + LN_EPS) * g.astype(jnp.float32) + b.astype(jnp.float32)
    return y.astype(x.dtype)


def _linear_scan(a, b):
    def combine(left, right):
        a_l, b_l = left
        a_r, b_r = right
        return a_r * a_l, a_r * b_l + b_r
    _, h = lax.associative_scan(combine, (a, b), axis=1)
    return h


def _s5_branch(u, a_re, a_im, log_dt, b_re, b_im, c_re, c_im, d_skip, w_glu, b_glu):
    bsz, L, _ = u.shape
    f32 = jnp.float32
    ug = u.astype(f32).reshape(bsz, L, S5_GROUPS, S5_GROUP)
    lam = lax.complex(a_re.astype(f32), a_im.astype(f32))
    dt = jnp.exp(log_dt.astype(f32))[:, None]
    lam_bar = jnp.exp(lam * dt)
    b_c = lax.complex(b_re.astype(f32), b_im.astype(f32))
    b_bar = ((lam_bar - 1.0) / lam)[:, :, None] * b_c
    bu = jnp.einsum('blgc,gpc->blgp', ug.astype(jnp.complex64), b_bar)
    h = _linear_scan(jnp.broadcast_to(lam_bar, bu.shape), bu)
    c_c = lax.complex(c_re.astype(f32), c_im.astype(f32))
    y = jnp.einsum('blgp,gcp->blgc', h, c_c).real + d_skip.astype(f32).reshape(S5_GROUPS, S5_GROUP) * ug
    y = jax.nn.gelu(y.reshape(bsz, L, S5_WIDTH))
    y = y * jax.nn.sigmoid(y @ w_glu.astype(f32) + b_glu.astype(f32))
    return y.astype(u.dtype)


def _rglru_branch(xb, gate, conv_w, conv_b, w_a, b_a, w_x, b_x, lam):
    bsz, L, _ = xb.shape
    f32 = jnp.float32
    xp = jnp.pad(xb, ((0, 0), (CONV_WIDTH - 1, 0), (0, 0)))
    xc = conv_b + sum(conv_w[k] * xp[:, k:k + L] for k in range(CONV_WIDTH))
    xh = xc.reshape(bsz, L, LRU_HEADS, LRU_HEAD_DIM)
    r = jax.nn.sigmoid(jnp.einsum('blhi,hij->blhj', xh, w_a) + b_a).astype(f32)
    i = jax.nn.sigmoid(jnp.einsum('blhi,hij->blhj', xh, w_x) + b_x).astype(f32)
    log_a = -LRU_C * jax.nn.softplus(-lam.astype(f32).reshape(LRU_HEADS, LRU_HEAD_DIM)) * r
    a = jnp.exp(log_a)
    mult = jnp.sqrt(-jnp.expm1(2.0 * log_a))
    h = _linear_scan(a, mult * (i * xh.astype(f32)))
    y = jax.nn.gelu(gate.astype(f32)) * h.reshape(bsz, L, LRU_WIDTH)
    return y.astype(xb.dtype)


def _fox_branch(q, k, v, fg_logit, b_f):
    bsz, L, _ = q.shape
    f32 = jnp.float32
    q = q.reshape(bsz, L, FOX_HEADS, FOX_HEAD_DIM)
    k = k.reshape(bsz, L, FOX_HEADS, FOX_HEAD_DIM)
    v = v.reshape(bsz, L, FOX_HEADS, FOX_HEAD_DIM)
    log_f = jax.nn.log_sigmoid((fg_logit + b_f).astype(f32))
    cum = jnp.cumsum(log_f, axis=1).transpose(0, 2, 1)
    kpos = jnp.arange(L)
    scale = FOX_HEAD_DIM ** -0.5

    def one_block(blk):
        start = blk * Q_BLOCK
        qb = lax.dynamic_slice_in_dim(q, start, Q_BLOCK, axis=1)
        cq = lax.dynamic_slice_in_dim(cum, start, Q_BLOCK, axis=2)
        s = jnp.einsum('bqhd,bkhd->bhqk', qb, k).astype(f32) * scale
        s = s + cq[..., None] - cum[:, :, None, :]
        qpos = start + jnp.arange(Q_BLOCK)
        s = jnp.where(kpos[None, :] <= qpos[:, None], s, -jnp.inf)
        p = jax.nn.softmax(s, axis=-1)
        return jnp.einsum('bhqk,bkhd->bqhd', p.astype(v.dtype), v)

    out = lax.map(one_block, jnp.arange(L // Q_BLOCK))
    return out.transpose(1, 0, 2, 3, 4).reshape(bsz, L, FOX_WIDTH)


def setup_inputs(seed: int = 0) -> dict:
    key = jax.random.key(seed)
    ks = iter(jax.random.split(key, 40))
    f32 = jnp.float32

    def nrm(shape, scale):
        return scale * jax.random.normal(next(ks), shape, f32)

    n = jnp.arange(S5_STATE, dtype=f32)
    x = nrm((BATCH, SEQ, D_MODEL), 1.0)
    w_in = nrm((DEPTH, D_MODEL, IN_TOTAL), D_MODEL ** -0.5)
    b_f = FOX_FORGET_BIAS + nrm((DEPTH, FOX_HEADS), 0.1)
    b_gate = nrm((DEPTH, N_BRANCH * D_MODEL), 0.01)
    s5_a_re = -0.5 + nrm((DEPTH, S5_GROUPS, S5_STATE), 0.01)
    s5_a_im = math.pi * n + nrm((DEPTH, S5_GROUPS, S5_STATE), 0.01)
    s5_log_dt = jax.random.uniform(next(ks), (DEPTH, S5_GROUPS), f32, math.log(S5_DT_MIN), math.log(S5_DT_MAX))
    s5_b_re = nrm((DEPTH, S5_GROUPS, S5_STATE, S5_GROUP), (2 * S5_GROUP) ** -0.5)
    s5_b_im = nrm((DEPTH, S5_GROUPS, S5_STATE, S5_GROUP), (2 * S5_GROUP) ** -0.5)
    s5_c_re = nrm((DEPTH, S5_GROUPS, S5_GROUP, S5_STATE), (2 * S5_STATE) ** -0.5)
    s5_c_im = nrm((DEPTH, S5_GROUPS, S5_GROUP, S5_STATE), (2 * S5_STATE) ** -0.5)
    s5_d = nrm((DEPTH, S5_WIDTH), 1.0)
    s5_w_glu = nrm((DEPTH, S5_WIDTH, S5_WIDTH), S5_WIDTH ** -0.5)
    s5_b_glu = nrm((DEPTH, S5_WIDTH), 0.01)
    lru_conv_w = nrm((DEPTH, CONV_WIDTH, LRU_WIDTH), CONV_WIDTH ** -0.5)
    lru_conv_b = nrm((DEPTH, LRU_WIDTH), 0.01)
    lru_w_a = nrm((DEPTH, LRU_HEADS, LRU_HEAD_DIM, LRU_HEAD_DIM), LRU_HEAD_DIM ** -0.5)
    lru_b_a = nrm((DEPTH, LRU_HEADS, LRU_HEAD_DIM), 0.01)
    lru_w_x = nrm((DEPTH, LRU_HEADS, LRU_HEAD_DIM, LRU_HEAD_DIM), LRU_HEAD_DIM ** -0.5)
    lru_b_x = nrm((DEPTH, LRU_HEADS, LRU_HEAD_DIM), 0.01)
    a_c = jax.random.uniform(next(ks), (DEPTH, LRU_WIDTH), f32, LRU_A_MIN, LRU_A_MAX)
    sig = a_c ** (1.0 / LRU_C)
    lru_lambda = jnp.log(sig) - jnp.log1p(-sig)
    w_branch = nrm((DEPTH, N_BRANCH, BRANCH_WIDTH, D_MODEL), BRANCH_WIDTH ** -0.5)
    w_out = nrm((DEPTH, D_MODEL, D_MODEL), BETA * D_MODEL ** -0.5)
    ln1_g = 1.0 + nrm((DEPTH, D_MODEL), 0.01)
    ln1_b = nrm((DEPTH, D_MODEL), 0.01)
    w_ffn_gate = nrm((DEPTH, D_MODEL, FFN_HIDDEN), D_MODEL ** -0.5)
    w_ffn_up = nrm((DEPTH, D_MODEL, FFN_HIDDEN), D_MODEL ** -0.5)
    w_ffn_down = nrm((DEPTH, FFN_HIDDEN, D_MODEL), BETA * FFN_HIDDEN ** -0.5)
    ln2_g = 1.0 + nrm((DEPTH, D_MODEL), 0.01)
    ln2_b = nrm((DEPTH, D_MODEL), 0.01)
    return {"x": x, "w_in": w_in, "b_f": b_f, "b_gate": b_gate,
            "s5_a_re": s5_a_re, "s5_a_im": s5_a_im, "s5_log_dt": s5_log_dt,
            "s5_b_re": s5_b_re, "s5_b_im": s5_b_im, "s5_c_re": s5_c_re, "s5_c_im": s5_c_im,
            "s5_d": s5_d, "s5_w_glu": s5_w_glu, "s5_b_glu": s5_b_glu,
            "lru_conv_w": lru_conv_w, "lru_conv_b": lru_conv_b,
            "lru_w_a": lru_w_a, "lru_b_a": lru_b_a, "lru_w_x": lru_w_x, "lru_b_x": lru_b_x,
            "lru_lambda": lru_lambda, "w_branch": w_branch, "w_out": w_out,
            "ln1_g": ln1_g, "ln1_b": ln1_b,
            "w_ffn_gate": w_ffn_gate, "w_ffn_up": w_ffn_up, "w_ffn_down": w_ffn_down,
            "ln2_g": ln2_g, "ln2_b": ln2_b}


def reference(x, w_in, b_f, b_gate, s5_a_re, s5_a_im, s5_log_dt, s5_b_re, s5_b_im, s5_c_re, s5_c_im,
              s5_d, s5_w_glu, s5_b_glu, lru_conv_w, lru_conv_b, lru_w_a, lru_b_a, lru_w_x, lru_b_x,
              lru_lambda, w_branch, w_out, ln1_g, ln1_b, w_ffn_gate, w_ffn_up, w_ffn_down, ln2_g, ln2_b):
    split_at = np.cumsum(IN_SIZES)[:-1].tolist()
    bsz, L, _ = x.shape
    for l in range(DEPTH):
        z = x @ w_in[l]
        u_s5, x_lru, g_lru, q, k, v, fg, gate_logits = jnp.split(z, split_at, axis=-1)
        y_s5 = _s5_branch(u_s5, s5_a_re[l], s5_a_im[l], s5_log_dt[l], s5_b_re[l], s5_b_im[l],
                          s5_c_re[l], s5_c_im[l], s5_d[l], s5_w_glu[l], s5_b_glu[l])
        y_lru = _rglru_branch(x_lru, g_lru, lru_conv_w[l], lru_conv_b[l], lru_w_a[l], lru_b_a[l],
                              lru_w_x[l], lru_b_x[l], lru_lambda[l])
        y_fox = _fox_branch(q, k, v, fg, b_f[l])
        ys = jnp.stack([y_s5, y_lru, y_fox], axis=2)
        proj = jnp.einsum('blkc,kcd->blkd', ys, w_branch[l])
        gates = jax.nn.sigmoid(gate_logits + b_gate[l]).reshape(bsz, L, N_BRANCH, D_MODEL)
        mixed = jnp.sum(gates * proj, axis=2) @ w_out[l]
        x = _layer_norm(ALPHA * x + mixed, ln1_g[l], ln1_b[l])
        hid = jax.nn.silu(x @ w_ffn_gate[l]) * (x @ w_ffn_up[l])
        x = _layer_norm(ALPHA * x + hid @ w_ffn_down[l], ln2_g[l], ln2_b[l])
    return x
```

```python
import math
import numpy as np
import concourse.bass as bass
import concourse.mybir as mybir
from concourse.bass_utils import run_bass_kernel_spmd

F32 = mybir.dt.float32
BF16 = mybir.dt.bfloat16
AF = mybir.ActivationFunctionType
ALU = mybir.AluOpType

D = 1024
L = 4096
DEPTH = 4
NB = 8
TB = 512
IN_TOTAL = 6152
FFN = 2816
NHC = 22
ALPHA = (2.0 * DEPTH) ** 0.25
LN_EPS = 1e-5
MAGIC = 12582912.0
TWO_PI = 2.0 * math.pi


class Buf:
    __slots__ = ("ap", "w", "r", "name")

    def __init__(self, ap, name=""):
        self.ap = ap
        self.w = {}
        self.r = {}
        self.name = name


class Ctx:
    def __init__(self, nc):
        self.nc = nc
        self.E = {"pe": nc.tensor, "act": nc.scalar, "dve": nc.vector, "pool": nc.gpsimd, "sp": nc.sync}
        self.sem = {}
        self.cnt = {}
        self.nsem = 0
        for e in ("pe", "act", "dve", "pool"):
            self._new_sem(e)
        self.seen = {e: {} for e in self.E}
        self.dma_sems = [nc.alloc_semaphore(f"dq{i}") for i in range(40)]
        self.dma_cnt = [0] * len(self.dma_sems)
        self.dma_rr = 0
        self.semobj = {}
        self.uid = 0

    def _new_sem(self, e):
        s = self.nc.alloc_semaphore(f"s_{e}_{self.nsem}")
        self.nsem += 1
        self.sem[e] = s
        self.cnt[e] = 0

    def _key(self, s):
        k = id(s)
        self.semobj[k] = s
        return k

    def _wait(self, e, deps):
        seen = self.seen[e]
        for k, v in deps.items():
            if seen.get(k, 0) >= v:
                continue
            self.E[e].wait_ge(self.semobj[k], v)
            seen[k] = v

    @staticmethod
    def _merge(dst, src):
        for k, v in src.items():
            if dst.get(k, 0) < v:
                dst[k] = v

    def _deps(self, reads, writes):
        deps = {}
        for b in reads:
            self._merge(deps, b.w)
        for b in writes:
            self._merge(deps, b.w)
            self._merge(deps, b.r)
        return deps

    def _commit(self, tok, reads, writes):
        for b in reads:
            self._merge(b.r, tok)
        for b in writes:
            b.w = dict(tok)
            b.r = {}

    def op(self, e, emit, reads=(), writes=()):
        self._wait(e, self._deps(reads, writes))
        ins = emit(self.E[e])
        if self.cnt[e] >= 30000:
            self._new_sem(e)
        s = self.sem[e]
        self.cnt[e] += 1
        ins.then_inc(s, 1)
        tok = {self._key(s): self.cnt[e]}
        self._commit(tok, reads, writes)
        return tok

    def dma(self, e, out, in_, reads=(), writes=(), **kw):
        self._wait(e, self._deps(reads, writes))
        i = self.dma_rr
        self.dma_rr = (self.dma_rr + 1) % len(self.dma_sems)
        if self.dma_cnt[i] >= 30000:
            self.dma_sems[i] = self.nc.alloc_semaphore(f"dqx{self.nsem}")
            self.nsem += 1
            self.dma_cnt[i] = 0
        s = self.dma_sems[i]
        self.dma_cnt[i] += 16
        self.E[e].dma_start(out=out, in_=in_, **kw).then_inc(s, 16)
        tok = {self._key(s): self.dma_cnt[i]}
        self._commit(tok, reads, writes)
        return tok

    def barrier(self):
        allt = {}
        for e in ("pe", "act", "dve", "pool"):
            if self.cnt[e] > 0:
                allt[self._key(self.sem[e])] = self.cnt[e]
        for i, s in enumerate(self.dma_sems):
            if self.dma_cnt[i] > 0:
                allt[self._key(s)] = self.dma_cnt[i]
        for e in self.E:
            self._wait(e, allt)


class Scope:
    def __init__(self, cx):
        self.cx = cx
        self.guards = []

    def sb(self, shape, dt=F32, name=None):
        self.cx.uid += 1
        g = self.cx.nc.sbuf_tensor(f"{name or 't'}_{self.cx.uid}", list(shape), dt)
        t = g.__enter__()
        self.guards.append(g)
        return t.ap()

    def buf(self, shape, dt=F32, name=None):
        return Buf(self.sb(shape, dt, name), name or "")

    def close(self):
        for g in reversed(self.guards):
            g.__exit__(None, None, None)
        self.guards = []


def build(nc, n_layers=DEPTH, dbg=False, stop_after=None):
    cx = Ctx(nc)
    kind_dbg = "ExternalOutput" if dbg else "Internal"

    def din(name, shape):
        return nc.dram_tensor(name, list(shape), F32, kind="ExternalInput").ap()

    def dscr(name, shape, dt, k="Internal"):
        return nc.dram_tensor(name, list(shape), dt, kind=k).ap()

    xT = din("xT", [D, L])
    w_in = din("w_in", [n_layers, D, IN_TOTAL])
    w_branch = din("w_branch", [n_layers, 1536, D])
    w_out = din("w_out", [n_layers, D, D])
    w_g = din("w_ffn_gate", [n_layers, D, FFN])
    w_u = din("w_ffn_up", [n_layers, D, FFN])
    w_d = din("w_ffn_down", [n_layers, FFN, D])
    w_glu = din("s5_w_glu", [n_layers, 512, 512])
    lru_w_a = din("lru_w_a", [n_layers, 8, 64, 64])
    lru_w_x = din("lru_w_x", [n_layers, 8, 64, 64])
    b_f = din("b_f", [n_layers, 8])
    b_gate = din("b_gate", [n_layers, 3072])
    s5_a_re = din("s5_a_re", [n_layers, 32, 64])
    s5_a_im = din("s5_a_im", [n_layers, 32, 64])
    s5_log_dt = din("s5_log_dt", [n_layers, 32])
    s5_b_re = din("s5_b_re", [n_layers, 32, 64, 16])
    s5_b_im = din("s5_b_im", [n_layers, 32, 64, 16])
    s5_c_re = din("s5_c_re", [n_layers, 512, 64])
    s5_c_im = din("s5_c_im", [n_layers, 512, 64])
    s5_d = din("s5_d", [n_layers, 512])
    s5_b_glu = din("s5_b_glu", [n_layers, 512])
    lru_conv_w = din("lru_conv_w", [n_layers, 4, 512])
    lru_conv_b = din("lru_conv_b", [n_layers, 512])
    lru_b_a = din("lru_b_a", [n_layers, 512])
    lru_b_x = din("lru_b_x", [n_layers, 512])
    lru_lambda = din("lru_lambda", [n_layers, 512])
    ln1_g = din("ln1_g", [n_layers, D])
    ln1_b = din("ln1_b", [n_layers, D])
    ln2_g = din("ln2_g", [n_layers, D])
    ln2_b = din("ln2_b", [n_layers, D])
    outT = nc.dram_tensor("outT", [D, L], F32, kind="ExternalOutput").ap()

    wb_in = dscr("wb_in", [n_layers, D, IN_TOTAL], BF16)
    wb_branch = dscr("wb_branch", [n_layers, 1536, D], BF16)
    wb_out = dscr("wb_out", [n_layers, D, D], BF16)
    wb_g = dscr("wb_g", [n_layers, D, FFN], BF16)
    wb_u = dscr("wb_u", [n_layers, D, FFN], BF16)
    wb_d = dscr("wb_d", [n_layers, FFN, D], BF16)
    wb_glu = dscr("wb_glu", [n_layers, 512, 512], BF16)
    XBF = dscr("XBF", [D, L], BF16)
    XRES = dscr("XRES", [D, L], F32, kind_dbg)
    X1BF = dscr("X1BF", [D, L], BF16)
    X1RES = dscr("X1RES", [D, L], F32, kind_dbg)
    UT = dscr("UT", [512, L], BF16, kind_dbg)
    XL = dscr("XL", [512, L], F32, kind_dbg)
    GL = dscr("GL", [512, L], F32, kind_dbg)
    QA = dscr("QA", [8, 70, L], BF16, kind_dbg)
    KA = dscr("KA", [8, 70, L], BF16, kind_dbg)
    VA = dscr("VA", [8, 128, 32, 65], BF16, kind_dbg)
    YS = dscr("YS", [1536, L], BF16, kind_dbg)

    B_wb = {}
    for nm in ("in", "branch", "out", "g", "u", "d", "glu"):
        for l in range(n_layers):
            B_wb[(nm, l)] = Buf(None, f"wb_{nm}{l}")
    B_XBF = [Buf(None, f"XBF{t}") for t in range(NB)]
    B_XRES = [Buf(None, f"XRES{t}") for t in range(NB)]
    B_X1BF = [Buf(None, f"X1BF{t}") for t in range(NB)]
    B_X1RES = [Buf(None, f"X1RES{t}") for t in range(NB)]
    B_UT = Buf(None, "UT")
    B_XL = Buf(None, "XL")
    B_GL = Buf(None, "GL")
    B_QA = Buf(None, "QA")
    B_KA = Buf(None, "KA")
    B_VA = Buf(None, "VA")
    B_YS = [Buf(None, f"YS{k}") for k in range(3)]
    B_OUT = Buf(None, "out")

    PS = [Buf(nc.alloc_psum_tensor(f"psb{i}", [128, 512], F32).ap(), f"ps{i}") for i in range(8)]

    cs = Scope(cx)
    identf = cs.buf([128, 128], F32, "identf")
    identb = cs.buf([128, 128], BF16, "identb")
    onesD = cs.buf([128, 128], F32, "onesD")
    ones1 = cs.buf([128, 64], F32, "ones1")
    mask8 = cs.buf([128, 128], F32, "mask8")
    tri = cs.buf([128, 128], BF16, "tri")
    SELD = dscr("SELD", [2, 128, 8, 8, 128], BF16)
    B_SELD = Buf(None, "SELD")
    cs0 = Scope(cx)
    Sel = cs0.buf([128, 8, 8, 128], BF16, "Sel")
    SelT = cs0.buf([128, 8, 8, 128], BF16, "SelT")

    def pool_fill(buf, val):
        cx.op("pool", lambda e: e.memset(buf.ap, val), writes=[buf])

    def pool_sel(buf, ap, pattern, cmp, base, cm):
        cx.op("pool", lambda e: e.affine_select(out=ap, in_=ap, pattern=pattern, compare_op=cmp, fill=0.0,
                                                base=base, channel_multiplier=cm), reads=[buf], writes=[buf])

    pool_fill(identf, 1.0)
    pool_sel(identf, identf.ap, [[1, 128]], ALU.is_equal, 0, -1)
    pool_fill(identb, 1.0)
    pool_sel(identb, identb.ap, [[1, 128]], ALU.is_equal, 0, -1)
    pool_fill(onesD, 1.0 / D)
    pool_fill(ones1, 1.0)
    pool_fill(mask8, 1.0)
    pool_sel(mask8, mask8.ap.rearrange("p (t c) -> p t c", c=16), [[16, 8], [0, 16]], ALU.is_ge, 15, -1)
    pool_fill(tri, 1.0)
    pool_sel(tri, tri.ap, [[1, 128]], ALU.is_ge, 0, -1)
    pool_fill(Sel, 1.0)
    for gg in range(8):
        a4 = Sel.ap[:, gg, :, :].rearrange("p t (u c) -> p t u c", c=16)
        pool_sel(Sel, a4, [[0, 8], [0, 8], [-1, 16]], ALU.is_equal, -16 * gg, 1)
        pool_sel(Sel, a4, [[-1, 8], [1, 8], [0, 16]], ALU.is_equal, 0, 0)
    pool_fill(SelT, 1.0)
    for gg in range(8):
        a3 = SelT.ap[:, gg, :, :]
        pool_sel(SelT, a3, [[16, 8], [1, 128]], ALU.is_equal, -16 * gg, -1)
        pool_sel(SelT, a3, [[-16, 8], [0, 128]], ALU.is_ge, 0, 1)
        pool_sel(SelT, a3, [[16, 8], [0, 128]], ALU.is_ge, 15, -1)

    cx.dma("sp", SELD[0], Sel.ap, reads=[Sel], writes=[B_SELD])
    cx.dma("sp", SELD[1], SelT.ap, reads=[SelT], writes=[B_SELD])
    cx.barrier()
    cs0.close()

    def convert(src, dst, rows, key):
        r = 0
        while r < rows:
            n = min(128, rows - r)
            cx.dma("pool", dst[r:r + n, :], src[r:r + n, :], writes=[B_wb[key]])
            r += n

    for l in range(n_layers):
        convert(w_in[l], wb_in[l], D, ("in", l))
        convert(w_glu[l], wb_glu[l], 512, ("glu", l))
        convert(w_branch[l], wb_branch[l], 1536, ("branch", l))
        convert(w_out[l], wb_out[l], D, ("out", l))
        convert(w_g[l], wb_g[l], D, ("g", l))
        convert(w_u[l], wb_u[l], D, ("u", l))
        convert(w_d[l], wb_d[l], FFN, ("d", l))
    for t in range(NB):
        for kc in range(8):
            cx.dma("pool", XBF[kc * 128:(kc + 1) * 128, t * TB:(t + 1) * TB],
                   xT[kc * 128:(kc + 1) * 128, t * TB:(t + 1) * TB], writes=[B_XBF[t]])

    def kview(ap2d):
        return ap2d.rearrange("(kc p) n -> p kc n", p=128)

    def blk(t):
        return slice(t * TB, (t + 1) * TB)

    def phase_proj(l):
        sc_fg = Scope(cx)
        fgT = sc_fg.buf([8, L], F32, "fgT")
        sc = Scope(cx)
        xb = [sc.buf([128, 8, TB], BF16, f"xb{t}") for t in range(NB)]
        for t in range(NB):
            cx.dma("sp", xb[t].ap, kview(XBF)[:, :, blk(t)], reads=[B_XBF[t]], writes=[xb[t]])
        wt = [sc.buf([128, 8, 512], BF16, f"wt{i}") for i in range(2)]
        wfg = sc.buf([128, 8, 8], BF16, "wfg")
        cx.dma("sp", wfg.ap, kview(wb_in[l])[:, :, 3072:3080], reads=[B_wb[("in", l)]], writes=[wfg])
        st32 = [sc.buf([128, 4, TB], F32, f"st32_{i}") for i in range(2)]
        st16 = [sc.buf([128, 4, TB], BF16, f"st16_{i}") for i in range(2)]
        stqk = [sc.buf([64, 8, TB], BF16, f"stqk_{i}") for i in range(2)]
        vst = [sc.buf([128, 8, 65], BF16, f"vst_{i}") for i in range(2)]
        for v in vst:
            cx.op("pool", lambda e, v=v: e.memset(v.ap, 1.0), writes=[v])
        nps = 0
        nst = 0
        for cg in range(6):
            w = wt[cg % 2]
            cx.dma("sp", w.ap, kview(wb_in[l])[:, :, cg * 512:(cg + 1) * 512], reads=[B_wb[("in", l)]], writes=[w])
            if cg < 3:
                for t in range(NB):
                    stb = (st16 if cg == 0 else st32)[nst % 2]
                    nst += 1
                    for ct in range(4):
                        ps = PS[nps % 4]
                        nps += 1

                        def mm(e, ps=ps, ct=ct, t=t, w=w):
                            for kc in range(8):
                                i = e.matmul(ps.ap, lhsT=w.ap[:, kc, ct * 128:(ct + 1) * 128], rhs=xb[t].ap[:, kc, :],
                                             start=(kc == 0), stop=(kc == 7))
                            return i
                        cx.op("pe", mm, reads=[w, xb[t]], writes=[ps])
                        fn = AF.Gelu_apprx_tanh if cg == 2 else AF.Identity
                        cx.op("act", lambda e, ps=ps, stb=stb, ct=ct, fn=fn: e.activation(out=stb.ap[:, ct, :], in_=ps.ap, func=fn),
                              reads=[ps], writes=[stb])
                    dst, bd = [(UT, B_UT), (XL, B_XL), (GL, B_GL)][cg]
                    cx.dma("sp", dst.rearrange("(c p) t -> p c t", p=128)[:, :, blk(t)], stb.ap, reads=[stb], writes=[bd])
            elif cg < 5:
                for t in range(NB):
                    stb = stqk[nst % 2]
                    nst += 1
                    for h in range(8):
                        ps = PS[nps % 4]
                        nps += 1

                        def mm(e, ps=ps, h=h, t=t, w=w):
                            for kc in range(8):
                                i = e.matmul(ps.ap[0:64, :], lhsT=w.ap[:, kc, h * 64:(h + 1) * 64], rhs=xb[t].ap[:, kc, :],
                                             start=(kc == 0), stop=(kc == 7))
                            return i
                        cx.op("pe", mm, reads=[w, xb[t]], writes=[ps])
                        sc_ = 0.125 if cg == 3 else 1.0
                        cx.op("act", lambda e, ps=ps, stb=stb, h=h, sc_=sc_: e.activation(out=stb.ap[:, h, :], in_=ps.ap[0:64, :],
                                                                                         func=AF.Identity, scale=sc_),
                              reads=[ps], writes=[stb])
                    dst, bd = (QA, B_QA) if cg == 3 else (KA, B_KA)
                    cx.dma("sp", dst[:, 0:64, blk(t)].rearrange("h d t -> d h t"), stb.ap, reads=[stb], writes=[bd])
            else:
                for tt in range(32):
                    ps = PS[nps % 4]
                    nps += 1
                    t = tt // 4
                    vs = vst[tt % 2]

                    def mm(e, ps=ps, tt=tt, t=t, w=w):
                        o = (tt % 4) * 128
                        for kc in range(8):
                            i = e.matmul(ps.ap, lhsT=xb[t].ap[:, kc, o:o + 128], rhs=w.ap[:, kc, :],
                                         start=(kc == 0), stop=(kc == 7))
                        return i
                    cx.op("pe", mm, reads=[w, xb[t]], writes=[ps])
                    cx.op("act", lambda e, ps=ps, vs=vs: e.activation(out=vs.ap[:, :, 0:64], in_=ps.ap.rearrange("p (h d) -> p h d", d=64),
                                                                      func=AF.Identity), reads=[ps], writes=[vs])
                    cx.dma("sp", VA[:, :, tt, :].rearrange("h p e -> p h e"), vs.ap, reads=[vs], writes=[B_VA])
        for t in range(NB):
            ps = PS[nps % 4]
            nps += 1

            def mm(e, ps=ps, t=t):
                for kc in range(8):
                    i = e.matmul(ps.ap[0:8, :], lhsT=wfg.ap[:, kc, :], rhs=xb[t].ap[:, kc, :], start=(kc == 0), stop=(kc == 7))
                return i
            cx.op("pe", mm, reads=[wfg, xb[t]], writes=[ps])
            cx.op("act", lambda e, ps=ps, t=t: e.activation(out=fgT.ap[:, blk(t)], in_=ps.ap[0:8, :], func=AF.Identity),
                  reads=[ps], writes=[fgT])
        cx.barrier()
        sc.close()
        sc = Scope(cx)
        bf = sc.buf([8, 1], F32, "bf")
        cx.dma("sp", bf.ap, b_f[l].rearrange("(h o) -> h o", o=1), writes=[bf])
        nbf = sc.buf([8, 1], F32, "nbf")
        cx.op("dve", lambda e: e.tensor_scalar(out=nbf.ap, in0=bf.ap, scalar1=-1.0, scalar2=None, op0=ALU.mult), reads=[bf], writes=[nbf])
        one8 = sc.buf([8, 1], F32, "one8")
        cx.op("dve", lambda e: e.memset(one8.ap, 1.0), writes=[one8])
        ex = sc.buf([8, L], F32, "ex")
        cx.op("act", lambda e: e.activation(out=ex.ap, in_=fgT.ap, func=AF.Exp, bias=nbf.ap, scale=-1.0), reads=[fgT, nbf], writes=[ex])
        cx.op("act", lambda e: e.activation(out=ex.ap, in_=ex.ap, func=AF.Ln, bias=one8.ap, scale=1.0), reads=[ex, one8], writes=[ex])
        csum = sc.buf([8, L], F32, "csum")
        cx.op("dve", lambda e: e.tensor_tensor_scan(out=csum.ap, data0=one8.ap.to_broadcast([8, L]), data1=ex.ap, initial=0.0,
                                                    op0=ALU.mult, op1=ALU.add), reads=[ex, one8], writes=[csum])
        pcs = [sc.buf([8, L], BF16, f"pc{j}") for j in range(3)]
        ncs = [sc.buf([8, L], BF16, f"nc{j}") for j in range(3)]
        res = ex
        cx.op("dve", lambda e: e.tensor_copy(out=pcs[0].ap, in_=csum.ap), reads=[csum], writes=[pcs[0]])
        cx.op("dve", lambda e: e.tensor_tensor(out=res.ap, in0=csum.ap, in1=pcs[0].ap, op=ALU.subtract), reads=[csum, pcs[0]], writes=[res])
        cx.op("dve", lambda e: e.tensor_copy(out=pcs[1].ap, in_=res.ap), reads=[res], writes=[pcs[1]])
        cx.op("dve", lambda e: e.tensor_tensor(out=res.ap, in0=res.ap, in1=pcs[1].ap, op=ALU.subtract), reads=[res, pcs[1]], writes=[res])
        cx.op("dve", lambda e: e.tensor_copy(out=pcs[2].ap, in_=res.ap), reads=[res], writes=[pcs[2]])
        for j in range(3):
            cx.op("dve", lambda e, j=j: e.tensor_scalar(out=ncs[j].ap, in0=pcs[j].ap, scalar1=-1.0, scalar2=None, op0=ALU.mult),
                  reads=[pcs[j]], writes=[ncs[j]])
        onesb = sc.buf([8, L], BF16, "onesb")
        cx.op("pool", lambda e: e.memset(onesb.ap, 1.0), writes=[onesb])
        for j in range(3):
            cx.dma("sp", QA[:, 64 + j, :], ncs[j].ap, reads=[ncs[j]], writes=[B_QA])
            cx.dma("sp", QA[:, 67 + j, :], onesb.ap, reads=[onesb], writes=[B_QA])
            cx.dma("sp", KA[:, 64 + j, :], onesb.ap, reads=[onesb], writes=[B_KA])
            cx.dma("sp", KA[:, 67 + j, :], pcs[j].ap, reads=[pcs[j]], writes=[B_KA])
        cx.barrier()
        sc.close()
        sc_fg.close()

    def phase_attn(l):
        sc = Scope(cx)
        qa = [sc.buf([70, L], BF16, f"qa{i}") for i in range(2)]
        ka = [sc.buf([70, L], BF16, f"ka{i}") for i in range(2)]
        va = [sc.buf([128, 32, 65], BF16, f"va{i}") for i in range(2)]
        pt = [sc.buf([128, TB], BF16, f"pt{i}") for i in range(4)]
        rden = [sc.buf([128, TB], F32, f"rden{i}") for i in range(2)]
        rb = [sc.buf([64, TB], F32, f"rb{i}") for i in range(2)]
        ost = [sc.buf([64, TB], BF16, f"ost{i}") for i in range(2)]
        npt = 0
        nblk = 0
        for h in range(8):
            q, k, v = qa[h % 2], ka[h % 2], va[h % 2]
            cx.dma("sp", q.ap, QA[h], reads=[B_QA], writes=[q])
            cx.dma("sp", k.ap, KA[h], reads=[B_KA], writes=[k])
            cx.dma("sp", v.ap, VA[h], reads=[B_VA], writes=[v])
            for I in range(NB):
                po = PS[4 + nblk % 2]
                pr = PS[6]
                nkb = 4 * I + 4
                for j in range(nkb):
                    jj = max(0, j - 4 * I)
                    c0 = 128 * jj
                    ps = PS[npt % 4]
                    p = pt[npt % 4]
                    npt += 1
                    cx.op("pe", lambda e, ps=ps, j=j, c0=c0, I=I, k=k, q=q: e.matmul(
                        ps.ap[:, c0:TB], lhsT=k.ap[:, j * 128:(j + 1) * 128], rhs=q.ap[:, I * TB + c0:(I + 1) * TB],
                        start=True, stop=True), reads=[k, q], writes=[ps])
                    cx.op("act", lambda e, ps=ps, p=p, c0=c0: e.activation(out=p.ap[:, c0:TB], in_=ps.ap[:, c0:TB], func=AF.Exp),
                          reads=[ps], writes=[p])
                    if j >= 4 * I:
                        cx.op("pool", lambda e, p=p, c0=c0: e.tensor_tensor(out=p.ap[:, c0:c0 + 128], in0=p.ap[:, c0:c0 + 128],
                                                                            in1=tri.ap, op=ALU.mult), reads=[p, tri], writes=[p])
                    cx.op("pe", lambda e, po=po, p=p, c0=c0, j=j, nkb=nkb, v=v: e.matmul(
                        po.ap[0:65, c0:TB], lhsT=v.ap[:, j, :], rhs=p.ap[:, c0:TB], start=(j == 0), stop=(j == nkb - 1)),
                        reads=[v, p], writes=[po])
                rd = rden[nblk % 2]
                r_ = rb[nblk % 2]
                o_ = ost[nblk % 2]
                nblk += 1
                cx.op("dve", lambda e, po=po, rd=rd: e.reciprocal(out=rd.ap[64:65, :], in_=po.ap[64:65, :]), reads=[po], writes=[rd])
                cx.op("pe", lambda e, pr=pr, rd=rd: e.matmul(pr.ap[0:64, :], lhsT=ones1.ap[64:65, :], rhs=rd.ap[64:65, :], start=True, stop=True),
                      reads=[ones1, rd], writes=[pr])
                cx.op("act", lambda e, pr=pr, r_=r_: e.activation(out=r_.ap, in_=pr.ap[0:64, :], func=AF.Identity), reads=[pr], writes=[r_])
                cx.op("dve", lambda e, po=po, r_=r_, o_=o_: e.tensor_tensor(out=o_.ap, in0=po.ap[0:64, :], in1=r_.ap, op=ALU.mult),
                      reads=[po, r_], writes=[o_])
                cx.dma("sp", YS[1024 + h * 64:1024 + (h + 1) * 64, blk(I)], o_.ap, reads=[o_], writes=[B_YS[2]])
        cx.barrier()
        sc.close()

    def phase_lru(l):
        sc = Scope(cx)
        cw = sc.buf([128, 4, 4], F32, "cw")
        cb = sc.buf([128, 4], F32, "cb")
        ba = sc.buf([128, 4], F32, "ba")
        bx = sc.buf([128, 4], F32, "bx")
        lam = sc.buf([128, 4], F32, "lam")
        sneg = sc.buf([128, 4], F32, "sneg")
        one_c = sc.buf([128, 1], F32, "one_c")
        cx.op("dve", lambda e: e.memset(one_c.ap, 1.0), writes=[one_c])
        for k_ in range(4):
            cx.dma("sp", cw.ap[:, :, k_], lru_conv_w[l, k_].rearrange("(c p) -> p c", p=128), writes=[cw], allow_slow_non_contiguous=True)
        for (dst, src) in ((cb, lru_conv_b), (ba, lru_b_a), (bx, lru_b_x), (lam, lru_lambda)):
            cx.dma("sp", dst.ap, src[l].rearrange("(c p) -> p c", p=128), writes=[dst], allow_slow_non_contiguous=True)
        cx.op("act", lambda e: e.activation(out=sneg.ap, in_=lam.ap, func=AF.Exp, scale=-1.0), reads=[lam], writes=[sneg])
        cx.op("act", lambda e: e.activation(out=sneg.ap, in_=sneg.ap, func=AF.Ln, bias=one_c.ap, scale=1.0), reads=[sneg, one_c], writes=[sneg])
        cx.op("dve", lambda e: e.tensor_scalar(out=sneg.ap, in0=sneg.ap, scalar1=-8.0, scalar2=None, op0=ALU.mult), reads=[sneg], writes=[sneg])
        WA = sc.buf([128, 4, 128], BF16, "WA")
        WX = sc.buf([128, 4, 128], BF16, "WX")
        for Wm, src in ((WA, lru_w_a), (WX, lru_w_x)):
            cx.op("pool", lambda e, Wm=Wm: e.memset(Wm.ap, 0.0), writes=[Wm])
            for c in range(4):
                cx.dma("pool", Wm.ap[0:64, c, 0:64], src[l, 2 * c], writes=[Wm])
                cx.dma("pool", Wm.ap[64:128, c, 64:128], src[l, 2 * c + 1], writes=[Wm])
        xl = sc.buf([128, L + 3], F32, "xl")
        gl = sc.buf([128, L], F32, "gl")
        xc = sc.buf([128, L], F32, "xc")
        xcb = sc.buf([128, L], BF16, "xcb")
        a_all = sc.buf([128, L], F32, "a_all")
        b_all = sc.buf([128, L], F32, "b_all")
        yb = sc.buf([128, L], BF16, "yb")
        h_all = sc.buf([128, L], F32, "h_all")
        tr = [sc.buf([128, TB], F32, f"tr{i}") for i in range(2)]
        ti = [sc.buf([128, TB], F32, f"ti{i}") for i in range(2)]
        tm = [sc.buf([128, TB], F32, f"tm{i}") for i in range(2)]
        cx.op("pool", lambda e: e.memset(xl.ap[:, 0:3], 0.0), writes=[xl])
        n = 0
        for c in range(4):
            cx.dma("sp", xl.ap[:, 3:], XL[c * 128:(c + 1) * 128, :], reads=[B_XL], writes=[xl])
            cx.dma("sp", gl.ap, GL[c * 128:(c + 1) * 128, :], reads=[B_GL], writes=[gl])
            cx.op("dve", lambda e, c=c: e.tensor_scalar(out=xc.ap, in0=xl.ap[:, 0:L], scalar1=cw.ap[:, c, 0:1], scalar2=cb.ap[:, c:c + 1],
                                                        op0=ALU.mult, op1=ALU.add), reads=[xl, cw, cb], writes=[xc])
            for k_ in range(1, 4):
                eng = "dve"
                cx.op(eng, lambda e, c=c, k_=k_: e.scalar_tensor_tensor(out=xc.ap, in0=xl.ap[:, k_:k_ + L], scalar=cw.ap[:, c, k_:k_ + 1],
                                                                         in1=xc.ap, op0=ALU.mult, op1=ALU.add), reads=[xl, cw, xc], writes=[xc])
            cx.op("act", lambda e: e.activation(out=xcb.ap, in_=xc.ap, func=AF.Identity), reads=[xc], writes=[xcb])
            for t in range(NB):
                pa, px = PS[(2 * n) % 4], PS[(2 * n + 1) % 4]
                r_, i_, m_ = tr[n % 2], ti[n % 2], tm[n % 2]
                n += 1
                cx.op("pe", lambda e, pa=pa, c=c, t=t: e.matmul(pa.ap, lhsT=WA.ap[:, c, :], rhs=xcb.ap[:, blk(t)], start=True, stop=True),
                      reads=[WA, xcb], writes=[pa])
                cx.op("pe", lambda e, px=px, c=c, t=t: e.matmul(px.ap, lhsT=WX.ap[:, c, :], rhs=xcb.ap[:, blk(t)], start=True, stop=True),
                      reads=[WX, xcb], writes=[px])
                cx.op("act", lambda e, pa=pa, r_=r_, c=c: e.activation(out=r_.ap, in_=pa.ap, func=AF.Sigmoid, bias=ba.ap[:, c:c + 1], scale=1.0),
                      reads=[pa, ba], writes=[r_])
                cx.op("act", lambda e, px=px, i_=i_, c=c: e.activation(out=i_.ap, in_=px.ap, func=AF.Sigmoid, bias=bx.ap[:, c:c + 1], scale=1.0),
                      reads=[px, bx], writes=[i_])
                cx.op("act", lambda e, r_=r_, c=c, t=t: e.activation(out=a_all.ap[:, blk(t)], in_=r_.ap, func=AF.Exp, scale=sneg.ap[:, c:c + 1]),
                      reads=[r_, sneg], writes=[a_all])
                cx.op("act", lambda e, m_=m_, t=t: e.activation(out=m_.ap, in_=a_all.ap[:, blk(t)], func=AF.Square), reads=[a_all], writes=[m_])
                cx.op("act", lambda e, m_=m_: e.activation(out=m_.ap, in_=m_.ap, func=AF.Sqrt, bias=one_c.ap, scale=-1.0),
                      reads=[m_, one_c], writes=[m_])
                cx.op("dve", lambda e, m_=m_, i_=i_: e.tensor_tensor(out=m_.ap, in0=m_.ap, in1=i_.ap, op=ALU.mult), reads=[m_, i_], writes=[m_])
                cx.op("pool", lambda e, m_=m_, t=t: e.tensor_tensor(out=b_all.ap[:, blk(t)], in0=m_.ap, in1=xc.ap[:, blk(t)], op=ALU.mult),
                      reads=[m_, xc], writes=[b_all])
            cx.op("dve", lambda e: e.tensor_tensor_scan(out=h_all.ap, data0=a_all.ap, data1=b_all.ap, initial=0.0, op0=ALU.mult, op1=ALU.add),
                  reads=[a_all, b_all], writes=[h_all])
            cx.op("dve", lambda e: e.tensor_tensor(out=yb.ap, in0=h_all.ap, in1=gl.ap, op=ALU.mult), reads=[h_all, gl], writes=[yb])
            cx.dma("sp", YS[512 + c * 128:512 + (c + 1) * 128, :], yb.ap, reads=[yb], writes=[B_YS[1]])
        cx.barrier()
        sc.close()

    def cmul(eng_a, eng_b, sc_t, outr, outi, ar, ai, br, bi, reads, w_r, w_i, negi=False):
        t1, t2 = sc_t
        cx.op(eng_a, lambda e: e.tensor_tensor(out=t1.ap, in0=ar, in1=br, op=ALU.mult), reads=reads, writes=[t1])
        cx.op(eng_b, lambda e: e.tensor_tensor(out=t2.ap, in0=ai, in1=bi, op=ALU.mult), reads=reads, writes=[t2])
        cx.op(eng_a, lambda e: e.tensor_tensor(out=outr, in0=t1.ap, in1=t2.ap, op=ALU.subtract), reads=[t1, t2], writes=[w_r])
        cx.op(eng_a, lambda e: e.tensor_tensor(out=t1.ap, in0=ar, in1=bi, op=ALU.mult), reads=reads + [w_r], writes=[t1])
        cx.op(eng_b, lambda e: e.tensor_tensor(out=t2.ap, in0=ai, in1=br, op=ALU.mult), reads=reads + [w_r], writes=[t2])
        if negi:
            cx.op(eng_a, lambda e: e.scalar_tensor_tensor(out=outi, in0=t1.ap, scalar=-1.0, in1=t2.ap, op0=ALU.mult, op1=ALU.subtract),
                  reads=[t1, t2], writes=[w_i])
        else:
            cx.op(eng_a, lambda e: e.tensor_tensor(out=outi, in0=t1.ap, in1=t2.ap, op=ALU.add), reads=[t1, t2], writes=[w_i])

    def phase_s5(l):
        ws = Scope(cx)
        Mw = ws.buf([128, 32, 128], BF16, "Mw")
        W1r = ws.buf([128, 32, 64], BF16, "W1r")
        W1i = ws.buf([128, 32, 64], BF16, "W1i")
        W2r = ws.buf([128, 16, 128], BF16, "W2r")
        W2i = ws.buf([128, 16, 128], BF16, "W2i")
        rho = ws.buf([128, 16], F32, "rho")
        ph_r = ws.buf([128, 16], F32, "ph_r")
        ph_i = ws.buf([128, 16], F32, "ph_i")
        dcol = ws.buf([128, 32], F32, "dcol")
        bglu = ws.buf([128, 4], F32, "bglu")
        cx.dma("sp", bglu.ap, s5_b_glu[l].rearrange("(c p) -> p c", p=128), writes=[bglu], allow_slow_non_contiguous=True)
        for tau in range(8):
            cx.dma("sp", dcol.ap[16 * tau:16 * tau + 16, :], s5_d[l].rearrange("(g c) -> c g", c=16), writes=[dcol],
                   allow_slow_non_contiguous=True)
        sc = Scope(cx)
        N = 64
        araw = sc.buf([32, 64], F32, "araw")
        airaw = sc.buf([32, 64], F32, "airaw")
        cx.dma("sp", araw.ap, s5_a_re[l], writes=[araw])
        cx.dma("sp", airaw.ap, s5_a_im[l], writes=[airaw])
        are = sc.buf([N, 32], F32, "are")
        aim = sc.buf([N, 32], F32, "aim")
        for src, dst in ((araw, are), (airaw, aim)):
            cx.op("pe", lambda e, src=src: e.matmul(PS[0].ap[0:64, 0:32], lhsT=src.ap, rhs=identf.ap[0:32, 0:32], start=True, stop=True),
                  reads=[src, identf], writes=[PS[0]])
            cx.op("act", lambda e, dst=dst: e.activation(out=dst.ap, in_=PS[0].ap[0:64, 0:32], func=AF.Identity), reads=[PS[0]], writes=[dst])
        dt = sc.buf([N, 32], F32, "dt")
        cx.dma("sp", dt.ap, s5_log_dt[l].partition_broadcast(N), writes=[dt])
        Br = sc.buf([N, 32, 16], F32, "Br")
        Bi = sc.buf([N, 32, 16], F32, "Bi")
        cx.dma("sp", Br.ap, s5_b_re[l].rearrange("g n c -> n g c"), writes=[Br])
        cx.dma("sp", Bi.ap, s5_b_im[l].rearrange("g n c -> n g c"), writes=[Bi])
        Cr = sc.buf([N, 32, 16], F32, "Cr")
        Ci = sc.buf([N, 32, 16], F32, "Ci")
        craw = sc.buf([128, 4, 64], F32, "craw")
        for src, dst in ((s5_c_re, Cr), (s5_c_im, Ci)):
            cx.dma("sp", craw.ap, src[l].rearrange("(j p) n -> p j n", p=128), writes=[craw])

            def mm(e):
                for j in range(4):
                    i = e.matmul(PS[1].ap[0:64, j * 128:(j + 1) * 128], lhsT=craw.ap[:, j, :], rhs=identf.ap, start=True, stop=True)
                return i
            cx.op("pe", mm, reads=[craw, identf], writes=[PS[1]])
            cx.op("act", lambda e, dst=dst: e.activation(out=dst.ap.rearrange("n g c -> n (g c)"), in_=PS[1].ap[0:64, :], func=AF.Identity),
                  reads=[PS[1]], writes=[dst])

        def small(name):
            return sc.buf([N, 32], F32, name)
        ar, ang, mag, lbr, lbi = small("ar"), small("ang"), small("mag"), small("lbr"), small("lbi")
        t1, t2, t3 = small("t1"), small("t2"), small("t3")
        V = "dve"

        def tt(out, a, b, op_, eng=V):
            cx.op(eng, lambda e: e.tensor_tensor(out=out.ap, in0=a.ap, in1=b.ap, op=op_), reads=[a, b], writes=[out])

        def tsc(out, a, s1, op0, s2=None, op1=None, eng=V):
            if op1 is None:
                cx.op(eng, lambda e: e.tensor_scalar(out=out.ap, in0=a.ap, scalar1=s1, scalar2=None, op0=op0), reads=[a], writes=[out])
            else:
                cx.op(eng, lambda e: e.tensor_scalar(out=out.ap, in0=a.ap, scalar1=s1, scalar2=s2, op0=op0, op1=op1), reads=[a], writes=[out])

        def act(out, a, fn, scale=1.0, bias=None):
            if bias is None:
                cx.op("act", lambda e: e.activation(out=out.ap, in_=a.ap, func=fn, scale=scale), reads=[a], writes=[out])
            else:
                cx.op("act", lambda e: e.activation(out=out.ap, in_=a.ap, func=fn, scale=scale, bias=bias.ap), reads=[a, bias], writes=[out])

        zero_c = sc.buf([N, 1], F32, "zero_c")
        cx.op("dve", lambda e: e.memset(zero_c.ap, 0.0), writes=[zero_c])

        def sin_of(out, angle_buf, shift):
            tsc(t1, angle_buf, 1.0 / TWO_PI, ALU.mult, (shift / TWO_PI) + MAGIC, ALU.add)
            tsc(t1, t1, -MAGIC, ALU.add)
            cx.op(V, lambda e: e.scalar_tensor_tensor(out=t2.ap, in0=t1.ap, scalar=-TWO_PI, in1=angle_buf.ap, op0=ALU.mult, op1=ALU.add),
                  reads=[t1, angle_buf], writes=[t2])
            tsc(t2, t2, float(shift), ALU.add, math.pi - 1e-6, ALU.min)
            tsc(t2, t2, -(math.pi - 1e-6), ALU.max)
            act(out, t2, AF.Sin, bias=zero_c)

        em1, xr_, w_, sn, cm1, nn = small("em1"), small("xr_"), small("w_"), small("sn"), small("cm1"), small("nn")

        def nested(out, var, divs, sign):
            cx.op(V, lambda e: e.memset(out.ap, 1.0), writes=[out])
            for dv in divs:
                tt(t3, out, var, ALU.mult)
                tsc(out, t3, sign / dv, ALU.mult, 1.0, ALU.add)
        ld8 = small("ld8")
        tsc(ld8, dt, 0.125, ALU.mult)
        nested(dt, ld8, [float(k) for k in range(12, 0, -1)], 1.0)
        for _ in range(3):
            tt(dt, dt, dt, ALU.mult)
        tt(ar, are, dt, ALU.mult)
        tt(ang, aim, dt, ALU.mult)
        nested(nn, ar, [9.0, 8.0, 7.0, 6.0, 5.0, 4.0, 3.0, 2.0], 1.0)
        tt(em1, nn, ar, ALU.mult)
        tsc(mag, em1, 1.0, ALU.add)
        C1 = 6.28125
        C2 = TWO_PI - C1
        tsc(t1, ang, 1.0 / TWO_PI, ALU.mult, MAGIC, ALU.add)
        tsc(t1, t1, -MAGIC, ALU.add)
        cx.op(V, lambda e: e.scalar_tensor_tensor(out=xr_.ap, in0=t1.ap, scalar=-C1, in1=ang.ap, op0=ALU.mult, op1=ALU.add),
              reads=[t1, ang], writes=[xr_])
        cx.op(V, lambda e: e.scalar_tensor_tensor(out=xr_.ap, in0=t1.ap, scalar=-C2, in1=xr_.ap, op0=ALU.mult, op1=ALU.add),
              reads=[t1, xr_], writes=[xr_])
        tt(w_, xr_, xr_, ALU.mult)
        nested(nn, w_, [float((2 * k) * (2 * k + 1)) for k in range(10, 0, -1)], -1.0)
        tt(sn, nn, xr_, ALU.mult)
        nested(nn, w_, [float((2 * k + 1) * (2 * k + 2)) for k in range(10, 0, -1)], -1.0)
        tt(cm1, nn, w_, ALU.mult)
        tsc(cm1, cm1, -0.5, ALU.mult)
        lm1 = small("lm1")
        tt(t1, em1, cm1, ALU.mult)
        tt(t2, em1, cm1, ALU.add)
        tt(lm1, t1, t2, ALU.add)
        tsc(lbr, lm1, 1.0, ALU.add)
        tt(lbi, sn, mag, ALU.mult)
        den, qr, qi = small("den"), small("qr"), small("qi")
        tt(den, are, are, ALU.mult)
        tt(t1, aim, aim, ALU.mult)
        tt(den, den, t1, ALU.add)
        cx.op(V, lambda e: e.reciprocal(out=den.ap, in_=den.ap), reads=[den], writes=[den])
        tt(t1, lm1, are, ALU.mult)
        tt(t2, lbi, aim, ALU.mult)
        tt(qr, t1, t2, ALU.add)
        tt(qr, qr, den, ALU.mult)
        tt(t1, lbi, are, ALU.mult)
        tt(t2, lm1, aim, ALU.mult)
        tt(qi, t1, t2, ALU.subtract)
        tt(qi, qi, den, ALU.mult)
        Bbr = sc.buf([N, 32, 16], F32, "Bbr")
        Bbi = sc.buf([N, 32, 16], F32, "Bbi")
        tb1 = sc.buf([N, 32, 16], F32, "tb1")
        tb2 = sc.buf([N, 32, 16], F32, "tb2")

        def bc3(b):
            return b.ap.unsqueeze(2).to_broadcast([N, 32, 16])
        cmul("dve", "pool", (tb1, tb2), Bbr.ap, Bbi.ap, bc3(qr), bc3(qi), Br.ap, Bi.ap, [qr, qi, Br, Bi], Bbr, Bbi)
        Pr = sc.buf([N, 32, 9], F32, "Pr")
        Pi = sc.buf([N, 32, 9], F32, "Pi")
        Qr = sc.buf([N, 32, 8], F32, "Qr")
        Qi = sc.buf([N, 32, 8], F32, "Qi")
        ibr, ibi, im2 = small("ibr"), small("ibi"), small("im2")
        tt(im2, mag, mag, ALU.mult)
        cx.op(V, lambda e: e.reciprocal(out=im2.ap, in_=im2.ap), reads=[im2], writes=[im2])
        tt(ibr, lbr, im2, ALU.mult)
        tt(ibi, lbi, im2, ALU.mult)
        tsc(ibi, ibi, -1.0, ALU.mult)
        for (Xr, Xi, br_, bi_, n_) in ((Pr, Pi, lbr, lbi, 9), (Qr, Qi, ibr, ibi, 8)):
            cx.op(V, lambda e, Xr=Xr: e.memset(Xr.ap[:, :, 0:1], 1.0), writes=[Xr])
            cx.op(V, lambda e, Xi=Xi: e.memset(Xi.ap[:, :, 0:1], 0.0), writes=[Xi])
            for tau in range(1, n_):
                cmul("dve", "pool", (t1, t2), Xr.ap[:, :, tau], Xi.ap[:, :, tau], Xr.ap[:, :, tau - 1], Xi.ap[:, :, tau - 1],
                     br_.ap, bi_.ap, [Xr, Xi, br_, bi_], Xr, Xi)
        GH = 16
        big = [sc.buf([N, GH, 8, 16], F32, f"big{i}") for i in range(8)]
        Ar_, Ai_, Cmr, Cmin, T1, T2, A7r, A7i = big
        mtmp = [sc.buf([128, 128], F32, f"mtmp{i}") for i in range(2)]
        for gh in range(2):
            gs = slice(gh * GH, (gh + 1) * GH)

            def bq(b):
                return b.ap[:, gs, 0:8].unsqueeze(3).to_broadcast([N, GH, 8, 16])

            def bb(b):
                return b.ap[:, gs, :].unsqueeze(2).to_broadcast([N, GH, 8, 16])

            def b7(b, idx):
                return b.ap[:, gs, idx:idx + 1].unsqueeze(3).to_broadcast([N, GH, 8, 16])
            cmul("dve", "pool", (T1, T2), Ar_.ap, Ai_.ap, bq(Qr), bq(Qi), bb(Bbr), bb(Bbi), [Qr, Qi, Bbr, Bbi], Ar_, Ai_)
            cmul("dve", "pool", (T1, T2), Cmr.ap, Cmin.ap, bq(Pr), bq(Pi), bb(Cr), bb(Ci), [Pr, Pi, Cr, Ci], Cmr, Cmin, negi=True)
            cmul("dve", "pool", (T1, T2), A7r.ap, A7i.ap, b7(Pr, 7), b7(Pi, 7), Ar_.ap, Ai_.ap, [Pr, Pi, Ar_, Ai_], A7r, A7i)
            for src, dst in ((A7r, W1r), (A7i, W1i)):
                for g8 in range(2):
                    ps = PS[g8 % 4]

                    def mm(e, ps=ps, src=src, g8=g8):
                        for gi in range(8):
                            i = e.matmul(ps.ap[:, gi * 64:(gi + 1) * 64], lhsT=src.ap[:, g8 * 8 + gi].rearrange("n t c -> n (t c)"),
                                         rhs=identf.ap[0:64, 0:64], start=True, stop=True)
                        return i
                    cx.op("pe", mm, reads=[src, identf], writes=[ps])
                    g0 = gh * GH + g8 * 8
                    cx.op("act", lambda e, ps=ps, dst=dst, g0=g0: e.activation(
                        out=dst.ap[:, g0:g0 + 8, :], in_=ps.ap.rearrange("p (g n) -> p g n", n=64), func=AF.Identity),
                        reads=[ps], writes=[dst])
            for g4 in range(4):
                ps = PS[g4 % 4]

                def mm(e, ps=ps, g4=g4):
                    for gi in range(4):
                        gl_ = g4 * 4 + gi
                        e.matmul(ps.ap[:, gi * 128:(gi + 1) * 128], lhsT=Ar_.ap[:, gl_].rearrange("n t c -> n (t c)"),
                                 rhs=Cmr.ap[:, gl_].rearrange("n t c -> n (t c)"), start=True, stop=False)
                        i = e.matmul(ps.ap[:, gi * 128:(gi + 1) * 128], lhsT=Ai_.ap[:, gl_].rearrange("n t c -> n (t c)"),
                                     rhs=Cmin.ap[:, gl_].rearrange("n t c -> n (t c)"), start=False, stop=True)
                    return i
                cx.op("pe", mm, reads=[Ar_, Ai_, Cmr, Cmin], writes=[ps])
                for gi in range(4):
                    g = gh * GH + g4 * 4 + gi
                    mt = mtmp[g % 2]
                    cx.op("dve", lambda e, ps=ps, gi=gi, mt=mt: e.tensor_tensor(out=mt.ap, in0=ps.ap[:, gi * 128:(gi + 1) * 128], in1=mask8.ap,
                                                                                op=ALU.mult), reads=[ps, mask8], writes=[mt])
                    cx.op("dve", lambda e, g=g, mt=mt: e.scalar_tensor_tensor(out=Mw.ap[:, g, :], in0=identf.ap, scalar=dcol.ap[:, g:g + 1],
                                                                              in1=mt.ap, op0=ALU.mult, op1=ALU.add),
                          reads=[identf, dcol, mt], writes=[Mw])
            W2r_f, W2i_f = A7r, A7i
            cx.op("dve", lambda e: e.tensor_tensor(out=T1.ap, in0=b7(Pr, 1), in1=Cmr.ap, op=ALU.mult), reads=[Pr, Cmr], writes=[T1])
            cx.op("pool", lambda e: e.tensor_tensor(out=T2.ap, in0=b7(Pi, 1), in1=Cmin.ap, op=ALU.mult), reads=[Pi, Cmin], writes=[T2])
            cx.op("dve", lambda e: e.tensor_tensor(out=W2r_f.ap, in0=T1.ap, in1=T2.ap, op=ALU.add), reads=[T1, T2], writes=[W2r_f])
            cx.op("dve", lambda e: e.tensor_tensor(out=T1.ap, in0=b7(Pr, 1), in1=Cmin.ap, op=ALU.mult), reads=[Pr, Cmin, W2r_f], writes=[T1])
            cx.op("pool", lambda e: e.tensor_tensor(out=T2.ap, in0=b7(Pi, 1), in1=Cmr.ap, op=ALU.mult), reads=[Pi, Cmr, W2r_f], writes=[T2])
            cx.op("dve", lambda e: e.tensor_tensor(out=W2i_f.ap, in0=T1.ap, in1=T2.ap, op=ALU.subtract), reads=[T1, T2], writes=[W2i_f])
            for src, dst in ((W2r_f, W2r), (W2i_f, W2i)):
                v = src.ap.rearrange("n (q two) t c -> n q two (t c)", two=2)
                q0 = gh * (GH // 2)
                cx.op("act", lambda e, v=v, dst=dst, q0=q0: e.activation(out=dst.ap[0:64, q0:q0 + GH // 2, :], in_=v[:, :, 0, :], func=AF.Identity),
                      reads=[src], writes=[dst])
                cx.op("act", lambda e, v=v, dst=dst, q0=q0: e.activation(out=dst.ap[64:128, q0:q0 + GH // 2, :], in_=v[:, :, 1, :], func=AF.Identity),
                      reads=[src], writes=[dst])
        r8, e8, p8r, p8i = small("r8"), small("e8"), small("p8r"), small("p8i")
        tt(r8, mag, mag, ALU.mult)
        tt(r8, r8, r8, ALU.mult)
        tt(r8, r8, r8, ALU.mult)
        cx.op(V, lambda e: e.reciprocal(out=e8.ap, in_=r8.ap), reads=[r8], writes=[e8])
        cx.op(V, lambda e: e.tensor_tensor(out=p8r.ap, in0=Pr.ap[:, :, 8], in1=e8.ap, op=ALU.mult), reads=[Pr, e8], writes=[p8r])
        cx.op(V, lambda e: e.tensor_tensor(out=p8i.ap, in0=Pi.ap[:, :, 8], in1=e8.ap, op=ALU.mult), reads=[Pi, e8], writes=[p8i])
        for src, dst in ((r8, rho), (p8r, ph_r), (p8i, ph_i)):
            v = src.ap.rearrange("n (q two) -> n q two", two=2)
            cx.op("act", lambda e, v=v, dst=dst: e.activation(out=dst.ap[0:64], in_=v[:, :, 0], func=AF.Identity), reads=[src], writes=[dst])
            cx.op("act", lambda e, v=v, dst=dst: e.activation(out=dst.ap[64:128], in_=v[:, :, 1], func=AF.Identity), reads=[src], writes=[dst])
        cx.barrier()
        sc.close()
        sc = Scope(cx)
        gb = sc.buf([128, 4, L], BF16, "gb")
        SelS = sc.buf([128, 8, 8, 128], BF16, "SelS")
        SelTS = sc.buf([128, 8, 8, 128], BF16, "SelTS")
        cx.dma("sp", SelS.ap, SELD[0], reads=[B_SELD], writes=[SelS])
        cx.dma("sp", SelTS.ap, SELD[1], reads=[B_SELD], writes=[SelTS])
        NQ = 4
        for qt in range(4):
            s2 = Scope(cx)
            uT = s2.buf([128, L], BF16, "uT")
            cx.dma("sp", uT.ap, UT[qt * 128:(qt + 1) * 128, :], reads=[B_UT], writes=[uT])
            U3 = s2.buf([128, 8, TB], BF16, "U3")
            E1r = s2.buf([128, NQ, TB], F32, "E1r")
            E1i = s2.buf([128, NQ, TB], F32, "E1i")
            Hr = s2.buf([128, NQ, TB], BF16, "Hr")
            Hi = s2.buf([128, NQ, TB], BF16, "Hi")
            Y3 = s2.buf([128, 8, TB], BF16, "Y3")
            dd = [s2.buf([128, NQ, 256], F32, f"dd{i}") for i in range(4)]
            tmp = [s2.buf([128, TB], F32, f"s5t{i}") for i in range(8)]
            q0 = qt * NQ
            cx.op("dve", lambda e: e.tensor_copy(out=E1r.ap[:, :, 0], in_=ph_r.ap[:, q0:q0 + NQ]), reads=[ph_r], writes=[E1r])
            cx.op("dve", lambda e: e.tensor_copy(out=E1i.ap[:, :, 0], in_=ph_i.ap[:, q0:q0 + NQ]), reads=[ph_i], writes=[E1i])
            m = 1
            while m < TB:
                def bcm(b, m=m):
                    return b.ap[:, :, m - 1:m].to_broadcast([128, NQ, m])
                cx.op("dve", lambda e, m=m: e.tensor_tensor(out=dd[0].ap[:, :, 0:m], in0=E1r.ap[:, :, 0:m], in1=bcm(E1r), op=ALU.mult),
                      reads=[E1r], writes=[dd[0]])
                cx.op("pool", lambda e, m=m: e.tensor_tensor(out=dd[1].ap[:, :, 0:m], in0=E1i.ap[:, :, 0:m], in1=bcm(E1i), op=ALU.mult),
                      reads=[E1i], writes=[dd[1]])
                cx.op("dve", lambda e, m=m: e.tensor_tensor(out=dd[2].ap[:, :, 0:m], in0=E1r.ap[:, :, 0:m], in1=bcm(E1i), op=ALU.mult),
                      reads=[E1r, E1i], writes=[dd[2]])
                cx.op("pool", lambda e, m=m: e.tensor_tensor(out=dd[3].ap[:, :, 0:m], in0=E1i.ap[:, :, 0:m], in1=bcm(E1r), op=ALU.mult),
                      reads=[E1i, E1r], writes=[dd[3]])
                cx.op("dve", lambda e, m=m: e.tensor_tensor(out=E1r.ap[:, :, m:2 * m], in0=dd[0].ap[:, :, 0:m], in1=dd[1].ap[:, :, 0:m], op=ALU.subtract),
                      reads=[dd[0], dd[1]], writes=[E1r])
                cx.op("dve", lambda e, m=m: e.tensor_tensor(out=E1i.ap[:, :, m:2 * m], in0=dd[2].ap[:, :, 0:m], in1=dd[3].ap[:, :, 0:m], op=ALU.add),
                      reads=[dd[2], dd[3]], writes=[E1i])
                m *= 2
            npsu = 0
            for gl_ in range(8):
                ps = PS[npsu % 4]
                npsu += 1

                def mm(e, ps=ps, gl_=gl_):
                    for tau in range(8):
                        i = e.matmul(ps.ap, lhsT=SelS.ap[:, gl_, tau, :], rhs=uT.ap[:, tau::8], start=(tau == 0), stop=(tau == 7))
                    return i
                cx.op("pe", mm, reads=[SelS, uT], writes=[ps])
                cx.op("act", lambda e, ps=ps, gl_=gl_: e.activation(out=U3.ap[:, gl_, :], in_=ps.ap, func=AF.Identity), reads=[ps], writes=[U3])
            for ql in range(NQ):
                q_ = qt * NQ + ql
                ga, gb_ = 2 * ql, 2 * ql + 1
                pre, pim = PS[4 + (ql % 2) * 2], PS[5 + (ql % 2) * 2]

                def mm(e, pre=pre, pim=pim, ga=ga, gb_=gb_, qt=qt):
                    e.matmul(pre.ap[0:64, :], lhsT=W1r.ap[:, qt * 8 + ga, :], rhs=U3.ap[:, ga, :], start=True, stop=True)
                    e.matmul(pre.ap[64:128, :], lhsT=W1r.ap[:, qt * 8 + gb_, :], rhs=U3.ap[:, gb_, :], start=True, stop=True)
                    e.matmul(pim.ap[0:64, :], lhsT=W1i.ap[:, qt * 8 + ga, :], rhs=U3.ap[:, ga, :], start=True, stop=True)
                    return e.matmul(pim.ap[64:128, :], lhsT=W1i.ap[:, qt * 8 + gb_, :], rhs=U3.ap[:, gb_, :], start=True, stop=True)
                cx.op("pe", mm, reads=[W1r, W1i, U3], writes=[pre, pim])
                xr, xi, a1, a2, vr, vi, sr, si = tmp
                cx.op("act", lambda e, pre=pre: e.activation(out=xr.ap, in_=pre.ap, func=AF.Identity), reads=[pre], writes=[xr])
                cx.op("act", lambda e, pim=pim: e.activation(out=xi.ap, in_=pim.ap, func=AF.Identity), reads=[pim], writes=[xi])
                er, ei = E1r.ap[:, ql, :], E1i.ap[:, ql, :]
                cx.op("dve", lambda e, er=er: e.tensor_tensor(out=a1.ap, in0=xr.ap, in1=er, op=ALU.mult), reads=[xr, E1r], writes=[a1])
                cx.op("pool", lambda e, ei=ei: e.tensor_tensor(out=a2.ap, in0=xi.ap, in1=ei, op=ALU.mult), reads=[xi, E1i], writes=[a2])
                cx.op("dve", lambda e: e.tensor_tensor(out=vr.ap, in0=a1.ap, in1=a2.ap, op=ALU.add), reads=[a1, a2], writes=[vr])
                cx.op("dve", lambda e, er=er: e.tensor_tensor(out=a1.ap, in0=xi.ap, in1=er, op=ALU.mult), reads=[xi, E1r], writes=[a1])
                cx.op("pool", lambda e, ei=ei: e.tensor_tensor(out=a2.ap, in0=xr.ap, in1=ei, op=ALU.mult), reads=[xr, E1i], writes=[a2])
                cx.op("dve", lambda e: e.tensor_tensor(out=vi.ap, in0=a1.ap, in1=a2.ap, op=ALU.subtract), reads=[a1, a2], writes=[vi])
                rc = rho.ap[:, q_:q_ + 1].to_broadcast([128, TB])
                cx.op("dve", lambda e, rc=rc: e.tensor_tensor_scan(out=sr.ap, data0=rc, data1=vr.ap, initial=0.0, op0=ALU.mult, op1=ALU.add),
                      reads=[rho, vr], writes=[sr])
                cx.op("dve", lambda e, rc=rc: e.tensor_tensor_scan(out=si.ap, data0=rc, data1=vi.ap, initial=0.0, op0=ALU.mult, op1=ALU.add),
                      reads=[rho, vi], writes=[si])
                cx.op("dve", lambda e, er=er: e.tensor_tensor(out=a1.ap, in0=sr.ap, in1=er, op=ALU.mult), reads=[sr, E1r], writes=[a1])
                cx.op("pool", lambda e, ei=ei: e.tensor_tensor(out=a2.ap, in0=si.ap, in1=ei, op=ALU.mult), reads=[si, E1i], writes=[a2])
                cx.op("pool", lambda e, ql=ql: e.memset(Hr.ap[:, ql, 0:1], 0.0), writes=[Hr])
                cx.op("pool", lambda e, ql=ql: e.memset(Hi.ap[:, ql, 0:1], 0.0), writes=[Hi])
                cx.op("dve", lambda e, ql=ql: e.tensor_tensor(out=Hr.ap[:, ql, 1:TB], in0=a1.ap[:, 0:TB - 1], in1=a2.ap[:, 0:TB - 1], op=ALU.subtract),
                      reads=[a1, a2], writes=[Hr])
                cx.op("dve", lambda e, ei=ei: e.tensor_tensor(out=a1.ap, in0=sr.ap, in1=ei, op=ALU.mult), reads=[sr, E1i], writes=[a1])
                cx.op("pool", lambda e, er=er: e.tensor_tensor(out=a2.ap, in0=si.ap, in1=er, op=ALU.mult), reads=[si, E1r], writes=[a2])
                cx.op("dve", lambda e, ql=ql: e.tensor_tensor(out=Hi.ap[:, ql, 1:TB], in0=a1.ap[:, 0:TB - 1], in1=a2.ap[:, 0:TB - 1], op=ALU.add),
                      reads=[a1, a2], writes=[Hi])
            for gl_ in range(8):
                g = qt * 8 + gl_
                ql, half = gl_ // 2, gl_ % 2
                q_ = qt * NQ + ql
                ps = PS[npsu % 4]
                npsu += 1
                lo, hi = half * 64, half * 64 + 64

                def mm(e, ps=ps, g=g, gl_=gl_, ql=ql, q_=q_, lo=lo, hi=hi):
                    e.matmul(ps.ap, lhsT=Mw.ap[:, g, :], rhs=U3.ap[:, gl_, :], start=True, stop=False)
                    e.matmul(ps.ap, lhsT=W2r.ap[lo:hi, q_, :], rhs=Hr.ap[lo:hi, ql, :], start=False, stop=False)
                    return e.matmul(ps.ap, lhsT=W2i.ap[lo:hi, q_, :], rhs=Hi.ap[lo:hi, ql, :], start=False, stop=True)
                cx.op("pe", mm, reads=[Mw, U3, W2r, W2i, Hr, Hi], writes=[ps])
                cx.op("act", lambda e, ps=ps, gl_=gl_: e.activation(out=Y3.ap[:, gl_, :], in_=ps.ap, func=AF.Identity), reads=[ps], writes=[Y3])
            for tau in range(8):
                ps = PS[npsu % 4]
                npsu += 1

                def mm(e, ps=ps, tau=tau):
                    for gg in range(8):
                        i = e.matmul(ps.ap, lhsT=SelTS.ap[:, gg, tau, :], rhs=Y3.ap[:, gg, :], start=(gg == 0), stop=(gg == 7))
                    return i
                cx.op("pe", mm, reads=[SelTS, Y3], writes=[ps])
                cx.op("act", lambda e, ps=ps, qt=qt, tau=tau: e.activation(out=gb.ap[:, qt, tau::8], in_=ps.ap, func=AF.Gelu_apprx_tanh),
                      reads=[ps], writes=[gb])
            cx.barrier()
            s2.close()
        wgl = sc.buf([128, 4, 512], BF16, "wgl")
        cx.dma("sp", wgl.ap, wb_glu[l].rearrange("(kc p) n -> p kc n", p=128), reads=[B_wb[("glu", l)]], writes=[wgl])
        sg = [sc.buf([128, TB], F32, f"sg{i}") for i in range(2)]
        yst = [sc.buf([128, 4, TB], BF16, f"yst{i}") for i in range(2)]
        n = 0
        for t in range(NB):
            ys_ = yst[t % 2]
            for ct in range(4):
                ps = PS[n % 4]
                s_ = sg[n % 2]
                n += 1

                def mm(e, ps=ps, ct=ct, t=t):
                    for kc in range(4):
                        i = e.matmul(ps.ap, lhsT=wgl.ap[:, kc, ct * 128:(ct + 1) * 128], rhs=gb.ap[:, kc, blk(t)], start=(kc == 0), stop=(kc == 3))
                    return i
                cx.op("pe", mm, reads=[wgl, gb], writes=[ps])
                cx.op("act", lambda e, ps=ps, s_=s_, ct=ct: e.activation(out=s_.ap, in_=ps.ap, func=AF.Sigmoid, bias=bglu.ap[:, ct:ct + 1], scale=1.0),
                      reads=[ps, bglu], writes=[s_])
                cx.op("dve", lambda e, s_=s_, ys_=ys_, ct=ct, t=t: e.tensor_tensor(out=ys_.ap[:, ct, :], in0=gb.ap[:, ct, blk(t)], in1=s_.ap, op=ALU.mult),
                      reads=[gb, s_], writes=[ys_])
            cx.dma("sp", YS[0:512, blk(t)].rearrange("(c p) t -> p c t", p=128), ys_.ap, reads=[ys_], writes=[B_YS[0]])
        cx.barrier()
        sc.close()
        ws.close()

    def layer_norm(y, gcol, bcol, outb, tmp, stat):
        pm, pq = PS[6], PS[7]
        cx.op("act", lambda e: e.activation(out=tmp.ap, in_=y.ap, func=AF.Square), reads=[y], writes=[tmp])

        def mm1(e):
            for kc in range(8):
                i = e.matmul(pm.ap, lhsT=onesD.ap, rhs=y.ap[:, kc, :], start=(kc == 0), stop=(kc == 7))
            return i

        def mm2(e):
            for kc in range(8):
                i = e.matmul(pq.ap, lhsT=onesD.ap, rhs=tmp.ap[:, kc, :], start=(kc == 0), stop=(kc == 7))
            return i
        cx.op("pe", mm1, reads=[onesD, y], writes=[pm])
        cx.op("pe", mm2, reads=[onesD, tmp], writes=[pq])
        mean, rstd = stat
        cx.op("act", lambda e: e.activation(out=mean.ap, in_=pm.ap, func=AF.Identity), reads=[pm], writes=[mean])
        cx.op("act", lambda e: e.activation(out=rstd.ap, in_=pm.ap, func=AF.Square), reads=[pm], writes=[rstd])
        cx.op("dve", lambda e: e.tensor_tensor(out=rstd.ap, in0=pq.ap, in1=rstd.ap, op=ALU.subtract), reads=[pq, rstd], writes=[rstd])
        cx.op("dve", lambda e: e.tensor_scalar(out=rstd.ap, in0=rstd.ap, scalar1=0.0, scalar2=LN_EPS, op0=ALU.max, op1=ALU.add),
              reads=[rstd], writes=[rstd])
        cx.op("act", lambda e: e.activation(out=rstd.ap, in_=rstd.ap, func=AF.Sqrt), reads=[rstd], writes=[rstd])
        cx.op("dve", lambda e: e.reciprocal(out=rstd.ap, in_=rstd.ap), reads=[rstd], writes=[rstd])
        mb = mean.ap.unsqueeze(1).to_broadcast([128, 8, TB])
        rb_ = rstd.ap.unsqueeze(1).to_broadcast([128, 8, TB])
        cx.op("dve", lambda e: e.tensor_tensor(out=tmp.ap, in0=y.ap, in1=mb, op=ALU.subtract), reads=[y, mean], writes=[tmp])
        cx.op("pool", lambda e: e.tensor_tensor(out=tmp.ap, in0=tmp.ap, in1=rb_, op=ALU.mult), reads=[tmp, rstd], writes=[tmp])
        for kc in range(8):
            cx.op("dve", lambda e, kc=kc: e.tensor_scalar(out=y.ap[:, kc, :], in0=tmp.ap[:, kc, :], scalar1=gcol.ap[:, kc:kc + 1],
                                                          scalar2=bcol.ap[:, kc:kc + 1], op0=ALU.mult, op1=ALU.add),
                  reads=[tmp, gcol, bcol], writes=[y])
        cx.op("act", lambda e: e.activation(out=outb.ap, in_=y.ap, func=AF.Identity), reads=[y], writes=[outb])

    def load_cols(sc, src, l, name, n=8):
        b = sc.buf([128, n], F32, name)
        cx.dma("sp", b.ap, src[l].rearrange("(c p) -> p c", p=128), writes=[b], allow_slow_non_contiguous=True)
        return b

    def phase_mix(l):
        sc = Scope(cx)
        wbr = sc.buf([128, 12, D], BF16, "wbr")
        cx.dma("sp", wbr.ap, wb_branch[l].rearrange("(j p) n -> p j n", p=128), reads=[B_wb[("branch", l)]], writes=[wbr])
        wgt = sc.buf([128, 8, 3072], BF16, "wgt")
        for k3 in range(3):
            cx.dma("sp", wgt.ap[:, :, k3 * 1024:(k3 + 1) * 1024], kview(wb_in[l])[:, :, 3080 + k3 * 1024:3080 + (k3 + 1) * 1024],
                   reads=[B_wb[("in", l)]], writes=[wgt])
        wo = sc.buf([128, 8, D], BF16, "wo")
        cx.dma("sp", wo.ap, kview(wb_out[l]), reads=[B_wb[("out", l)]], writes=[wo])
        bg = load_cols(sc, b_gate, l, "bg", 24)
        g1 = load_cols(sc, ln1_g, l, "g1")
        b1 = load_cols(sc, ln1_b, l, "b1")
        xb = sc.buf([128, 8, TB], BF16, "mxb")
        xr = [sc.buf([128, TB], F32, f"mxr{i}") for i in range(2)]
        ys = sc.buf([128, 12, TB], BF16, "mys")
        mixb = sc.buf([128, 8, TB], BF16, "mixb")
        yv = sc.buf([128, 8, TB], F32, "yv")
        tmp = sc.buf([128, 8, TB], F32, "lntmp")
        stat = (sc.buf([128, TB], F32, "mean"), sc.buf([128, TB], F32, "rstd"))
        gsb = [sc.buf([128, TB], F32, f"gsb{i}") for i in range(3)]
        acc = [sc.buf([128, TB], F32, f"acc{i}") for i in range(2)]
        xres_src = xT if l == 0 else XRES
        n = 0
        nr = 0
        for t in range(NB):
            x_, y_ = xb, ys
            cx.dma("sp", x_.ap, kview(XBF)[:, :, blk(t)], reads=[B_XBF[t]], writes=[x_])
            cx.dma("sp", y_.ap, YS.rearrange("(j p) t -> p j t", p=128)[:, :, blk(t)], reads=B_YS, writes=[y_])
            for dc in range(8):
                a_ = acc[dc % 2]
                for k3 in range(3):
                    pp, pg = PS[(2 * n) % 6], PS[(2 * n + 1) % 6]
                    g_ = gsb[n % 3]
                    n += 1

                    def mmp(e, pp=pp, k3=k3, dc=dc, y_=y_):
                        for kc in range(4):
                            i = e.matmul(pp.ap, lhsT=wbr.ap[:, k3 * 4 + kc, dc * 128:(dc + 1) * 128], rhs=y_.ap[:, k3 * 4 + kc, :],
                                         start=(kc == 0), stop=(kc == 3))
                        return i

                    def mmg(e, pg=pg, k3=k3, dc=dc, x_=x_):
                        c0 = k3 * 1024 + dc * 128
                        for kc in range(8):
                            i = e.matmul(pg.ap, lhsT=wgt.ap[:, kc, c0:c0 + 128], rhs=x_.ap[:, kc, :], start=(kc == 0), stop=(kc == 7))
                        return i
                    cx.op("pe", mmg, reads=[wgt, x_], writes=[pg])
                    cx.op("pe", mmp, reads=[wbr, y_], writes=[pp])
                    cx.op("act", lambda e, pg=pg, g_=g_, k3=k3, dc=dc: e.activation(out=g_.ap, in_=pg.ap, func=AF.Sigmoid,
                                                                                   bias=bg.ap[:, k3 * 8 + dc:k3 * 8 + dc + 1], scale=1.0),
                          reads=[pg, bg], writes=[g_])
                    if k3 == 0:
                        cx.op("dve", lambda e, pp=pp, g_=g_, a_=a_: e.tensor_tensor(out=a_.ap, in0=pp.ap, in1=g_.ap, op=ALU.mult),
                              reads=[pp, g_], writes=[a_])
                    else:
                        cx.op("dve", lambda e, pp=pp, g_=g_: e.tensor_tensor(out=g_.ap, in0=pp.ap, in1=g_.ap, op=ALU.mult),
                              reads=[pp, g_], writes=[g_])
                        if k3 == 1:
                            cx.op("pool", lambda e, g_=g_, a_=a_: e.tensor_tensor(out=a_.ap, in0=a_.ap, in1=g_.ap, op=ALU.add),
                                  reads=[a_, g_], writes=[a_])
                        else:
                            cx.op("pool", lambda e, g_=g_, a_=a_, dc=dc: e.tensor_tensor(out=mixb.ap[:, dc, :], in0=a_.ap, in1=g_.ap, op=ALU.add),
                                  reads=[a_, g_], writes=[mixb])
            for dc in range(8):
                po = PS[6 + dc % 2]
                r_ = xr[nr % 2]
                nr += 1
                cx.dma("sp", r_.ap, xres_src[dc * 128:(dc + 1) * 128, blk(t)], reads=[B_XRES[t]], writes=[r_])

                def mmo(e, po=po, dc=dc):
                    for kc in range(8):
                        i = e.matmul(po.ap, lhsT=wo.ap[:, kc, dc * 128:(dc + 1) * 128], rhs=mixb.ap[:, kc, :], start=(kc == 0), stop=(kc == 7))
                    return i
                cx.op("pe", mmo, reads=[wo, mixb], writes=[po])
                cx.op("dve", lambda e, po=po, dc=dc, r_=r_: e.scalar_tensor_tensor(out=yv.ap[:, dc, :], in0=r_.ap, scalar=float(ALPHA),
                                                                                 in1=po.ap, op0=ALU.mult, op1=ALU.add),
                      reads=[r_, po], writes=[yv])
            layer_norm(yv, g1, b1, mixb, tmp, stat)
            cx.dma("sp", kview(X1RES)[:, :, blk(t)], yv.ap, reads=[yv], writes=[B_X1RES[t]])
            cx.dma("sp", kview(X1BF)[:, :, blk(t)], mixb.ap, reads=[mixb], writes=[B_X1BF[t]])
        cx.barrier()
        sc.close()

    def phase_ffn(l, last):
        sc = Scope(cx)
        wdn = sc.buf([128, NHC, D], BF16, "wdn")
        cx.dma("sp", wdn.ap, wb_d[l].rearrange("(j p) n -> p j n", p=128), reads=[B_wb[("d", l)]], writes=[wdn])
        g2 = load_cols(sc, ln2_g, l, "g2")
        b2 = load_cols(sc, ln2_b, l, "b2")
        xb = sc.buf([128, 8, TB], BF16, "fxb")
        hT = sc.buf([128, NHC, TB], BF16, "hT")
        wgu = [sc.buf([128, 2, 8, 256], BF16, f"wgu{i}") for i in range(2)]
        sl = [sc.buf([128, TB], F32, f"sl{i}") for i in range(2)]
        xr = [sc.buf([128, TB], F32, f"fxr{i}") for i in range(2)]
        yv = sc.buf([128, 8, TB], F32, "fyv")
        tmp = sc.buf([128, 8, TB], F32, "flntmp")
        o16 = sc.buf([128, 8, TB], BF16, "fo16")
        stat = (sc.buf([128, TB], F32, "fmean"), sc.buf([128, TB], F32, "frstd"))
        n = 0
        nw = 0
        nr = 0
        for t in range(NB):
            cx.dma("sp", xb.ap, kview(X1BF)[:, :, blk(t)], reads=[B_X1BF[t]], writes=[xb])
            for hp in range(NHC // 2):
                w = wgu[nw % 2]
                nw += 1
                cx.dma("sp", w.ap[:, 0], kview(wb_g[l])[:, :, hp * 256:(hp + 1) * 256], reads=[B_wb[("g", l)]], writes=[w])
                cx.dma("sp", w.ap[:, 1], kview(wb_u[l])[:, :, hp * 256:(hp + 1) * 256], reads=[B_wb[("u", l)]], writes=[w])
                for hh in range(2):
                    hc = hp * 2 + hh
                    pg, pu = PS[(2 * n) % 6], PS[(2 * n + 1) % 6]
                    s_ = sl[n % 2]
                    n += 1

                    def mmg(e, pg=pg, w=w, hh=hh):
                        for kc in range(8):
                            i = e.matmul(pg.ap, lhsT=w.ap[:, 0, kc, hh * 128:(hh + 1) * 128], rhs=xb.ap[:, kc, :], start=(kc == 0), stop=(kc == 7))
                        return i

                    def mmu(e, pu=pu, w=w, hh=hh):
                        for kc in range(8):
                            i = e.matmul(pu.ap, lhsT=w.ap[:, 1, kc, hh * 128:(hh + 1) * 128], rhs=xb.ap[:, kc, :], start=(kc == 0), stop=(kc == 7))
                        return i
                    cx.op("pe", mmg, reads=[w, xb], writes=[pg])
                    cx.op("pe", mmu, reads=[w, xb], writes=[pu])
                    cx.op("act", lambda e, pg=pg, s_=s_: e.activation(out=s_.ap, in_=pg.ap, func=AF.Silu), reads=[pg], writes=[s_])
                    cx.op("dve", lambda e, pu=pu, s_=s_, hc=hc: e.tensor_tensor(out=hT.ap[:, hc, :], in0=pu.ap, in1=s_.ap, op=ALU.mult),
                          reads=[pu, s_], writes=[hT])
            for dc in range(8):
                po = PS[6 + dc % 2]
                r_ = xr[nr % 2]
                nr += 1
                cx.dma("sp", r_.ap, X1RES[dc * 128:(dc + 1) * 128, blk(t)], reads=[B_X1RES[t]], writes=[r_])

                def mmo(e, po=po, dc=dc):
                    for hc in range(NHC):
                        i = e.matmul(po.ap, lhsT=wdn.ap[:, hc, dc * 128:(dc + 1) * 128], rhs=hT.ap[:, hc, :], start=(hc == 0), stop=(hc == NHC - 1))
                    return i
                cx.op("pe", mmo, reads=[wdn, hT], writes=[po])
                cx.op("dve", lambda e, po=po, dc=dc, r_=r_: e.scalar_tensor_tensor(out=yv.ap[:, dc, :], in0=r_.ap, scalar=float(ALPHA),
                                                                                 in1=po.ap, op0=ALU.mult, op1=ALU.add),
                      reads=[r_, po], writes=[yv])
            layer_norm(yv, g2, b2, o16, tmp, stat)
            if last:
                cx.dma("sp", kview(outT)[:, :, blk(t)], yv.ap, reads=[yv], writes=[B_OUT])
            else:
                cx.dma("sp", kview(XRES)[:, :, blk(t)], yv.ap, reads=[yv], writes=[B_XRES[t]])
                cx.dma("sp", kview(XBF)[:, :, blk(t)], o16.ap, reads=[o16], writes=[B_XBF[t]])
        cx.barrier()
        sc.close()

    cx.barrier()
    for l in range(n_layers):
        phase_proj(l)
        if stop_after == ("proj", l):
            break
        phase_attn(l)
        if stop_after == ("attn", l):
            break
        phase_lru(l)
        if stop_after == ("lru", l):
            break
        phase_s5(l)
        if stop_after == ("s5", l):
            break
        phase_mix(l)
        if stop_after == ("mix", l):
            break
        phase_ffn(l, last=(l == n_layers - 1))
    cx.barrier()
    return nc


INPUT_ORDER = ["w_in", "w_branch", "w_out", "w_ffn_gate", "w_ffn_up", "w_ffn_down", "s5_w_glu", "lru_w_a", "lru_w_x",
               "b_f", "b_gate", "s5_a_re", "s5_a_im", "s5_log_dt", "s5_b_re", "s5_b_im", "s5_c_re", "s5_c_im", "s5_d",
               "s5_b_glu", "lru_conv_w", "lru_conv_b", "lru_b_a", "lru_b_x", "lru_lambda", "ln1_g", "ln1_b", "ln2_g", "ln2_b"]


def layout_inputs(inputs, n_layers=DEPTH):
    f = lambda a: np.ascontiguousarray(np.asarray(a, dtype=np.float32)[:n_layers])
    shared = {}
    for k in INPUT_ORDER:
        a = f(inputs[k])
        if k == "w_branch":
            a = a.reshape(n_layers, 1536, D)
        elif k in ("s5_c_re", "s5_c_im"):
            a = a.reshape(n_layers, 512, 64)
        elif k in ("lru_b_a", "lru_b_x"):
            a = a.reshape(n_layers, 512)
        shared[k] = np.ascontiguousarray(a)
    return shared


def kernel(**inputs):
    x = np.asarray(inputs["x"], dtype=np.float32)
    shared = layout_inputs(inputs)
    nc = bass.Bass("TRN2", target_bir_lowering=False)
    build(nc)
    in_maps = []
    for c in range(8):
        m = dict(shared)
        m["xT"] = np.ascontiguousarray(x[c % 4].T)
        in_maps.append(m)
    res = run_bass_kernel_spmd(nc, in_maps, core_ids=list(range(8)))
    out = np.stack([np.ascontiguousarray(res.results[b]["outT"].T) for b in range(4)], axis=0)
    return out.astype(np.float32)
```

```python
import math
import numpy as np
import concourse.bass as bass
import concourse.mybir as mybir
from concourse.bass_utils import run_bass_kernel_spmd

F32 = mybir.dt.float32
BF16 = mybir.dt.bfloat16
AF = mybir.ActivationFunctionType
ALU = mybir.AluOpType

D = 1024
L = 4096
DEPTH = 4
NB = 8
TB = 512
IN_TOTAL = 6152
FFN = 2816
NHC = 22
ALPHA = (2.0 * DEPTH) ** 0.25
LN_EPS = 1e-5
MAGIC = 12582912.0
TWO_PI = 2.0 * math.pi


class Buf:
    __slots__ = ("ap", "w", "r", "name")

    def __init__(self, ap, name=""):
        self.ap = ap
        self.w = {}
        self.r = {}
        self.name = name


class Ctx:
    def __init__(self, nc):
        self.nc = nc
        self.E = {"pe": nc.tensor, "act": nc.scalar, "dve": nc.vector, "pool": nc.gpsimd, "sp": nc.sync}
        self.sem = {}
        self.cnt = {}
        self.nsem = 0
        for e in ("pe", "act", "dve", "pool"):
            self._new_sem(e)
        self.seen = {e: {} for e in self.E}
        self.dma_sems = {"sp": [nc.alloc_semaphore(f"dq{i}") for i in range(60)],
                         "pool": [nc.alloc_semaphore(f"dqs{i}") for i in range(16)]}
        self.dma_cnt = {k: [0] * len(v) for k, v in self.dma_sems.items()}
        self.dma_rr = {"sp": 0, "pool": 0}
        self.semobj = {}
        self.uid = 0

    def _new_sem(self, e):
        s = self.nc.alloc_semaphore(f"s_{e}_{self.nsem}")
        self.nsem += 1
        self.sem[e] = s
        self.cnt[e] = 0

    def _key(self, s):
        k = id(s)
        self.semobj[k] = s
        return k

    def _wait(self, e, deps):
        seen = self.seen[e]
        for k, v in deps.items():
            if seen.get(k, 0) >= v:
                continue
            self.E[e].wait_ge(self.semobj[k], v)
            seen[k] = v

    @staticmethod
    def _merge(dst, src):
        for k, v in src.items():
            if dst.get(k, 0) < v:
                dst[k] = v

    def _deps(self, reads, writes):
        deps = {}
        for b in reads:
            self._merge(deps, b.w)
        for b in writes:
            self._merge(deps, b.w)
            self._merge(deps, b.r)
        return deps

    def _commit(self, tok, reads, writes):
        for b in reads:
            self._merge(b.r, tok)
        for b in writes:
            b.w = dict(tok)
            b.r = {}

    def op(self, e, emit, reads=(), writes=()):
        self._wait(e, self._deps(reads, writes))
        ins = emit(self.E[e])
        if self.cnt[e] >= 30000:
            self._new_sem(e)
        s = self.sem[e]
        self.cnt[e] += 1
        ins.then_inc(s, 1)
        tok = {self._key(s): self.cnt[e]}
        self._commit(tok, reads, writes)
        return tok

    def dma(self, e, out, in_, reads=(), writes=(), **kw):
        self._wait(e, self._deps(reads, writes))
        sems, cnts = self.dma_sems[e], self.dma_cnt[e]
        i = self.dma_rr[e]
        self.dma_rr[e] = (i + 1) % len(sems)
        if cnts[i] >= 30000:
            sems[i] = self.nc.alloc_semaphore(f"dqx{self.nsem}")
            self.nsem += 1
            cnts[i] = 0
        s = sems[i]
        cnts[i] += 16
        self.E[e].dma_start(out=out, in_=in_, **kw).then_inc(s, 16)
        tok = {self._key(s): cnts[i]}
        self._commit(tok, reads, writes)
        return tok

    def barrier(self, skip_sw=False):
        allt = {}
        for e in ("pe", "act", "dve", "pool"):
            if self.cnt[e] > 0:
                allt[self._key(self.sem[e])] = self.cnt[e]
        for q in self.dma_sems:
            for i, s in enumerate(self.dma_sems[q]):
                if self.dma_cnt[q][i] > 0 and not (q == "pool" and skip_sw):
                    allt[self._key(s)] = self.dma_cnt[q][i]
        for e in self.E:
            self._wait(e, allt)


class Scope:
    def __init__(self, cx):
        self.cx = cx
        self.guards = []

    def sb(self, shape, dt=F32, name=None):
        self.cx.uid += 1
        g = self.cx.nc.sbuf_tensor(f"{name or 't'}_{self.cx.uid}", list(shape), dt)
        t = g.__enter__()
        self.guards.append(g)
        return t.ap()

    def buf(self, shape, dt=F32, name=None):
        return Buf(self.sb(shape, dt, name), name or "")

    def close(self):
        for g in reversed(self.guards):
            g.__exit__(None, None, None)
        self.guards = []


def build(nc, n_layers=DEPTH, dbg=False, stop_after=None):
    cx = Ctx(nc)
    kind_dbg = "ExternalOutput" if dbg else "Internal"

    def din(name, shape):
        return nc.dram_tensor(name, list(shape), F32, kind="ExternalInput").ap()

    def dscr(name, shape, dt, k="Internal"):
        return nc.dram_tensor(name, list(shape), dt, kind=k).ap()

    xT = din("xT", [D, L])
    w_in = din("w_in", [n_layers, D, IN_TOTAL])
    w_branch = din("w_branch", [n_layers, 1536, D])
    w_out = din("w_out", [n_layers, D, D])
    w_g = din("w_ffn_gate", [n_layers, D, FFN])
    w_u = din("w_ffn_up", [n_layers, D, FFN])
    w_d = din("w_ffn_down", [n_layers, FFN, D])
    w_glu = din("s5_w_glu", [n_layers, 512, 512])
    lru_w_a = din("lru_w_a", [n_layers, 8, 64, 64])
    lru_w_x = din("lru_w_x", [n_layers, 8, 64, 64])
    b_f = din("b_f", [n_layers, 8])
    b_gate = din("b_gate", [n_layers, 3072])
    s5_a_re = din("s5_a_re", [n_layers, 32, 64])
    s5_a_im = din("s5_a_im", [n_layers, 32, 64])
    s5_log_dt = din("s5_log_dt", [n_layers, 32])
    s5_b_re = din("s5_b_re", [n_layers, 32, 64, 16])
    s5_b_im = din("s5_b_im", [n_layers, 32, 64, 16])
    s5_c_re = din("s5_c_re", [n_layers, 512, 64])
    s5_c_im = din("s5_c_im", [n_layers, 512, 64])
    s5_d = din("s5_d", [n_layers, 512])
    s5_b_glu = din("s5_b_glu", [n_layers, 512])
    lru_conv_w = din("lru_conv_w", [n_layers, 4, 512])
    lru_conv_b = din("lru_conv_b", [n_layers, 512])
    lru_b_a = din("lru_b_a", [n_layers, 512])
    lru_b_x = din("lru_b_x", [n_layers, 512])
    lru_lambda = din("lru_lambda", [n_layers, 512])
    ln1_g = din("ln1_g", [n_layers, D])
    ln1_b = din("ln1_b", [n_layers, D])
    ln2_g = din("ln2_g", [n_layers, D])
    ln2_b = din("ln2_b", [n_layers, D])
    outT = nc.dram_tensor("outT", [D, L], F32, kind="ExternalOutput").ap()

    wb_in = dscr("wb_in", [n_layers, D, IN_TOTAL], BF16)
    wb_branch = dscr("wb_branch", [n_layers, 1536, D], BF16)
    wb_out = dscr("wb_out", [n_layers, D, D], BF16)
    wb_g = dscr("wb_g", [n_layers, D, FFN], BF16)
    wb_u = dscr("wb_u", [n_layers, D, FFN], BF16)
    wb_d = dscr("wb_d", [n_layers, FFN, D], BF16)
    wb_glu = dscr("wb_glu", [n_layers, 512, 512], BF16)
    XBF = dscr("XBF", [D, L], BF16)
    XRES = dscr("XRES", [D, L], F32, kind_dbg)
    X1BF = dscr("X1BF", [D, L], BF16)
    X1RES = dscr("X1RES", [D, L], F32, kind_dbg)
    UT = dscr("UT", [512, L], BF16, kind_dbg)
    XL = dscr("XL", [512, L], F32, kind_dbg)
    GL = dscr("GL", [512, L], F32, kind_dbg)
    QA = dscr("QA", [8, 70, L], BF16, kind_dbg)
    KA = dscr("KA", [8, 70, L], BF16, kind_dbg)
    VA = dscr("VA", [8, 128, 32, 65], BF16, kind_dbg)
    YS = dscr("YS", [1536, L], BF16, kind_dbg)

    B_wb = {}
    for nm in ("in", "branch", "out", "g", "u", "d", "glu"):
        for l in range(n_layers):
            B_wb[(nm, l)] = []
    B_XBF = [Buf(None, f"XBF{t}") for t in range(NB)]
    B_XRES = [Buf(None, f"XRES{t}") for t in range(NB)]
    B_X1BF = [Buf(None, f"X1BF{t}") for t in range(NB)]
    B_X1RES = [Buf(None, f"X1RES{t}") for t in range(NB)]
    B_UT = Buf(None, "UT")
    B_XL = Buf(None, "XL")
    B_GL = Buf(None, "GL")
    B_QA = Buf(None, "QA")
    B_KA = Buf(None, "KA")
    B_VA = Buf(None, "VA")
    B_YS = [Buf(None, f"YS{k}") for k in range(3)]
    B_OUT = Buf(None, "out")

    PS = [Buf(nc.alloc_psum_tensor(f"psb{i}", [128, 512], F32).ap(), f"ps{i}") for i in range(8)]

    cs = Scope(cx)
    identf = cs.buf([128, 128], F32, "identf")
    identb = cs.buf([128, 128], BF16, "identb")
    onesD = cs.buf([128, 128], F32, "onesD")
    ones1 = cs.buf([128, 64], F32, "ones1")
    mask8 = cs.buf([128, 128], F32, "mask8")
    tri = cs.buf([128, 128], BF16, "tri")
    SELD = dscr("SELD", [2, 128, 8, 8, 128], BF16)
    B_SELD = Buf(None, "SELD")
    cs0 = Scope(cx)
    Sel = cs0.buf([128, 8, 8, 128], BF16, "Sel")
    SelT = cs0.buf([128, 8, 8, 128], BF16, "SelT")

    def pool_fill(buf, val):
        cx.op("pool", lambda e: e.memset(buf.ap, val), writes=[buf])

    def pool_sel(buf, ap, pattern, cmp, base, cm):
        cx.op("pool", lambda e: e.affine_select(out=ap, in_=ap, pattern=pattern, compare_op=cmp, fill=0.0,
                                                base=base, channel_multiplier=cm), reads=[buf], writes=[buf])

    pool_fill(identf, 1.0)
    pool_sel(identf, identf.ap, [[1, 128]], ALU.is_equal, 0, -1)
    pool_fill(identb, 1.0)
    pool_sel(identb, identb.ap, [[1, 128]], ALU.is_equal, 0, -1)
    pool_fill(onesD, 1.0 / D)
    pool_fill(ones1, 1.0)
    pool_fill(mask8, 1.0)
    pool_sel(mask8, mask8.ap.rearrange("p (t c) -> p t c", c=16), [[16, 8], [0, 16]], ALU.is_ge, 15, -1)
    pool_fill(tri, 1.0)
    pool_sel(tri, tri.ap, [[1, 128]], ALU.is_ge, 0, -1)
    pool_fill(Sel, 1.0)
    for gg in range(8):
        a4 = Sel.ap[:, gg, :, :].rearrange("p t (u c) -> p t u c", c=16)
        pool_sel(Sel, a4, [[0, 8], [0, 8], [-1, 16]], ALU.is_equal, -16 * gg, 1)
        pool_sel(Sel, a4, [[-1, 8], [1, 8], [0, 16]], ALU.is_equal, 0, 0)
    pool_fill(SelT, 1.0)
    for gg in range(8):
        a3 = SelT.ap[:, gg, :, :]
        pool_sel(SelT, a3, [[16, 8], [1, 128]], ALU.is_equal, -16 * gg, -1)
        pool_sel(SelT, a3, [[-16, 8], [0, 128]], ALU.is_ge, 0, 1)
        pool_sel(SelT, a3, [[16, 8], [0, 128]], ALU.is_ge, 15, -1)

    cx.dma("sp", SELD[0], Sel.ap, reads=[Sel], writes=[B_SELD])
    cx.dma("sp", SELD[1], SelT.ap, reads=[SelT], writes=[B_SELD])
    cx.barrier()
    cs0.close()

    def convert(src, dst, rows, key):
        r = 0
        while r < rows:
            n = min(128, rows - r)
            bch = Buf(None, "wbch")
            B_wb[key].append(bch)
            cx.dma("pool", dst[r:r + n, :], src[r:r + n, :], writes=[bch])
            r += n

    def convert_layer(l):
        convert(w_in[l], wb_in[l], D, ("in", l))
        convert(w_glu[l], wb_glu[l], 512, ("glu", l))
        convert(w_branch[l], wb_branch[l], 1536, ("branch", l))
        convert(w_out[l], wb_out[l], D, ("out", l))
        convert(w_g[l], wb_g[l], D, ("g", l))
        convert(w_u[l], wb_u[l], D, ("u", l))
        convert(w_d[l], wb_d[l], FFN, ("d", l))

    for t in range(NB):
        for kc in range(8):
            cx.dma("pool", XBF[kc * 128:(kc + 1) * 128, t * TB:(t + 1) * TB],
                   xT[kc * 128:(kc + 1) * 128, t * TB:(t + 1) * TB], writes=[B_XBF[t]])

    convert_layer(0)

    def kview(ap2d):
        return ap2d.rearrange("(kc p) n -> p kc n", p=128)

    def blk(t):
        return slice(t * TB, (t + 1) * TB)

    def phase_proj(l):
        sc_fg = Scope(cx)
        fgT = sc_fg.buf([8, L], F32, "fgT")
        sc = Scope(cx)
        xb = [sc.buf([128, 8, TB], BF16, f"xb{t}") for t in range(NB)]
        for t in range(NB):
            cx.dma("sp", xb[t].ap, kview(XBF)[:, :, blk(t)], reads=[B_XBF[t]], writes=[xb[t]])
        wt = [sc.buf([128, 8, 512], BF16, f"wt{i}") for i in range(2)]
        wfg = sc.buf([128, 8, 8], BF16, "wfg")
        cx.dma("sp", wfg.ap, kview(wb_in[l])[:, :, 3072:3080], reads=B_wb[("in", l)], writes=[wfg])
        st32 = [sc.buf([128, 4, TB], F32, f"st32_{i}") for i in range(2)]
        st16 = [sc.buf([128, 4, TB], BF16, f"st16_{i}") for i in range(2)]
        stqk = [sc.buf([64, 8, TB], BF16, f"stqk_{i}") for i in range(2)]
        vst = [sc.buf([128, 8, 65], BF16, f"vst_{i}") for i in range(2)]
        for v in vst:
            cx.op("pool", lambda e, v=v: e.memset(v.ap, 1.0), writes=[v])
        nps = 0
        nst = 0
        for cg in range(6):
            w = wt[cg % 2]
            cx.dma("sp", w.ap, kview(wb_in[l])[:, :, cg * 512:(cg + 1) * 512], reads=B_wb[("in", l)], writes=[w])
            if cg < 3:
                for t in range(NB):
                    stb = (st16 if cg == 0 else st32)[nst % 2]
                    nst += 1
                    for ct in range(4):
                        ps = PS[nps % 4]
                        nps += 1

                        def mm(e, ps=ps, ct=ct, t=t, w=w):
                            for kc in range(8):
                                i = e.matmul(ps.ap, lhsT=w.ap[:, kc, ct * 128:(ct + 1) * 128], rhs=xb[t].ap[:, kc, :],
                                             start=(kc == 0), stop=(kc == 7))
                            return i
                        cx.op("pe", mm, reads=[w, xb[t]], writes=[ps])
                        fn = AF.Gelu_apprx_tanh if cg == 2 else AF.Identity
                        cx.op("act", lambda e, ps=ps, stb=stb, ct=ct, fn=fn: e.activation(out=stb.ap[:, ct, :], in_=ps.ap, func=fn),
                              reads=[ps], writes=[stb])
                    dst, bd = [(UT, B_UT), (XL, B_XL), (GL, B_GL)][cg]
                    cx.dma("sp", dst.rearrange("(c p) t -> p c t", p=128)[:, :, blk(t)], stb.ap, reads=[stb], writes=[bd])
            elif cg < 5:
                for t in range(NB):
                    stb = stqk[nst % 2]
                    nst += 1
                    for h in range(8):
                        ps = PS[nps % 4]
                        nps += 1

                        def mm(e, ps=ps, h=h, t=t, w=w):
                            for kc in range(8):
                                i = e.matmul(ps.ap[0:64, :], lhsT=w.ap[:, kc, h * 64:(h + 1) * 64], rhs=xb[t].ap[:, kc, :],
                                             start=(kc == 0), stop=(kc == 7))
                            return i
                        cx.op("pe", mm, reads=[w, xb[t]], writes=[ps])
                        sc_ = 0.125 if cg == 3 else 1.0
                        cx.op("act", lambda e, ps=ps, stb=stb, h=h, sc_=sc_: e.activation(out=stb.ap[:, h, :], in_=ps.ap[0:64, :],
                                                                                         func=AF.Identity, scale=sc_),
                              reads=[ps], writes=[stb])
                    dst, bd = (QA, B_QA) if cg == 3 else (KA, B_KA)
                    cx.dma("sp", dst[:, 0:64, blk(t)].rearrange("h d t -> d h t"), stb.ap, reads=[stb], writes=[bd])
            else:
                for tt in range(32):
                    ps = PS[nps % 4]
                    nps += 1
                    t = tt // 4
                    vs = vst[tt % 2]

                    def mm(e, ps=ps, tt=tt, t=t, w=w):
                        o = (tt % 4) * 128
                        for kc in range(8):
                            i = e.matmul(ps.ap, lhsT=xb[t].ap[:, kc, o:o + 128], rhs=w.ap[:, kc, :],
                                         start=(kc == 0), stop=(kc == 7))
                        return i
                    cx.op("pe", mm, reads=[w, xb[t]], writes=[ps])
                    cx.op("act", lambda e, ps=ps, vs=vs: e.activation(out=vs.ap[:, :, 0:64], in_=ps.ap.rearrange("p (h d) -> p h d", d=64),
                                                                      func=AF.Identity), reads=[ps], writes=[vs])
                    cx.dma("sp", VA[:, :, tt, :].rearrange("h p e -> p h e"), vs.ap, reads=[vs], writes=[B_VA])
        for t in range(NB):
            ps = PS[nps % 4]
            nps += 1

            def mm(e, ps=ps, t=t):
                for kc in range(8):
                    i = e.matmul(ps.ap[0:8, :], lhsT=wfg.ap[:, kc, :], rhs=xb[t].ap[:, kc, :], start=(kc == 0), stop=(kc == 7))
                return i
            cx.op("pe", mm, reads=[wfg, xb[t]], writes=[ps])
            cx.op("act", lambda e, ps=ps, t=t: e.activation(out=fgT.ap[:, blk(t)], in_=ps.ap[0:8, :], func=AF.Identity),
                  reads=[ps], writes=[fgT])
        cx.barrier(skip_sw=True)
        sc.close()
        sc = Scope(cx)
        bf = sc.buf([8, 1], F32, "bf")
        cx.dma("sp", bf.ap, b_f[l].rearrange("(h o) -> h o", o=1), writes=[bf])
        nbf = sc.buf([8, 1], F32, "nbf")
        cx.op("dve", lambda e: e.tensor_scalar(out=nbf.ap, in0=bf.ap, scalar1=-1.0, scalar2=None, op0=ALU.mult), reads=[bf], writes=[nbf])
        one8 = sc.buf([8, 1], F32, "one8")
        cx.op("dve", lambda e: e.memset(one8.ap, 1.0), writes=[one8])
        ex = sc.buf([8, L], F32, "ex")
        cx.op("act", lambda e: e.activation(out=ex.ap, in_=fgT.ap, func=AF.Exp, bias=nbf.ap, scale=-1.0), reads=[fgT, nbf], writes=[ex])
        cx.op("act", lambda e: e.activation(out=ex.ap, in_=ex.ap, func=AF.Ln, bias=one8.ap, scale=1.0), reads=[ex, one8], writes=[ex])
        csum = sc.buf([8, L], F32, "csum")
        cx.op("dve", lambda e: e.tensor_tensor_scan(out=csum.ap, data0=one8.ap.to_broadcast([8, L]), data1=ex.ap, initial=0.0,
                                                    op0=ALU.mult, op1=ALU.add), reads=[ex, one8], writes=[csum])
        pcs = [sc.buf([8, L], BF16, f"pc{j}") for j in range(3)]
        ncs = [sc.buf([8, L], BF16, f"nc{j}") for j in range(3)]
        res = ex
        cx.op("dve", lambda e: e.tensor_copy(out=pcs[0].ap, in_=csum.ap), reads=[csum], writes=[pcs[0]])
        cx.op("dve", lambda e: e.tensor_tensor(out=res.ap, in0=csum.ap, in1=pcs[0].ap, op=ALU.subtract), reads=[csum, pcs[0]], writes=[res])
        cx.op("dve", lambda e: e.tensor_copy(out=pcs[1].ap, in_=res.ap), reads=[res], writes=[pcs[1]])
        cx.op("dve", lambda e: e.tensor_tensor(out=res.ap, in0=res.ap, in1=pcs[1].ap, op=ALU.subtract), reads=[res, pcs[1]], writes=[res])
        cx.op("dve", lambda e: e.tensor_copy(out=pcs[2].ap, in_=res.ap), reads=[res], writes=[pcs[2]])
        for j in range(3):
            cx.op("dve", lambda e, j=j: e.tensor_scalar(out=ncs[j].ap, in0=pcs[j].ap, scalar1=-1.0, scalar2=None, op0=ALU.mult),
                  reads=[pcs[j]], writes=[ncs[j]])
        onesb = sc.buf([8, L], BF16, "onesb")
        cx.op("pool", lambda e: e.memset(onesb.ap, 1.0), writes=[onesb])
        for j in range(3):
            cx.dma("sp", QA[:, 64 + j, :], ncs[j].ap, reads=[ncs[j]], writes=[B_QA])
            cx.dma("sp", QA[:, 67 + j, :], onesb.ap, reads=[onesb], writes=[B_QA])
            cx.dma("sp", KA[:, 64 + j, :], onesb.ap, reads=[onesb], writes=[B_KA])
            cx.dma("sp", KA[:, 67 + j, :], pcs[j].ap, reads=[pcs[j]], writes=[B_KA])
        cx.barrier(skip_sw=True)
        sc.close()
        sc_fg.close()

    def phase_attn(l):
        sc = Scope(cx)
        qa = [sc.buf([70, L], BF16, f"qa{i}") for i in range(2)]
        ka = [sc.buf([70, L], BF16, f"ka{i}") for i in range(2)]
        va = [sc.buf([128, 32, 65], BF16, f"va{i}") for i in range(2)]
        NPT = 6
        pt = [sc.buf([128, TB], BF16, f"pt{i}") for i in range(NPT)]
        rden = [sc.buf([128, TB], F32, f"rden{i}") for i in range(2)]
        rb = [sc.buf([64, TB], F32, f"rb{i}") for i in range(2)]
        ost = [sc.buf([64, TB], BF16, f"ost{i}") for i in range(2)]

        def load_head(h):
            cx.dma("sp", qa[h % 2].ap, QA[h], reads=[B_QA], writes=[qa[h % 2]])
            cx.dma("sp", ka[h % 2].ap, KA[h], reads=[B_KA], writes=[ka[h % 2]])
            cx.dma("sp", va[h % 2].ap, VA[h], reads=[B_VA], writes=[va[h % 2]])
        items = []
        nb = 0
        for h in range(8):
            for I in range(NB):
                nkb = 4 * I + 4
                for j in range(nkb):
                    items.append((h, I, j, nkb, nb))
                nb += 1
        LA = 3

        def emit_S(i):
            h, I, j, nkb, b_ = items[i]
            c0 = 128 * max(0, j - 4 * I)
            ps = PS[i % 4]
            k, q = ka[h % 2], qa[h % 2]
            cx.op("pe", lambda e: e.matmul(ps.ap[:, c0:TB], lhsT=k.ap[:, j * 128:(j + 1) * 128], rhs=q.ap[:, I * TB + c0:(I + 1) * TB],
                                           start=True, stop=True), reads=[k, q], writes=[ps])

        def finalize(h, I, b_):
            po = PS[4 + b_ % 2]
            pr = PS[6 + b_ % 2]
            rd, r_, o_ = rden[b_ % 2], rb[b_ % 2], ost[b_ % 2]
            cx.op("dve", lambda e: e.reciprocal(out=rd.ap[64:65, :], in_=po.ap[64:65, :]), reads=[po], writes=[rd])
            cx.op("pe", lambda e: e.matmul(pr.ap[0:64, :], lhsT=ones1.ap[64:65, :], rhs=rd.ap[64:65, :], start=True, stop=True),
                  reads=[ones1, rd], writes=[pr])
            cx.op("act", lambda e: e.activation(out=r_.ap, in_=pr.ap[0:64, :], func=AF.Identity), reads=[pr], writes=[r_])
            cx.op("dve", lambda e: e.tensor_tensor(out=o_.ap, in0=po.ap[0:64, :], in1=r_.ap, op=ALU.mult), reads=[po, r_], writes=[o_])
            cx.dma("sp", YS[1024 + h * 64:1024 + (h + 1) * 64, blk(I)], o_.ap, reads=[o_], writes=[B_YS[2]])
        load_head(0)
        for i in range(min(LA, len(items))):
            emit_S(i)
        pending = []
        for i, (h, I, j, nkb, b_) in enumerate(items):
            if I == 0 and j == 0 and h + 1 < 8:
                load_head(h + 1)
            if i + LA < len(items):
                emit_S(i + LA)
            c0 = 128 * max(0, j - 4 * I)
            ps = PS[i % 4]
            p = pt[i % NPT]
            v = va[h % 2]
            po = PS[4 + b_ % 2]
            cx.op("act", lambda e: e.activation(out=p.ap[:, c0:TB], in_=ps.ap[:, c0:TB], func=AF.Exp), reads=[ps], writes=[p])
            if j >= 4 * I:
                cx.op("pool", lambda e: e.tensor_tensor(out=p.ap[:, c0:c0 + 128], in0=p.ap[:, c0:c0 + 128], in1=tri.ap, op=ALU.mult),
                      reads=[p, tri], writes=[p])
            cx.op("pe", lambda e: e.matmul(po.ap[0:65, c0:TB], lhsT=v.ap[:, j, :], rhs=p.ap[:, c0:TB], start=(j == 0), stop=(j == nkb - 1)),
                  reads=[v, p], writes=[po])
            pending = [(cnt - 1, args) for (cnt, args) in pending]
            for cnt, args in [x for x in pending if x[0] <= 0]:
                finalize(*args)
            pending = [x for x in pending if x[0] > 0]
            if j == nkb - 1:
                pending.append((2, (h, I, b_)))
        for cnt, args in pending:
            finalize(*args)
        cx.barrier(skip_sw=True)
        sc.close()

    def phase_lru(l):
        sc = Scope(cx)
        cw = sc.buf([128, 4, 4], F32, "cw")
        cb = sc.buf([128, 4], F32, "cb")
        ba = sc.buf([128, 4], F32, "ba")
        bx = sc.buf([128, 4], F32, "bx")
        lam = sc.buf([128, 4], F32, "lam")
        sneg = sc.buf([128, 4], F32, "sneg")
        one_c = sc.buf([128, 1], F32, "one_c")
        cx.op("dve", lambda e: e.memset(one_c.ap, 1.0), writes=[one_c])
        for k_ in range(4):
            cx.dma("sp", cw.ap[:, :, k_], lru_conv_w[l, k_].rearrange("(c p) -> p c", p=128), writes=[cw], allow_slow_non_contiguous=True)
        for (dst, src) in ((cb, lru_conv_b), (ba, lru_b_a), (bx, lru_b_x), (lam, lru_lambda)):
            cx.dma("sp", dst.ap, src[l].rearrange("(c p) -> p c", p=128), writes=[dst], allow_slow_non_contiguous=True)
        cx.op("act", lambda e: e.activation(out=sneg.ap, in_=lam.ap, func=AF.Exp, scale=-1.0), reads=[lam], writes=[sneg])
        cx.op("act", lambda e: e.activation(out=sneg.ap, in_=sneg.ap, func=AF.Ln, bias=one_c.ap, scale=1.0), reads=[sneg, one_c], writes=[sneg])
        cx.op("dve", lambda e: e.tensor_scalar(out=sneg.ap, in0=sneg.ap, scalar1=-8.0, scalar2=None, op0=ALU.mult), reads=[sneg], writes=[sneg])
        WA = sc.buf([128, 4, 128], BF16, "WA")
        WX = sc.buf([128, 4, 128], BF16, "WX")
        for Wm, src in ((WA, lru_w_a), (WX, lru_w_x)):
            cx.op("pool", lambda e, Wm=Wm: e.memset(Wm.ap, 0.0), writes=[Wm])
            for c in range(4):
                cx.dma("pool", Wm.ap[0:64, c, 0:64], src[l, 2 * c], writes=[Wm])
                cx.dma("pool", Wm.ap[64:128, c, 64:128], src[l, 2 * c + 1], writes=[Wm])
        xl = sc.buf([128, L + 3], F32, "xl")
        gl = sc.buf([128, L], F32, "gl")
        xc = sc.buf([128, L], F32, "xc")
        xcb = sc.buf([128, L], BF16, "xcb")
        a_all = sc.buf([128, L], F32, "a_all")
        b_all = sc.buf([128, L], F32, "b_all")
        yb = sc.buf([128, L], BF16, "yb")
        h_all = sc.buf([128, L], F32, "h_all")
        tr = [sc.buf([128, TB], F32, f"tr{i}") for i in range(2)]
        ti = [sc.buf([128, TB], F32, f"ti{i}") for i in range(2)]
        tm = [sc.buf([128, TB], F32, f"tm{i}") for i in range(2)]
        cx.op("pool", lambda e: e.memset(xl.ap[:, 0:3], 0.0), writes=[xl])
        n = 0
        for c in range(4):
            cx.dma("sp", xl.ap[:, 3:], XL[c * 128:(c + 1) * 128, :], reads=[B_XL], writes=[xl])
            cx.dma("sp", gl.ap, GL[c * 128:(c + 1) * 128, :], reads=[B_GL], writes=[gl])
            cx.op("dve", lambda e, c=c: e.tensor_scalar(out=xc.ap, in0=xl.ap[:, 0:L], scalar1=cw.ap[:, c, 0:1], scalar2=cb.ap[:, c:c + 1],
                                                        op0=ALU.mult, op1=ALU.add), reads=[xl, cw, cb], writes=[xc])
            for k_ in range(1, 4):
                eng = "dve"
                cx.op(eng, lambda e, c=c, k_=k_: e.scalar_tensor_tensor(out=xc.ap, in0=xl.ap[:, k_:k_ + L], scalar=cw.ap[:, c, k_:k_ + 1],
                                                                         in1=xc.ap, op0=ALU.mult, op1=ALU.add), reads=[xl, cw, xc], writes=[xc])
            cx.op("act", lambda e: e.activation(out=xcb.ap, in_=xc.ap, func=AF.Identity), reads=[xc], writes=[xcb])
            for t in range(NB):
                pa, px = PS[(2 * n) % 4], PS[(2 * n + 1) % 4]
                r_, i_, m_ = tr[n % 2], ti[n % 2], tm[n % 2]
                n += 1
                cx.op("pe", lambda e, pa=pa, c=c, t=t: e.matmul(pa.ap, lhsT=WA.ap[:, c, :], rhs=xcb.ap[:, blk(t)], start=True, stop=True),
                      reads=[WA, xcb], writes=[pa])
                cx.op("pe", lambda e, px=px, c=c, t=t: e.matmul(px.ap, lhsT=WX.ap[:, c, :], rhs=xcb.ap[:, blk(t)], start=True, stop=True),
                      reads=[WX, xcb], writes=[px])
                cx.op("act", lambda e, pa=pa, r_=r_, c=c: e.activation(out=r_.ap, in_=pa.ap, func=AF.Sigmoid, bias=ba.ap[:, c:c + 1], scale=1.0),
                      reads=[pa, ba], writes=[r_])
                cx.op("act", lambda e, px=px, i_=i_, c=c: e.activation(out=i_.ap, in_=px.ap, func=AF.Sigmoid, bias=bx.ap[:, c:c + 1], scale=1.0),
                      reads=[px, bx], writes=[i_])
                cx.op("act", lambda e, r_=r_, c=c, t=t: e.activation(out=a_all.ap[:, blk(t)], in_=r_.ap, func=AF.Exp, scale=sneg.ap[:, c:c + 1]),
                      reads=[r_, sneg], writes=[a_all])
                cx.op("act", lambda e, m_=m_, t=t: e.activation(out=m_.ap, in_=a_all.ap[:, blk(t)], func=AF.Square), reads=[a_all], writes=[m_])
                cx.op("act", lambda e, m_=m_: e.activation(out=m_.ap, in_=m_.ap, func=AF.Sqrt, bias=one_c.ap, scale=-1.0),
                      reads=[m_, one_c], writes=[m_])
                cx.op("dve", lambda e, m_=m_, i_=i_: e.tensor_tensor(out=m_.ap, in0=m_.ap, in1=i_.ap, op=ALU.mult), reads=[m_, i_], writes=[m_])
                cx.op("pool", lambda e, m_=m_, t=t: e.tensor_tensor(out=b_all.ap[:, blk(t)], in0=m_.ap, in1=xc.ap[:, blk(t)], op=ALU.mult),
                      reads=[m_, xc], writes=[b_all])
            cx.op("dve", lambda e: e.tensor_tensor_scan(out=h_all.ap, data0=a_all.ap, data1=b_all.ap, initial=0.0, op0=ALU.mult, op1=ALU.add),
                  reads=[a_all, b_all], writes=[h_all])
            cx.op("dve", lambda e: e.tensor_tensor(out=yb.ap, in0=h_all.ap, in1=gl.ap, op=ALU.mult), reads=[h_all, gl], writes=[yb])
            cx.dma("sp", YS[512 + c * 128:512 + (c + 1) * 128, :], yb.ap, reads=[yb], writes=[B_YS[1]])
        cx.barrier(skip_sw=True)
        sc.close()

    def cmul(eng_a, eng_b, sc_t, outr, outi, ar, ai, br, bi, reads, w_r, w_i, negi=False):
        t1, t2 = sc_t
        cx.op(eng_a, lambda e: e.tensor_tensor(out=t1.ap, in0=ar, in1=br, op=ALU.mult), reads=reads, writes=[t1])
        cx.op(eng_b, lambda e: e.tensor_tensor(out=t2.ap, in0=ai, in1=bi, op=ALU.mult), reads=reads, writes=[t2])
        cx.op(eng_a, lambda e: e.tensor_tensor(out=outr, in0=t1.ap, in1=t2.ap, op=ALU.subtract), reads=[t1, t2], writes=[w_r])
        cx.op(eng_a, lambda e: e.tensor_tensor(out=t1.ap, in0=ar, in1=bi, op=ALU.mult), reads=reads + [w_r], writes=[t1])
        cx.op(eng_b, lambda e: e.tensor_tensor(out=t2.ap, in0=ai, in1=br, op=ALU.mult), reads=reads + [w_r], writes=[t2])
        if negi:
            cx.op(eng_a, lambda e: e.scalar_tensor_tensor(out=outi, in0=t1.ap, scalar=-1.0, in1=t2.ap, op0=ALU.mult, op1=ALU.subtract),
                  reads=[t1, t2], writes=[w_i])
        else:
            cx.op(eng_a, lambda e: e.tensor_tensor(out=outi, in0=t1.ap, in1=t2.ap, op=ALU.add), reads=[t1, t2], writes=[w_i])

    def phase_s5(l):
        ws = Scope(cx)
        Mw = ws.buf([128, 32, 128], BF16, "Mw")
        W1r = ws.buf([128, 32, 64], BF16, "W1r")
        W1i = ws.buf([128, 32, 64], BF16, "W1i")
        W2r = ws.buf([128, 16, 128], BF16, "W2r")
        W2i = ws.buf([128, 16, 128], BF16, "W2i")
        rho = ws.buf([128, 16], F32, "rho")
        ph_r = ws.buf([128, 16], F32, "ph_r")
        ph_i = ws.buf([128, 16], F32, "ph_i")
        dcol = ws.buf([128, 32], F32, "dcol")
        bglu = ws.buf([128, 4], F32, "bglu")
        cx.dma("sp", bglu.ap, s5_b_glu[l].rearrange("(c p) -> p c", p=128), writes=[bglu], allow_slow_non_contiguous=True)
        for tau in range(8):
            cx.dma("sp", dcol.ap[16 * tau:16 * tau + 16, :], s5_d[l].rearrange("(g c) -> c g", c=16), writes=[dcol],
                   allow_slow_non_contiguous=True)
        sc = Scope(cx)
        N = 64
        araw = sc.buf([32, 64], F32, "araw")
        airaw = sc.buf([32, 64], F32, "airaw")
        cx.dma("sp", araw.ap, s5_a_re[l], writes=[araw])
        cx.dma("sp", airaw.ap, s5_a_im[l], writes=[airaw])
        are = sc.buf([N, 32], F32, "are")
        aim = sc.buf([N, 32], F32, "aim")
        for src, dst in ((araw, are), (airaw, aim)):
            cx.op("pe", lambda e, src=src: e.matmul(PS[0].ap[0:64, 0:32], lhsT=src.ap, rhs=identf.ap[0:32, 0:32], start=True, stop=True),
                  reads=[src, identf], writes=[PS[0]])
            cx.op("act", lambda e, dst=dst: e.activation(out=dst.ap, in_=PS[0].ap[0:64, 0:32], func=AF.Identity), reads=[PS[0]], writes=[dst])
        dt = sc.buf([N, 32], F32, "dt")
        cx.dma("sp", dt.ap, s5_log_dt[l].partition_broadcast(N), writes=[dt])
        Br = sc.buf([N, 32, 16], F32, "Br")
        Bi = sc.buf([N, 32, 16], F32, "Bi")
        cx.dma("sp", Br.ap, s5_b_re[l].rearrange("g n c -> n g c"), writes=[Br])
        cx.dma("sp", Bi.ap, s5_b_im[l].rearrange("g n c -> n g c"), writes=[Bi])
        Cr = sc.buf([N, 32, 16], F32, "Cr")
        Ci = sc.buf([N, 32, 16], F32, "Ci")
        craw = sc.buf([128, 4, 64], F32, "craw")
        for src, dst in ((s5_c_re, Cr), (s5_c_im, Ci)):
            cx.dma("sp", craw.ap, src[l].rearrange("(j p) n -> p j n", p=128), writes=[craw])

            def mm(e):
                for j in range(4):
                    i = e.matmul(PS[1].ap[0:64, j * 128:(j + 1) * 128], lhsT=craw.ap[:, j, :], rhs=identf.ap, start=True, stop=True)
                return i
            cx.op("pe", mm, reads=[craw, identf], writes=[PS[1]])
            cx.op("act", lambda e, dst=dst: e.activation(out=dst.ap.rearrange("n g c -> n (g c)"), in_=PS[1].ap[0:64, :], func=AF.Identity),
                  reads=[PS[1]], writes=[dst])

        def small(name):
            return sc.buf([N, 32], F32, name)
        ar, ang, mag, lbr, lbi = small("ar"), small("ang"), small("mag"), small("lbr"), small("lbi")
        t1, t2, t3 = small("t1"), small("t2"), small("t3")
        V = "dve"

        def tt(out, a, b, op_, eng=V):
            cx.op(eng, lambda e: e.tensor_tensor(out=out.ap, in0=a.ap, in1=b.ap, op=op_), reads=[a, b], writes=[out])

        def tsc(out, a, s1, op0, s2=None, op1=None, eng=V):
            if op1 is None:
                cx.op(eng, lambda e: e.tensor_scalar(out=out.ap, in0=a.ap, scalar1=s1, scalar2=None, op0=op0), reads=[a], writes=[out])
            else:
                cx.op(eng, lambda e: e.tensor_scalar(out=out.ap, in0=a.ap, scalar1=s1, scalar2=s2, op0=op0, op1=op1), reads=[a], writes=[out])

        def act(out, a, fn, scale=1.0, bias=None):
            if bias is None:
                cx.op("act", lambda e: e.activation(out=out.ap, in_=a.ap, func=fn, scale=scale), reads=[a], writes=[out])
            else:
                cx.op("act", lambda e: e.activation(out=out.ap, in_=a.ap, func=fn, scale=scale, bias=bias.ap), reads=[a, bias], writes=[out])

        zero_c = sc.buf([N, 1], F32, "zero_c")
        cx.op("dve", lambda e: e.memset(zero_c.ap, 0.0), writes=[zero_c])

        def sin_of(out, angle_buf, shift):
            tsc(t1, angle_buf, 1.0 / TWO_PI, ALU.mult, (shift / TWO_PI) + MAGIC, ALU.add)
            tsc(t1, t1, -MAGIC, ALU.add)
            cx.op(V, lambda e: e.scalar_tensor_tensor(out=t2.ap, in0=t1.ap, scalar=-TWO_PI, in1=angle_buf.ap, op0=ALU.mult, op1=ALU.add),
                  reads=[t1, angle_buf], writes=[t2])
            tsc(t2, t2, float(shift), ALU.add, math.pi - 1e-6, ALU.min)
            tsc(t2, t2, -(math.pi - 1e-6), ALU.max)
            act(out, t2, AF.Sin, bias=zero_c)

        em1, xr_, w_, sn, cm1, nn = small("em1"), small("xr_"), small("w_"), small("sn"), small("cm1"), small("nn")

        def nested(out, var, divs, sign):
            cx.op(V, lambda e: e.memset(out.ap, 1.0), writes=[out])
            for dv in divs:
                tt(t3, out, var, ALU.mult)
                tsc(out, t3, sign / dv, ALU.mult, 1.0, ALU.add)
        ld8 = small("ld8")
        tsc(ld8, dt, 0.125, ALU.mult)
        nested(dt, ld8, [float(k) for k in range(12, 0, -1)], 1.0)
        for _ in range(3):
            tt(dt, dt, dt, ALU.mult)
        tt(ar, are, dt, ALU.mult)
        tt(ang, aim, dt, ALU.mult)
        nested(nn, ar, [9.0, 8.0, 7.0, 6.0, 5.0, 4.0, 3.0, 2.0], 1.0)
        tt(em1, nn, ar, ALU.mult)
        tsc(mag, em1, 1.0, ALU.add)
        C1 = 6.28125
        C2 = TWO_PI - C1
        tsc(t1, ang, 1.0 / TWO_PI, ALU.mult, MAGIC, ALU.add)
        tsc(t1, t1, -MAGIC, ALU.add)
        cx.op(V, lambda e: e.scalar_tensor_tensor(out=xr_.ap, in0=t1.ap, scalar=-C1, in1=ang.ap, op0=ALU.mult, op1=ALU.add),
              reads=[t1, ang], writes=[xr_])
        cx.op(V, lambda e: e.scalar_tensor_tensor(out=xr_.ap, in0=t1.ap, scalar=-C2, in1=xr_.ap, op0=ALU.mult, op1=ALU.add),
              reads=[t1, xr_], writes=[xr_])
        tt(w_, xr_, xr_, ALU.mult)
        nested(nn, w_, [float((2 * k) * (2 * k + 1)) for k in range(10, 0, -1)], -1.0)
        tt(sn, nn, xr_, ALU.mult)
        nested(nn, w_, [float((2 * k + 1) * (2 * k + 2)) for k in range(10, 0, -1)], -1.0)
        tt(cm1, nn, w_, ALU.mult)
        tsc(cm1, cm1, -0.5, ALU.mult)
        lm1 = small("lm1")
        tt(t1, em1, cm1, ALU.mult)
        tt(t2, em1, cm1, ALU.add)
        tt(lm1, t1, t2, ALU.add)
        tsc(lbr, lm1, 1.0, ALU.add)
        tt(lbi, sn, mag, ALU.mult)
        den, qr, qi = small("den"), small("qr"), small("qi")
        tt(den, are, are, ALU.mult)
        tt(t1, aim, aim, ALU.mult)
        tt(den, den, t1, ALU.add)
        cx.op(V, lambda e: e.reciprocal(out=den.ap, in_=den.ap), reads=[den], writes=[den])
        tt(t1, lm1, are, ALU.mult)
        tt(t2, lbi, aim, ALU.mult)
        tt(qr, t1, t2, ALU.add)
        tt(qr, qr, den, ALU.mult)
        tt(t1, lbi, are, ALU.mult)
        tt(t2, lm1, aim, ALU.mult)
        tt(qi, t1, t2, ALU.subtract)
        tt(qi, qi, den, ALU.mult)
        Bbr = sc.buf([N, 32, 16], F32, "Bbr")
        Bbi = sc.buf([N, 32, 16], F32, "Bbi")
        tb1 = sc.buf([N, 32, 16], F32, "tb1")
        tb2 = sc.buf([N, 32, 16], F32, "tb2")

        def bc3(b):
            return b.ap.unsqueeze(2).to_broadcast([N, 32, 16])
        cmul("dve", "pool", (tb1, tb2), Bbr.ap, Bbi.ap, bc3(qr), bc3(qi), Br.ap, Bi.ap, [qr, qi, Br, Bi], Bbr, Bbi)
        Pr = sc.buf([N, 32, 9], F32, "Pr")
        Pi = sc.buf([N, 32, 9], F32, "Pi")
        Qr = sc.buf([N, 32, 8], F32, "Qr")
        Qi = sc.buf([N, 32, 8], F32, "Qi")
        ibr, ibi, im2 = small("ibr"), small("ibi"), small("im2")
        tt(im2, mag, mag, ALU.mult)
        cx.op(V, lambda e: e.reciprocal(out=im2.ap, in_=im2.ap), reads=[im2], writes=[im2])
        tt(ibr, lbr, im2, ALU.mult)
        tt(ibi, lbi, im2, ALU.mult)
        tsc(ibi, ibi, -1.0, ALU.mult)
        for (Xr, Xi, br_, bi_, n_) in ((Pr, Pi, lbr, lbi, 9), (Qr, Qi, ibr, ibi, 8)):
            cx.op(V, lambda e, Xr=Xr: e.memset(Xr.ap[:, :, 0:1], 1.0), writes=[Xr])
            cx.op(V, lambda e, Xi=Xi: e.memset(Xi.ap[:, :, 0:1], 0.0), writes=[Xi])
            for tau in range(1, n_):
                cmul("dve", "pool", (t1, t2), Xr.ap[:, :, tau], Xi.ap[:, :, tau], Xr.ap[:, :, tau - 1], Xi.ap[:, :, tau - 1],
                     br_.ap, bi_.ap, [Xr, Xi, br_, bi_], Xr, Xi)
        GH = 16
        big = [sc.buf([N, GH, 8, 16], F32, f"big{i}") for i in range(8)]
        Ar_, Ai_, Cmr, Cmin, T1, T2, A7r, A7i = big
        mtmp = [sc.buf([128, 128], F32, f"mtmp{i}") for i in range(2)]
        for gh in range(2):
            gs = slice(gh * GH, (gh + 1) * GH)

            def bq(b):
                return b.ap[:, gs, 0:8].unsqueeze(3).to_broadcast([N, GH, 8, 16])

            def bb(b):
                return b.ap[:, gs, :].unsqueeze(2).to_broadcast([N, GH, 8, 16])

            def b7(b, idx):
                return b.ap[:, gs, idx:idx + 1].unsqueeze(3).to_broadcast([N, GH, 8, 16])
            cmul("dve", "pool", (T1, T2), Ar_.ap, Ai_.ap, bq(Qr), bq(Qi), bb(Bbr), bb(Bbi), [Qr, Qi, Bbr, Bbi], Ar_, Ai_)
            cmul("dve", "pool", (T1, T2), Cmr.ap, Cmin.ap, bq(Pr), bq(Pi), bb(Cr), bb(Ci), [Pr, Pi, Cr, Ci], Cmr, Cmin, negi=True)
            cmul("dve", "pool", (T1, T2), A7r.ap, A7i.ap, b7(Pr, 7), b7(Pi, 7), Ar_.ap, Ai_.ap, [Pr, Pi, Ar_, Ai_], A7r, A7i)
            for src, dst in ((A7r, W1r), (A7i, W1i)):
                for g8 in range(2):
                    ps = PS[g8 % 4]

                    def mm(e, ps=ps, src=src, g8=g8):
                        for gi in range(8):
                            i = e.matmul(ps.ap[:, gi * 64:(gi + 1) * 64], lhsT=src.ap[:, g8 * 8 + gi].rearrange("n t c -> n (t c)"),
                                         rhs=identf.ap[0:64, 0:64], start=True, stop=True)
                        return i
                    cx.op("pe", mm, reads=[src, identf], writes=[ps])
                    g0 = gh * GH + g8 * 8
                    cx.op("act", lambda e, ps=ps, dst=dst, g0=g0: e.activation(
                        out=dst.ap[:, g0:g0 + 8, :], in_=ps.ap.rearrange("p (g n) -> p g n", n=64), func=AF.Identity),
                        reads=[ps], writes=[dst])
            for g4 in range(4):
                ps = PS[g4 % 4]

                def mm(e, ps=ps, g4=g4):
                    for gi in range(4):
                        gl_ = g4 * 4 + gi
                        e.matmul(ps.ap[:, gi * 128:(gi + 1) * 128], lhsT=Ar_.ap[:, gl_].rearrange("n t c -> n (t c)"),
                                 rhs=Cmr.ap[:, gl_].rearrange("n t c -> n (t c)"), start=True, stop=False)
                        i = e.matmul(ps.ap[:, gi * 128:(gi + 1) * 128], lhsT=Ai_.ap[:, gl_].rearrange("n t c -> n (t c)"),
                                     rhs=Cmin.ap[:, gl_].rearrange("n t c -> n (t c)"), start=False, stop=True)
                    return i
                cx.op("pe", mm, reads=[Ar_, Ai_, Cmr, Cmin], writes=[ps])
                for gi in range(4):
                    g = gh * GH + g4 * 4 + gi
                    mt = mtmp[g % 2]
                    cx.op("dve", lambda e, ps=ps, gi=gi, mt=mt: e.tensor_tensor(out=mt.ap, in0=ps.ap[:, gi * 128:(gi + 1) * 128], in1=mask8.ap,
                                                                                op=ALU.mult), reads=[ps, mask8], writes=[mt])
                    cx.op("dve", lambda e, g=g, mt=mt: e.scalar_tensor_tensor(out=Mw.ap[:, g, :], in0=identf.ap, scalar=dcol.ap[:, g:g + 1],
                                                                              in1=mt.ap, op0=ALU.mult, op1=ALU.add),
                          reads=[identf, dcol, mt], writes=[Mw])
            W2r_f, W2i_f = A7r, A7i
            cx.op("dve", lambda e: e.tensor_tensor(out=T1.ap, in0=b7(Pr, 1), in1=Cmr.ap, op=ALU.mult), reads=[Pr, Cmr], writes=[T1])
            cx.op("pool", lambda e: e.tensor_tensor(out=T2.ap, in0=b7(Pi, 1), in1=Cmin.ap, op=ALU.mult), reads=[Pi, Cmin], writes=[T2])
            cx.op("dve", lambda e: e.tensor_tensor(out=W2r_f.ap, in0=T1.ap, in1=T2.ap, op=ALU.add), reads=[T1, T2], writes=[W2r_f])
            cx.op("dve", lambda e: e.tensor_tensor(out=T1.ap, in0=b7(Pr, 1), in1=Cmin.ap, op=ALU.mult), reads=[Pr, Cmin, W2r_f], writes=[T1])
            cx.op("pool", lambda e: e.tensor_tensor(out=T2.ap, in0=b7(Pi, 1), in1=Cmr.ap, op=ALU.mult), reads=[Pi, Cmr, W2r_f], writes=[T2])
            cx.op("dve", lambda e: e.tensor_tensor(out=W2i_f.ap, in0=T1.ap, in1=T2.ap, op=ALU.subtract), reads=[T1, T2], writes=[W2i_f])
            for src, dst in ((W2r_f, W2r), (W2i_f, W2i)):
                v = src.ap.rearrange("n (q two) t c -> n q two (t c)", two=2)
                q0 = gh * (GH // 2)
                cx.op("act", lambda e, v=v, dst=dst, q0=q0: e.activation(out=dst.ap[0:64, q0:q0 + GH // 2, :], in_=v[:, :, 0, :], func=AF.Identity),
                      reads=[src], writes=[dst])
                cx.op("act", lambda e, v=v, dst=dst, q0=q0: e.activation(out=dst.ap[64:128, q0:q0 + GH // 2, :], in_=v[:, :, 1, :], func=AF.Identity),
                      reads=[src], writes=[dst])
        r8, e8, p8r, p8i = small("r8"), small("e8"), small("p8r"), small("p8i")
        tt(r8, mag, mag, ALU.mult)
        tt(r8, r8, r8, ALU.mult)
        tt(r8, r8, r8, ALU.mult)
        cx.op(V, lambda e: e.reciprocal(out=e8.ap, in_=r8.ap), reads=[r8], writes=[e8])
        cx.op(V, lambda e: e.tensor_tensor(out=p8r.ap, in0=Pr.ap[:, :, 8], in1=e8.ap, op=ALU.mult), reads=[Pr, e8], writes=[p8r])
        cx.op(V, lambda e: e.tensor_tensor(out=p8i.ap, in0=Pi.ap[:, :, 8], in1=e8.ap, op=ALU.mult), reads=[Pi, e8], writes=[p8i])
        for src, dst in ((r8, rho), (p8r, ph_r), (p8i, ph_i)):
            v = src.ap.rearrange("n (q two) -> n q two", two=2)
            cx.op("act", lambda e, v=v, dst=dst: e.activation(out=dst.ap[0:64], in_=v[:, :, 0], func=AF.Identity), reads=[src], writes=[dst])
            cx.op("act", lambda e, v=v, dst=dst: e.activation(out=dst.ap[64:128], in_=v[:, :, 1], func=AF.Identity), reads=[src], writes=[dst])
        cx.barrier(skip_sw=True)
        sc.close()
        sc = Scope(cx)
        gb = sc.buf([128, 4, L], BF16, "gb")
        SelS = sc.buf([128, 8, 8, 128], BF16, "SelS")
        SelTS = sc.buf([128, 8, 8, 128], BF16, "SelTS")
        cx.dma("sp", SelS.ap, SELD[0], reads=[B_SELD], writes=[SelS])
        cx.dma("sp", SelTS.ap, SELD[1], reads=[B_SELD], writes=[SelTS])
        NQ = 4
        for qt in range(4):
            s2 = Scope(cx)
            uT = s2.buf([128, L], BF16, "uT")
            cx.dma("sp", uT.ap, UT[qt * 128:(qt + 1) * 128, :], reads=[B_UT], writes=[uT])
            U3 = s2.buf([128, 8, TB], BF16, "U3")
            E1r = s2.buf([128, NQ, TB], F32, "E1r")
            E1i = s2.buf([128, NQ, TB], F32, "E1i")
            Hr = s2.buf([128, NQ, TB], BF16, "Hr")
            Hi = s2.buf([128, NQ, TB], BF16, "Hi")
            Y3 = s2.buf([128, 8, TB], BF16, "Y3")
            dd = [s2.buf([128, NQ, 256], F32, f"dd{i}") for i in range(4)]
            tmp = [s2.buf([128, TB], F32, f"s5t{i}") for i in range(8)]
            q0 = qt * NQ
            cx.op("dve", lambda e: e.tensor_copy(out=E1r.ap[:, :, 0], in_=ph_r.ap[:, q0:q0 + NQ]), reads=[ph_r], writes=[E1r])
            cx.op("dve", lambda e: e.tensor_copy(out=E1i.ap[:, :, 0], in_=ph_i.ap[:, q0:q0 + NQ]), reads=[ph_i], writes=[E1i])
            m = 1
            while m < TB:
                def bcm(b, m=m):
                    return b.ap[:, :, m - 1:m].to_broadcast([128, NQ, m])
                cx.op("dve", lambda e, m=m: e.tensor_tensor(out=dd[0].ap[:, :, 0:m], in0=E1r.ap[:, :, 0:m], in1=bcm(E1r), op=ALU.mult),
                      reads=[E1r], writes=[dd[0]])
                cx.op("pool", lambda e, m=m: e.tensor_tensor(out=dd[1].ap[:, :, 0:m], in0=E1i.ap[:, :, 0:m], in1=bcm(E1i), op=ALU.mult),
                      reads=[E1i], writes=[dd[1]])
                cx.op("dve", lambda e, m=m: e.tensor_tensor(out=dd[2].ap[:, :, 0:m], in0=E1r.ap[:, :, 0:m], in1=bcm(E1i), op=ALU.mult),
                      reads=[E1r, E1i], writes=[dd[2]])
                cx.op("pool", lambda e, m=m: e.tensor_tensor(out=dd[3].ap[:, :, 0:m], in0=E1i.ap[:, :, 0:m], in1=bcm(E1r), op=ALU.mult),
                      reads=[E1i, E1r], writes=[dd[3]])
                cx.op("dve", lambda e, m=m: e.tensor_tensor(out=E1r.ap[:, :, m:2 * m], in0=dd[0].ap[:, :, 0:m], in1=dd[1].ap[:, :, 0:m], op=ALU.subtract),
                      reads=[dd[0], dd[1]], writes=[E1r])
                cx.op("dve", lambda e, m=m: e.tensor_tensor(out=E1i.ap[:, :, m:2 * m], in0=dd[2].ap[:, :, 0:m], in1=dd[3].ap[:, :, 0:m], op=ALU.add),
                      reads=[dd[2], dd[3]], writes=[E1i])
                m *= 2
            npsu = 0
            for gl_ in range(8):
                ps = PS[npsu % 4]
                npsu += 1

                def mm(e, ps=ps, gl_=gl_):
                    for tau in range(8):
                        i = e.matmul(ps.ap, lhsT=SelS.ap[:, gl_, tau, :], rhs=uT.ap[:, tau::8], start=(tau == 0), stop=(tau == 7))
                    return i
                cx.op("pe", mm, reads=[SelS, uT], writes=[ps])
                cx.op("act", lambda e, ps=ps, gl_=gl_: e.activation(out=U3.ap[:, gl_, :], in_=ps.ap, func=AF.Identity), reads=[ps], writes=[U3])
            for ql in range(NQ):
                q_ = qt * NQ + ql
                ga, gb_ = 2 * ql, 2 * ql + 1
                pre, pim = PS[4 + (ql % 2) * 2], PS[5 + (ql % 2) * 2]

                def mm(e, pre=pre, pim=pim, ga=ga, gb_=gb_, qt=qt):
                    e.matmul(pre.ap[0:64, :], lhsT=W1r.ap[:, qt * 8 + ga, :], rhs=U3.ap[:, ga, :], start=True, stop=True)
                    e.matmul(pre.ap[64:128, :], lhsT=W1r.ap[:, qt * 8 + gb_, :], rhs=U3.ap[:, gb_, :], start=True, stop=True)
                    e.matmul(pim.ap[0:64, :], lhsT=W1i.ap[:, qt * 8 + ga, :], rhs=U3.ap[:, ga, :], start=True, stop=True)
                    return e.matmul(pim.ap[64:128, :], lhsT=W1i.ap[:, qt * 8 + gb_, :], rhs=U3.ap[:, gb_, :], start=True, stop=True)
                cx.op("pe", mm, reads=[W1r, W1i, U3], writes=[pre, pim])
                xr, xi, a1, a2, vr, vi, sr, si = tmp
                cx.op("act", lambda e, pre=pre: e.activation(out=xr.ap, in_=pre.ap, func=AF.Identity), reads=[pre], writes=[xr])
                cx.op("act", lambda e, pim=pim: e.activation(out=xi.ap, in_=pim.ap, func=AF.Identity), reads=[pim], writes=[xi])
                er, ei = E1r.ap[:, ql, :], E1i.ap[:, ql, :]
                cx.op("dve", lambda e, er=er: e.tensor_tensor(out=a1.ap, in0=xr.ap, in1=er, op=ALU.mult), reads=[xr, E1r], writes=[a1])
                cx.op("pool", lambda e, ei=ei: e.tensor_tensor(out=a2.ap, in0=xi.ap, in1=ei, op=ALU.mult), reads=[xi, E1i], writes=[a2])
                cx.op("dve", lambda e: e.tensor_tensor(out=vr.ap, in0=a1.ap, in1=a2.ap, op=ALU.add), reads=[a1, a2], writes=[vr])
                cx.op("dve", lambda e, er=er: e.tensor_tensor(out=a1.ap, in0=xi.ap, in1=er, op=ALU.mult), reads=[xi, E1r], writes=[a1])
                cx.op("pool", lambda e, ei=ei: e.tensor_tensor(out=a2.ap, in0=xr.ap, in1=ei, op=ALU.mult), reads=[xr, E1i], writes=[a2])
                cx.op("dve", lambda e: e.tensor_tensor(out=vi.ap, in0=a1.ap, in1=a2.ap, op=ALU.subtract), reads=[a1, a2], writes=[vi])
                rc = rho.ap[:, q_:q_ + 1].to_broadcast([128, TB])
                cx.op("dve", lambda e, rc=rc: e.tensor_tensor_scan(out=sr.ap, data0=rc, data1=vr.ap, initial=0.0, op0=ALU.mult, op1=ALU.add),
                      reads=[rho, vr], writes=[sr])
                cx.op("dve", lambda e, rc=rc: e.tensor_tensor_scan(out=si.ap, data0=rc, data1=vi.ap, initial=0.0, op0=ALU.mult, op1=ALU.add),
                      reads=[rho, vi], writes=[si])
                cx.op("dve", lambda e, er=er: e.tensor_tensor(out=a1.ap, in0=sr.ap, in1=er, op=ALU.mult), reads=[sr, E1r], writes=[a1])
                cx.op("pool", lambda e, ei=ei: e.tensor_tensor(out=a2.ap, in0=si.ap, in1=ei, op=ALU.mult), reads=[si, E1i], writes=[a2])
                cx.op("pool", lambda e, ql=ql: e.memset(Hr.ap[:, ql, 0:1], 0.0), writes=[Hr])
                cx.op("pool", lambda e, ql=ql: e.memset(Hi.ap[:, ql, 0:1], 0.0), writes=[Hi])
                cx.op("dve", lambda e, ql=ql: e.tensor_tensor(out=Hr.ap[:, ql, 1:TB], in0=a1.ap[:, 0:TB - 1], in1=a2.ap[:, 0:TB - 1], op=ALU.subtract),
                      reads=[a1, a2], writes=[Hr])
                cx.op("dve", lambda e, ei=ei: e.tensor_tensor(out=a1.ap, in0=sr.ap, in1=ei, op=ALU.mult), reads=[sr, E1i], writes=[a1])
                cx.op("pool", lambda e, er=er: e.tensor_tensor(out=a2.ap, in0=si.ap, in1=er, op=ALU.mult), reads=[si, E1r], writes=[a2])
                cx.op("dve", lambda e, ql=ql: e.tensor_tensor(out=Hi.ap[:, ql, 1:TB], in0=a1.ap[:, 0:TB - 1], in1=a2.ap[:, 0:TB - 1], op=ALU.add),
                      reads=[a1, a2], writes=[Hi])
            for gl_ in range(8):
                g = qt * 8 + gl_
                ql, half = gl_ // 2, gl_ % 2
                q_ = qt * NQ + ql
                ps = PS[npsu % 4]
                npsu += 1
                lo, hi = half * 64, half * 64 + 64

                def mm(e, ps=ps, g=g, gl_=gl_, ql=ql, q_=q_, lo=lo, hi=hi):
                    e.matmul(ps.ap, lhsT=Mw.ap[:, g, :], rhs=U3.ap[:, gl_, :], start=True, stop=False)
                    e.matmul(ps.ap, lhsT=W2r.ap[lo:hi, q_, :], rhs=Hr.ap[lo:hi, ql, :], start=False, stop=False)
                    return e.matmul(ps.ap, lhsT=W2i.ap[lo:hi, q_, :], rhs=Hi.ap[lo:hi, ql, :], start=False, stop=True)
                cx.op("pe", mm, reads=[Mw, U3, W2r, W2i, Hr, Hi], writes=[ps])
                cx.op("act", lambda e, ps=ps, gl_=gl_: e.activation(out=Y3.ap[:, gl_, :], in_=ps.ap, func=AF.Identity), reads=[ps], writes=[Y3])
            for tau in range(8):
                ps = PS[npsu % 4]
                npsu += 1

                def mm(e, ps=ps, tau=tau):
                    for gg in range(8):
                        i = e.matmul(ps.ap, lhsT=SelTS.ap[:, gg, tau, :], rhs=Y3.ap[:, gg, :], start=(gg == 0), stop=(gg == 7))
                    return i
                cx.op("pe", mm, reads=[SelTS, Y3], writes=[ps])
                cx.op("act", lambda e, ps=ps, qt=qt, tau=tau: e.activation(out=gb.ap[:, qt, tau::8], in_=ps.ap, func=AF.Gelu_apprx_tanh),
                      reads=[ps], writes=[gb])
            cx.barrier(skip_sw=True)
            s2.close()
        wgl = sc.buf([128, 4, 512], BF16, "wgl")
        cx.dma("sp", wgl.ap, wb_glu[l].rearrange("(kc p) n -> p kc n", p=128), reads=B_wb[("glu", l)], writes=[wgl])
        sg = [sc.buf([128, TB], F32, f"sg{i}") for i in range(2)]
        yst = [sc.buf([128, 4, TB], BF16, f"yst{i}") for i in range(2)]
        n = 0
        for t in range(NB):
            ys_ = yst[t % 2]
            for ct in range(4):
                ps = PS[n % 4]
                s_ = sg[n % 2]
                n += 1

                def mm(e, ps=ps, ct=ct, t=t):
                    for kc in range(4):
                        i = e.matmul(ps.ap, lhsT=wgl.ap[:, kc, ct * 128:(ct + 1) * 128], rhs=gb.ap[:, kc, blk(t)], start=(kc == 0), stop=(kc == 3))
                    return i
                cx.op("pe", mm, reads=[wgl, gb], writes=[ps])
                cx.op("act", lambda e, ps=ps, s_=s_, ct=ct: e.activation(out=s_.ap, in_=ps.ap, func=AF.Sigmoid, bias=bglu.ap[:, ct:ct + 1], scale=1.0),
                      reads=[ps, bglu], writes=[s_])
                cx.op("dve", lambda e, s_=s_, ys_=ys_, ct=ct, t=t: e.tensor_tensor(out=ys_.ap[:, ct, :], in0=gb.ap[:, ct, blk(t)], in1=s_.ap, op=ALU.mult),
                      reads=[gb, s_], writes=[ys_])
            cx.dma("sp", YS[0:512, blk(t)].rearrange("(c p) t -> p c t", p=128), ys_.ap, reads=[ys_], writes=[B_YS[0]])
        cx.barrier(skip_sw=True)
        sc.close()
        ws.close()

    def layer_norm(y, gcol, bcol, outb, tmp, stat):
        pm, pq = PS[6], PS[7]
        cx.op("act", lambda e: e.activation(out=tmp.ap, in_=y.ap, func=AF.Square), reads=[y], writes=[tmp])

        def mm1(e):
            for kc in range(8):
                i = e.matmul(pm.ap, lhsT=onesD.ap, rhs=y.ap[:, kc, :], start=(kc == 0), stop=(kc == 7))
            return i

        def mm2(e):
            for kc in range(8):
                i = e.matmul(pq.ap, lhsT=onesD.ap, rhs=tmp.ap[:, kc, :], start=(kc == 0), stop=(kc == 7))
            return i
        cx.op("pe", mm1, reads=[onesD, y], writes=[pm])
        cx.op("pe", mm2, reads=[onesD, tmp], writes=[pq])
        mean, rstd = stat
        cx.op("act", lambda e: e.activation(out=mean.ap, in_=pm.ap, func=AF.Identity), reads=[pm], writes=[mean])
        cx.op("act", lambda e: e.activation(out=rstd.ap, in_=pm.ap, func=AF.Square), reads=[pm], writes=[rstd])
        cx.op("dve", lambda e: e.tensor_tensor(out=rstd.ap, in0=pq.ap, in1=rstd.ap, op=ALU.subtract), reads=[pq, rstd], writes=[rstd])
        cx.op("dve", lambda e: e.tensor_scalar(out=rstd.ap, in0=rstd.ap, scalar1=0.0, scalar2=LN_EPS, op0=ALU.max, op1=ALU.add),
              reads=[rstd], writes=[rstd])
        cx.op("act", lambda e: e.activation(out=rstd.ap, in_=rstd.ap, func=AF.Sqrt), reads=[rstd], writes=[rstd])
        cx.op("dve", lambda e: e.reciprocal(out=rstd.ap, in_=rstd.ap), reads=[rstd], writes=[rstd])
        mb = mean.ap.unsqueeze(1).to_broadcast([128, 8, TB])
        rb_ = rstd.ap.unsqueeze(1).to_broadcast([128, 8, TB])
        cx.op("dve", lambda e: e.tensor_tensor(out=tmp.ap, in0=y.ap, in1=mb, op=ALU.subtract), reads=[y, mean], writes=[tmp])
        cx.op("pool", lambda e: e.tensor_tensor(out=tmp.ap, in0=tmp.ap, in1=rb_, op=ALU.mult), reads=[tmp, rstd], writes=[tmp])
        for kc in range(8):
            cx.op("dve", lambda e, kc=kc: e.tensor_scalar(out=y.ap[:, kc, :], in0=tmp.ap[:, kc, :], scalar1=gcol.ap[:, kc:kc + 1],
                                                          scalar2=bcol.ap[:, kc:kc + 1], op0=ALU.mult, op1=ALU.add),
                  reads=[tmp, gcol, bcol], writes=[y])
        cx.op("act", lambda e: e.activation(out=outb.ap, in_=y.ap, func=AF.Identity), reads=[y], writes=[outb])

    def load_cols(sc, src, l, name, n=8):
        b = sc.buf([128, n], F32, name)
        cx.dma("sp", b.ap, src[l].rearrange("(c p) -> p c", p=128), writes=[b], allow_slow_non_contiguous=True)
        return b

    def phase_mix(l):
        sc = Scope(cx)
        wbr = sc.buf([128, 12, D], BF16, "wbr")
        cx.dma("sp", wbr.ap, wb_branch[l].rearrange("(j p) n -> p j n", p=128), reads=B_wb[("branch", l)], writes=[wbr])
        wgt = sc.buf([128, 8, 3072], BF16, "wgt")
        for k3 in range(3):
            cx.dma("sp", wgt.ap[:, :, k3 * 1024:(k3 + 1) * 1024], kview(wb_in[l])[:, :, 3080 + k3 * 1024:3080 + (k3 + 1) * 1024],
                   reads=B_wb[("in", l)], writes=[wgt])
        wo = sc.buf([128, 8, D], BF16, "wo")
        cx.dma("sp", wo.ap, kview(wb_out[l]), reads=B_wb[("out", l)], writes=[wo])
        bg = load_cols(sc, b_gate, l, "bg", 24)
        g1 = load_cols(sc, ln1_g, l, "g1")
        b1 = load_cols(sc, ln1_b, l, "b1")
        xb = sc.buf([128, 8, TB], BF16, "mxb")
        xr = [sc.buf([128, TB], F32, f"mxr{i}") for i in range(2)]
        ys = sc.buf([128, 12, TB], BF16, "mys")
        mixb = sc.buf([128, 8, TB], BF16, "mixb")
        yv = sc.buf([128, 8, TB], F32, "yv")
        tmp = sc.buf([128, 8, TB], F32, "lntmp")
        stat = (sc.buf([128, TB], F32, "mean"), sc.buf([128, TB], F32, "rstd"))
        gsb = [sc.buf([128, TB], F32, f"gsb{i}") for i in range(3)]
        acc = [sc.buf([128, TB], F32, f"acc{i}") for i in range(2)]
        xres_src = xT if l == 0 else XRES
        n = 0
        nr = 0
        for t in range(NB):
            x_, y_ = xb, ys
            cx.dma("sp", x_.ap, kview(XBF)[:, :, blk(t)], reads=[B_XBF[t]], writes=[x_])
            cx.dma("sp", y_.ap, YS.rearrange("(j p) t -> p j t", p=128)[:, :, blk(t)], reads=B_YS, writes=[y_])
            for dc in range(8):
                a_ = acc[dc % 2]
                for k3 in range(3):
                    pp, pg = PS[(2 * n) % 6], PS[(2 * n + 1) % 6]
                    g_ = gsb[n % 3]
                    n += 1

                    def mmp(e, pp=pp, k3=k3, dc=dc, y_=y_):
                        for kc in range(4):
                            i = e.matmul(pp.ap, lhsT=wbr.ap[:, k3 * 4 + kc, dc * 128:(dc + 1) * 128], rhs=y_.ap[:, k3 * 4 + kc, :],
                                         start=(kc == 0), stop=(kc == 3))
                        return i

                    def mmg(e, pg=pg, k3=k3, dc=dc, x_=x_):
                        c0 = k3 * 1024 + dc * 128
                        for kc in range(8):
                            i = e.matmul(pg.ap, lhsT=wgt.ap[:, kc, c0:c0 + 128], rhs=x_.ap[:, kc, :], start=(kc == 0), stop=(kc == 7))
                        return i
                    cx.op("pe", mmg, reads=[wgt, x_], writes=[pg])
                    cx.op("pe", mmp, reads=[wbr, y_], writes=[pp])
                    cx.op("act", lambda e, pg=pg, g_=g_, k3=k3, dc=dc: e.activation(out=g_.ap, in_=pg.ap, func=AF.Sigmoid,
                                                                                   bias=bg.ap[:, k3 * 8 + dc:k3 * 8 + dc + 1], scale=1.0),
                          reads=[pg, bg], writes=[g_])
                    if k3 == 0:
                        cx.op("dve", lambda e, pp=pp, g_=g_, a_=a_: e.tensor_tensor(out=a_.ap, in0=pp.ap, in1=g_.ap, op=ALU.mult),
                              reads=[pp, g_], writes=[a_])
                    else:
                        cx.op("dve", lambda e, pp=pp, g_=g_: e.tensor_tensor(out=g_.ap, in0=pp.ap, in1=g_.ap, op=ALU.mult),
                              reads=[pp, g_], writes=[g_])
                        if k3 == 1:
                            cx.op("pool", lambda e, g_=g_, a_=a_: e.tensor_tensor(out=a_.ap, in0=a_.ap, in1=g_.ap, op=ALU.add),
                                  reads=[a_, g_], writes=[a_])
                        else:
                            cx.op("pool", lambda e, g_=g_, a_=a_, dc=dc: e.tensor_tensor(out=mixb.ap[:, dc, :], in0=a_.ap, in1=g_.ap, op=ALU.add),
                                  reads=[a_, g_], writes=[mixb])
            for dc in range(8):
                po = PS[6 + dc % 2]
                r_ = xr[nr % 2]
                nr += 1
                cx.dma("sp", r_.ap, xres_src[dc * 128:(dc + 1) * 128, blk(t)], reads=[B_XRES[t]], writes=[r_])

                def mmo(e, po=po, dc=dc):
                    for kc in range(8):
                        i = e.matmul(po.ap, lhsT=wo.ap[:, kc, dc * 128:(dc + 1) * 128], rhs=mixb.ap[:, kc, :], start=(kc == 0), stop=(kc == 7))
                    return i
                cx.op("pe", mmo, reads=[wo, mixb], writes=[po])
                cx.op("dve", lambda e, po=po, dc=dc, r_=r_: e.scalar_tensor_tensor(out=yv.ap[:, dc, :], in0=r_.ap, scalar=float(ALPHA),
                                                                                 in1=po.ap, op0=ALU.mult, op1=ALU.add),
                      reads=[r_, po], writes=[yv])
            layer_norm(yv, g1, b1, mixb, tmp, stat)
            cx.dma("sp", kview(X1RES)[:, :, blk(t)], yv.ap, reads=[yv], writes=[B_X1RES[t]])
            cx.dma("sp", kview(X1BF)[:, :, blk(t)], mixb.ap, reads=[mixb], writes=[B_X1BF[t]])
        cx.barrier(skip_sw=True)
        sc.close()

    def phase_ffn(l, last):
        sc = Scope(cx)
        wdn = sc.buf([128, NHC, D], BF16, "wdn")
        cx.dma("sp", wdn.ap, wb_d[l].rearrange("(j p) n -> p j n", p=128), reads=B_wb[("d", l)], writes=[wdn])
        g2 = load_cols(sc, ln2_g, l, "g2")
        b2 = load_cols(sc, ln2_b, l, "b2")
        xb = sc.buf([128, 8, TB], BF16, "fxb")
        hT = sc.buf([128, NHC, TB], BF16, "hT")
        wgu = [sc.buf([128, 2, 8, 256], BF16, f"wgu{i}") for i in range(2)]
        sl = [sc.buf([128, TB], F32, f"sl{i}") for i in range(2)]
        xr = [sc.buf([128, TB], F32, f"fxr{i}") for i in range(2)]
        yv = sc.buf([128, 8, TB], F32, "fyv")
        tmp = sc.buf([128, 8, TB], F32, "flntmp")
        o16 = sc.buf([128, 8, TB], BF16, "fo16")
        stat = (sc.buf([128, TB], F32, "fmean"), sc.buf([128, TB], F32, "frstd"))
        n = 0
        nw = 0
        nr = 0
        for t in range(NB):
            cx.dma("sp", xb.ap, kview(X1BF)[:, :, blk(t)], reads=[B_X1BF[t]], writes=[xb])
            for hp in range(NHC // 2):
                w = wgu[nw % 2]
                nw += 1
                cx.dma("sp", w.ap[:, 0], kview(wb_g[l])[:, :, hp * 256:(hp + 1) * 256], reads=B_wb[("g", l)], writes=[w])
                cx.dma("sp", w.ap[:, 1], kview(wb_u[l])[:, :, hp * 256:(hp + 1) * 256], reads=B_wb[("u", l)], writes=[w])
                for hh in range(2):
                    hc = hp * 2 + hh
                    pg, pu = PS[(2 * n) % 6], PS[(2 * n + 1) % 6]
                    s_ = sl[n % 2]
                    n += 1

                    def mmg(e, pg=pg, w=w, hh=hh):
                        for kc in range(8):
                            i = e.matmul(pg.ap, lhsT=w.ap[:, 0, kc, hh * 128:(hh + 1) * 128], rhs=xb.ap[:, kc, :], start=(kc == 0), stop=(kc == 7))
                        return i

                    def mmu(e, pu=pu, w=w, hh=hh):
                        for kc in range(8):
                            i = e.matmul(pu.ap, lhsT=w.ap[:, 1, kc, hh * 128:(hh + 1) * 128], rhs=xb.ap[:, kc, :], start=(kc == 0), stop=(kc == 7))
                        return i
                    cx.op("pe", mmg, reads=[w, xb], writes=[pg])
                    cx.op("pe", mmu, reads=[w, xb], writes=[pu])
                    cx.op("act", lambda e, pg=pg, s_=s_: e.activation(out=s_.ap, in_=pg.ap, func=AF.Silu), reads=[pg], writes=[s_])
                    cx.op("dve", lambda e, pu=pu, s_=s_, hc=hc: e.tensor_tensor(out=hT.ap[:, hc, :], in0=pu.ap, in1=s_.ap, op=ALU.mult),
                          reads=[pu, s_], writes=[hT])
            for dc in range(8):
                po = PS[6 + dc % 2]
                r_ = xr[nr % 2]
                nr += 1
                cx.dma("sp", r_.ap, X1RES[dc * 128:(dc + 1) * 128, blk(t)], reads=[B_X1RES[t]], writes=[r_])

                def mmo(e, po=po, dc=dc):
                    for hc in range(NHC):
                        i = e.matmul(po.ap, lhsT=wdn.ap[:, hc, dc * 128:(dc + 1) * 128], rhs=hT.ap[:, hc, :], start=(hc == 0), stop=(hc == NHC - 1))
                    return i
                cx.op("pe", mmo, reads=[wdn, hT], writes=[po])
                cx.op("dve", lambda e, po=po, dc=dc, r_=r_: e.scalar_tensor_tensor(out=yv.ap[:, dc, :], in0=r_.ap, scalar=float(ALPHA),
                                                                                 in1=po.ap, op0=ALU.mult, op1=ALU.add),
                      reads=[r_, po], writes=[yv])
            layer_norm(yv, g2, b2, o16, tmp, stat)
            if last:
                cx.dma("sp", kview(outT)[:, :, blk(t)], yv.ap, reads=[yv], writes=[B_OUT])
            else:
                cx.dma("sp", kview(XRES)[:, :, blk(t)], yv.ap, reads=[yv], writes=[B_XRES[t]])
                cx.dma("sp", kview(XBF)[:, :, blk(t)], o16.ap, reads=[o16], writes=[B_XBF[t]])
        cx.barrier(skip_sw=True)
        sc.close()

    cx.barrier(skip_sw=True)
    for l in range(n_layers):
        if stop_after == ("setup", l):
            break
        phase_proj(l)
        if l + 1 < n_layers:
            convert_layer(l + 1)
        if stop_after == ("proj", l):
            break
        phase_attn(l)
        if stop_after == ("attn", l):
            break
        phase_lru(l)
        if stop_after == ("lru", l):
            break
        phase_s5(l)
        if stop_after == ("s5", l):
            break
        phase_mix(l)
        if stop_after == ("mix", l):
            break
        phase_ffn(l, last=(l == n_layers - 1))
    cx.barrier()
    return nc


INPUT_ORDER = ["w_in", "w_branch", "w_out", "w_ffn_gate", "w_ffn_up", "w_ffn_down", "s5_w_glu", "lru_w_a", "lru_w_x",
               "b_f", "b_gate", "s5_a_re", "s5_a_im", "s5_log_dt", "s5_b_re", "s5_b_im", "s5_c_re", "s5_c_im", "s5_d",
               "s5_b_glu", "lru_conv_w", "lru_conv_b", "lru_b_a", "lru_b_x", "lru_lambda", "ln1_g", "ln1_b", "ln2_g", "ln2_b"]


def layout_inputs(inputs, n_layers=DEPTH):
    f = lambda a: np.ascontiguousarray(np.asarray(a, dtype=np.float32)[:n_layers])
    shared = {}
    for k in INPUT_ORDER:
        a = f(inputs[k])
        if k == "w_branch":
            a = a.reshape(n_layers, 1536, D)
        elif k in ("s5_c_re", "s5_c_im"):
            a = a.reshape(n_layers, 512, 64)
        elif k in ("lru_b_a", "lru_b_x"):
            a = a.reshape(n_layers, 512)
        shared[k] = np.ascontiguousarray(a)
    return shared


def kernel(**inputs):
    x = np.asarray(inputs["x"], dtype=np.float32)
    shared = layout_inputs(inputs)
    nc = bass.Bass("TRN2", target_bir_lowering=False)
    build(nc)
    in_maps = []
    for c in range(8):
        m = dict(shared)
        m["xT"] = np.ascontiguousarray(x[c % 4].T)
        in_maps.append(m)
    res = run_bass_kernel_spmd(nc, in_maps, core_ids=list(range(8)))
    out = np.stack([np.ascontiguousarray(res.results[b]["outT"].T) for b in range(4)], axis=0)
    return out.astype(np.float32)
```

```python
import math
import numpy as np
import concourse.bass as bass
import concourse.mybir as mybir
from concourse.bass_utils import run_bass_kernel_spmd

F32 = mybir.dt.float32
BF16 = mybir.dt.bfloat16
AF = mybir.ActivationFunctionType
ALU = mybir.AluOpType

D = 1024
L = 4096
DEPTH = 4
NB = 8
TB = 512
IN_TOTAL = 6152
FFN = 2816
NHC = 22
ALPHA = (2.0 * DEPTH) ** 0.25
LN_EPS = 1e-5
MAGIC = 12582912.0
TWO_PI = 2.0 * math.pi


class Buf:
    __slots__ = ("ap", "w", "r", "name")

    def __init__(self, ap, name=""):
        self.ap = ap
        self.w = {}
        self.r = {}
        self.name = name


class Ctx:
    def __init__(self, nc):
        self.nc = nc
        self.E = {"pe": nc.tensor, "act": nc.scalar, "dve": nc.vector, "pool": nc.gpsimd, "sp": nc.sync}
        self.sem = {}
        self.cnt = {}
        self.nsem = 0
        for e in ("pe", "act", "dve", "pool"):
            self._new_sem(e)
        self.seen = {e: {} for e in self.E}
        self.dma_sems = {"sp": [nc.alloc_semaphore(f"dq{i}") for i in range(60)],
                         "pool": [nc.alloc_semaphore(f"dqs{i}") for i in range(16)]}
        self.dma_cnt = {k: [0] * len(v) for k, v in self.dma_sems.items()}
        self.dma_rr = {"sp": 0, "pool": 0}
        self.semobj = {}
        self.uid = 0

    def _new_sem(self, e):
        s = self.nc.alloc_semaphore(f"s_{e}_{self.nsem}")
        self.nsem += 1
        self.sem[e] = s
        self.cnt[e] = 0

    def _key(self, s):
        k = id(s)
        self.semobj[k] = s
        return k

    def _wait(self, e, deps):
        seen = self.seen[e]
        for k, v in deps.items():
            if seen.get(k, 0) >= v:
                continue
            self.E[e].wait_ge(self.semobj[k], v)
            seen[k] = v

    @staticmethod
    def _merge(dst, src):
        for k, v in src.items():
            if dst.get(k, 0) < v:
                dst[k] = v

    def _deps(self, reads, writes):
        deps = {}
        for b in reads:
            self._merge(deps, b.w)
        for b in writes:
            self._merge(deps, b.w)
            self._merge(deps, b.r)
        return deps

    def _commit(self, tok, reads, writes):
        for b in reads:
            self._merge(b.r, tok)
        for b in writes:
            b.w = dict(tok)
            b.r = {}

    def op(self, e, emit, reads=(), writes=()):
        self._wait(e, self._deps(reads, writes))
        ins = emit(self.E[e])
        if self.cnt[e] >= 30000:
            self._new_sem(e)
        s = self.sem[e]
        self.cnt[e] += 1
        ins.then_inc(s, 1)
        tok = {self._key(s): self.cnt[e]}
        self._commit(tok, reads, writes)
        return tok

    def dma(self, e, out, in_, reads=(), writes=(), **kw):
        self._wait(e, self._deps(reads, writes))
        sems, cnts = self.dma_sems[e], self.dma_cnt[e]
        i = self.dma_rr[e]
        self.dma_rr[e] = (i + 1) % len(sems)
        if cnts[i] >= 30000:
            sems[i] = self.nc.alloc_semaphore(f"dqx{self.nsem}")
            self.nsem += 1
            cnts[i] = 0
        s = sems[i]
        if cnts[i] > 0:
            self._wait(e, {self._key(s): cnts[i]})
        cnts[i] += 16
        self.E[e].dma_start(out=out, in_=in_, **kw).then_inc(s, 16)
        tok = {self._key(s): cnts[i]}
        self._commit(tok, reads, writes)
        return tok

    def barrier(self, skip_sw=False):
        allt = {}
        for e in ("pe", "act", "dve", "pool"):
            if self.cnt[e] > 0:
                allt[self._key(self.sem[e])] = self.cnt[e]
        for q in self.dma_sems:
            for i, s in enumerate(self.dma_sems[q]):
                if self.dma_cnt[q][i] > 0 and not (q == "pool" and skip_sw):
                    allt[self._key(s)] = self.dma_cnt[q][i]
        for e in self.E:
            self._wait(e, allt)


class Scope:
    def __init__(self, cx):
        self.cx = cx
        self.guards = []

    def sb(self, shape, dt=F32, name=None):
        self.cx.uid += 1
        g = self.cx.nc.sbuf_tensor(f"{name or 't'}_{self.cx.uid}", list(shape), dt)
        t = g.__enter__()
        self.guards.append(g)
        return t.ap()

    def buf(self, shape, dt=F32, name=None):
        return Buf(self.sb(shape, dt, name), name or "")

    def close(self):
        for g in reversed(self.guards):
            g.__exit__(None, None, None)
        self.guards = []


def build(nc, n_layers=DEPTH, dbg=False, stop_after=None):
    cx = Ctx(nc)
    kind_dbg = "ExternalOutput" if dbg else "Internal"

    def din(name, shape):
        return nc.dram_tensor(name, list(shape), F32, kind="ExternalInput").ap()

    def dscr(name, shape, dt, k="Internal"):
        return nc.dram_tensor(name, list(shape), dt, kind=k).ap()

    xT = din("xT", [D, L])
    w_in = din("w_in", [n_layers, D, IN_TOTAL])
    w_branch = din("w_branch", [n_layers, 1536, D])
    w_out = din("w_out", [n_layers, D, D])
    w_g = din("w_ffn_gate", [n_layers, D, FFN])
    w_u = din("w_ffn_up", [n_layers, D, FFN])
    w_d = din("w_ffn_down", [n_layers, FFN, D])
    w_glu = din("s5_w_glu", [n_layers, 512, 512])
    lru_w_a = din("lru_w_a", [n_layers, 8, 64, 64])
    lru_w_x = din("lru_w_x", [n_layers, 8, 64, 64])
    b_f = din("b_f", [n_layers, 8])
    b_gate = din("b_gate", [n_layers, 3072])
    s5_a_re = din("s5_a_re", [n_layers, 32, 64])
    s5_a_im = din("s5_a_im", [n_layers, 32, 64])
    s5_log_dt = din("s5_log_dt", [n_layers, 32])
    s5_b_re = din("s5_b_re", [n_layers, 32, 64, 16])
    s5_b_im = din("s5_b_im", [n_layers, 32, 64, 16])
    s5_c_re = din("s5_c_re", [n_layers, 512, 64])
    s5_c_im = din("s5_c_im", [n_layers, 512, 64])
    s5_d = din("s5_d", [n_layers, 512])
    s5_b_glu = din("s5_b_glu", [n_layers, 512])
    lru_conv_w = din("lru_conv_w", [n_layers, 4, 512])
    lru_conv_b = din("lru_conv_b", [n_layers, 512])
    lru_b_a = din("lru_b_a", [n_layers, 512])
    lru_b_x = din("lru_b_x", [n_layers, 512])
    lru_lambda = din("lru_lambda", [n_layers, 512])
    ln1_g = din("ln1_g", [n_layers, D])
    ln1_b = din("ln1_b", [n_layers, D])
    ln2_g = din("ln2_g", [n_layers, D])
    ln2_b = din("ln2_b", [n_layers, D])
    outT = nc.dram_tensor("outT", [D, L], F32, kind="ExternalOutput").ap()

    wb_in = dscr("wb_in", [n_layers, D, IN_TOTAL], BF16)
    wb_branch = dscr("wb_branch", [n_layers, 1536, D], BF16)
    wb_out = dscr("wb_out", [n_layers, D, D], BF16)
    wb_g = dscr("wb_g", [n_layers, D, FFN], BF16)
    wb_u = dscr("wb_u", [n_layers, D, FFN], BF16)
    wb_d = dscr("wb_d", [n_layers, FFN, D], BF16)
    wb_glu = dscr("wb_glu", [n_layers, 512, 512], BF16)
    XBF = dscr("XBF", [D, L], BF16)
    XRES = dscr("XRES", [D, L], F32, kind_dbg)
    X1BF = dscr("X1BF", [D, L], BF16)
    X1RES = dscr("X1RES", [D, L], F32, kind_dbg)
    UT = dscr("UT", [512, L], BF16, kind_dbg)
    XL = dscr("XL", [512, L], F32, kind_dbg)
    GL = dscr("GL", [512, L], F32, kind_dbg)
    QA = dscr("QA", [8, 70, L], BF16, kind_dbg)
    KA = dscr("KA", [8, 70, L], BF16, kind_dbg)
    VA = dscr("VA", [8, 128, 32, 65], BF16, kind_dbg)
    YS = dscr("YS", [1536, L], BF16, kind_dbg)

    B_wb = {}
    for nm in ("in", "branch", "out", "g", "u", "d", "glu"):
        for l in range(n_layers):
            B_wb[(nm, l)] = []
    B_XBF = [Buf(None, f"XBF{t}") for t in range(NB)]
    B_XRES = [Buf(None, f"XRES{t}") for t in range(NB)]
    B_X1BF = [Buf(None, f"X1BF{t}") for t in range(NB)]
    B_X1RES = [Buf(None, f"X1RES{t}") for t in range(NB)]
    B_UT = Buf(None, "UT")
    B_XL = Buf(None, "XL")
    B_GL = Buf(None, "GL")
    B_QA = Buf(None, "QA")
    B_KA = Buf(None, "KA")
    B_VA = Buf(None, "VA")
    B_YS = [Buf(None, f"YS{k}") for k in range(3)]
    B_OUT = Buf(None, "out")

    PS = [Buf(nc.alloc_psum_tensor(f"psb{i}", [128, 512], F32).ap(), f"ps{i}") for i in range(8)]

    cs = Scope(cx)
    identf = cs.buf([128, 128], F32, "identf")
    identb = cs.buf([128, 128], BF16, "identb")
    onesD = cs.buf([128, 128], F32, "onesD")
    ones1 = cs.buf([128, 64], F32, "ones1")
    mask8 = cs.buf([128, 128], F32, "mask8")
    tri = cs.buf([128, 128], BF16, "tri")
    SELD = dscr("SELD", [2, 128, 8, 8, 128], BF16)
    B_SELD = Buf(None, "SELD")
    cs0 = Scope(cx)
    Sel = cs0.buf([128, 8, 8, 128], BF16, "Sel")
    SelT = cs0.buf([128, 8, 8, 128], BF16, "SelT")

    def pool_fill(buf, val):
        cx.op("pool", lambda e: e.memset(buf.ap, val), writes=[buf])

    def pool_sel(buf, ap, pattern, cmp, base, cm):
        cx.op("pool", lambda e: e.affine_select(out=ap, in_=ap, pattern=pattern, compare_op=cmp, fill=0.0,
                                                base=base, channel_multiplier=cm), reads=[buf], writes=[buf])

    pool_fill(identf, 1.0)
    pool_sel(identf, identf.ap, [[1, 128]], ALU.is_equal, 0, -1)
    pool_fill(identb, 1.0)
    pool_sel(identb, identb.ap, [[1, 128]], ALU.is_equal, 0, -1)
    pool_fill(onesD, 1.0 / D)
    pool_fill(ones1, 1.0)
    pool_fill(mask8, 1.0)
    pool_sel(mask8, mask8.ap.rearrange("p (t c) -> p t c", c=16), [[16, 8], [0, 16]], ALU.is_ge, 15, -1)
    pool_fill(tri, 1.0)
    pool_sel(tri, tri.ap, [[1, 128]], ALU.is_ge, 0, -1)
    pool_fill(Sel, 1.0)
    for gg in range(8):
        a4 = Sel.ap[:, gg, :, :].rearrange("p t (u c) -> p t u c", c=16)
        pool_sel(Sel, a4, [[0, 8], [0, 8], [-1, 16]], ALU.is_equal, -16 * gg, 1)
        pool_sel(Sel, a4, [[-1, 8], [1, 8], [0, 16]], ALU.is_equal, 0, 0)
    pool_fill(SelT, 1.0)
    for gg in range(8):
        a3 = SelT.ap[:, gg, :, :]
        pool_sel(SelT, a3, [[16, 8], [1, 128]], ALU.is_equal, -16 * gg, -1)
        pool_sel(SelT, a3, [[-16, 8], [0, 128]], ALU.is_ge, 0, 1)
        pool_sel(SelT, a3, [[16, 8], [0, 128]], ALU.is_ge, 15, -1)

    cx.dma("sp", SELD[0], Sel.ap, reads=[Sel], writes=[B_SELD])
    cx.dma("sp", SELD[1], SelT.ap, reads=[SelT], writes=[B_SELD])
    cx.barrier()
    cs0.close()

    def convert(src, dst, rows, key):
        r = 0
        while r < rows:
            n = min(128, rows - r)
            bch = Buf(None, "wbch")
            B_wb[key].append(bch)
            cx.dma("pool", dst[r:r + n, :], src[r:r + n, :], writes=[bch])
            r += n

    def convert_layer(l):
        convert(w_in[l], wb_in[l], D, ("in", l))
        convert(w_glu[l], wb_glu[l], 512, ("glu", l))
        convert(w_branch[l], wb_branch[l], 1536, ("branch", l))
        convert(w_out[l], wb_out[l], D, ("out", l))
        convert(w_g[l], wb_g[l], D, ("g", l))
        convert(w_u[l], wb_u[l], D, ("u", l))
        convert(w_d[l], wb_d[l], FFN, ("d", l))

    for t in range(NB):
        for kc in range(8):
            cx.dma("pool", XBF[kc * 128:(kc + 1) * 128, t * TB:(t + 1) * TB],
                   xT[kc * 128:(kc + 1) * 128, t * TB:(t + 1) * TB], writes=[B_XBF[t]])

    convert_layer(0)

    def kview(ap2d):
        return ap2d.rearrange("(kc p) n -> p kc n", p=128)

    def blk(t):
        return slice(t * TB, (t + 1) * TB)

    def phase_proj(l):
        sc_fg = Scope(cx)
        fgT = sc_fg.buf([8, L], F32, "fgT")
        sc = Scope(cx)
        xb = [sc.buf([128, 8, TB], BF16, f"xb{t}") for t in range(NB)]
        for t in range(NB):
            cx.dma("sp", xb[t].ap, kview(XBF)[:, :, blk(t)], reads=[B_XBF[t]], writes=[xb[t]])
        wt = [sc.buf([128, 8, 512], BF16, f"wt{i}") for i in range(2)]
        wfg = sc.buf([128, 8, 8], BF16, "wfg")
        cx.dma("sp", wfg.ap, kview(wb_in[l])[:, :, 3072:3080], reads=B_wb[("in", l)], writes=[wfg])
        st32 = [sc.buf([128, 4, TB], F32, f"st32_{i}") for i in range(2)]
        st16 = [sc.buf([128, 4, TB], BF16, f"st16_{i}") for i in range(2)]
        stqk = [sc.buf([64, 8, TB], BF16, f"stqk_{i}") for i in range(2)]
        vst = [sc.buf([128, 8, 65], BF16, f"vst_{i}") for i in range(2)]
        for v in vst:
            cx.op("pool", lambda e, v=v: e.memset(v.ap, 1.0), writes=[v])
        nps = 0
        nst = 0
        for cg in range(6):
            w = wt[cg % 2]
            cx.dma("sp", w.ap, kview(wb_in[l])[:, :, cg * 512:(cg + 1) * 512], reads=B_wb[("in", l)], writes=[w])
            if cg < 3:
                for t in range(NB):
                    stb = (st16 if cg == 0 else st32)[nst % 2]
                    nst += 1
                    for ct in range(4):
                        ps = PS[nps % 4]
                        nps += 1

                        def mm(e, ps=ps, ct=ct, t=t, w=w):
                            for kc in range(8):
                                i = e.matmul(ps.ap, lhsT=w.ap[:, kc, ct * 128:(ct + 1) * 128], rhs=xb[t].ap[:, kc, :],
                                             start=(kc == 0), stop=(kc == 7))
                            return i
                        cx.op("pe", mm, reads=[w, xb[t]], writes=[ps])
                        fn = AF.Gelu_apprx_tanh if cg == 2 else AF.Identity
                        cx.op("act", lambda e, ps=ps, stb=stb, ct=ct, fn=fn: e.activation(out=stb.ap[:, ct, :], in_=ps.ap, func=fn),
                              reads=[ps], writes=[stb])
                    dst, bd = [(UT, B_UT), (XL, B_XL), (GL, B_GL)][cg]
                    cx.dma("sp", dst.rearrange("(c p) t -> p c t", p=128)[:, :, blk(t)], stb.ap, reads=[stb], writes=[bd])
            elif cg < 5:
                for t in range(NB):
                    stb = stqk[nst % 2]
                    nst += 1
                    for h in range(8):
                        ps = PS[nps % 4]
                        nps += 1

                        def mm(e, ps=ps, h=h, t=t, w=w):
                            for kc in range(8):
                                i = e.matmul(ps.ap[0:64, :], lhsT=w.ap[:, kc, h * 64:(h + 1) * 64], rhs=xb[t].ap[:, kc, :],
                                             start=(kc == 0), stop=(kc == 7))
                            return i
                        cx.op("pe", mm, reads=[w, xb[t]], writes=[ps])
                        sc_ = 0.125 if cg == 3 else 1.0
                        cx.op("act", lambda e, ps=ps, stb=stb, h=h, sc_=sc_: e.activation(out=stb.ap[:, h, :], in_=ps.ap[0:64, :],
                                                                                         func=AF.Identity, scale=sc_),
                              reads=[ps], writes=[stb])
                    dst, bd = (QA, B_QA) if cg == 3 else (KA, B_KA)
                    cx.dma("sp", dst[:, 0:64, blk(t)].rearrange("h d t -> d h t"), stb.ap, reads=[stb], writes=[bd])
            else:
                for tt in range(32):
                    ps = PS[nps % 4]
                    nps += 1
                    t = tt // 4
                    vs = vst[tt % 2]

                    def mm(e, ps=ps, tt=tt, t=t, w=w):
                        o = (tt % 4) * 128
                        for kc in range(8):
                            i = e.matmul(ps.ap, lhsT=xb[t].ap[:, kc, o:o + 128], rhs=w.ap[:, kc, :],
                                         start=(kc == 0), stop=(kc == 7))
                        return i
                    cx.op("pe", mm, reads=[w, xb[t]], writes=[ps])
                    cx.op("act", lambda e, ps=ps, vs=vs: e.activation(out=vs.ap[:, :, 0:64], in_=ps.ap.rearrange("p (h d) -> p h d", d=64),
                                                                      func=AF.Identity), reads=[ps], writes=[vs])
                    cx.dma("sp", VA[:, :, tt, :].rearrange("h p e -> p h e"), vs.ap, reads=[vs], writes=[B_VA])
        for t in range(NB):
            ps = PS[nps % 4]
            nps += 1

            def mm(e, ps=ps, t=t):
                for kc in range(8):
                    i = e.matmul(ps.ap[0:8, :], lhsT=wfg.ap[:, kc, :], rhs=xb[t].ap[:, kc, :], start=(kc == 0), stop=(kc == 7))
                return i
            cx.op("pe", mm, reads=[wfg, xb[t]], writes=[ps])
            cx.op("act", lambda e, ps=ps, t=t: e.activation(out=fgT.ap[:, blk(t)], in_=ps.ap[0:8, :], func=AF.Identity),
                  reads=[ps], writes=[fgT])
        cx.barrier(skip_sw=True)
        sc.close()
        sc = Scope(cx)
        bf = sc.buf([8, 1], F32, "bf")
        cx.dma("sp", bf.ap, b_f[l].rearrange("(h o) -> h o", o=1), writes=[bf])
        nbf = sc.buf([8, 1], F32, "nbf")
        cx.op("dve", lambda e: e.tensor_scalar(out=nbf.ap, in0=bf.ap, scalar1=-1.0, scalar2=None, op0=ALU.mult), reads=[bf], writes=[nbf])
        one8 = sc.buf([8, 1], F32, "one8")
        cx.op("dve", lambda e: e.memset(one8.ap, 1.0), writes=[one8])
        ex = sc.buf([8, L], F32, "ex")
        cx.op("act", lambda e: e.activation(out=ex.ap, in_=fgT.ap, func=AF.Exp, bias=nbf.ap, scale=-1.0), reads=[fgT, nbf], writes=[ex])
        cx.op("act", lambda e: e.activation(out=ex.ap, in_=ex.ap, func=AF.Ln, bias=one8.ap, scale=1.0), reads=[ex, one8], writes=[ex])
        csum = sc.buf([8, L], F32, "csum")
        cx.op("dve", lambda e: e.tensor_tensor_scan(out=csum.ap, data0=one8.ap.to_broadcast([8, L]), data1=ex.ap, initial=0.0,
                                                    op0=ALU.mult, op1=ALU.add), reads=[ex, one8], writes=[csum])
        pcs = [sc.buf([8, L], BF16, f"pc{j}") for j in range(3)]
        ncs = [sc.buf([8, L], BF16, f"nc{j}") for j in range(3)]
        res = ex
        cx.op("dve", lambda e: e.tensor_copy(out=pcs[0].ap, in_=csum.ap), reads=[csum], writes=[pcs[0]])
        cx.op("dve", lambda e: e.tensor_tensor(out=res.ap, in0=csum.ap, in1=pcs[0].ap, op=ALU.subtract), reads=[csum, pcs[0]], writes=[res])
        cx.op("dve", lambda e: e.tensor_copy(out=pcs[1].ap, in_=res.ap), reads=[res], writes=[pcs[1]])
        cx.op("dve", lambda e: e.tensor_tensor(out=res.ap, in0=res.ap, in1=pcs[1].ap, op=ALU.subtract), reads=[res, pcs[1]], writes=[res])
        cx.op("dve", lambda e: e.tensor_copy(out=pcs[2].ap, in_=res.ap), reads=[res], writes=[pcs[2]])
        for j in range(3):
            cx.op("dve", lambda e, j=j: e.tensor_scalar(out=ncs[j].ap, in0=pcs[j].ap, scalar1=-1.0, scalar2=None, op0=ALU.mult),
                  reads=[pcs[j]], writes=[ncs[j]])
        onesb = sc.buf([8, L], BF16, "onesb")
        cx.op("pool", lambda e: e.memset(onesb.ap, 1.0), writes=[onesb])
        for j in range(3):
            cx.dma("sp", QA[:, 64 + j, :], ncs[j].ap, reads=[ncs[j]], writes=[B_QA])
            cx.dma("sp", QA[:, 67 + j, :], onesb.ap, reads=[onesb], writes=[B_QA])
            cx.dma("sp", KA[:, 64 + j, :], onesb.ap, reads=[onesb], writes=[B_KA])
            cx.dma("sp", KA[:, 67 + j, :], pcs[j].ap, reads=[pcs[j]], writes=[B_KA])
        cx.barrier(skip_sw=True)
        sc.close()
        sc_fg.close()

    def phase_attn(l):
        sc = Scope(cx)
        qa = [sc.buf([70, L], BF16, f"qa{i}") for i in range(2)]
        ka = [sc.buf([70, L], BF16, f"ka{i}") for i in range(2)]
        va = [sc.buf([128, 32, 65], BF16, f"va{i}") for i in range(2)]
        NPT = 6
        pt = [sc.buf([128, TB], BF16, f"pt{i}") for i in range(NPT)]
        rden = [sc.buf([128, TB], F32, f"rden{i}") for i in range(2)]
        rb = [sc.buf([64, TB], F32, f"rb{i}") for i in range(2)]
        ost = [sc.buf([64, TB], BF16, f"ost{i}") for i in range(2)]

        def load_head(h):
            cx.dma("sp", qa[h % 2].ap, QA[h], reads=[B_QA], writes=[qa[h % 2]])
            cx.dma("sp", ka[h % 2].ap, KA[h], reads=[B_KA], writes=[ka[h % 2]])
            cx.dma("sp", va[h % 2].ap, VA[h], reads=[B_VA], writes=[va[h % 2]])
        items = []
        nb = 0
        for h in range(8):
            for I in range(NB):
                nkb = 4 * I + 4
                for j in range(nkb):
                    items.append((h, I, j, nkb, nb))
                nb += 1
        LA = 3

        def emit_S(i):
            h, I, j, nkb, b_ = items[i]
            c0 = 128 * max(0, j - 4 * I)
            ps = PS[i % 4]
            k, q = ka[h % 2], qa[h % 2]
            cx.op("pe", lambda e: e.matmul(ps.ap[:, c0:TB], lhsT=k.ap[:, j * 128:(j + 1) * 128], rhs=q.ap[:, I * TB + c0:(I + 1) * TB],
                                           start=True, stop=True), reads=[k, q], writes=[ps])

        def finalize(h, I, b_):
            po = PS[4 + b_ % 2]
            pr = PS[6 + b_ % 2]
            rd, r_, o_ = rden[b_ % 2], rb[b_ % 2], ost[b_ % 2]
            cx.op("dve", lambda e: e.reciprocal(out=rd.ap[64:65, :], in_=po.ap[64:65, :]), reads=[po], writes=[rd])
            cx.op("pe", lambda e: e.matmul(pr.ap[0:64, :], lhsT=ones1.ap[64:65, :], rhs=rd.ap[64:65, :], start=True, stop=True),
                  reads=[ones1, rd], writes=[pr])
            cx.op("act", lambda e: e.activation(out=r_.ap, in_=pr.ap[0:64, :], func=AF.Identity), reads=[pr], writes=[r_])
            cx.op("dve", lambda e: e.tensor_tensor(out=o_.ap, in0=po.ap[0:64, :], in1=r_.ap, op=ALU.mult), reads=[po, r_], writes=[o_])
            cx.dma("sp", YS[1024 + h * 64:1024 + (h + 1) * 64, blk(I)], o_.ap, reads=[o_], writes=[B_YS[2]])
        load_head(0)
        for i in range(min(LA, len(items))):
            emit_S(i)
        pending = []
        for i, (h, I, j, nkb, b_) in enumerate(items):
            if I == 0 and j == 0 and h + 1 < 8:
                load_head(h + 1)
            if i + LA < len(items):
                emit_S(i + LA)
            c0 = 128 * max(0, j - 4 * I)
            ps = PS[i % 4]
            p = pt[i % NPT]
            v = va[h % 2]
            po = PS[4 + b_ % 2]
            cx.op("act", lambda e: e.activation(out=p.ap[:, c0:TB], in_=ps.ap[:, c0:TB], func=AF.Exp), reads=[ps], writes=[p])
            if j >= 4 * I:
                cx.op("dve", lambda e: e.tensor_tensor(out=p.ap[:, c0:c0 + 128], in0=p.ap[:, c0:c0 + 128], in1=tri.ap, op=ALU.mult),
                      reads=[p, tri], writes=[p])
            cx.op("pe", lambda e: e.matmul(po.ap[0:65, c0:TB], lhsT=v.ap[:, j, :], rhs=p.ap[:, c0:TB], start=(j == 0), stop=(j == nkb - 1)),
                  reads=[v, p], writes=[po])
            pending = [(cnt - 1, args) for (cnt, args) in pending]
            for cnt, args in [x for x in pending if x[0] <= 0]:
                finalize(*args)
            pending = [x for x in pending if x[0] > 0]
            if j == nkb - 1:
                pending.append((2, (h, I, b_)))
        for cnt, args in pending:
            finalize(*args)
        cx.barrier(skip_sw=True)
        sc.close()

    def phase_lru(l):
        sc = Scope(cx)
        cw = sc.buf([128, 4, 4], F32, "cw")
        cb = sc.buf([128, 4], F32, "cb")
        ba = sc.buf([128, 4], F32, "ba")
        bx = sc.buf([128, 4], F32, "bx")
        lam = sc.buf([128, 4], F32, "lam")
        sneg = sc.buf([128, 4], F32, "sneg")
        one_c = sc.buf([128, 1], F32, "one_c")
        cx.op("dve", lambda e: e.memset(one_c.ap, 1.0), writes=[one_c])
        for k_ in range(4):
            cx.dma("sp", cw.ap[:, :, k_], lru_conv_w[l, k_].rearrange("(c p) -> p c", p=128), writes=[cw], allow_slow_non_contiguous=True)
        for (dst, src) in ((cb, lru_conv_b), (ba, lru_b_a), (bx, lru_b_x), (lam, lru_lambda)):
            cx.dma("sp", dst.ap, src[l].rearrange("(c p) -> p c", p=128), writes=[dst], allow_slow_non_contiguous=True)
        cx.op("act", lambda e: e.activation(out=sneg.ap, in_=lam.ap, func=AF.Exp, scale=-1.0), reads=[lam], writes=[sneg])
        cx.op("act", lambda e: e.activation(out=sneg.ap, in_=sneg.ap, func=AF.Ln, bias=one_c.ap, scale=1.0), reads=[sneg, one_c], writes=[sneg])
        cx.op("dve", lambda e: e.tensor_scalar(out=sneg.ap, in0=sneg.ap, scalar1=-8.0, scalar2=None, op0=ALU.mult), reads=[sneg], writes=[sneg])
        WA = sc.buf([128, 4, 128], BF16, "WA")
        WX = sc.buf([128, 4, 128], BF16, "WX")
        for Wm, src in ((WA, lru_w_a), (WX, lru_w_x)):
            cx.op("pool", lambda e, Wm=Wm: e.memset(Wm.ap, 0.0), writes=[Wm])
            for c in range(4):
                cx.dma("pool", Wm.ap[0:64, c, 0:64], src[l, 2 * c], writes=[Wm])
                cx.dma("pool", Wm.ap[64:128, c, 64:128], src[l, 2 * c + 1], writes=[Wm])
        xl = sc.buf([128, L + 3], F32, "xl")
        gl = sc.buf([128, L], F32, "gl")
        xc = sc.buf([128, L], F32, "xc")
        xcb = sc.buf([128, L], BF16, "xcb")
        a_all = sc.buf([128, L], F32, "a_all")
        b_all = sc.buf([128, L], F32, "b_all")
        yb = sc.buf([128, L], BF16, "yb")
        h_all = sc.buf([128, L], F32, "h_all")
        tr = [sc.buf([128, TB], F32, f"tr{i}") for i in range(2)]
        ti = [sc.buf([128, TB], F32, f"ti{i}") for i in range(2)]
        tm = [sc.buf([128, TB], F32, f"tm{i}") for i in range(2)]
        cx.op("pool", lambda e: e.memset(xl.ap[:, 0:3], 0.0), writes=[xl])
        n = 0
        for c in range(4):
            cx.dma("sp", xl.ap[:, 3:], XL[c * 128:(c + 1) * 128, :], reads=[B_XL], writes=[xl])
            cx.dma("sp", gl.ap, GL[c * 128:(c + 1) * 128, :], reads=[B_GL], writes=[gl])
            cx.op("dve", lambda e, c=c: e.tensor_scalar(out=xc.ap, in0=xl.ap[:, 0:L], scalar1=cw.ap[:, c, 0:1], scalar2=cb.ap[:, c:c + 1],
                                                        op0=ALU.mult, op1=ALU.add), reads=[xl, cw, cb], writes=[xc])
            for k_ in range(1, 4):
                eng = "dve"
                cx.op(eng, lambda e, c=c, k_=k_: e.scalar_tensor_tensor(out=xc.ap, in0=xl.ap[:, k_:k_ + L], scalar=cw.ap[:, c, k_:k_ + 1],
                                                                         in1=xc.ap, op0=ALU.mult, op1=ALU.add), reads=[xl, cw, xc], writes=[xc])
            cx.op("act", lambda e: e.activation(out=xcb.ap, in_=xc.ap, func=AF.Identity), reads=[xc], writes=[xcb])
            for t in range(NB):
                pa, px = PS[(2 * n) % 4], PS[(2 * n + 1) % 4]
                r_, i_, m_ = tr[n % 2], ti[n % 2], tm[n % 2]
                n += 1
                cx.op("pe", lambda e, pa=pa, c=c, t=t: e.matmul(pa.ap, lhsT=WA.ap[:, c, :], rhs=xcb.ap[:, blk(t)], start=True, stop=True),
                      reads=[WA, xcb], writes=[pa])
                cx.op("pe", lambda e, px=px, c=c, t=t: e.matmul(px.ap, lhsT=WX.ap[:, c, :], rhs=xcb.ap[:, blk(t)], start=True, stop=True),
                      reads=[WX, xcb], writes=[px])
                cx.op("act", lambda e, pa=pa, r_=r_, c=c: e.activation(out=r_.ap, in_=pa.ap, func=AF.Sigmoid, bias=ba.ap[:, c:c + 1], scale=1.0),
                      reads=[pa, ba], writes=[r_])
                cx.op("act", lambda e, px=px, i_=i_, c=c: e.activation(out=i_.ap, in_=px.ap, func=AF.Sigmoid, bias=bx.ap[:, c:c + 1], scale=1.0),
                      reads=[px, bx], writes=[i_])
                cx.op("act", lambda e, r_=r_, c=c, t=t: e.activation(out=a_all.ap[:, blk(t)], in_=r_.ap, func=AF.Exp, scale=sneg.ap[:, c:c + 1]),
                      reads=[r_, sneg], writes=[a_all])
                cx.op("act", lambda e, m_=m_, t=t: e.activation(out=m_.ap, in_=a_all.ap[:, blk(t)], func=AF.Square), reads=[a_all], writes=[m_])
                cx.op("act", lambda e, m_=m_: e.activation(out=m_.ap, in_=m_.ap, func=AF.Sqrt, bias=one_c.ap, scale=-1.0),
                      reads=[m_, one_c], writes=[m_])
                cx.op("dve", lambda e, m_=m_, i_=i_: e.tensor_tensor(out=m_.ap, in0=m_.ap, in1=i_.ap, op=ALU.mult), reads=[m_, i_], writes=[m_])
                cx.op("pool", lambda e, m_=m_, t=t: e.tensor_tensor(out=b_all.ap[:, blk(t)], in0=m_.ap, in1=xc.ap[:, blk(t)], op=ALU.mult),
                      reads=[m_, xc], writes=[b_all])
            cx.op("dve", lambda e: e.tensor_tensor_scan(out=h_all.ap, data0=a_all.ap, data1=b_all.ap, initial=0.0, op0=ALU.mult, op1=ALU.add),
                  reads=[a_all, b_all], writes=[h_all])
            cx.op("dve", lambda e: e.tensor_tensor(out=yb.ap, in0=h_all.ap, in1=gl.ap, op=ALU.mult), reads=[h_all, gl], writes=[yb])
            cx.dma("sp", YS[512 + c * 128:512 + (c + 1) * 128, :], yb.ap, reads=[yb], writes=[B_YS[1]])
        cx.barrier(skip_sw=True)
        sc.close()

    def cmul(eng_a, eng_b, sc_t, outr, outi, ar, ai, br, bi, reads, w_r, w_i, negi=False):
        t1, t2 = sc_t
        cx.op(eng_a, lambda e: e.tensor_tensor(out=t1.ap, in0=ar, in1=br, op=ALU.mult), reads=reads, writes=[t1])
        cx.op(eng_b, lambda e: e.tensor_tensor(out=t2.ap, in0=ai, in1=bi, op=ALU.mult), reads=reads, writes=[t2])
        cx.op(eng_a, lambda e: e.tensor_tensor(out=outr, in0=t1.ap, in1=t2.ap, op=ALU.subtract), reads=[t1, t2], writes=[w_r])
        cx.op(eng_a, lambda e: e.tensor_tensor(out=t1.ap, in0=ar, in1=bi, op=ALU.mult), reads=reads + [w_r], writes=[t1])
        cx.op(eng_b, lambda e: e.tensor_tensor(out=t2.ap, in0=ai, in1=br, op=ALU.mult), reads=reads + [w_r], writes=[t2])
        if negi:
            cx.op(eng_a, lambda e: e.scalar_tensor_tensor(out=outi, in0=t1.ap, scalar=-1.0, in1=t2.ap, op0=ALU.mult, op1=ALU.subtract),
                  reads=[t1, t2], writes=[w_i])
        else:
            cx.op(eng_a, lambda e: e.tensor_tensor(out=outi, in0=t1.ap, in1=t2.ap, op=ALU.add), reads=[t1, t2], writes=[w_i])

    def phase_s5(l):
        ws = Scope(cx)
        Mw = ws.buf([128, 32, 128], BF16, "Mw")
        W1r = ws.buf([128, 32, 64], BF16, "W1r")
        W1i = ws.buf([128, 32, 64], BF16, "W1i")
        W2r = ws.buf([128, 16, 128], BF16, "W2r")
        W2i = ws.buf([128, 16, 128], BF16, "W2i")
        rho = ws.buf([128, 16], F32, "rho")
        ph_r = ws.buf([128, 16], F32, "ph_r")
        ph_i = ws.buf([128, 16], F32, "ph_i")
        dcol = ws.buf([128, 32], F32, "dcol")
        bglu = ws.buf([128, 4], F32, "bglu")
        cx.dma("sp", bglu.ap, s5_b_glu[l].rearrange("(c p) -> p c", p=128), writes=[bglu], allow_slow_non_contiguous=True)
        for tau in range(8):
            cx.dma("sp", dcol.ap[16 * tau:16 * tau + 16, :], s5_d[l].rearrange("(g c) -> c g", c=16), writes=[dcol],
                   allow_slow_non_contiguous=True)
        sc = Scope(cx)
        N = 64
        araw = sc.buf([32, 64], F32, "araw")
        airaw = sc.buf([32, 64], F32, "airaw")
        cx.dma("sp", araw.ap, s5_a_re[l], writes=[araw])
        cx.dma("sp", airaw.ap, s5_a_im[l], writes=[airaw])
        are = sc.buf([N, 32], F32, "are")
        aim = sc.buf([N, 32], F32, "aim")
        for src, dst in ((araw, are), (airaw, aim)):
            cx.op("pe", lambda e, src=src: e.matmul(PS[0].ap[0:64, 0:32], lhsT=src.ap, rhs=identf.ap[0:32, 0:32], start=True, stop=True),
                  reads=[src, identf], writes=[PS[0]])
            cx.op("act", lambda e, dst=dst: e.activation(out=dst.ap, in_=PS[0].ap[0:64, 0:32], func=AF.Identity), reads=[PS[0]], writes=[dst])
        dt = sc.buf([N, 32], F32, "dt")
        cx.dma("sp", dt.ap, s5_log_dt[l].partition_broadcast(N), writes=[dt])
        Br = sc.buf([N, 32, 16], F32, "Br")
        Bi = sc.buf([N, 32, 16], F32, "Bi")
        cx.dma("sp", Br.ap, s5_b_re[l].rearrange("g n c -> n g c"), writes=[Br])
        cx.dma("sp", Bi.ap, s5_b_im[l].rearrange("g n c -> n g c"), writes=[Bi])
        Cr = sc.buf([N, 32, 16], F32, "Cr")
        Ci = sc.buf([N, 32, 16], F32, "Ci")
        craw = sc.buf([128, 4, 64], F32, "craw")
        for src, dst in ((s5_c_re, Cr), (s5_c_im, Ci)):
            cx.dma("sp", craw.ap, src[l].rearrange("(j p) n -> p j n", p=128), writes=[craw])

            def mm(e):
                for j in range(4):
                    i = e.matmul(PS[1].ap[0:64, j * 128:(j + 1) * 128], lhsT=craw.ap[:, j, :], rhs=identf.ap, start=True, stop=True)
                return i
            cx.op("pe", mm, reads=[craw, identf], writes=[PS[1]])
            cx.op("act", lambda e, dst=dst: e.activation(out=dst.ap.rearrange("n g c -> n (g c)"), in_=PS[1].ap[0:64, :], func=AF.Identity),
                  reads=[PS[1]], writes=[dst])

        def small(name):
            return sc.buf([N, 32], F32, name)
        ar, ang, mag, lbr, lbi = small("ar"), small("ang"), small("mag"), small("lbr"), small("lbi")
        t1, t2, t3 = small("t1"), small("t2"), small("t3")
        V = "dve"

        def tt(out, a, b, op_, eng=V):
            cx.op(eng, lambda e: e.tensor_tensor(out=out.ap, in0=a.ap, in1=b.ap, op=op_), reads=[a, b], writes=[out])

        def tsc(out, a, s1, op0, s2=None, op1=None, eng=V):
            if op1 is None:
                cx.op(eng, lambda e: e.tensor_scalar(out=out.ap, in0=a.ap, scalar1=s1, scalar2=None, op0=op0), reads=[a], writes=[out])
            else:
                cx.op(eng, lambda e: e.tensor_scalar(out=out.ap, in0=a.ap, scalar1=s1, scalar2=s2, op0=op0, op1=op1), reads=[a], writes=[out])

        def act(out, a, fn, scale=1.0, bias=None):
            if bias is None:
                cx.op("act", lambda e: e.activation(out=out.ap, in_=a.ap, func=fn, scale=scale), reads=[a], writes=[out])
            else:
                cx.op("act", lambda e: e.activation(out=out.ap, in_=a.ap, func=fn, scale=scale, bias=bias.ap), reads=[a, bias], writes=[out])

        zero_c = sc.buf([N, 1], F32, "zero_c")
        cx.op("dve", lambda e: e.memset(zero_c.ap, 0.0), writes=[zero_c])

        def sin_of(out, angle_buf, shift):
            tsc(t1, angle_buf, 1.0 / TWO_PI, ALU.mult, (shift / TWO_PI) + MAGIC, ALU.add)
            tsc(t1, t1, -MAGIC, ALU.add)
            cx.op(V, lambda e: e.scalar_tensor_tensor(out=t2.ap, in0=t1.ap, scalar=-TWO_PI, in1=angle_buf.ap, op0=ALU.mult, op1=ALU.add),
                  reads=[t1, angle_buf], writes=[t2])
            tsc(t2, t2, float(shift), ALU.add, math.pi - 1e-6, ALU.min)
            tsc(t2, t2, -(math.pi - 1e-6), ALU.max)
            act(out, t2, AF.Sin, bias=zero_c)

        em1, xr_, w_, sn, cm1, nn = small("em1"), small("xr_"), small("w_"), small("sn"), small("cm1"), small("nn")

        def nested(out, var, divs, sign):
            cx.op(V, lambda e: e.memset(out.ap, 1.0), writes=[out])
            for dv in divs:
                tt(t3, out, var, ALU.mult)
                tsc(out, t3, sign / dv, ALU.mult, 1.0, ALU.add)
        ld8 = small("ld8")
        tsc(ld8, dt, 0.125, ALU.mult)
        nested(dt, ld8, [float(k) for k in range(12, 0, -1)], 1.0)
        for _ in range(3):
            tt(dt, dt, dt, ALU.mult)
        tt(ar, are, dt, ALU.mult)
        tt(ang, aim, dt, ALU.mult)
        nested(nn, ar, [9.0, 8.0, 7.0, 6.0, 5.0, 4.0, 3.0, 2.0], 1.0)
        tt(em1, nn, ar, ALU.mult)
        tsc(mag, em1, 1.0, ALU.add)
        C1 = 6.28125
        C2 = TWO_PI - C1
        tsc(t1, ang, 1.0 / TWO_PI, ALU.mult, MAGIC, ALU.add)
        tsc(t1, t1, -MAGIC, ALU.add)
        cx.op(V, lambda e: e.scalar_tensor_tensor(out=xr_.ap, in0=t1.ap, scalar=-C1, in1=ang.ap, op0=ALU.mult, op1=ALU.add),
              reads=[t1, ang], writes=[xr_])
        cx.op(V, lambda e: e.scalar_tensor_tensor(out=xr_.ap, in0=t1.ap, scalar=-C2, in1=xr_.ap, op0=ALU.mult, op1=ALU.add),
              reads=[t1, xr_], writes=[xr_])
        tt(w_, xr_, xr_, ALU.mult)
        nested(nn, w_, [float((2 * k) * (2 * k + 1)) for k in range(10, 0, -1)], -1.0)
        tt(sn, nn, xr_, ALU.mult)
        nested(nn, w_, [float((2 * k + 1) * (2 * k + 2)) for k in range(10, 0, -1)], -1.0)
        tt(cm1, nn, w_, ALU.mult)
        tsc(cm1, cm1, -0.5, ALU.mult)
        lm1 = small("lm1")
        tt(t1, em1, cm1, ALU.mult)
        tt(t2, em1, cm1, ALU.add)
        tt(lm1, t1, t2, ALU.add)
        tsc(lbr, lm1, 1.0, ALU.add)
        tt(lbi, sn, mag, ALU.mult)
        den, qr, qi = small("den"), small("qr"), small("qi")
        tt(den, are, are, ALU.mult)
        tt(t1, aim, aim, ALU.mult)
        tt(den, den, t1, ALU.add)
        cx.op(V, lambda e: e.reciprocal(out=den.ap, in_=den.ap), reads=[den], writes=[den])
        tt(t1, lm1, are, ALU.mult)
        tt(t2, lbi, aim, ALU.mult)
        tt(qr, t1, t2, ALU.add)
        tt(qr, qr, den, ALU.mult)
        tt(t1, lbi, are, ALU.mult)
        tt(t2, lm1, aim, ALU.mult)
        tt(qi, t1, t2, ALU.subtract)
        tt(qi, qi, den, ALU.mult)
        Bbr = sc.buf([N, 32, 16], F32, "Bbr")
        Bbi = sc.buf([N, 32, 16], F32, "Bbi")
        tb1 = sc.buf([N, 32, 16], F32, "tb1")
        tb2 = sc.buf([N, 32, 16], F32, "tb2")

        def bc3(b):
            return b.ap.unsqueeze(2).to_broadcast([N, 32, 16])
        cmul("dve", "pool", (tb1, tb2), Bbr.ap, Bbi.ap, bc3(qr), bc3(qi), Br.ap, Bi.ap, [qr, qi, Br, Bi], Bbr, Bbi)
        Pr = sc.buf([N, 32, 9], F32, "Pr")
        Pi = sc.buf([N, 32, 9], F32, "Pi")
        Qr = sc.buf([N, 32, 8], F32, "Qr")
        Qi = sc.buf([N, 32, 8], F32, "Qi")
        ibr, ibi, im2 = small("ibr"), small("ibi"), small("im2")
        tt(im2, mag, mag, ALU.mult)
        cx.op(V, lambda e: e.reciprocal(out=im2.ap, in_=im2.ap), reads=[im2], writes=[im2])
        tt(ibr, lbr, im2, ALU.mult)
        tt(ibi, lbi, im2, ALU.mult)
        tsc(ibi, ibi, -1.0, ALU.mult)
        for (Xr, Xi, br_, bi_, n_) in ((Pr, Pi, lbr, lbi, 9), (Qr, Qi, ibr, ibi, 8)):
            cx.op(V, lambda e, Xr=Xr: e.memset(Xr.ap[:, :, 0:1], 1.0), writes=[Xr])
            cx.op(V, lambda e, Xi=Xi: e.memset(Xi.ap[:, :, 0:1], 0.0), writes=[Xi])
            for tau in range(1, n_):
                cmul("dve", "pool", (t1, t2), Xr.ap[:, :, tau], Xi.ap[:, :, tau], Xr.ap[:, :, tau - 1], Xi.ap[:, :, tau - 1],
                     br_.ap, bi_.ap, [Xr, Xi, br_, bi_], Xr, Xi)
        GH = 16
        big = [sc.buf([N, GH, 8, 16], F32, f"big{i}") for i in range(8)]
        Ar_, Ai_, Cmr, Cmin, T1, T2, A7r, A7i = big
        mtmp = [sc.buf([128, 128], F32, f"mtmp{i}") for i in range(2)]
        for gh in range(2):
            gs = slice(gh * GH, (gh + 1) * GH)

            def bq(b):
                return b.ap[:, gs, 0:8].unsqueeze(3).to_broadcast([N, GH, 8, 16])

            def bb(b):
                return b.ap[:, gs, :].unsqueeze(2).to_broadcast([N, GH, 8, 16])

            def b7(b, idx):
                return b.ap[:, gs, idx:idx + 1].unsqueeze(3).to_broadcast([N, GH, 8, 16])
            cmul("dve", "pool", (T1, T2), Ar_.ap, Ai_.ap, bq(Qr), bq(Qi), bb(Bbr), bb(Bbi), [Qr, Qi, Bbr, Bbi], Ar_, Ai_)
            cmul("dve", "pool", (T1, T2), Cmr.ap, Cmin.ap, bq(Pr), bq(Pi), bb(Cr), bb(Ci), [Pr, Pi, Cr, Ci], Cmr, Cmin, negi=True)
            cmul("dve", "pool", (T1, T2), A7r.ap, A7i.ap, b7(Pr, 7), b7(Pi, 7), Ar_.ap, Ai_.ap, [Pr, Pi, Ar_, Ai_], A7r, A7i)
            for src, dst in ((A7r, W1r), (A7i, W1i)):
                for g8 in range(2):
                    ps = PS[g8 % 4]

                    def mm(e, ps=ps, src=src, g8=g8):
                        for gi in range(8):
                            i = e.matmul(ps.ap[:, gi * 64:(gi + 1) * 64], lhsT=src.ap[:, g8 * 8 + gi].rearrange("n t c -> n (t c)"),
                                         rhs=identf.ap[0:64, 0:64], start=True, stop=True)
                        return i
                    cx.op("pe", mm, reads=[src, identf], writes=[ps])
                    g0 = gh * GH + g8 * 8
                    cx.op("act", lambda e, ps=ps, dst=dst, g0=g0: e.activation(
                        out=dst.ap[:, g0:g0 + 8, :], in_=ps.ap.rearrange("p (g n) -> p g n", n=64), func=AF.Identity),
                        reads=[ps], writes=[dst])
            for g4 in range(4):
                ps = PS[g4 % 4]

                def mm(e, ps=ps, g4=g4):
                    for gi in range(4):
                        gl_ = g4 * 4 + gi
                        e.matmul(ps.ap[:, gi * 128:(gi + 1) * 128], lhsT=Ar_.ap[:, gl_].rearrange("n t c -> n (t c)"),
                                 rhs=Cmr.ap[:, gl_].rearrange("n t c -> n (t c)"), start=True, stop=False)
                        i = e.matmul(ps.ap[:, gi * 128:(gi + 1) * 128], lhsT=Ai_.ap[:, gl_].rearrange("n t c -> n (t c)"),
                                     rhs=Cmin.ap[:, gl_].rearrange("n t c -> n (t c)"), start=False, stop=True)
                    return i
                cx.op("pe", mm, reads=[Ar_, Ai_, Cmr, Cmin], writes=[ps])
                for gi in range(4):
                    g = gh * GH + g4 * 4 + gi
                    mt = mtmp[g % 2]
                    cx.op("dve", lambda e, ps=ps, gi=gi, mt=mt: e.tensor_tensor(out=mt.ap, in0=ps.ap[:, gi * 128:(gi + 1) * 128], in1=mask8.ap,
                                                                                op=ALU.mult), reads=[ps, mask8], writes=[mt])
                    cx.op("dve", lambda e, g=g, mt=mt: e.scalar_tensor_tensor(out=Mw.ap[:, g, :], in0=identf.ap, scalar=dcol.ap[:, g:g + 1],
                                                                              in1=mt.ap, op0=ALU.mult, op1=ALU.add),
                          reads=[identf, dcol, mt], writes=[Mw])
            W2r_f, W2i_f = A7r, A7i
            cx.op("dve", lambda e: e.tensor_tensor(out=T1.ap, in0=b7(Pr, 1), in1=Cmr.ap, op=ALU.mult), reads=[Pr, Cmr], writes=[T1])
            cx.op("pool", lambda e: e.tensor_tensor(out=T2.ap, in0=b7(Pi, 1), in1=Cmin.ap, op=ALU.mult), reads=[Pi, Cmin], writes=[T2])
            cx.op("dve", lambda e: e.tensor_tensor(out=W2r_f.ap, in0=T1.ap, in1=T2.ap, op=ALU.add), reads=[T1, T2], writes=[W2r_f])
            cx.op("dve", lambda e: e.tensor_tensor(out=T1.ap, in0=b7(Pr, 1), in1=Cmin.ap, op=ALU.mult), reads=[Pr, Cmin, W2r_f], writes=[T1])
            cx.op("pool", lambda e: e.tensor_tensor(out=T2.ap, in0=b7(Pi, 1), in1=Cmr.ap, op=ALU.mult), reads=[Pi, Cmr, W2r_f], writes=[T2])
            cx.op("dve", lambda e: e.tensor_tensor(out=W2i_f.ap, in0=T1.ap, in1=T2.ap, op=ALU.subtract), reads=[T1, T2], writes=[W2i_f])
            for src, dst in ((W2r_f, W2r), (W2i_f, W2i)):
                v = src.ap.rearrange("n (q two) t c -> n q two (t c)", two=2)
                q0 = gh * (GH // 2)
                cx.op("act", lambda e, v=v, dst=dst, q0=q0: e.activation(out=dst.ap[0:64, q0:q0 + GH // 2, :], in_=v[:, :, 0, :], func=AF.Identity),
                      reads=[src], writes=[dst])
                cx.op("act", lambda e, v=v, dst=dst, q0=q0: e.activation(out=dst.ap[64:128, q0:q0 + GH // 2, :], in_=v[:, :, 1, :], func=AF.Identity),
                      reads=[src], writes=[dst])
        r8, e8, p8r, p8i = small("r8"), small("e8"), small("p8r"), small("p8i")
        tt(r8, mag, mag, ALU.mult)
        tt(r8, r8, r8, ALU.mult)
        tt(r8, r8, r8, ALU.mult)
        cx.op(V, lambda e: e.reciprocal(out=e8.ap, in_=r8.ap), reads=[r8], writes=[e8])
        cx.op(V, lambda e: e.tensor_tensor(out=p8r.ap, in0=Pr.ap[:, :, 8], in1=e8.ap, op=ALU.mult), reads=[Pr, e8], writes=[p8r])
        cx.op(V, lambda e: e.tensor_tensor(out=p8i.ap, in0=Pi.ap[:, :, 8], in1=e8.ap, op=ALU.mult), reads=[Pi, e8], writes=[p8i])
        for src, dst in ((r8, rho), (p8r, ph_r), (p8i, ph_i)):
            v = src.ap.rearrange("n (q two) -> n q two", two=2)
            cx.op("act", lambda e, v=v, dst=dst: e.activation(out=dst.ap[0:64], in_=v[:, :, 0], func=AF.Identity), reads=[src], writes=[dst])
            cx.op("act", lambda e, v=v, dst=dst: e.activation(out=dst.ap[64:128], in_=v[:, :, 1], func=AF.Identity), reads=[src], writes=[dst])
        cx.barrier(skip_sw=True)
        sc.close()
        sc = Scope(cx)
        gb = sc.buf([128, 4, L], BF16, "gb")
        SelS = sc.buf([128, 8, 8, 128], BF16, "SelS")
        SelTS = sc.buf([128, 8, 8, 128], BF16, "SelTS")
        cx.dma("sp", SelS.ap, SELD[0], reads=[B_SELD], writes=[SelS])
        cx.dma("sp", SelTS.ap, SELD[1], reads=[B_SELD], writes=[SelTS])
        NQ = 4
        for qt in range(4):
            s2 = Scope(cx)
            uT = s2.buf([128, L], BF16, "uT")
            cx.dma("sp", uT.ap, UT[qt * 128:(qt + 1) * 128, :], reads=[B_UT], writes=[uT])
            U3 = s2.buf([128, 8, TB], BF16, "U3")
            E1r = s2.buf([128, NQ, TB], F32, "E1r")
            E1i = s2.buf([128, NQ, TB], F32, "E1i")
            Hr = s2.buf([128, NQ, TB], BF16, "Hr")
            Hi = s2.buf([128, NQ, TB], BF16, "Hi")
            Y3 = s2.buf([128, 8, TB], BF16, "Y3")
            dd = [s2.buf([128, NQ, 256], F32, f"dd{i}") for i in range(4)]
            tmp = [s2.buf([128, TB], F32, f"s5t{i}") for i in range(8)]
            q0 = qt * NQ
            cx.op("dve", lambda e: e.tensor_copy(out=E1r.ap[:, :, 0], in_=ph_r.ap[:, q0:q0 + NQ]), reads=[ph_r], writes=[E1r])
            cx.op("dve", lambda e: e.tensor_copy(out=E1i.ap[:, :, 0], in_=ph_i.ap[:, q0:q0 + NQ]), reads=[ph_i], writes=[E1i])
            m = 1
            while m < TB:
                def bcm(b, m=m):
                    return b.ap[:, :, m - 1:m].to_broadcast([128, NQ, m])
                cx.op("dve", lambda e, m=m: e.tensor_tensor(out=dd[0].ap[:, :, 0:m], in0=E1r.ap[:, :, 0:m], in1=bcm(E1r), op=ALU.mult),
                      reads=[E1r], writes=[dd[0]])
                cx.op("pool", lambda e, m=m: e.tensor_tensor(out=dd[1].ap[:, :, 0:m], in0=E1i.ap[:, :, 0:m], in1=bcm(E1i), op=ALU.mult),
                      reads=[E1i], writes=[dd[1]])
                cx.op("dve", lambda e, m=m: e.tensor_tensor(out=dd[2].ap[:, :, 0:m], in0=E1r.ap[:, :, 0:m], in1=bcm(E1i), op=ALU.mult),
                      reads=[E1r, E1i], writes=[dd[2]])
                cx.op("pool", lambda e, m=m: e.tensor_tensor(out=dd[3].ap[:, :, 0:m], in0=E1i.ap[:, :, 0:m], in1=bcm(E1r), op=ALU.mult),
                      reads=[E1i, E1r], writes=[dd[3]])
                cx.op("dve", lambda e, m=m: e.tensor_tensor(out=E1r.ap[:, :, m:2 * m], in0=dd[0].ap[:, :, 0:m], in1=dd[1].ap[:, :, 0:m], op=ALU.subtract),
                      reads=[dd[0], dd[1]], writes=[E1r])
                cx.op("dve", lambda e, m=m: e.tensor_tensor(out=E1i.ap[:, :, m:2 * m], in0=dd[2].ap[:, :, 0:m], in1=dd[3].ap[:, :, 0:m], op=ALU.add),
                      reads=[dd[2], dd[3]], writes=[E1i])
                m *= 2
            npsu = 0
            for gl_ in range(8):
                ps = PS[npsu % 4]
                npsu += 1

                def mm(e, ps=ps, gl_=gl_):
                    for tau in range(8):
                        i = e.matmul(ps.ap, lhsT=SelS.ap[:, gl_, tau, :], rhs=uT.ap[:, tau::8], start=(tau == 0), stop=(tau == 7))
                    return i
                cx.op("pe", mm, reads=[SelS, uT], writes=[ps])
                cx.op("act", lambda e, ps=ps, gl_=gl_: e.activation(out=U3.ap[:, gl_, :], in_=ps.ap, func=AF.Identity), reads=[ps], writes=[U3])
            for ql in range(NQ):
                q_ = qt * NQ + ql
                ga, gb_ = 2 * ql, 2 * ql + 1
                pre, pim = PS[4 + (ql % 2) * 2], PS[5 + (ql % 2) * 2]

                def mm(e, pre=pre, pim=pim, ga=ga, gb_=gb_, qt=qt):
                    e.matmul(pre.ap[0:64, :], lhsT=W1r.ap[:, qt * 8 + ga, :], rhs=U3.ap[:, ga, :], start=True, stop=True)
                    e.matmul(pre.ap[64:128, :], lhsT=W1r.ap[:, qt * 8 + gb_, :], rhs=U3.ap[:, gb_, :], start=True, stop=True)
                    e.matmul(pim.ap[0:64, :], lhsT=W1i.ap[:, qt * 8 + ga, :], rhs=U3.ap[:, ga, :], start=True, stop=True)
                    return e.matmul(pim.ap[64:128, :], lhsT=W1i.ap[:, qt * 8 + gb_, :], rhs=U3.ap[:, gb_, :], start=True, stop=True)
                cx.op("pe", mm, reads=[W1r, W1i, U3], writes=[pre, pim])
                xr, xi, a1, a2, vr, vi, sr, si = tmp
                cx.op("act", lambda e, pre=pre: e.activation(out=xr.ap, in_=pre.ap, func=AF.Identity), reads=[pre], writes=[xr])
                cx.op("act", lambda e, pim=pim: e.activation(out=xi.ap, in_=pim.ap, func=AF.Identity), reads=[pim], writes=[xi])
                er, ei = E1r.ap[:, ql, :], E1i.ap[:, ql, :]
                cx.op("dve", lambda e, er=er: e.tensor_tensor(out=a1.ap, in0=xr.ap, in1=er, op=ALU.mult), reads=[xr, E1r], writes=[a1])
                cx.op("pool", lambda e, ei=ei: e.tensor_tensor(out=a2.ap, in0=xi.ap, in1=ei, op=ALU.mult), reads=[xi, E1i], writes=[a2])
                cx.op("dve", lambda e: e.tensor_tensor(out=vr.ap, in0=a1.ap, in1=a2.ap, op=ALU.add), reads=[a1, a2], writes=[vr])
                cx.op("dve", lambda e, er=er: e.tensor_tensor(out=a1.ap, in0=xi.ap, in1=er, op=ALU.mult), reads=[xi, E1r], writes=[a1])
                cx.op("pool", lambda e, ei=ei: e.tensor_tensor(out=a2.ap, in0=xr.ap, in1=ei, op=ALU.mult), reads=[xr, E1i], writes=[a2])
                cx.op("dve", lambda e: e.tensor_tensor(out=vi.ap, in0=a1.ap, in1=a2.ap, op=ALU.subtract), reads=[a1, a2], writes=[vi])
                rc = rho.ap[:, q_:q_ + 1].to_broadcast([128, TB])
                cx.op("dve", lambda e, rc=rc: e.tensor_tensor_scan(out=sr.ap, data0=rc, data1=vr.ap, initial=0.0, op0=ALU.mult, op1=ALU.add),
                      reads=[rho, vr], writes=[sr])
                cx.op("dve", lambda e, rc=rc: e.tensor_tensor_scan(out=si.ap, data0=rc, data1=vi.ap, initial=0.0, op0=ALU.mult, op1=ALU.add),
                      reads=[rho, vi], writes=[si])
                cx.op("dve", lambda e, er=er: e.tensor_tensor(out=a1.ap, in0=sr.ap, in1=er, op=ALU.mult), reads=[sr, E1r], writes=[a1])
                cx.op("pool", lambda e, ei=ei: e.tensor_tensor(out=a2.ap, in0=si.ap, in1=ei, op=ALU.mult), reads=[si, E1i], writes=[a2])
                cx.op("pool", lambda e, ql=ql: e.memset(Hr.ap[:, ql, 0:1], 0.0), writes=[Hr])
                cx.op("pool", lambda e, ql=ql: e.memset(Hi.ap[:, ql, 0:1], 0.0), writes=[Hi])
                cx.op("dve", lambda e, ql=ql: e.tensor_tensor(out=Hr.ap[:, ql, 1:TB], in0=a1.ap[:, 0:TB - 1], in1=a2.ap[:, 0:TB - 1], op=ALU.subtract),
                      reads=[a1, a2], writes=[Hr])
                cx.op("dve", lambda e, ei=ei: e.tensor_tensor(out=a1.ap, in0=sr.ap, in1=ei, op=ALU.mult), reads=[sr, E1i], writes=[a1])
                cx.op("pool", lambda e, er=er: e.tensor_tensor(out=a2.ap, in0=si.ap, in1=er, op=ALU.mult), reads=[si, E1r], writes=[a2])
                cx.op("dve", lambda e, ql=ql: e.tensor_tensor(out=Hi.ap[:, ql, 1:TB], in0=a1.ap[:, 0:TB - 1], in1=a2.ap[:, 0:TB - 1], op=ALU.add),
                      reads=[a1, a2], writes=[Hi])
            for gl_ in range(8):
                g = qt * 8 + gl_
                ql, half = gl_ // 2, gl_ % 2
                q_ = qt * NQ + ql
                ps = PS[npsu % 4]
                npsu += 1
                lo, hi = half * 64, half * 64 + 64

                def mm(e, ps=ps, g=g, gl_=gl_, ql=ql, q_=q_, lo=lo, hi=hi):
                    e.matmul(ps.ap, lhsT=Mw.ap[:, g, :], rhs=U3.ap[:, gl_, :], start=True, stop=False)
                    e.matmul(ps.ap, lhsT=W2r.ap[lo:hi, q_, :], rhs=Hr.ap[lo:hi, ql, :], start=False, stop=False)
                    return e.matmul(ps.ap, lhsT=W2i.ap[lo:hi, q_, :], rhs=Hi.ap[lo:hi, ql, :], start=False, stop=True)
                cx.op("pe", mm, reads=[Mw, U3, W2r, W2i, Hr, Hi], writes=[ps])
                cx.op("act", lambda e, ps=ps, gl_=gl_: e.activation(out=Y3.ap[:, gl_, :], in_=ps.ap, func=AF.Identity), reads=[ps], writes=[Y3])
            for tau in range(8):
                ps = PS[npsu % 4]
                npsu += 1

                def mm(e, ps=ps, tau=tau):
                    for gg in range(8):
                        i = e.matmul(ps.ap, lhsT=SelTS.ap[:, gg, tau, :], rhs=Y3.ap[:, gg, :], start=(gg == 0), stop=(gg == 7))
                    return i
                cx.op("pe", mm, reads=[SelTS, Y3], writes=[ps])
                cx.op("act", lambda e, ps=ps, qt=qt, tau=tau: e.activation(out=gb.ap[:, qt, tau::8], in_=ps.ap, func=AF.Gelu_apprx_tanh),
                      reads=[ps], writes=[gb])
            cx.barrier(skip_sw=True)
            s2.close()
        wgl = sc.buf([128, 4, 512], BF16, "wgl")
        cx.dma("sp", wgl.ap, wb_glu[l].rearrange("(kc p) n -> p kc n", p=128), reads=B_wb[("glu", l)], writes=[wgl])
        sg = [sc.buf([128, TB], F32, f"sg{i}") for i in range(2)]
        yst = [sc.buf([128, 4, TB], BF16, f"yst{i}") for i in range(2)]
        n = 0
        for t in range(NB):
            ys_ = yst[t % 2]
            for ct in range(4):
                ps = PS[n % 4]
                s_ = sg[n % 2]
                n += 1

                def mm(e, ps=ps, ct=ct, t=t):
                    for kc in range(4):
                        i = e.matmul(ps.ap, lhsT=wgl.ap[:, kc, ct * 128:(ct + 1) * 128], rhs=gb.ap[:, kc, blk(t)], start=(kc == 0), stop=(kc == 3))
                    return i
                cx.op("pe", mm, reads=[wgl, gb], writes=[ps])
                cx.op("act", lambda e, ps=ps, s_=s_, ct=ct: e.activation(out=s_.ap, in_=ps.ap, func=AF.Sigmoid, bias=bglu.ap[:, ct:ct + 1], scale=1.0),
                      reads=[ps, bglu], writes=[s_])
                cx.op("dve", lambda e, s_=s_, ys_=ys_, ct=ct, t=t: e.tensor_tensor(out=ys_.ap[:, ct, :], in0=gb.ap[:, ct, blk(t)], in1=s_.ap, op=ALU.mult),
                      reads=[gb, s_], writes=[ys_])
            cx.dma("sp", YS[0:512, blk(t)].rearrange("(c p) t -> p c t", p=128), ys_.ap, reads=[ys_], writes=[B_YS[0]])
        cx.barrier(skip_sw=True)
        sc.close()
        ws.close()

    def run_interleaved(gens):
        active = [g for g in gens if g is not None]
        while active:
            for g in list(active):
                try:
                    next(g)
                except StopIteration:
                    active.remove(g)

    def layer_norm_gen(y, gcol, bcol, outb, tmp, stat):
        pm, pq = PS[6], PS[7]
        for h2 in range(2):
            cx.op("act", lambda e, h2=h2: e.activation(out=tmp.ap[:, h2 * 4:(h2 + 1) * 4, :], in_=y.ap[:, h2 * 4:(h2 + 1) * 4, :], func=AF.Square),
                  reads=[y], writes=[tmp])
            yield

        def mm1(e):
            for kc in range(8):
                i = e.matmul(pm.ap, lhsT=onesD.ap, rhs=y.ap[:, kc, :], start=(kc == 0), stop=(kc == 7))
            return i

        def mm2(e):
            for kc in range(8):
                i = e.matmul(pq.ap, lhsT=onesD.ap, rhs=tmp.ap[:, kc, :], start=(kc == 0), stop=(kc == 7))
            return i
        cx.op("pe", mm1, reads=[onesD, y], writes=[pm])
        yield
        cx.op("pe", mm2, reads=[onesD, tmp], writes=[pq])
        yield
        mean, rstd = stat
        cx.op("act", lambda e: e.activation(out=mean.ap, in_=pm.ap, func=AF.Identity), reads=[pm], writes=[mean])
        cx.op("act", lambda e: e.activation(out=rstd.ap, in_=pm.ap, func=AF.Square), reads=[pm], writes=[rstd])
        yield
        cx.op("dve", lambda e: e.tensor_tensor(out=rstd.ap, in0=pq.ap, in1=rstd.ap, op=ALU.subtract), reads=[pq, rstd], writes=[rstd])
        cx.op("dve", lambda e: e.tensor_scalar(out=rstd.ap, in0=rstd.ap, scalar1=0.0, scalar2=LN_EPS, op0=ALU.max, op1=ALU.add),
              reads=[rstd], writes=[rstd])
        yield
        cx.op("act", lambda e: e.activation(out=rstd.ap, in_=rstd.ap, func=AF.Sqrt), reads=[rstd], writes=[rstd])
        cx.op("dve", lambda e: e.reciprocal(out=rstd.ap, in_=rstd.ap), reads=[rstd], writes=[rstd])
        yield
        for h2 in range(2):
            sl_ = slice(h2 * 4, (h2 + 1) * 4)
            mb = mean.ap.unsqueeze(1).to_broadcast([128, 4, TB])
            rb_ = rstd.ap.unsqueeze(1).to_broadcast([128, 4, TB])
            cx.op("dve", lambda e: e.tensor_tensor(out=tmp.ap[:, sl_, :], in0=y.ap[:, sl_, :], in1=mb, op=ALU.subtract), reads=[y, mean], writes=[tmp])
            yield
            cx.op("pool", lambda e: e.tensor_tensor(out=tmp.ap[:, sl_, :], in0=tmp.ap[:, sl_, :], in1=rb_, op=ALU.mult), reads=[tmp, rstd], writes=[tmp])
            yield
        for kc in range(8):
            cx.op("dve", lambda e, kc=kc: e.tensor_scalar(out=y.ap[:, kc, :], in0=tmp.ap[:, kc, :], scalar1=gcol.ap[:, kc:kc + 1],
                                                          scalar2=bcol.ap[:, kc:kc + 1], op0=ALU.mult, op1=ALU.add),
                  reads=[tmp, gcol, bcol], writes=[y])
            yield
        for h2 in range(2):
            sl_ = slice(h2 * 4, (h2 + 1) * 4)
            cx.op("act", lambda e: e.activation(out=outb.ap[:, sl_, :], in_=y.ap[:, sl_, :], func=AF.Identity), reads=[y], writes=[outb])
            yield

    def load_cols(sc, src, l, name, n=8):
        b = sc.buf([128, n], F32, name)
        cx.dma("sp", b.ap, src[l].rearrange("(c p) -> p c", p=128), writes=[b], allow_slow_non_contiguous=True)
        return b

    def phase_mix(l):
        sc = Scope(cx)
        wbr = sc.buf([128, 12, D], BF16, "wbr")
        cx.dma("sp", wbr.ap, wb_branch[l].rearrange("(j p) n -> p j n", p=128), reads=B_wb[("branch", l)], writes=[wbr])
        wgd = [sc.buf([128, 8, 3, 128], BF16, f"wgd{i}") for i in range(2)]
        wgsrc = kview(wb_in[l])[:, :, 3080:6152].rearrange("p kc (k3 dc j) -> p kc k3 dc j", k3=3, dc=8)
        wo = sc.buf([128, 8, D], BF16, "wo")
        cx.dma("sp", wo.ap, kview(wb_out[l]), reads=B_wb[("out", l)], writes=[wo])
        bg = load_cols(sc, b_gate, l, "bg", 24)
        g1 = load_cols(sc, ln1_g, l, "g1")
        b1 = load_cols(sc, ln1_b, l, "b1")
        xb = sc.buf([128, 8, TB], BF16, "mxb")
        xr = [sc.buf([128, TB], F32, f"mxr{i}") for i in range(2)]
        ys = sc.buf([128, 12, TB], BF16, "mys")
        mixb = sc.buf([128, 8, TB], BF16, "mixb")
        yvs = [sc.buf([128, 8, TB], F32, f"yv{i}") for i in range(2)]
        o16 = sc.buf([128, 8, TB], BF16, "mo16")
        tmp = sc.buf([128, 8, TB], F32, "lntmp")
        stat = (sc.buf([128, TB], F32, "mean"), sc.buf([128, TB], F32, "rstd"))
        gsb = [sc.buf([128, TB], F32, f"gsb{i}") for i in range(3)]
        acc = [sc.buf([128, TB], F32, f"acc{i}") for i in range(2)]
        xres_src = xT if l == 0 else XRES
        st = {"n": 0, "nr": 0, "nw": 0}

        def genA(t):
            x_, y_ = xb, ys
            yv = yvs[t % 2]
            cx.dma("sp", x_.ap, kview(XBF)[:, :, blk(t)], reads=[B_XBF[t]], writes=[x_])
            cx.dma("sp", y_.ap, YS.rearrange("(j p) t -> p j t", p=128)[:, :, blk(t)], reads=B_YS, writes=[y_])
            for dc in range(8):
                a_ = acc[dc % 2]
                wg_ = wgd[st["nw"] % 2]
                st["nw"] += 1
                for k3_ in range(3):
                    cx.dma("sp", wg_.ap[:, :, k3_, :], wgsrc[:, :, k3_, dc, :], reads=B_wb[("in", l)], writes=[wg_])
                for k3 in range(3):
                    n = st["n"]
                    st["n"] += 1
                    pp, pg = PS[(2 * n) % 6], PS[(2 * n + 1) % 6]
                    g_ = gsb[n % 3]

                    def mmp(e, pp=pp, k3=k3, dc=dc, y_=y_):
                        for kc in range(4):
                            i = e.matmul(pp.ap, lhsT=wbr.ap[:, k3 * 4 + kc, dc * 128:(dc + 1) * 128], rhs=y_.ap[:, k3 * 4 + kc, :],
                                         start=(kc == 0), stop=(kc == 3))
                        return i

                    def mmg(e, pg=pg, k3=k3, x_=x_, wg_=wg_):
                        for kc in range(8):
                            i = e.matmul(pg.ap, lhsT=wg_.ap[:, kc, k3, :], rhs=x_.ap[:, kc, :], start=(kc == 0), stop=(kc == 7))
                        return i
                    cx.op("pe", mmg, reads=[wg_, x_], writes=[pg])
                    cx.op("pe", mmp, reads=[wbr, y_], writes=[pp])
                    cx.op("act", lambda e, pg=pg, g_=g_, k3=k3, dc=dc: e.activation(out=g_.ap, in_=pg.ap, func=AF.Sigmoid,
                                                                                   bias=bg.ap[:, k3 * 8 + dc:k3 * 8 + dc + 1], scale=1.0),
                          reads=[pg, bg], writes=[g_])
                    if k3 == 0:
                        cx.op("dve", lambda e, pp=pp, g_=g_, a_=a_: e.tensor_tensor(out=a_.ap, in0=pp.ap, in1=g_.ap, op=ALU.mult),
                              reads=[pp, g_], writes=[a_])
                    else:
                        cx.op("dve", lambda e, pp=pp, g_=g_: e.tensor_tensor(out=g_.ap, in0=pp.ap, in1=g_.ap, op=ALU.mult),
                              reads=[pp, g_], writes=[g_])
                        if k3 == 1:
                            cx.op("pool", lambda e, g_=g_, a_=a_: e.tensor_tensor(out=a_.ap, in0=a_.ap, in1=g_.ap, op=ALU.add),
                                  reads=[a_, g_], writes=[a_])
                        else:
                            cx.op("pool", lambda e, g_=g_, a_=a_, dc=dc: e.tensor_tensor(out=mixb.ap[:, dc, :], in0=a_.ap, in1=g_.ap, op=ALU.add),
                                  reads=[a_, g_], writes=[mixb])
                    yield
            for dc in range(8):
                po = PS[6 + dc % 2]
                r_ = xr[st["nr"] % 2]
                st["nr"] += 1
                cx.dma("sp", r_.ap, xres_src[dc * 128:(dc + 1) * 128, blk(t)], reads=[B_XRES[t]], writes=[r_])

                def mmo(e, po=po, dc=dc):
                    for kc in range(8):
                        i = e.matmul(po.ap, lhsT=wo.ap[:, kc, dc * 128:(dc + 1) * 128], rhs=mixb.ap[:, kc, :], start=(kc == 0), stop=(kc == 7))
                    return i
                cx.op("pe", mmo, reads=[wo, mixb], writes=[po])
                cx.op("dve", lambda e, po=po, dc=dc, r_=r_: e.scalar_tensor_tensor(out=yv.ap[:, dc, :], in0=r_.ap, scalar=float(ALPHA),
                                                                                 in1=po.ap, op0=ALU.mult, op1=ALU.add),
                      reads=[r_, po], writes=[yv])
                yield

        def genB(t):
            yv = yvs[t % 2]
            yield from layer_norm_gen(yv, g1, b1, o16, tmp, stat)
            cx.dma("sp", kview(X1RES)[:, :, blk(t)], yv.ap, reads=[yv], writes=[B_X1RES[t]])
            cx.dma("sp", kview(X1BF)[:, :, blk(t)], o16.ap, reads=[o16], writes=[B_X1BF[t]])
            yield
        run_interleaved([genA(0)])
        for t in range(NB):
            run_interleaved([genA(t + 1) if t + 1 < NB else None, genB(t)])
        cx.barrier(skip_sw=True)
        sc.close()

    def phase_ffn(l, last):
        sc = Scope(cx)
        wdn = sc.buf([128, NHC, D], BF16, "wdn")
        cx.dma("sp", wdn.ap, wb_d[l].rearrange("(j p) n -> p j n", p=128), reads=B_wb[("d", l)], writes=[wdn])
        g2 = load_cols(sc, ln2_g, l, "g2")
        b2 = load_cols(sc, ln2_b, l, "b2")
        xb = sc.buf([128, 8, TB], BF16, "fxb")
        hT = sc.buf([128, NHC, TB], BF16, "hT")
        wgu = [sc.buf([128, 2, 8, 256], BF16, f"wgu{i}") for i in range(2)]
        sl = [sc.buf([128, TB], F32, f"sl{i}") for i in range(2)]
        xr = [sc.buf([128, TB], F32, f"fxr{i}") for i in range(2)]
        yvs = [sc.buf([128, 8, TB], F32, f"fyv{i}") for i in range(2)]
        tmp = sc.buf([128, 8, TB], F32, "flntmp")
        o16 = sc.buf([128, 8, TB], BF16, "fo16")
        stat = (sc.buf([128, TB], F32, "fmean"), sc.buf([128, TB], F32, "frstd"))
        st = {"n": 0, "nr": 0, "nw": 0}

        def genA(t):
            yv = yvs[t % 2]
            cx.dma("sp", xb.ap, kview(X1BF)[:, :, blk(t)], reads=[B_X1BF[t]], writes=[xb])
            for hp in range(NHC // 2):
                w = wgu[st["nw"] % 2]
                st["nw"] += 1
                cx.dma("sp", w.ap[:, 0], kview(wb_g[l])[:, :, hp * 256:(hp + 1) * 256], reads=B_wb[("g", l)], writes=[w])
                cx.dma("sp", w.ap[:, 1], kview(wb_u[l])[:, :, hp * 256:(hp + 1) * 256], reads=B_wb[("u", l)], writes=[w])
                for hh in range(2):
                    hc = hp * 2 + hh
                    n = st["n"]
                    st["n"] += 1
                    pg, pu = PS[(2 * n) % 6], PS[(2 * n + 1) % 6]
                    s_ = sl[n % 2]

                    def mmg(e, pg=pg, w=w, hh=hh):
                        for kc in range(8):
                            i = e.matmul(pg.ap, lhsT=w.ap[:, 0, kc, hh * 128:(hh + 1) * 128], rhs=xb.ap[:, kc, :], start=(kc == 0), stop=(kc == 7))
                        return i

                    def mmu(e, pu=pu, w=w, hh=hh):
                        for kc in range(8):
                            i = e.matmul(pu.ap, lhsT=w.ap[:, 1, kc, hh * 128:(hh + 1) * 128], rhs=xb.ap[:, kc, :], start=(kc == 0), stop=(kc == 7))
                        return i
                    cx.op("pe", mmg, reads=[w, xb], writes=[pg])
                    cx.op("pe", mmu, reads=[w, xb], writes=[pu])
                    cx.op("act", lambda e, pg=pg, s_=s_: e.activation(out=s_.ap, in_=pg.ap, func=AF.Silu), reads=[pg], writes=[s_])
                    cx.op("dve", lambda e, pu=pu, s_=s_, hc=hc: e.tensor_tensor(out=hT.ap[:, hc, :], in0=pu.ap, in1=s_.ap, op=ALU.mult),
                          reads=[pu, s_], writes=[hT])
                    yield
            for dc in range(8):
                po = PS[6 + dc % 2]
                r_ = xr[st["nr"] % 2]
                st["nr"] += 1
                cx.dma("sp", r_.ap, X1RES[dc * 128:(dc + 1) * 128, blk(t)], reads=[B_X1RES[t]], writes=[r_])

                def mmo(e, po=po, dc=dc):
                    for hc in range(NHC):
                        i = e.matmul(po.ap, lhsT=wdn.ap[:, hc, dc * 128:(dc + 1) * 128], rhs=hT.ap[:, hc, :], start=(hc == 0), stop=(hc == NHC - 1))
                    return i
                cx.op("pe", mmo, reads=[wdn, hT], writes=[po])
                cx.op("dve", lambda e, po=po, dc=dc, r_=r_: e.scalar_tensor_tensor(out=yv.ap[:, dc, :], in0=r_.ap, scalar=float(ALPHA),
                                                                                 in1=po.ap, op0=ALU.mult, op1=ALU.add),
                      reads=[r_, po], writes=[yv])
                yield

        def genB(t):
            yv = yvs[t % 2]
            yield from layer_norm_gen(yv, g2, b2, o16, tmp, stat)
            if last:
                cx.dma("sp", kview(outT)[:, :, blk(t)], yv.ap, reads=[yv], writes=[B_OUT])
            else:
                cx.dma("sp", kview(XRES)[:, :, blk(t)], yv.ap, reads=[yv], writes=[B_XRES[t]])
                cx.dma("sp", kview(XBF)[:, :, blk(t)], o16.ap, reads=[o16], writes=[B_XBF[t]])
            yield
        run_interleaved([genA(0)])
        for t in range(NB):
            run_interleaved([genA(t + 1) if t + 1 < NB else None, genB(t)])
        cx.barrier(skip_sw=True)
        sc.close()

    cx.barrier(skip_sw=True)
    for l in range(n_layers):
        if stop_after == ("setup", l):
            break
        phase_proj(l)
        if l + 1 < n_layers:
            convert_layer(l + 1)
        if stop_after == ("proj", l):
            break
        phase_attn(l)
        if stop_after == ("attn", l):
            break
        phase_lru(l)
        if stop_after == ("lru", l):
            break
        phase_s5(l)
        if stop_after == ("s5", l):
            break
        phase_mix(l)
        if stop_after == ("mix", l):
            break
        phase_ffn(l, last=(l == n_layers - 1))
    cx.barrier()
    return nc


INPUT_ORDER = ["w_in", "w_branch", "w_out", "w_ffn_gate", "w_ffn_up", "w_ffn_down", "s5_w_glu", "lru_w_a", "lru_w_x",
               "b_f", "b_gate", "s5_a_re", "s5_a_im", "s5_log_dt", "s5_b_re", "s5_b_im", "s5_c_re", "s5_c_im", "s5_d",
               "s5_b_glu", "lru_conv_w", "lru_conv_b", "lru_b_a", "lru_b_x", "lru_lambda", "ln1_g", "ln1_b", "ln2_g", "ln2_b"]


def layout_inputs(inputs, n_layers=DEPTH):
    f = lambda a: np.ascontiguousarray(np.asarray(a, dtype=np.float32)[:n_layers])
    shared = {}
    for k in INPUT_ORDER:
        a = f(inputs[k])
        if k == "w_branch":
            a = a.reshape(n_layers, 1536, D)
        elif k in ("s5_c_re", "s5_c_im"):
            a = a.reshape(n_layers, 512, 64)
        elif k in ("lru_b_a", "lru_b_x"):
            a = a.reshape(n_layers, 512)
        shared[k] = np.ascontiguousarray(a)
    return shared


def kernel(**inputs):
    x = np.asarray(inputs["x"], dtype=np.float32)
    shared = layout_inputs(inputs)
    nc = bass.Bass("TRN2", target_bir_lowering=False)
    build(nc)
    in_maps = []
    for c in range(8):
        m = dict(shared)
        m["xT"] = np.ascontiguousarray(x[c % 4].T)
        in_maps.append(m)
    res = run_bass_kernel_spmd(nc, in_maps, core_ids=list(range(8)))
    out = np.stack([np.ascontiguousarray(res.results[b]["outT"].T) for b in range(4)], axis=0)
    return out.astype(np.float32)
```

```python
import math
import numpy as np
import concourse.bass as bass
import concourse.mybir as mybir
from concourse.bass_utils import run_bass_kernel_spmd

F32 = mybir.dt.float32
BF16 = mybir.dt.bfloat16
AF = mybir.ActivationFunctionType
ALU = mybir.AluOpType

D = 1024
L = 4096
DEPTH = 4
NB = 8
TB = 512
IN_TOTAL = 6152
FFN = 2816
NHC = 22
ALPHA = (2.0 * DEPTH) ** 0.25
LN_EPS = 1e-5
MAGIC = 12582912.0
TWO_PI = 2.0 * math.pi


class Buf:
    __slots__ = ("ap", "w", "r", "name")

    def __init__(self, ap, name=""):
        self.ap = ap
        self.w = {}
        self.r = {}
        self.name = name


class Ctx:
    def __init__(self, nc):
        self.nc = nc
        self.E = {"pe": nc.tensor, "act": nc.scalar, "dve": nc.vector, "pool": nc.gpsimd, "sp": nc.sync}
        self.sem = {}
        self.cnt = {}
        self.nsem = 0
        for e in ("pe", "act", "dve", "pool"):
            self._new_sem(e)
        self.seen = {e: {} for e in self.E}
        self.dma_sems = {"sp": [nc.alloc_semaphore(f"dq{i}") for i in range(60)],
                         "pool": [nc.alloc_semaphore(f"dqs{i}") for i in range(16)]}
        self.dma_cnt = {k: [0] * len(v) for k, v in self.dma_sems.items()}
        self.dma_rr = {"sp": 0, "pool": 0}
        self.semobj = {}
        self.uid = 0

    def _new_sem(self, e):
        s = self.nc.alloc_semaphore(f"s_{e}_{self.nsem}")
        self.nsem += 1
        self.sem[e] = s
        self.cnt[e] = 0

    def _key(self, s):
        k = id(s)
        self.semobj[k] = s
        return k

    def _wait(self, e, deps):
        seen = self.seen[e]
        for k, v in deps.items():
            if seen.get(k, 0) >= v:
                continue
            self.E[e].wait_ge(self.semobj[k], v)
            seen[k] = v

    @staticmethod
    def _merge(dst, src):
        for k, v in src.items():
            if dst.get(k, 0) < v:
                dst[k] = v

    def _deps(self, reads, writes):
        deps = {}
        for b in reads:
            self._merge(deps, b.w)
        for b in writes:
            self._merge(deps, b.w)
            self._merge(deps, b.r)
        return deps

    def _commit(self, tok, reads, writes):
        for b in reads:
            self._merge(b.r, tok)
        for b in writes:
            b.w = dict(tok)
            b.r = {}

    def op(self, e, emit, reads=(), writes=()):
        self._wait(e, self._deps(reads, writes))
        ins = emit(self.E[e])
        if self.cnt[e] >= 30000:
            self._new_sem(e)
        s = self.sem[e]
        self.cnt[e] += 1
        ins.then_inc(s, 1)
        tok = {self._key(s): self.cnt[e]}
        self._commit(tok, reads, writes)
        return tok

    def dma(self, e, out, in_, reads=(), writes=(), **kw):
        self._wait(e, self._deps(reads, writes))
        sems, cnts = self.dma_sems[e], self.dma_cnt[e]
        i = self.dma_rr[e]
        self.dma_rr[e] = (i + 1) % len(sems)
        if cnts[i] >= 30000:
            sems[i] = self.nc.alloc_semaphore(f"dqx{self.nsem}")
            self.nsem += 1
            cnts[i] = 0
        s = sems[i]
        if cnts[i] > 0:
            self._wait(e, {self._key(s): cnts[i]})
        cnts[i] += 16
        self.E[e].dma_start(out=out, in_=in_, **kw).then_inc(s, 16)
        tok = {self._key(s): cnts[i]}
        self._commit(tok, reads, writes)
        return tok

    def barrier(self, skip_sw=False):
        allt = {}
        for e in ("pe", "act", "dve", "pool"):
            if self.cnt[e] > 0:
                allt[self._key(self.sem[e])] = self.cnt[e]
        for q in self.dma_sems:
            for i, s in enumerate(self.dma_sems[q]):
                if self.dma_cnt[q][i] > 0 and not (q == "pool" and skip_sw):
                    allt[self._key(s)] = self.dma_cnt[q][i]
        for e in self.E:
            self._wait(e, allt)


class Scope:
    def __init__(self, cx):
        self.cx = cx
        self.guards = []

    def sb(self, shape, dt=F32, name=None):
        self.cx.uid += 1
        g = self.cx.nc.sbuf_tensor(f"{name or 't'}_{self.cx.uid}", list(shape), dt)
        t = g.__enter__()
        self.guards.append(g)
        return t.ap()

    def buf(self, shape, dt=F32, name=None):
        return Buf(self.sb(shape, dt, name), name or "")

    def close(self):
        for g in reversed(self.guards):
            g.__exit__(None, None, None)
        self.guards = []


def build(nc, n_layers=DEPTH, dbg=False, stop_after=None):
    cx = Ctx(nc)
    kind_dbg = "ExternalOutput" if dbg else "Internal"

    def din(name, shape):
        return nc.dram_tensor(name, list(shape), F32, kind="ExternalInput").ap()

    def dscr(name, shape, dt, k="Internal"):
        return nc.dram_tensor(name, list(shape), dt, kind=k).ap()

    xT = din("xT", [D, L])
    w_in = din("w_in", [n_layers, D, IN_TOTAL])
    w_branch = din("w_branch", [n_layers, 1536, D])
    w_out = din("w_out", [n_layers, D, D])
    w_g = din("w_ffn_gate", [n_layers, D, FFN])
    w_u = din("w_ffn_up", [n_layers, D, FFN])
    w_d = din("w_ffn_down", [n_layers, FFN, D])
    w_glu = din("s5_w_glu", [n_layers, 512, 512])
    lru_w_a = din("lru_w_a", [n_layers, 8, 64, 64])
    lru_w_x = din("lru_w_x", [n_layers, 8, 64, 64])
    b_f = din("b_f", [n_layers, 8])
    b_gate = din("b_gate", [n_layers, 3072])
    s5_a_re = din("s5_a_re", [n_layers, 32, 64])
    s5_a_im = din("s5_a_im", [n_layers, 32, 64])
    s5_log_dt = din("s5_log_dt", [n_layers, 32])
    s5_b_re = din("s5_b_re", [n_layers, 32, 64, 16])
    s5_b_im = din("s5_b_im", [n_layers, 32, 64, 16])
    s5_c_re = din("s5_c_re", [n_layers, 512, 64])
    s5_c_im = din("s5_c_im", [n_layers, 512, 64])
    s5_d = din("s5_d", [n_layers, 512])
    s5_b_glu = din("s5_b_glu", [n_layers, 512])
    lru_conv_w = din("lru_conv_w", [n_layers, 4, 512])
    lru_conv_b = din("lru_conv_b", [n_layers, 512])
    lru_b_a = din("lru_b_a", [n_layers, 512])
    lru_b_x = din("lru_b_x", [n_layers, 512])
    lru_lambda = din("lru_lambda", [n_layers, 512])
    ln1_g = din("ln1_g", [n_layers, D])
    ln1_b = din("ln1_b", [n_layers, D])
    ln2_g = din("ln2_g", [n_layers, D])
    ln2_b = din("ln2_b", [n_layers, D])
    outT = nc.dram_tensor("outT", [D, L], F32, kind="ExternalOutput").ap()

    wb_in = dscr("wb_in", [n_layers, D, IN_TOTAL], BF16)
    wb_branch = dscr("wb_branch", [n_layers, 1536, D], BF16)
    wb_out = dscr("wb_out", [n_layers, D, D], BF16)
    wb_g = dscr("wb_g", [n_layers, D, FFN], BF16)
    wb_u = dscr("wb_u", [n_layers, D, FFN], BF16)
    wb_d = dscr("wb_d", [n_layers, FFN, D], BF16)
    wb_glu = dscr("wb_glu", [n_layers, 512, 512], BF16)
    XBF = dscr("XBF", [D, L], BF16)
    XRES = dscr("XRES", [D, L], F32, kind_dbg)
    X1BF = dscr("X1BF", [D, L], BF16)
    X1RES = dscr("X1RES", [D, L], F32, kind_dbg)
    UT = dscr("UT", [512, L], BF16, kind_dbg)
    XL = dscr("XL", [512, L], F32, kind_dbg)
    GL = dscr("GL", [512, L], F32, kind_dbg)
    QA = dscr("QA", [8, 70, L], BF16, kind_dbg)
    KA = dscr("KA", [8, 70, L], BF16, kind_dbg)
    VA = dscr("VA", [8, 128, 32, 65], BF16, kind_dbg)
    YS = dscr("YS", [1536, L], BF16, kind_dbg)

    B_wb = {}
    for nm in ("in", "branch", "out", "g", "u", "d", "glu"):
        for l in range(n_layers):
            B_wb[(nm, l)] = []
    B_XBF = [Buf(None, f"XBF{t}") for t in range(NB)]
    B_XRES = [Buf(None, f"XRES{t}") for t in range(NB)]
    B_X1BF = [Buf(None, f"X1BF{t}") for t in range(NB)]
    B_X1RES = [Buf(None, f"X1RES{t}") for t in range(NB)]
    B_UT = Buf(None, "UT")
    B_XL = Buf(None, "XL")
    B_GL = Buf(None, "GL")
    B_QA = Buf(None, "QA")
    B_KA = Buf(None, "KA")
    B_VA = Buf(None, "VA")
    B_YS = [Buf(None, f"YS{k}") for k in range(3)]
    B_OUT = Buf(None, "out")

    PS = [Buf(nc.alloc_psum_tensor(f"psb{i}", [128, 512], F32).ap(), f"ps{i}") for i in range(8)]

    cs = Scope(cx)
    identf = cs.buf([128, 128], F32, "identf")
    identb = cs.buf([128, 128], BF16, "identb")
    onesD = cs.buf([128, 128], F32, "onesD")
    ones1 = cs.buf([128, 64], F32, "ones1")
    mask8 = cs.buf([128, 128], F32, "mask8")
    negtri = cs.buf([128, 128], BF16, "negtri")
    SELD = dscr("SELD", [2, 128, 8, 8, 128], BF16)
    B_SELD = Buf(None, "SELD")
    cs0 = Scope(cx)
    Sel = cs0.buf([128, 8, 8, 128], BF16, "Sel")
    SelT = cs0.buf([128, 8, 8, 128], BF16, "SelT")

    def pool_fill(buf, val):
        cx.op("pool", lambda e: e.memset(buf.ap, val), writes=[buf])

    def pool_sel(buf, ap, pattern, cmp, base, cm, fill=0.0):
        cx.op("pool", lambda e: e.affine_select(out=ap, in_=ap, pattern=pattern, compare_op=cmp, fill=fill,
                                                base=base, channel_multiplier=cm), reads=[buf], writes=[buf])

    pool_fill(identf, 1.0)
    pool_sel(identf, identf.ap, [[1, 128]], ALU.is_equal, 0, -1)
    pool_fill(identb, 1.0)
    pool_sel(identb, identb.ap, [[1, 128]], ALU.is_equal, 0, -1)
    pool_fill(onesD, 1.0 / D)
    pool_fill(ones1, 1.0)
    pool_fill(mask8, 1.0)
    pool_sel(mask8, mask8.ap.rearrange("p (t c) -> p t c", c=16), [[16, 8], [0, 16]], ALU.is_ge, 15, -1)
    pool_fill(negtri, 0.0)
    pool_sel(negtri, negtri.ap, [[1, 128]], ALU.is_ge, 0, -1, fill=-30000.0)
    pool_fill(Sel, 1.0)
    for gg in range(8):
        a4 = Sel.ap[:, gg, :, :].rearrange("p t (u c) -> p t u c", c=16)
        pool_sel(Sel, a4, [[0, 8], [0, 8], [-1, 16]], ALU.is_equal, -16 * gg, 1)
        pool_sel(Sel, a4, [[-1, 8], [1, 8], [0, 16]], ALU.is_equal, 0, 0)
    pool_fill(SelT, 1.0)
    for gg in range(8):
        a3 = SelT.ap[:, gg, :, :]
        pool_sel(SelT, a3, [[16, 8], [1, 128]], ALU.is_equal, -16 * gg, -1)
        pool_sel(SelT, a3, [[-16, 8], [0, 128]], ALU.is_ge, 0, 1)
        pool_sel(SelT, a3, [[16, 8], [0, 128]], ALU.is_ge, 15, -1)

    cx.dma("sp", SELD[0], Sel.ap, reads=[Sel], writes=[B_SELD])
    cx.dma("sp", SELD[1], SelT.ap, reads=[SelT], writes=[B_SELD])
    cx.barrier()
    cs0.close()

    def convert(src, dst, rows, key):
        r = 0
        while r < rows:
            n = min(128, rows - r)
            bch = Buf(None, "wbch")
            B_wb[key].append(bch)
            cx.dma("pool", dst[r:r + n, :], src[r:r + n, :], writes=[bch])
            r += n

    def convert_layer(l):
        convert(w_in[l], wb_in[l], D, ("in", l))
        convert(w_glu[l], wb_glu[l], 512, ("glu", l))
        convert(w_branch[l], wb_branch[l], 1536, ("branch", l))
        convert(w_out[l], wb_out[l], D, ("out", l))
        convert(w_g[l], wb_g[l], D, ("g", l))
        convert(w_u[l], wb_u[l], D, ("u", l))
        convert(w_d[l], wb_d[l], FFN, ("d", l))

    for t in range(NB):
        for kc in range(8):
            cx.dma("pool", XBF[kc * 128:(kc + 1) * 128, t * TB:(t + 1) * TB],
                   xT[kc * 128:(kc + 1) * 128, t * TB:(t + 1) * TB], writes=[B_XBF[t]])

    convert_layer(0)

    def kview(ap2d):
        return ap2d.rearrange("(kc p) n -> p kc n", p=128)

    def blk(t):
        return slice(t * TB, (t + 1) * TB)

    def phase_proj(l):
        sc_fg = Scope(cx)
        fgT = sc_fg.buf([8, L], F32, "fgT")
        sc = Scope(cx)
        xb = [sc.buf([128, 8, TB], BF16, f"xb{t}") for t in range(NB)]
        for t in range(NB):
            cx.dma("sp", xb[t].ap, kview(XBF)[:, :, blk(t)], reads=[B_XBF[t]], writes=[xb[t]])
        wt = [sc.buf([128, 8, 512], BF16, f"wt{i}") for i in range(2)]
        wfg = sc.buf([128, 8, 8], BF16, "wfg")
        cx.dma("sp", wfg.ap, kview(wb_in[l])[:, :, 3072:3080], reads=B_wb[("in", l)], writes=[wfg])
        st32 = [sc.buf([128, 4, TB], F32, f"st32_{i}") for i in range(2)]
        st16 = [sc.buf([128, 4, TB], BF16, f"st16_{i}") for i in range(2)]
        stqk = [sc.buf([64, 8, TB], BF16, f"stqk_{i}") for i in range(2)]
        vst = [sc.buf([128, 8, 65], BF16, f"vst_{i}") for i in range(2)]
        for v in vst:
            cx.op("pool", lambda e, v=v: e.memset(v.ap, 1.0), writes=[v])
        nps = 0
        nst = 0
        for cg in range(6):
            w = wt[cg % 2]
            cx.dma("sp", w.ap, kview(wb_in[l])[:, :, cg * 512:(cg + 1) * 512], reads=B_wb[("in", l)], writes=[w])
            if cg < 3:
                for t in range(NB):
                    stb = (st16 if cg == 0 else st32)[nst % 2]
                    nst += 1
                    for ct in range(4):
                        ps = PS[nps % 4]
                        nps += 1

                        def mm(e, ps=ps, ct=ct, t=t, w=w):
                            for kc in range(8):
                                i = e.matmul(ps.ap, lhsT=w.ap[:, kc, ct * 128:(ct + 1) * 128], rhs=xb[t].ap[:, kc, :],
                                             start=(kc == 0), stop=(kc == 7))
                            return i
                        cx.op("pe", mm, reads=[w, xb[t]], writes=[ps])
                        fn = AF.Gelu_apprx_tanh if cg == 2 else AF.Identity
                        cx.op("act", lambda e, ps=ps, stb=stb, ct=ct, fn=fn: e.activation(out=stb.ap[:, ct, :], in_=ps.ap, func=fn),
                              reads=[ps], writes=[stb])
                    dst, bd = [(UT, B_UT), (XL, B_XL), (GL, B_GL)][cg]
                    cx.dma("sp", dst.rearrange("(c p) t -> p c t", p=128)[:, :, blk(t)], stb.ap, reads=[stb], writes=[bd])
            elif cg < 5:
                for t in range(NB):
                    stb = stqk[nst % 2]
                    nst += 1
                    for h in range(8):
                        ps = PS[nps % 4]
                        nps += 1

                        def mm(e, ps=ps, h=h, t=t, w=w):
                            for kc in range(8):
                                i = e.matmul(ps.ap[0:64, :], lhsT=w.ap[:, kc, h * 64:(h + 1) * 64], rhs=xb[t].ap[:, kc, :],
                                             start=(kc == 0), stop=(kc == 7))
                            return i
                        cx.op("pe", mm, reads=[w, xb[t]], writes=[ps])
                        sc_ = 0.125 if cg == 3 else 1.0
                        cx.op("act", lambda e, ps=ps, stb=stb, h=h, sc_=sc_: e.activation(out=stb.ap[:, h, :], in_=ps.ap[0:64, :],
                                                                                         func=AF.Identity, scale=sc_),
                              reads=[ps], writes=[stb])
                    dst, bd = (QA, B_QA) if cg == 3 else (KA, B_KA)
                    cx.dma("sp", dst[:, 0:64, blk(t)].rearrange("h d t -> d h t"), stb.ap, reads=[stb], writes=[bd])
            else:
                for tt in range(32):
                    ps = PS[nps % 4]
                    nps += 1
                    t = tt // 4
                    vs = vst[tt % 2]

                    def mm(e, ps=ps, tt=tt, t=t, w=w):
                        o = (tt % 4) * 128
                        for kc in range(8):
                            i = e.matmul(ps.ap, lhsT=xb[t].ap[:, kc, o:o + 128], rhs=w.ap[:, kc, :],
                                         start=(kc == 0), stop=(kc == 7))
                        return i
                    cx.op("pe", mm, reads=[w, xb[t]], writes=[ps])
                    cx.op("act", lambda e, ps=ps, vs=vs: e.activation(out=vs.ap[:, :, 0:64], in_=ps.ap.rearrange("p (h d) -> p h d", d=64),
                                                                      func=AF.Identity), reads=[ps], writes=[vs])
                    cx.dma("sp", VA[:, :, tt, :].rearrange("h p e -> p h e"), vs.ap, reads=[vs], writes=[B_VA])
        for t in range(NB):
            ps = PS[nps % 4]
            nps += 1

            def mm(e, ps=ps, t=t):
                for kc in range(8):
                    i = e.matmul(ps.ap[0:8, :], lhsT=wfg.ap[:, kc, :], rhs=xb[t].ap[:, kc, :], start=(kc == 0), stop=(kc == 7))
                return i
            cx.op("pe", mm, reads=[wfg, xb[t]], writes=[ps])
            cx.op("act", lambda e, ps=ps, t=t: e.activation(out=fgT.ap[:, blk(t)], in_=ps.ap[0:8, :], func=AF.Identity),
                  reads=[ps], writes=[fgT])
        cx.barrier(skip_sw=True)
        sc.close()
        sc = Scope(cx)
        bf = sc.buf([8, 1], F32, "bf")
        cx.dma("sp", bf.ap, b_f[l].rearrange("(h o) -> h o", o=1), writes=[bf])
        nbf = sc.buf([8, 1], F32, "nbf")
        cx.op("dve", lambda e: e.tensor_scalar(out=nbf.ap, in0=bf.ap, scalar1=-1.0, scalar2=None, op0=ALU.mult), reads=[bf], writes=[nbf])
        one8 = sc.buf([8, 1], F32, "one8")
        cx.op("dve", lambda e: e.memset(one8.ap, 1.0), writes=[one8])
        ex = sc.buf([8, L], F32, "ex")
        cx.op("act", lambda e: e.activation(out=ex.ap, in_=fgT.ap, func=AF.Exp, bias=nbf.ap, scale=-1.0), reads=[fgT, nbf], writes=[ex])
        cx.op("act", lambda e: e.activation(out=ex.ap, in_=ex.ap, func=AF.Ln, bias=one8.ap, scale=1.0), reads=[ex, one8], writes=[ex])
        csum = sc.buf([8, L], F32, "csum")
        cx.op("dve", lambda e: e.tensor_tensor_scan(out=csum.ap, data0=one8.ap.to_broadcast([8, L]), data1=ex.ap, initial=0.0,
                                                    op0=ALU.mult, op1=ALU.add), reads=[ex, one8], writes=[csum])
        pcs = [sc.buf([8, L], BF16, f"pc{j}") for j in range(3)]
        ncs = [sc.buf([8, L], BF16, f"nc{j}") for j in range(3)]
        res = ex
        cx.op("dve", lambda e: e.tensor_copy(out=pcs[0].ap, in_=csum.ap), reads=[csum], writes=[pcs[0]])
        cx.op("dve", lambda e: e.tensor_tensor(out=res.ap, in0=csum.ap, in1=pcs[0].ap, op=ALU.subtract), reads=[csum, pcs[0]], writes=[res])
        cx.op("dve", lambda e: e.tensor_copy(out=pcs[1].ap, in_=res.ap), reads=[res], writes=[pcs[1]])
        cx.op("dve", lambda e: e.tensor_tensor(out=res.ap, in0=res.ap, in1=pcs[1].ap, op=ALU.subtract), reads=[res, pcs[1]], writes=[res])
        cx.op("dve", lambda e: e.tensor_copy(out=pcs[2].ap, in_=res.ap), reads=[res], writes=[pcs[2]])
        for j in range(3):
            cx.op("dve", lambda e, j=j: e.tensor_scalar(out=ncs[j].ap, in0=pcs[j].ap, scalar1=-1.0, scalar2=None, op0=ALU.mult),
                  reads=[pcs[j]], writes=[ncs[j]])
        onesb = sc.buf([8, L], BF16, "onesb")
        cx.op("pool", lambda e: e.memset(onesb.ap, 1.0), writes=[onesb])
        for j in range(3):
            cx.dma("sp", QA[:, 64 + j, :], ncs[j].ap, reads=[ncs[j]], writes=[B_QA])
            cx.dma("sp", QA[:, 67 + j, :], onesb.ap, reads=[onesb], writes=[B_QA])
            cx.dma("sp", KA[:, 64 + j, :], onesb.ap, reads=[onesb], writes=[B_KA])
            cx.dma("sp", KA[:, 67 + j, :], pcs[j].ap, reads=[pcs[j]], writes=[B_KA])
        cx.barrier(skip_sw=True)
        sc.close()
        sc_fg.close()

    def phase_attn(l):
        sc = Scope(cx)
        qa = [sc.buf([70, L], BF16, f"qa{i}") for i in range(2)]
        ka = [sc.buf([70, L], BF16, f"ka{i}") for i in range(2)]
        va = [sc.buf([128, 32, 65], BF16, f"va{i}") for i in range(2)]
        NPT = 6
        pt = [sc.buf([128, TB], BF16, f"pt{i}") for i in range(NPT)]
        rden = [sc.buf([128, TB], F32, f"rden{i}") for i in range(2)]
        rb = [sc.buf([64, TB], F32, f"rb{i}") for i in range(2)]
        ost = [sc.buf([64, TB], BF16, f"ost{i}") for i in range(2)]

        def load_head(h):
            cx.dma("sp", qa[h % 2].ap, QA[h], reads=[B_QA], writes=[qa[h % 2]])
            cx.dma("sp", ka[h % 2].ap, KA[h], reads=[B_KA], writes=[ka[h % 2]])
            cx.dma("sp", va[h % 2].ap, VA[h], reads=[B_VA], writes=[va[h % 2]])
        items = []
        nb = 0
        for h in range(8):
            for I in range(NB):
                nkb = 4 * I + 4
                for j in range(nkb):
                    items.append((h, I, j, nkb, nb))
                nb += 1
        LA = 3

        def emit_S(i):
            h, I, j, nkb, b_ = items[i]
            c0 = 128 * max(0, j - 4 * I)
            diag = j >= 4 * I
            ps = PS[i % 4]
            k, q = ka[h % 2], qa[h % 2]

            def mm(e):
                i_ = e.matmul(ps.ap[:, c0:TB], lhsT=k.ap[:, j * 128:(j + 1) * 128], rhs=q.ap[:, I * TB + c0:(I + 1) * TB],
                              start=True, stop=not diag)
                if diag:
                    i_ = e.matmul(ps.ap[:, c0:c0 + 128], lhsT=identb.ap, rhs=negtri.ap, start=False, stop=True)
                return i_
            cx.op("pe", mm, reads=[k, q, identb, negtri], writes=[ps])

        def finalize(h, I, b_):
            po = PS[4 + b_ % 2]
            pr = PS[6 + b_ % 2]
            rd, r_, o_ = rden[b_ % 2], rb[b_ % 2], ost[b_ % 2]
            cx.op("dve", lambda e: e.reciprocal(out=rd.ap[64:65, :], in_=po.ap[64:65, :]), reads=[po], writes=[rd])
            cx.op("pe", lambda e: e.matmul(pr.ap[0:64, :], lhsT=ones1.ap[64:65, :], rhs=rd.ap[64:65, :], start=True, stop=True),
                  reads=[ones1, rd], writes=[pr])
            cx.op("act", lambda e: e.activation(out=r_.ap, in_=pr.ap[0:64, :], func=AF.Identity), reads=[pr], writes=[r_])
            cx.op("dve", lambda e: e.tensor_tensor(out=o_.ap, in0=po.ap[0:64, :], in1=r_.ap, op=ALU.mult), reads=[po, r_], writes=[o_])
            cx.dma("sp", YS[1024 + h * 64:1024 + (h + 1) * 64, blk(I)], o_.ap, reads=[o_], writes=[B_YS[2]])
        load_head(0)
        for i in range(min(LA, len(items))):
            emit_S(i)
        pending = []
        for i, (h, I, j, nkb, b_) in enumerate(items):
            if I == 0 and j == 0 and h + 1 < 8:
                load_head(h + 1)
            if i + LA < len(items):
                emit_S(i + LA)
            c0 = 128 * max(0, j - 4 * I)
            ps = PS[i % 4]
            p = pt[i % NPT]
            v = va[h % 2]
            po = PS[4 + b_ % 2]
            cx.op("act", lambda e: e.activation(out=p.ap[:, c0:TB], in_=ps.ap[:, c0:TB], func=AF.Exp), reads=[ps], writes=[p])
            cx.op("pe", lambda e: e.matmul(po.ap[0:65, c0:TB], lhsT=v.ap[:, j, :], rhs=p.ap[:, c0:TB], start=(j == 0), stop=(j == nkb - 1)),
                  reads=[v, p], writes=[po])
            pending = [(cnt - 1, args) for (cnt, args) in pending]
            for cnt, args in [x for x in pending if x[0] <= 0]:
                finalize(*args)
            pending = [x for x in pending if x[0] > 0]
            if j == nkb - 1:
                pending.append((2, (h, I, b_)))
        for cnt, args in pending:
            finalize(*args)
        cx.barrier(skip_sw=True)
        sc.close()

    def phase_lru(l):
        sc = Scope(cx)
        cw = sc.buf([128, 4, 4], F32, "cw")
        cb = sc.buf([128, 4], F32, "cb")
        ba = sc.buf([128, 4], F32, "ba")
        bx = sc.buf([128, 4], F32, "bx")
        lam = sc.buf([128, 4], F32, "lam")
        sneg = sc.buf([128, 4], F32, "sneg")
        one_c = sc.buf([128, 1], F32, "one_c")
        cx.op("dve", lambda e: e.memset(one_c.ap, 1.0), writes=[one_c])
        for k_ in range(4):
            cx.dma("sp", cw.ap[:, :, k_], lru_conv_w[l, k_].rearrange("(c p) -> p c", p=128), writes=[cw], allow_slow_non_contiguous=True)
        for (dst, src) in ((cb, lru_conv_b), (ba, lru_b_a), (bx, lru_b_x), (lam, lru_lambda)):
            cx.dma("sp", dst.ap, src[l].rearrange("(c p) -> p c", p=128), writes=[dst], allow_slow_non_contiguous=True)
        cx.op("act", lambda e: e.activation(out=sneg.ap, in_=lam.ap, func=AF.Exp, scale=-1.0), reads=[lam], writes=[sneg])
        cx.op("act", lambda e: e.activation(out=sneg.ap, in_=sneg.ap, func=AF.Ln, bias=one_c.ap, scale=1.0), reads=[sneg, one_c], writes=[sneg])
        cx.op("dve", lambda e: e.tensor_scalar(out=sneg.ap, in0=sneg.ap, scalar1=-8.0, scalar2=None, op0=ALU.mult), reads=[sneg], writes=[sneg])
        WA = sc.buf([128, 4, 128], BF16, "WA")
        WX = sc.buf([128, 4, 128], BF16, "WX")
        for Wm, src in ((WA, lru_w_a), (WX, lru_w_x)):
            cx.op("pool", lambda e, Wm=Wm: e.memset(Wm.ap, 0.0), writes=[Wm])
            for c in range(4):
                cx.dma("pool", Wm.ap[0:64, c, 0:64], src[l, 2 * c], writes=[Wm])
                cx.dma("pool", Wm.ap[64:128, c, 64:128], src[l, 2 * c + 1], writes=[Wm])
        hba = sc.buf([128, 4], F32, "hba")
        hbx = sc.buf([128, 4], F32, "hbx")
        hsn = sc.buf([128, 4], F32, "hsn")
        for dst, src in ((hba, ba), (hbx, bx), (hsn, sneg)):
            cx.op("dve", lambda e, dst=dst, src=src: e.tensor_scalar(out=dst.ap, in0=src.ap, scalar1=0.5, scalar2=None, op0=ALU.mult),
                  reads=[src], writes=[dst])
        xl = sc.buf([128, L + 3], F32, "xl")
        gl = sc.buf([128, L], F32, "gl")
        xc = sc.buf([128, L], F32, "xc")
        xcb = sc.buf([128, L], BF16, "xcb")
        a_all = sc.buf([128, L], F32, "a_all")
        tr_all = sc.buf([128, L], F32, "tr_all")
        ti_all = sc.buf([128, L], F32, "ti_all")
        h_all = sc.buf([128, L], F32, "h_all")
        yb = sc.buf([128, L], BF16, "yb")
        cx.op("pool", lambda e: e.memset(xl.ap[:, 0:3], 0.0), writes=[xl])
        n = 0
        for c in range(4):
            cx.dma("sp", xl.ap[:, 3:], XL[c * 128:(c + 1) * 128, :], reads=[B_XL], writes=[xl])
            cx.dma("sp", gl.ap, GL[c * 128:(c + 1) * 128, :], reads=[B_GL], writes=[gl])
            cx.op("dve", lambda e, c=c: e.tensor_scalar(out=xc.ap, in0=xl.ap[:, 0:L], scalar1=cw.ap[:, c, 0:1], scalar2=cb.ap[:, c:c + 1],
                                                        op0=ALU.mult, op1=ALU.add), reads=[xl, cw, cb], writes=[xc])
            for k_ in range(1, 4):
                cx.op("dve", lambda e, c=c, k_=k_: e.scalar_tensor_tensor(out=xc.ap, in0=xl.ap[:, k_:k_ + L], scalar=cw.ap[:, c, k_:k_ + 1],
                                                                         in1=xc.ap, op0=ALU.mult, op1=ALU.add), reads=[xl, cw, xc], writes=[xc])
            for hh_ in range(2):
                hs_ = slice(hh_ * (L // 2), (hh_ + 1) * (L // 2))
                cx.op("act", lambda e, hs_=hs_: e.activation(out=xcb.ap[:, hs_], in_=xc.ap[:, hs_], func=AF.Identity), reads=[xc], writes=[xcb])
            for t in range(NB):
                pa, px = PS[(2 * n) % 4], PS[(2 * n + 1) % 4]
                n += 1
                cx.op("pe", lambda e, pa=pa, c=c, t=t: e.matmul(pa.ap, lhsT=WA.ap[:, c, :], rhs=xcb.ap[:, blk(t)], start=True, stop=True),
                      reads=[WA, xcb], writes=[pa])
                cx.op("pe", lambda e, px=px, c=c, t=t: e.matmul(px.ap, lhsT=WX.ap[:, c, :], rhs=xcb.ap[:, blk(t)], start=True, stop=True),
                      reads=[WX, xcb], writes=[px])
                cx.op("act", lambda e, pa=pa, c=c, t=t: e.activation(out=tr_all.ap[:, blk(t)], in_=pa.ap, func=AF.Tanh, bias=hba.ap[:, c:c + 1], scale=0.5),
                      reads=[pa, hba], writes=[tr_all])
                cx.op("act", lambda e, px=px, c=c, t=t: e.activation(out=ti_all.ap[:, blk(t)], in_=px.ap, func=AF.Tanh, bias=hbx.ap[:, c:c + 1], scale=0.5),
                      reads=[px, hbx], writes=[ti_all])
            for hh_ in range(2):
                hs_ = slice(hh_ * (L // 2), (hh_ + 1) * (L // 2))
                cx.op("act", lambda e, c=c, hs_=hs_: e.activation(out=a_all.ap[:, hs_], in_=tr_all.ap[:, hs_], func=AF.Exp, bias=hsn.ap[:, c:c + 1],
                                                               scale=hsn.ap[:, c:c + 1]), reads=[tr_all, hsn], writes=[a_all])
            for hh_ in range(2):
                hs_ = slice(hh_ * (L // 2), (hh_ + 1) * (L // 2))
                cx.op("act", lambda e, hs_=hs_: e.activation(out=tr_all.ap[:, hs_], in_=a_all.ap[:, hs_], func=AF.Square), reads=[a_all], writes=[tr_all])
            cx.op("dve", lambda e: e.tensor_scalar(out=ti_all.ap, in0=ti_all.ap, scalar1=0.5, scalar2=0.5, op0=ALU.mult, op1=ALU.add),
                  reads=[ti_all], writes=[ti_all])
            cx.op("pool", lambda e: e.tensor_tensor(out=ti_all.ap, in0=ti_all.ap, in1=xc.ap, op=ALU.mult), reads=[ti_all, xc], writes=[ti_all])
            for hh_ in range(2):
                hs_ = slice(hh_ * (L // 2), (hh_ + 1) * (L // 2))
                cx.op("act", lambda e, hs_=hs_: e.activation(out=tr_all.ap[:, hs_], in_=tr_all.ap[:, hs_], func=AF.Sqrt, bias=one_c.ap, scale=-1.0),
                      reads=[tr_all, one_c], writes=[tr_all])
            cx.op("dve", lambda e: e.tensor_tensor(out=ti_all.ap, in0=ti_all.ap, in1=tr_all.ap, op=ALU.mult), reads=[ti_all, tr_all], writes=[ti_all])
            cx.op("dve", lambda e: e.tensor_tensor_scan(out=h_all.ap, data0=a_all.ap, data1=ti_all.ap, initial=0.0, op0=ALU.mult, op1=ALU.add),
                  reads=[a_all, ti_all], writes=[h_all])
            cx.op("dve", lambda e: e.tensor_tensor(out=yb.ap, in0=h_all.ap, in1=gl.ap, op=ALU.mult), reads=[h_all, gl], writes=[yb])
            cx.dma("sp", YS[512 + c * 128:512 + (c + 1) * 128, :], yb.ap, reads=[yb], writes=[B_YS[1]])
        cx.barrier(skip_sw=True)
        sc.close()

    def cmul(eng_a, eng_b, sc_t, outr, outi, ar, ai, br, bi, reads, w_r, w_i, negi=False):
        t1, t2 = sc_t
        cx.op(eng_a, lambda e: e.tensor_tensor(out=t1.ap, in0=ar, in1=br, op=ALU.mult), reads=reads, writes=[t1])
        cx.op(eng_b, lambda e: e.tensor_tensor(out=t2.ap, in0=ai, in1=bi, op=ALU.mult), reads=reads, writes=[t2])
        cx.op(eng_a, lambda e: e.tensor_tensor(out=outr, in0=t1.ap, in1=t2.ap, op=ALU.subtract), reads=[t1, t2], writes=[w_r])
        cx.op(eng_a, lambda e: e.tensor_tensor(out=t1.ap, in0=ar, in1=bi, op=ALU.mult), reads=reads + [w_r], writes=[t1])
        cx.op(eng_b, lambda e: e.tensor_tensor(out=t2.ap, in0=ai, in1=br, op=ALU.mult), reads=reads + [w_r], writes=[t2])
        if negi:
            cx.op(eng_a, lambda e: e.scalar_tensor_tensor(out=outi, in0=t1.ap, scalar=-1.0, in1=t2.ap, op0=ALU.mult, op1=ALU.subtract),
                  reads=[t1, t2], writes=[w_i])
        else:
            cx.op(eng_a, lambda e: e.tensor_tensor(out=outi, in0=t1.ap, in1=t2.ap, op=ALU.add), reads=[t1, t2], writes=[w_i])

    def phase_s5(l):
        ws = Scope(cx)
        Mw = ws.buf([128, 32, 128], BF16, "Mw")
        W1r = ws.buf([128, 32, 64], BF16, "W1r")
        W1i = ws.buf([128, 32, 64], BF16, "W1i")
        W2r = ws.buf([128, 16, 128], BF16, "W2r")
        W2i = ws.buf([128, 16, 128], BF16, "W2i")
        rho = ws.buf([128, 16], F32, "rho")
        ph_r = ws.buf([128, 16], F32, "ph_r")
        ph_i = ws.buf([128, 16], F32, "ph_i")
        dcol = ws.buf([128, 32], F32, "dcol")
        bglu = ws.buf([128, 4], F32, "bglu")
        cx.dma("sp", bglu.ap, s5_b_glu[l].rearrange("(c p) -> p c", p=128), writes=[bglu], allow_slow_non_contiguous=True)
        for tau in range(8):
            cx.dma("sp", dcol.ap[16 * tau:16 * tau + 16, :], s5_d[l].rearrange("(g c) -> c g", c=16), writes=[dcol],
                   allow_slow_non_contiguous=True)
        sc = Scope(cx)
        N = 64
        araw = sc.buf([32, 64], F32, "araw")
        airaw = sc.buf([32, 64], F32, "airaw")
        cx.dma("sp", araw.ap, s5_a_re[l], writes=[araw])
        cx.dma("sp", airaw.ap, s5_a_im[l], writes=[airaw])
        are = sc.buf([N, 32], F32, "are")
        aim = sc.buf([N, 32], F32, "aim")
        for src, dst in ((araw, are), (airaw, aim)):
            cx.op("pe", lambda e, src=src: e.matmul(PS[0].ap[0:64, 0:32], lhsT=src.ap, rhs=identf.ap[0:32, 0:32], start=True, stop=True),
                  reads=[src, identf], writes=[PS[0]])
            cx.op("act", lambda e, dst=dst: e.activation(out=dst.ap, in_=PS[0].ap[0:64, 0:32], func=AF.Identity), reads=[PS[0]], writes=[dst])
        dt = sc.buf([N, 32], F32, "dt")
        cx.dma("sp", dt.ap, s5_log_dt[l].partition_broadcast(N), writes=[dt])
        Br = sc.buf([N, 32, 16], F32, "Br")
        Bi = sc.buf([N, 32, 16], F32, "Bi")
        cx.dma("sp", Br.ap, s5_b_re[l].rearrange("g n c -> n g c"), writes=[Br])
        cx.dma("sp", Bi.ap, s5_b_im[l].rearrange("g n c -> n g c"), writes=[Bi])
        Cr = sc.buf([N, 32, 16], F32, "Cr")
        Ci = sc.buf([N, 32, 16], F32, "Ci")
        craw = sc.buf([128, 4, 64], F32, "craw")
        for src, dst in ((s5_c_re, Cr), (s5_c_im, Ci)):
            cx.dma("sp", craw.ap, src[l].rearrange("(j p) n -> p j n", p=128), writes=[craw])

            def mm(e):
                for j in range(4):
                    i = e.matmul(PS[1].ap[0:64, j * 128:(j + 1) * 128], lhsT=craw.ap[:, j, :], rhs=identf.ap, start=True, stop=True)
                return i
            cx.op("pe", mm, reads=[craw, identf], writes=[PS[1]])
            cx.op("act", lambda e, dst=dst: e.activation(out=dst.ap.rearrange("n g c -> n (g c)"), in_=PS[1].ap[0:64, :], func=AF.Identity),
                  reads=[PS[1]], writes=[dst])

        def small(name):
            return sc.buf([N, 32], F32, name)
        ar, ang, mag, lbr, lbi = small("ar"), small("ang"), small("mag"), small("lbr"), small("lbi")
        t1, t2, t3 = small("t1"), small("t2"), small("t3")
        V = "dve"

        def tt(out, a, b, op_, eng=V):
            cx.op(eng, lambda e: e.tensor_tensor(out=out.ap, in0=a.ap, in1=b.ap, op=op_), reads=[a, b], writes=[out])

        def tsc(out, a, s1, op0, s2=None, op1=None, eng=V):
            if op1 is None:
                cx.op(eng, lambda e: e.tensor_scalar(out=out.ap, in0=a.ap, scalar1=s1, scalar2=None, op0=op0), reads=[a], writes=[out])
            else:
                cx.op(eng, lambda e: e.tensor_scalar(out=out.ap, in0=a.ap, scalar1=s1, scalar2=s2, op0=op0, op1=op1), reads=[a], writes=[out])

        def act(out, a, fn, scale=1.0, bias=None):
            if bias is None:
                cx.op("act", lambda e: e.activation(out=out.ap, in_=a.ap, func=fn, scale=scale), reads=[a], writes=[out])
            else:
                cx.op("act", lambda e: e.activation(out=out.ap, in_=a.ap, func=fn, scale=scale, bias=bias.ap), reads=[a, bias], writes=[out])

        zero_c = sc.buf([N, 1], F32, "zero_c")
        cx.op("dve", lambda e: e.memset(zero_c.ap, 0.0), writes=[zero_c])

        def sin_of(out, angle_buf, shift):
            tsc(t1, angle_buf, 1.0 / TWO_PI, ALU.mult, (shift / TWO_PI) + MAGIC, ALU.add)
            tsc(t1, t1, -MAGIC, ALU.add)
            cx.op(V, lambda e: e.scalar_tensor_tensor(out=t2.ap, in0=t1.ap, scalar=-TWO_PI, in1=angle_buf.ap, op0=ALU.mult, op1=ALU.add),
                  reads=[t1, angle_buf], writes=[t2])
            tsc(t2, t2, float(shift), ALU.add, math.pi - 1e-6, ALU.min)
            tsc(t2, t2, -(math.pi - 1e-6), ALU.max)
            act(out, t2, AF.Sin, bias=zero_c)

        em1, xr_, w_, sn, cm1, nn = small("em1"), small("xr_"), small("w_"), small("sn"), small("cm1"), small("nn")

        def nested(out, var, divs, sign):
            cx.op(V, lambda e: e.memset(out.ap, 1.0), writes=[out])
            for dv in divs:
                tt(t3, out, var, ALU.mult)
                tsc(out, t3, sign / dv, ALU.mult, 1.0, ALU.add)
        ld8 = small("ld8")
        tsc(ld8, dt, 0.125, ALU.mult)
        nested(dt, ld8, [float(k) for k in range(12, 0, -1)], 1.0)
        for _ in range(3):
            tt(dt, dt, dt, ALU.mult)
        tt(ar, are, dt, ALU.mult)
        tt(ang, aim, dt, ALU.mult)
        nested(nn, ar, [9.0, 8.0, 7.0, 6.0, 5.0, 4.0, 3.0, 2.0], 1.0)
        tt(em1, nn, ar, ALU.mult)
        tsc(mag, em1, 1.0, ALU.add)
        C1 = 6.28125
        C2 = TWO_PI - C1
        tsc(t1, ang, 1.0 / TWO_PI, ALU.mult, MAGIC, ALU.add)
        tsc(t1, t1, -MAGIC, ALU.add)
        cx.op(V, lambda e: e.scalar_tensor_tensor(out=xr_.ap, in0=t1.ap, scalar=-C1, in1=ang.ap, op0=ALU.mult, op1=ALU.add),
              reads=[t1, ang], writes=[xr_])
        cx.op(V, lambda e: e.scalar_tensor_tensor(out=xr_.ap, in0=t1.ap, scalar=-C2, in1=xr_.ap, op0=ALU.mult, op1=ALU.add),
              reads=[t1, xr_], writes=[xr_])
        tt(w_, xr_, xr_, ALU.mult)
        nested(nn, w_, [float((2 * k) * (2 * k + 1)) for k in range(10, 0, -1)], -1.0)
        tt(sn, nn, xr_, ALU.mult)
        nested(nn, w_, [float((2 * k + 1) * (2 * k + 2)) for k in range(10, 0, -1)], -1.0)
        tt(cm1, nn, w_, ALU.mult)
        tsc(cm1, cm1, -0.5, ALU.mult)
        lm1 = small("lm1")
        tt(t1, em1, cm1, ALU.mult)
        tt(t2, em1, cm1, ALU.add)
        tt(lm1, t1, t2, ALU.add)
        tsc(lbr, lm1, 1.0, ALU.add)
        tt(lbi, sn, mag, ALU.mult)
        den, qr, qi = small("den"), small("qr"), small("qi")
        tt(den, are, are, ALU.mult)
        tt(t1, aim, aim, ALU.mult)
        tt(den, den, t1, ALU.add)
        cx.op(V, lambda e: e.reciprocal(out=den.ap, in_=den.ap), reads=[den], writes=[den])
        tt(t1, lm1, are, ALU.mult)
        tt(t2, lbi, aim, ALU.mult)
        tt(qr, t1, t2, ALU.add)
        tt(qr, qr, den, ALU.mult)
        tt(t1, lbi, are, ALU.mult)
        tt(t2, lm1, aim, ALU.mult)
        tt(qi, t1, t2, ALU.subtract)
        tt(qi, qi, den, ALU.mult)
        Bbr = sc.buf([N, 32, 16], F32, "Bbr")
        Bbi = sc.buf([N, 32, 16], F32, "Bbi")
        tb1 = sc.buf([N, 32, 16], F32, "tb1")
        tb2 = sc.buf([N, 32, 16], F32, "tb2")

        def bc3(b):
            return b.ap.unsqueeze(2).to_broadcast([N, 32, 16])
        cmul("dve", "pool", (tb1, tb2), Bbr.ap, Bbi.ap, bc3(qr), bc3(qi), Br.ap, Bi.ap, [qr, qi, Br, Bi], Bbr, Bbi)
        Pr = sc.buf([N, 32, 9], F32, "Pr")
        Pi = sc.buf([N, 32, 9], F32, "Pi")
        Qr = sc.buf([N, 32, 8], F32, "Qr")
        Qi = sc.buf([N, 32, 8], F32, "Qi")
        ibr, ibi, im2 = small("ibr"), small("ibi"), small("im2")
        tt(im2, mag, mag, ALU.mult)
        cx.op(V, lambda e: e.reciprocal(out=im2.ap, in_=im2.ap), reads=[im2], writes=[im2])
        tt(ibr, lbr, im2, ALU.mult)
        tt(ibi, lbi, im2, ALU.mult)
        tsc(ibi, ibi, -1.0, ALU.mult)
        for (Xr, Xi, br_, bi_, n_) in ((Pr, Pi, lbr, lbi, 9), (Qr, Qi, ibr, ibi, 8)):
            cx.op(V, lambda e, Xr=Xr: e.memset(Xr.ap[:, :, 0:1], 1.0), writes=[Xr])
            cx.op(V, lambda e, Xi=Xi: e.memset(Xi.ap[:, :, 0:1], 0.0), writes=[Xi])
            for tau in range(1, n_):
                cmul("dve", "pool", (t1, t2), Xr.ap[:, :, tau], Xi.ap[:, :, tau], Xr.ap[:, :, tau - 1], Xi.ap[:, :, tau - 1],
                     br_.ap, bi_.ap, [Xr, Xi, br_, bi_], Xr, Xi)
        GH = 16
        big = [sc.buf([N, GH, 8, 16], F32, f"big{i}") for i in range(8)]
        Ar_, Ai_, Cmr, Cmin, T1, T2, A7r, A7i = big
        mtmp = [sc.buf([128, 128], F32, f"mtmp{i}") for i in range(2)]
        for gh in range(2):
            gs = slice(gh * GH, (gh + 1) * GH)

            def bq(b):
                return b.ap[:, gs, 0:8].unsqueeze(3).to_broadcast([N, GH, 8, 16])

            def bb(b):
                return b.ap[:, gs, :].unsqueeze(2).to_broadcast([N, GH, 8, 16])

            def b7(b, idx):
                return b.ap[:, gs, idx:idx + 1].unsqueeze(3).to_broadcast([N, GH, 8, 16])
            cmul("dve", "pool", (T1, T2), Ar_.ap, Ai_.ap, bq(Qr), bq(Qi), bb(Bbr), bb(Bbi), [Qr, Qi, Bbr, Bbi], Ar_, Ai_)
            cmul("dve", "pool", (T1, T2), Cmr.ap, Cmin.ap, bq(Pr), bq(Pi), bb(Cr), bb(Ci), [Pr, Pi, Cr, Ci], Cmr, Cmin, negi=True)
            cmul("dve", "pool", (T1, T2), A7r.ap, A7i.ap, b7(Pr, 7), b7(Pi, 7), Ar_.ap, Ai_.ap, [Pr, Pi, Ar_, Ai_], A7r, A7i)
            for src, dst in ((A7r, W1r), (A7i, W1i)):
                for g8 in range(2):
                    ps = PS[g8 % 4]

                    def mm(e, ps=ps, src=src, g8=g8):
                        for gi in range(8):
                            i = e.matmul(ps.ap[:, gi * 64:(gi + 1) * 64], lhsT=src.ap[:, g8 * 8 + gi].rearrange("n t c -> n (t c)"),
                                         rhs=identf.ap[0:64, 0:64], start=True, stop=True)
                        return i
                    cx.op("pe", mm, reads=[src, identf], writes=[ps])
                    g0 = gh * GH + g8 * 8
                    cx.op("act", lambda e, ps=ps, dst=dst, g0=g0: e.activation(
                        out=dst.ap[:, g0:g0 + 8, :], in_=ps.ap.rearrange("p (g n) -> p g n", n=64), func=AF.Identity),
                        reads=[ps], writes=[dst])
            for g4 in range(4):
                ps = PS[g4 % 4]

                def mm(e, ps=ps, g4=g4):
                    for gi in range(4):
                        gl_ = g4 * 4 + gi
                        e.matmul(ps.ap[:, gi * 128:(gi + 1) * 128], lhsT=Ar_.ap[:, gl_].rearrange("n t c -> n (t c)"),
                                 rhs=Cmr.ap[:, gl_].rearrange("n t c -> n (t c)"), start=True, stop=False)
                        i = e.matmul(ps.ap[:, gi * 128:(gi + 1) * 128], lhsT=Ai_.ap[:, gl_].rearrange("n t c -> n (t c)"),
                                     rhs=Cmin.ap[:, gl_].rearrange("n t c -> n (t c)"), start=False, stop=True)
                    return i
                cx.op("pe", mm, reads=[Ar_, Ai_, Cmr, Cmin], writes=[ps])
                for gi in range(4):
                    g = gh * GH + g4 * 4 + gi
                    mt = mtmp[g % 2]
                    cx.op("dve", lambda e, ps=ps, gi=gi, mt=mt: e.tensor_tensor(out=mt.ap, in0=ps.ap[:, gi * 128:(gi + 1) * 128], in1=mask8.ap,
                                                                                op=ALU.mult), reads=[ps, mask8], writes=[mt])
                    cx.op("dve", lambda e, g=g, mt=mt: e.scalar_tensor_tensor(out=Mw.ap[:, g, :], in0=identf.ap, scalar=dcol.ap[:, g:g + 1],
                                                                              in1=mt.ap, op0=ALU.mult, op1=ALU.add),
                          reads=[identf, dcol, mt], writes=[Mw])
            W2r_f, W2i_f = A7r, A7i
            cx.op("dve", lambda e: e.tensor_tensor(out=T1.ap, in0=b7(Pr, 1), in1=Cmr.ap, op=ALU.mult), reads=[Pr, Cmr], writes=[T1])
            cx.op("pool", lambda e: e.tensor_tensor(out=T2.ap, in0=b7(Pi, 1), in1=Cmin.ap, op=ALU.mult), reads=[Pi, Cmin], writes=[T2])
            cx.op("dve", lambda e: e.tensor_tensor(out=W2r_f.ap, in0=T1.ap, in1=T2.ap, op=ALU.add), reads=[T1, T2], writes=[W2r_f])
            cx.op("dve", lambda e: e.tensor_tensor(out=T1.ap, in0=b7(Pr, 1), in1=Cmin.ap, op=ALU.mult), reads=[Pr, Cmin, W2r_f], writes=[T1])
            cx.op("pool", lambda e: e.tensor_tensor(out=T2.ap, in0=b7(Pi, 1), in1=Cmr.ap, op=ALU.mult), reads=[Pi, Cmr, W2r_f], writes=[T2])
            cx.op("dve", lambda e: e.tensor_tensor(out=W2i_f.ap, in0=T1.ap, in1=T2.ap, op=ALU.subtract), reads=[T1, T2], writes=[W2i_f])
            for src, dst in ((W2r_f, W2r), (W2i_f, W2i)):
                v = src.ap.rearrange("n (q two) t c -> n q two (t c)", two=2)
                q0 = gh * (GH // 2)
                cx.op("act", lambda e, v=v, dst=dst, q0=q0: e.activation(out=dst.ap[0:64, q0:q0 + GH // 2, :], in_=v[:, :, 0, :], func=AF.Identity),
                      reads=[src], writes=[dst])
                cx.op("act", lambda e, v=v, dst=dst, q0=q0: e.activation(out=dst.ap[64:128, q0:q0 + GH // 2, :], in_=v[:, :, 1, :], func=AF.Identity),
                      reads=[src], writes=[dst])
        r8, e8, p8r, p8i = small("r8"), small("e8"), small("p8r"), small("p8i")
        tt(r8, mag, mag, ALU.mult)
        tt(r8, r8, r8, ALU.mult)
        tt(r8, r8, r8, ALU.mult)
        cx.op(V, lambda e: e.reciprocal(out=e8.ap, in_=r8.ap), reads=[r8], writes=[e8])
        cx.op(V, lambda e: e.tensor_tensor(out=p8r.ap, in0=Pr.ap[:, :, 8], in1=e8.ap, op=ALU.mult), reads=[Pr, e8], writes=[p8r])
        cx.op(V, lambda e: e.tensor_tensor(out=p8i.ap, in0=Pi.ap[:, :, 8], in1=e8.ap, op=ALU.mult), reads=[Pi, e8], writes=[p8i])
        for src, dst in ((r8, rho), (p8r, ph_r), (p8i, ph_i)):
            v = src.ap.rearrange("n (q two) -> n q two", two=2)
            cx.op("act", lambda e, v=v, dst=dst: e.activation(out=dst.ap[0:64], in_=v[:, :, 0], func=AF.Identity), reads=[src], writes=[dst])
            cx.op("act", lambda e, v=v, dst=dst: e.activation(out=dst.ap[64:128], in_=v[:, :, 1], func=AF.Identity), reads=[src], writes=[dst])
        cx.barrier(skip_sw=True)
        sc.close()
        if stop_after == ("s5prep", l):
            ws.close()
            return
        sc = Scope(cx)
        gb = sc.buf([128, 4, L], BF16, "gb")
        SelS = sc.buf([128, 8, 8, 128], BF16, "SelS")
        SelTS = sc.buf([128, 8, 8, 128], BF16, "SelTS")
        cx.dma("sp", SelS.ap, SELD[0], reads=[B_SELD], writes=[SelS])
        cx.dma("sp", SelTS.ap, SELD[1], reads=[B_SELD], writes=[SelTS])
        NQ = 4
        for qt in range(4):
            s2 = Scope(cx)
            uT = s2.buf([128, L], BF16, "uT")
            cx.dma("sp", uT.ap, UT[qt * 128:(qt + 1) * 128, :], reads=[B_UT], writes=[uT])
            U3 = s2.buf([128, 8, TB], BF16, "U3")
            E1r = s2.buf([128, NQ, TB], F32, "E1r")
            E1i = s2.buf([128, NQ, TB], F32, "E1i")
            Hr = s2.buf([128, NQ, TB], BF16, "Hr")
            Hi = s2.buf([128, NQ, TB], BF16, "Hi")
            Y3 = s2.buf([128, 8, TB], BF16, "Y3")
            dd = [s2.buf([128, NQ, 256], F32, f"dd{i}") for i in range(4)]
            tmp = [s2.buf([128, TB], F32, f"s5t{i}") for i in range(8)]
            q0 = qt * NQ
            cx.op("dve", lambda e: e.tensor_copy(out=E1r.ap[:, :, 0], in_=ph_r.ap[:, q0:q0 + NQ]), reads=[ph_r], writes=[E1r])
            cx.op("dve", lambda e: e.tensor_copy(out=E1i.ap[:, :, 0], in_=ph_i.ap[:, q0:q0 + NQ]), reads=[ph_i], writes=[E1i])
            m = 1
            while m < TB:
                def bcm(b, m=m):
                    return b.ap[:, :, m - 1:m].to_broadcast([128, NQ, m])
                cx.op("dve", lambda e, m=m: e.tensor_tensor(out=dd[0].ap[:, :, 0:m], in0=E1r.ap[:, :, 0:m], in1=bcm(E1r), op=ALU.mult),
                      reads=[E1r], writes=[dd[0]])
                cx.op("pool", lambda e, m=m: e.tensor_tensor(out=dd[1].ap[:, :, 0:m], in0=E1i.ap[:, :, 0:m], in1=bcm(E1i), op=ALU.mult),
                      reads=[E1i], writes=[dd[1]])
                cx.op("dve", lambda e, m=m: e.tensor_tensor(out=dd[2].ap[:, :, 0:m], in0=E1r.ap[:, :, 0:m], in1=bcm(E1i), op=ALU.mult),
                      reads=[E1r, E1i], writes=[dd[2]])
                cx.op("pool", lambda e, m=m: e.tensor_tensor(out=dd[3].ap[:, :, 0:m], in0=E1i.ap[:, :, 0:m], in1=bcm(E1r), op=ALU.mult),
                      reads=[E1i, E1r], writes=[dd[3]])
                cx.op("dve", lambda e, m=m: e.tensor_tensor(out=E1r.ap[:, :, m:2 * m], in0=dd[0].ap[:, :, 0:m], in1=dd[1].ap[:, :, 0:m], op=ALU.subtract),
                      reads=[dd[0], dd[1]], writes=[E1r])
                cx.op("dve", lambda e, m=m: e.tensor_tensor(out=E1i.ap[:, :, m:2 * m], in0=dd[2].ap[:, :, 0:m], in1=dd[3].ap[:, :, 0:m], op=ALU.add),
                      reads=[dd[2], dd[3]], writes=[E1i])
                m *= 2
            npsu = 0
            for gl_ in range(8):
                ps = PS[npsu % 4]
                npsu += 1

                def mm(e, ps=ps, gl_=gl_):
                    for tau in range(8):
                        i = e.matmul(ps.ap, lhsT=SelS.ap[:, gl_, tau, :], rhs=uT.ap[:, tau::8], start=(tau == 0), stop=(tau == 7))
                    return i
                cx.op("pe", mm, reads=[SelS, uT], writes=[ps])
                cx.op("act", lambda e, ps=ps, gl_=gl_: e.activation(out=U3.ap[:, gl_, :], in_=ps.ap, func=AF.Identity), reads=[ps], writes=[U3])
            for ql in range(NQ):
                q_ = qt * NQ + ql
                ga, gb_ = 2 * ql, 2 * ql + 1
                pre, pim = PS[4 + (ql % 2) * 2], PS[5 + (ql % 2) * 2]

                def mm(e, pre=pre, pim=pim, ga=ga, gb_=gb_, qt=qt):
                    e.matmul(pre.ap[0:64, :], lhsT=W1r.ap[:, qt * 8 + ga, :], rhs=U3.ap[:, ga, :], start=True, stop=True)
                    e.matmul(pre.ap[64:128, :], lhsT=W1r.ap[:, qt * 8 + gb_, :], rhs=U3.ap[:, gb_, :], start=True, stop=True)
                    e.matmul(pim.ap[0:64, :], lhsT=W1i.ap[:, qt * 8 + ga, :], rhs=U3.ap[:, ga, :], start=True, stop=True)
                    return e.matmul(pim.ap[64:128, :], lhsT=W1i.ap[:, qt * 8 + gb_, :], rhs=U3.ap[:, gb_, :], start=True, stop=True)
                cx.op("pe", mm, reads=[W1r, W1i, U3], writes=[pre, pim])
                xr, xi, a1, a2, vr, vi, sr, si = tmp
                cx.op("act", lambda e, pre=pre: e.activation(out=xr.ap, in_=pre.ap, func=AF.Identity), reads=[pre], writes=[xr])
                cx.op("act", lambda e, pim=pim: e.activation(out=xi.ap, in_=pim.ap, func=AF.Identity), reads=[pim], writes=[xi])
                er, ei = E1r.ap[:, ql, :], E1i.ap[:, ql, :]
                cx.op("dve", lambda e, er=er: e.tensor_tensor(out=a1.ap, in0=xr.ap, in1=er, op=ALU.mult), reads=[xr, E1r], writes=[a1])
                cx.op("pool", lambda e, ei=ei: e.tensor_tensor(out=a2.ap, in0=xi.ap, in1=ei, op=ALU.mult), reads=[xi, E1i], writes=[a2])
                cx.op("dve", lambda e: e.tensor_tensor(out=vr.ap, in0=a1.ap, in1=a2.ap, op=ALU.add), reads=[a1, a2], writes=[vr])
                cx.op("dve", lambda e, er=er: e.tensor_tensor(out=a1.ap, in0=xi.ap, in1=er, op=ALU.mult), reads=[xi, E1r], writes=[a1])
                cx.op("pool", lambda e, ei=ei: e.tensor_tensor(out=a2.ap, in0=xr.ap, in1=ei, op=ALU.mult), reads=[xr, E1i], writes=[a2])
                cx.op("dve", lambda e: e.tensor_tensor(out=vi.ap, in0=a1.ap, in1=a2.ap, op=ALU.subtract), reads=[a1, a2], writes=[vi])
                rc = rho.ap[:, q_:q_ + 1].to_broadcast([128, TB])
                cx.op("dve", lambda e, rc=rc: e.tensor_tensor_scan(out=sr.ap, data0=rc, data1=vr.ap, initial=0.0, op0=ALU.mult, op1=ALU.add),
                      reads=[rho, vr], writes=[sr])
                cx.op("dve", lambda e, rc=rc: e.tensor_tensor_scan(out=si.ap, data0=rc, data1=vi.ap, initial=0.0, op0=ALU.mult, op1=ALU.add),
                      reads=[rho, vi], writes=[si])
                cx.op("dve", lambda e, er=er: e.tensor_tensor(out=a1.ap, in0=sr.ap, in1=er, op=ALU.mult), reads=[sr, E1r], writes=[a1])
                cx.op("pool", lambda e, ei=ei: e.tensor_tensor(out=a2.ap, in0=si.ap, in1=ei, op=ALU.mult), reads=[si, E1i], writes=[a2])
                cx.op("pool", lambda e, ql=ql: e.memset(Hr.ap[:, ql, 0:1], 0.0), writes=[Hr])
                cx.op("pool", lambda e, ql=ql: e.memset(Hi.ap[:, ql, 0:1], 0.0), writes=[Hi])
                cx.op("dve", lambda e, ql=ql: e.tensor_tensor(out=Hr.ap[:, ql, 1:TB], in0=a1.ap[:, 0:TB - 1], in1=a2.ap[:, 0:TB - 1], op=ALU.subtract),
                      reads=[a1, a2], writes=[Hr])
                cx.op("dve", lambda e, ei=ei: e.tensor_tensor(out=a1.ap, in0=sr.ap, in1=ei, op=ALU.mult), reads=[sr, E1i], writes=[a1])
                cx.op("pool", lambda e, er=er: e.tensor_tensor(out=a2.ap, in0=si.ap, in1=er, op=ALU.mult), reads=[si, E1r], writes=[a2])
                cx.op("dve", lambda e, ql=ql: e.tensor_tensor(out=Hi.ap[:, ql, 1:TB], in0=a1.ap[:, 0:TB - 1], in1=a2.ap[:, 0:TB - 1], op=ALU.add),
                      reads=[a1, a2], writes=[Hi])
            for gl_ in range(8):
                g = qt * 8 + gl_
                ql, half = gl_ // 2, gl_ % 2
                q_ = qt * NQ + ql
                ps = PS[npsu % 4]
                npsu += 1
                lo, hi = half * 64, half * 64 + 64

                def mm(e, ps=ps, g=g, gl_=gl_, ql=ql, q_=q_, lo=lo, hi=hi):
                    e.matmul(ps.ap, lhsT=Mw.ap[:, g, :], rhs=U3.ap[:, gl_, :], start=True, stop=False)
                    e.matmul(ps.ap, lhsT=W2r.ap[lo:hi, q_, :], rhs=Hr.ap[lo:hi, ql, :], start=False, stop=False)
                    return e.matmul(ps.ap, lhsT=W2i.ap[lo:hi, q_, :], rhs=Hi.ap[lo:hi, ql, :], start=False, stop=True)
                cx.op("pe", mm, reads=[Mw, U3, W2r, W2i, Hr, Hi], writes=[ps])
                cx.op("act", lambda e, ps=ps, gl_=gl_: e.activation(out=Y3.ap[:, gl_, :], in_=ps.ap, func=AF.Identity), reads=[ps], writes=[Y3])
            for tau in range(8):
                ps = PS[npsu % 4]
                npsu += 1

                def mm(e, ps=ps, tau=tau):
                    for gg in range(8):
                        i = e.matmul(ps.ap, lhsT=SelTS.ap[:, gg, tau, :], rhs=Y3.ap[:, gg, :], start=(gg == 0), stop=(gg == 7))
                    return i
                cx.op("pe", mm, reads=[SelTS, Y3], writes=[ps])
                cx.op("act", lambda e, ps=ps, qt=qt, tau=tau: e.activation(out=gb.ap[:, qt, tau::8], in_=ps.ap, func=AF.Gelu_apprx_tanh),
                      reads=[ps], writes=[gb])
            cx.barrier(skip_sw=True)
            s2.close()
        wgl = sc.buf([128, 4, 512], BF16, "wgl")
        cx.dma("sp", wgl.ap, wb_glu[l].rearrange("(kc p) n -> p kc n", p=128), reads=B_wb[("glu", l)], writes=[wgl])
        sg = [sc.buf([128, TB], F32, f"sg{i}") for i in range(2)]
        yst = [sc.buf([128, 4, TB], BF16, f"yst{i}") for i in range(2)]
        n = 0
        for t in range(NB):
            ys_ = yst[t % 2]
            for ct in range(4):
                ps = PS[n % 4]
                s_ = sg[n % 2]
                n += 1

                def mm(e, ps=ps, ct=ct, t=t):
                    for kc in range(4):
                        i = e.matmul(ps.ap, lhsT=wgl.ap[:, kc, ct * 128:(ct + 1) * 128], rhs=gb.ap[:, kc, blk(t)], start=(kc == 0), stop=(kc == 3))
                    return i
                cx.op("pe", mm, reads=[wgl, gb], writes=[ps])
                cx.op("act", lambda e, ps=ps, s_=s_, ct=ct: e.activation(out=s_.ap, in_=ps.ap, func=AF.Sigmoid, bias=bglu.ap[:, ct:ct + 1], scale=1.0),
                      reads=[ps, bglu], writes=[s_])
                cx.op("dve", lambda e, s_=s_, ys_=ys_, ct=ct, t=t: e.tensor_tensor(out=ys_.ap[:, ct, :], in0=gb.ap[:, ct, blk(t)], in1=s_.ap, op=ALU.mult),
                      reads=[gb, s_], writes=[ys_])
            cx.dma("sp", YS[0:512, blk(t)].rearrange("(c p) t -> p c t", p=128), ys_.ap, reads=[ys_], writes=[B_YS[0]])
        cx.barrier(skip_sw=True)
        sc.close()
        ws.close()

    def run_interleaved(gens):
        active = [g for g in gens if g is not None]
        while active:
            for g in list(active):
                try:
                    next(g)
                except StopIteration:
                    active.remove(g)

    def layer_norm_gen(y, gcol, bcol, outb, tmp, stat):
        pm, pq = PS[6], PS[7]
        for h2 in range(2):
            cx.op("act", lambda e, h2=h2: e.activation(out=tmp.ap[:, h2 * 4:(h2 + 1) * 4, :], in_=y.ap[:, h2 * 4:(h2 + 1) * 4, :], func=AF.Square),
                  reads=[y], writes=[tmp])
            yield

        def mm1(e):
            for kc in range(8):
                i = e.matmul(pm.ap, lhsT=onesD.ap, rhs=y.ap[:, kc, :], start=(kc == 0), stop=(kc == 7))
            return i

        def mm2(e):
            for kc in range(8):
                i = e.matmul(pq.ap, lhsT=onesD.ap, rhs=tmp.ap[:, kc, :], start=(kc == 0), stop=(kc == 7))
            return i
        cx.op("pe", mm1, reads=[onesD, y], writes=[pm])
        yield
        cx.op("pe", mm2, reads=[onesD, tmp], writes=[pq])
        yield
        mean, rstd = stat
        cx.op("act", lambda e: e.activation(out=mean.ap, in_=pm.ap, func=AF.Identity), reads=[pm], writes=[mean])
        cx.op("act", lambda e: e.activation(out=rstd.ap, in_=pm.ap, func=AF.Square), reads=[pm], writes=[rstd])
        yield
        cx.op("dve", lambda e: e.tensor_tensor(out=rstd.ap, in0=pq.ap, in1=rstd.ap, op=ALU.subtract), reads=[pq, rstd], writes=[rstd])
        cx.op("dve", lambda e: e.tensor_scalar(out=rstd.ap, in0=rstd.ap, scalar1=0.0, scalar2=LN_EPS, op0=ALU.max, op1=ALU.add),
              reads=[rstd], writes=[rstd])
        yield
        cx.op("act", lambda e: e.activation(out=rstd.ap, in_=rstd.ap, func=AF.Sqrt), reads=[rstd], writes=[rstd])
        cx.op("dve", lambda e: e.reciprocal(out=rstd.ap, in_=rstd.ap), reads=[rstd], writes=[rstd])
        yield
        for h2 in range(2):
            sl_ = slice(h2 * 4, (h2 + 1) * 4)
            mb = mean.ap.unsqueeze(1).to_broadcast([128, 4, TB])
            rb_ = rstd.ap.unsqueeze(1).to_broadcast([128, 4, TB])
            cx.op("dve", lambda e: e.tensor_tensor(out=tmp.ap[:, sl_, :], in0=y.ap[:, sl_, :], in1=mb, op=ALU.subtract), reads=[y, mean], writes=[tmp])
            yield
            cx.op("pool", lambda e: e.tensor_tensor(out=tmp.ap[:, sl_, :], in0=tmp.ap[:, sl_, :], in1=rb_, op=ALU.mult), reads=[tmp, rstd], writes=[tmp])
            yield
        for kc in range(8):
            cx.op("dve", lambda e, kc=kc: e.tensor_scalar(out=y.ap[:, kc, :], in0=tmp.ap[:, kc, :], scalar1=gcol.ap[:, kc:kc + 1],
                                                          scalar2=bcol.ap[:, kc:kc + 1], op0=ALU.mult, op1=ALU.add),
                  reads=[tmp, gcol, bcol], writes=[y])
            yield
        for h2 in range(2):
            sl_ = slice(h2 * 4, (h2 + 1) * 4)
            cx.op("act", lambda e: e.activation(out=outb.ap[:, sl_, :], in_=y.ap[:, sl_, :], func=AF.Identity), reads=[y], writes=[outb])
            yield

    def load_cols(sc, src, l, name, n=8):
        b = sc.buf([128, n], F32, name)
        cx.dma("sp", b.ap, src[l].rearrange("(c p) -> p c", p=128), writes=[b], allow_slow_non_contiguous=True)
        return b

    def phase_mix(l):
        sc = Scope(cx)
        wbr = sc.buf([128, 12, D], BF16, "wbr")
        cx.dma("sp", wbr.ap, wb_branch[l].rearrange("(j p) n -> p j n", p=128), reads=B_wb[("branch", l)], writes=[wbr])
        wgd = [sc.buf([128, 8, 3, 128], BF16, f"wgd{i}") for i in range(2)]
        wgsrc = kview(wb_in[l])[:, :, 3080:6152].rearrange("p kc (k3 dc j) -> p kc k3 dc j", k3=3, dc=8)
        wo = sc.buf([128, 8, D], BF16, "wo")
        cx.dma("sp", wo.ap, kview(wb_out[l]), reads=B_wb[("out", l)], writes=[wo])
        bg = load_cols(sc, b_gate, l, "bg", 24)
        g1 = load_cols(sc, ln1_g, l, "g1")
        b1 = load_cols(sc, ln1_b, l, "b1")
        xb = sc.buf([128, 8, TB], BF16, "mxb")
        xr = [sc.buf([128, TB], F32, f"mxr{i}") for i in range(2)]
        ys = sc.buf([128, 12, TB], BF16, "mys")
        mixb = sc.buf([128, 8, TB], BF16, "mixb")
        yvs = [sc.buf([128, 8, TB], F32, f"yv{i}") for i in range(2)]
        o16 = sc.buf([128, 8, TB], BF16, "mo16")
        tmp = sc.buf([128, 8, TB], F32, "lntmp")
        stat = (sc.buf([128, TB], F32, "mean"), sc.buf([128, TB], F32, "rstd"))
        gsb = [sc.buf([128, TB], F32, f"gsb{i}") for i in range(3)]
        acc = [sc.buf([128, TB], F32, f"acc{i}") for i in range(2)]
        xres_src = xT if l == 0 else XRES
        st = {"n": 0, "nr": 0, "nw": 0}

        def genA(t):
            x_, y_ = xb, ys
            yv = yvs[t % 2]
            cx.dma("sp", x_.ap, kview(XBF)[:, :, blk(t)], reads=[B_XBF[t]], writes=[x_])
            cx.dma("sp", y_.ap, YS.rearrange("(j p) t -> p j t", p=128)[:, :, blk(t)], reads=B_YS, writes=[y_])
            for dc in range(8):
                a_ = acc[dc % 2]
                wg_ = wgd[st["nw"] % 2]
                st["nw"] += 1
                for k3_ in range(3):
                    cx.dma("sp", wg_.ap[:, :, k3_, :], wgsrc[:, :, k3_, dc, :], reads=B_wb[("in", l)], writes=[wg_])
                for k3 in range(3):
                    n = st["n"]
                    st["n"] += 1
                    pp, pg = PS[(2 * n) % 6], PS[(2 * n + 1) % 6]
                    g_ = gsb[n % 3]

                    def mmp(e, pp=pp, k3=k3, dc=dc, y_=y_):
                        for kc in range(4):
                            i = e.matmul(pp.ap, lhsT=wbr.ap[:, k3 * 4 + kc, dc * 128:(dc + 1) * 128], rhs=y_.ap[:, k3 * 4 + kc, :],
                                         start=(kc == 0), stop=(kc == 3))
                        return i

                    def mmg(e, pg=pg, k3=k3, x_=x_, wg_=wg_):
                        for kc in range(8):
                            i = e.matmul(pg.ap, lhsT=wg_.ap[:, kc, k3, :], rhs=x_.ap[:, kc, :], start=(kc == 0), stop=(kc == 7))
                        return i
                    cx.op("pe", mmg, reads=[wg_, x_], writes=[pg])
                    cx.op("pe", mmp, reads=[wbr, y_], writes=[pp])
                    cx.op("act", lambda e, pg=pg, g_=g_, k3=k3, dc=dc: e.activation(out=g_.ap, in_=pg.ap, func=AF.Sigmoid,
                                                                                   bias=bg.ap[:, k3 * 8 + dc:k3 * 8 + dc + 1], scale=1.0),
                          reads=[pg, bg], writes=[g_])
                    if k3 == 0:
                        cx.op("dve", lambda e, pp=pp, g_=g_, a_=a_: e.tensor_tensor(out=a_.ap, in0=pp.ap, in1=g_.ap, op=ALU.mult),
                              reads=[pp, g_], writes=[a_])
                    else:
                        cx.op("dve", lambda e, pp=pp, g_=g_: e.tensor_tensor(out=g_.ap, in0=pp.ap, in1=g_.ap, op=ALU.mult),
                              reads=[pp, g_], writes=[g_])
                        if k3 == 1:
                            cx.op("pool", lambda e, g_=g_, a_=a_: e.tensor_tensor(out=a_.ap, in0=a_.ap, in1=g_.ap, op=ALU.add),
                                  reads=[a_, g_], writes=[a_])
                        else:
                            cx.op("pool", lambda e, g_=g_, a_=a_, dc=dc: e.tensor_tensor(out=mixb.ap[:, dc, :], in0=a_.ap, in1=g_.ap, op=ALU.add),
                                  reads=[a_, g_], writes=[mixb])
                    yield
            for dc in range(8):
                po = PS[6 + dc % 2]
                r_ = xr[st["nr"] % 2]
                st["nr"] += 1
                cx.dma("sp", r_.ap, xres_src[dc * 128:(dc + 1) * 128, blk(t)], reads=[B_XRES[t]], writes=[r_])

                def mmo(e, po=po, dc=dc):
                    for kc in range(8):
                        i = e.matmul(po.ap, lhsT=wo.ap[:, kc, dc * 128:(dc + 1) * 128], rhs=mixb.ap[:, kc, :], start=(kc == 0), stop=(kc == 7))
                    return i
                cx.op("pe", mmo, reads=[wo, mixb], writes=[po])
                cx.op("dve", lambda e, po=po, dc=dc, r_=r_: e.scalar_tensor_tensor(out=yv.ap[:, dc, :], in0=r_.ap, scalar=float(ALPHA),
                                                                                 in1=po.ap, op0=ALU.mult, op1=ALU.add),
                      reads=[r_, po], writes=[yv])
                yield

        def genB(t):
            yv = yvs[t % 2]
            yield from layer_norm_gen(yv, g1, b1, o16, tmp, stat)
            cx.dma("sp", kview(X1RES)[:, :, blk(t)], yv.ap, reads=[yv], writes=[B_X1RES[t]])
            cx.dma("sp", kview(X1BF)[:, :, blk(t)], o16.ap, reads=[o16], writes=[B_X1BF[t]])
            yield
        run_interleaved([genA(0)])
        for t in range(NB):
            run_interleaved([genA(t + 1) if t + 1 < NB else None, genB(t)])
        cx.barrier(skip_sw=True)
        sc.close()

    def phase_ffn(l, last):
        sc = Scope(cx)
        wdn = sc.buf([128, NHC, D], BF16, "wdn")
        cx.dma("sp", wdn.ap, wb_d[l].rearrange("(j p) n -> p j n", p=128), reads=B_wb[("d", l)], writes=[wdn])
        g2 = load_cols(sc, ln2_g, l, "g2")
        b2 = load_cols(sc, ln2_b, l, "b2")
        xb = sc.buf([128, 8, TB], BF16, "fxb")
        hT = sc.buf([128, NHC, TB], BF16, "hT")
        wgu = [sc.buf([128, 2, 8, 256], BF16, f"wgu{i}") for i in range(2)]
        sl = [sc.buf([128, TB], F32, f"sl{i}") for i in range(2)]
        xr = [sc.buf([128, TB], F32, f"fxr{i}") for i in range(2)]
        yvs = [sc.buf([128, 8, TB], F32, f"fyv{i}") for i in range(2)]
        tmp = sc.buf([128, 8, TB], F32, "flntmp")
        o16 = sc.buf([128, 8, TB], BF16, "fo16")
        stat = (sc.buf([128, TB], F32, "fmean"), sc.buf([128, TB], F32, "frstd"))
        st = {"n": 0, "nr": 0, "nw": 0}

        def genA(t):
            yv = yvs[t % 2]
            cx.dma("sp", xb.ap, kview(X1BF)[:, :, blk(t)], reads=[B_X1BF[t]], writes=[xb])
            for hp in range(NHC // 2):
                w = wgu[st["nw"] % 2]
                st["nw"] += 1
                cx.dma("sp", w.ap[:, 0], kview(wb_g[l])[:, :, hp * 256:(hp + 1) * 256], reads=B_wb[("g", l)], writes=[w])
                cx.dma("sp", w.ap[:, 1], kview(wb_u[l])[:, :, hp * 256:(hp + 1) * 256], reads=B_wb[("u", l)], writes=[w])
                for hh in range(2):
                    hc = hp * 2 + hh
                    n = st["n"]
                    st["n"] += 1
                    pg, pu = PS[(2 * n) % 6], PS[(2 * n + 1) % 6]
                    s_ = sl[n % 2]

                    def mmg(e, pg=pg, w=w, hh=hh):
                        for kc in range(8):
                            i = e.matmul(pg.ap, lhsT=w.ap[:, 0, kc, hh * 128:(hh + 1) * 128], rhs=xb.ap[:, kc, :], start=(kc == 0), stop=(kc == 7))
                        return i

                    def mmu(e, pu=pu, w=w, hh=hh):
                        for kc in range(8):
                            i = e.matmul(pu.ap, lhsT=w.ap[:, 1, kc, hh * 128:(hh + 1) * 128], rhs=xb.ap[:, kc, :], start=(kc == 0), stop=(kc == 7))
                        return i
                    cx.op("pe", mmg, reads=[w, xb], writes=[pg])
                    cx.op("pe", mmu, reads=[w, xb], writes=[pu])
                    cx.op("act", lambda e, pg=pg, s_=s_: e.activation(out=s_.ap, in_=pg.ap, func=AF.Silu), reads=[pg], writes=[s_])
                    cx.op("dve", lambda e, pu=pu, s_=s_, hc=hc: e.tensor_tensor(out=hT.ap[:, hc, :], in0=pu.ap, in1=s_.ap, op=ALU.mult),
                          reads=[pu, s_], writes=[hT])
                    yield
            for dc in range(8):
                po = PS[6 + dc % 2]
                r_ = xr[st["nr"] % 2]
                st["nr"] += 1
                cx.dma("sp", r_.ap, X1RES[dc * 128:(dc + 1) * 128, blk(t)], reads=[B_X1RES[t]], writes=[r_])

                def mmo(e, po=po, dc=dc):
                    for hc in range(NHC):
                        i = e.matmul(po.ap, lhsT=wdn.ap[:, hc, dc * 128:(dc + 1) * 128], rhs=hT.ap[:, hc, :], start=(hc == 0), stop=(hc == NHC - 1))
                    return i
                cx.op("pe", mmo, reads=[wdn, hT], writes=[po])
                cx.op("dve", lambda e, po=po, dc=dc, r_=r_: e.scalar_tensor_tensor(out=yv.ap[:, dc, :], in0=r_.ap, scalar=float(ALPHA),
                                                                                 in1=po.ap, op0=ALU.mult, op1=ALU.add),
                      reads=[r_, po], writes=[yv])
                yield

        def genB(t):
            yv = yvs[t % 2]
            yield from layer_norm_gen(yv, g2, b2, o16, tmp, stat)
            if last:
                cx.dma("sp", kview(outT)[:, :, blk(t)], yv.ap, reads=[yv], writes=[B_OUT])
            else:
                cx.dma("sp", kview(XRES)[:, :, blk(t)], yv.ap, reads=[yv], writes=[B_XRES[t]])
                cx.dma("sp", kview(XBF)[:, :, blk(t)], o16.ap, reads=[o16], writes=[B_XBF[t]])
            yield
        run_interleaved([genA(0)])
        for t in range(NB):
            run_interleaved([genA(t + 1) if t + 1 < NB else None, genB(t)])
        cx.barrier(skip_sw=True)
        sc.close()

    cx.barrier(skip_sw=True)
    for l in range(n_layers):
        if stop_after == ("setup", l):
            break
        phase_proj(l)
        if l + 1 < n_layers:
            convert_layer(l + 1)
        if stop_after == ("proj", l):
            break
        phase_attn(l)
        if stop_after == ("attn", l):
            break
        phase_lru(l)
        if stop_after == ("lru", l):
            break
        phase_s5(l)
        if stop_after in (("s5", l), ("s5prep", l)):
            break
        phase_mix(l)
        if stop_after == ("mix", l):
            break
        phase_ffn(l, last=(l == n_layers - 1))
    cx.barrier()
    return nc


INPUT_ORDER = ["w_in", "w_branch", "w_out", "w_ffn_gate", "w_ffn_up", "w_ffn_down", "s5_w_glu", "lru_w_a", "lru_w_x",
               "b_f", "b_gate", "s5_a_re", "s5_a_im", "s5_log_dt", "s5_b_re", "s5_b_im", "s5_c_re", "s5_c_im", "s5_d",
               "s5_b_glu", "lru_conv_w", "lru_conv_b", "lru_b_a", "lru_b_x", "lru_lambda", "ln1_g", "ln1_b", "ln2_g", "ln2_b"]


def layout_inputs(inputs, n_layers=DEPTH):
    f = lambda a: np.ascontiguousarray(np.asarray(a, dtype=np.float32)[:n_layers])
    shared = {}
    for k in INPUT_ORDER:
        a = f(inputs[k])
        if k == "w_branch":
            a = a.reshape(n_layers, 1536, D)
        elif k in ("s5_c_re", "s5_c_im"):
            a = a.reshape(n_layers, 512, 64)
        elif k in ("lru_b_a", "lru_b_x"):
            a = a.reshape(n_layers, 512)
        shared[k] = np.ascontiguousarray(a)
    return shared


def kernel(**inputs):
    x = np.asarray(inputs["x"], dtype=np.float32)
    shared = layout_inputs(inputs)
    nc = bass.Bass("TRN2", target_bir_lowering=False)
    build(nc)
    in_maps = []
    for c in range(8):
        m = dict(shared)
        m["xT"] = np.ascontiguousarray(x[c % 4].T)
        in_maps.append(m)
    res = run_bass_kernel_spmd(nc, in_maps, core_ids=list(range(8)))
    out = np.stack([np.ascontiguousarray(res.results[b]["outT"].T) for b in range(4)], axis=0)
    return out.astype(np.float32)
```

```python
import math
import numpy as np
import concourse.bass as bass
import concourse.mybir as mybir
from concourse.bass_utils import run_bass_kernel_spmd

F32 = mybir.dt.float32
BF16 = mybir.dt.bfloat16
AF = mybir.ActivationFunctionType
ALU = mybir.AluOpType

D = 1024
L = 4096
DEPTH = 4
NB = 8
TB = 512
IN_TOTAL = 6152
FFN = 2816
NHC = 22
ALPHA = (2.0 * DEPTH) ** 0.25
LN_EPS = 1e-5
MAGIC = 12582912.0
TWO_PI = 2.0 * math.pi


class Buf:
    __slots__ = ("ap", "w", "r", "name")

    def __init__(self, ap, name=""):
        self.ap = ap
        self.w = {}
        self.r = {}
        self.name = name


class Ctx:
    def __init__(self, nc):
        self.nc = nc
        self.E = {"pe": nc.tensor, "act": nc.scalar, "dve": nc.vector, "pool": nc.gpsimd, "sp": nc.sync}
        self.sem = {}
        self.cnt = {}
        self.nsem = 0
        for e in ("pe", "act", "dve", "pool"):
            self._new_sem(e)
        self.seen = {e: {} for e in self.E}
        self.dma_sems = {"sp": [nc.alloc_semaphore(f"dq{i}") for i in range(60)],
                         "pool": [nc.alloc_semaphore(f"dqs{i}") for i in range(16)]}
        self.dma_cnt = {k: [0] * len(v) for k, v in self.dma_sems.items()}
        self.dma_rr = {"sp": 0, "pool": 0}
        self.semobj = {}
        self.uid = 0

    def _new_sem(self, e):
        s = self.nc.alloc_semaphore(f"s_{e}_{self.nsem}")
        self.nsem += 1
        self.sem[e] = s
        self.cnt[e] = 0

    def _key(self, s):
        k = id(s)
        self.semobj[k] = s
        return k

    def _wait(self, e, deps):
        seen = self.seen[e]
        for k, v in deps.items():
            if seen.get(k, 0) >= v:
                continue
            self.E[e].wait_ge(self.semobj[k], v)
            seen[k] = v

    @staticmethod
    def _merge(dst, src):
        for k, v in src.items():
            if dst.get(k, 0) < v:
                dst[k] = v

    def _deps(self, reads, writes):
        deps = {}
        for b in reads:
            self._merge(deps, b.w)
        for b in writes:
            self._merge(deps, b.w)
            self._merge(deps, b.r)
        return deps

    def _commit(self, tok, reads, writes):
        for b in reads:
            self._merge(b.r, tok)
        for b in writes:
            b.w = dict(tok)
            b.r = {}

    def op(self, e, emit, reads=(), writes=()):
        self._wait(e, self._deps(reads, writes))
        ins = emit(self.E[e])
        if self.cnt[e] >= 30000:
            self._new_sem(e)
        s = self.sem[e]
        self.cnt[e] += 1
        ins.then_inc(s, 1)
        tok = {self._key(s): self.cnt[e]}
        self._commit(tok, reads, writes)
        return tok

    def dma(self, e, out, in_, reads=(), writes=(), **kw):
        self._wait(e, self._deps(reads, writes))
        sems, cnts = self.dma_sems[e], self.dma_cnt[e]
        i = self.dma_rr[e]
        self.dma_rr[e] = (i + 1) % len(sems)
        if cnts[i] >= 30000:
            sems[i] = self.nc.alloc_semaphore(f"dqx{self.nsem}")
            self.nsem += 1
            cnts[i] = 0
        s = sems[i]
        if cnts[i] > 0:
            self._wait(e, {self._key(s): cnts[i]})
        cnts[i] += 16
        self.E[e].dma_start(out=out, in_=in_, **kw).then_inc(s, 16)
        tok = {self._key(s): cnts[i]}
        self._commit(tok, reads, writes)
        return tok

    def barrier(self, skip_sw=False):
        allt = {}
        for e in ("pe", "act", "dve", "pool"):
            if self.cnt[e] > 0:
                allt[self._key(self.sem[e])] = self.cnt[e]
        for q in self.dma_sems:
            for i, s in enumerate(self.dma_sems[q]):
                if self.dma_cnt[q][i] > 0 and not (q == "pool" and skip_sw):
                    allt[self._key(s)] = self.dma_cnt[q][i]
        for e in self.E:
            self._wait(e, allt)


class Scope:
    def __init__(self, cx):
        self.cx = cx
        self.guards = []

    def sb(self, shape, dt=F32, name=None):
        self.cx.uid += 1
        g = self.cx.nc.sbuf_tensor(f"{name or 't'}_{self.cx.uid}", list(shape), dt)
        t = g.__enter__()
        self.guards.append(g)
        return t.ap()

    def buf(self, shape, dt=F32, name=None):
        return Buf(self.sb(shape, dt, name), name or "")

    def close(self):
        for g in reversed(self.guards):
            g.__exit__(None, None, None)
        self.guards = []


def build(nc, n_layers=DEPTH, dbg=False, stop_after=None):
    cx = Ctx(nc)
    kind_dbg = "ExternalOutput" if dbg else "Internal"

    def din(name, shape):
        return nc.dram_tensor(name, list(shape), F32, kind="ExternalInput").ap()

    def dscr(name, shape, dt, k="Internal"):
        return nc.dram_tensor(name, list(shape), dt, kind=k).ap()

    xT = din("xT", [D, L])
    w_in = din("w_in", [n_layers, D, IN_TOTAL])
    w_branch = din("w_branch", [n_layers, 1536, D])
    w_out = din("w_out", [n_layers, D, D])
    w_g = din("w_ffn_gate", [n_layers, D, FFN])
    w_u = din("w_ffn_up", [n_layers, D, FFN])
    w_d = din("w_ffn_down", [n_layers, FFN, D])
    w_glu = din("s5_w_glu", [n_layers, 512, 512])
    lru_w_a = din("lru_w_a", [n_layers, 8, 64, 64])
    lru_w_x = din("lru_w_x", [n_layers, 8, 64, 64])
    b_f = din("b_f", [n_layers, 8])
    b_gate = din("b_gate", [n_layers, 3072])
    s5_a_re = din("s5_a_re", [n_layers, 32, 64])
    s5_a_im = din("s5_a_im", [n_layers, 32, 64])
    s5_log_dt = din("s5_log_dt", [n_layers, 32])
    s5_b_re = din("s5_b_re", [n_layers, 32, 64, 16])
    s5_b_im = din("s5_b_im", [n_layers, 32, 64, 16])
    s5_c_re = din("s5_c_re", [n_layers, 512, 64])
    s5_c_im = din("s5_c_im", [n_layers, 512, 64])
    s5_d = din("s5_d", [n_layers, 512])
    s5_b_glu = din("s5_b_glu", [n_layers, 512])
    lru_conv_w = din("lru_conv_w", [n_layers, 4, 512])
    lru_conv_b = din("lru_conv_b", [n_layers, 512])
    lru_b_a = din("lru_b_a", [n_layers, 512])
    lru_b_x = din("lru_b_x", [n_layers, 512])
    lru_lambda = din("lru_lambda", [n_layers, 512])
    ln1_g = din("ln1_g", [n_layers, D])
    ln1_b = din("ln1_b", [n_layers, D])
    ln2_g = din("ln2_g", [n_layers, D])
    ln2_b = din("ln2_b", [n_layers, D])
    outT = nc.dram_tensor("outT", [D, L], F32, kind="ExternalOutput").ap()

    wb_in = dscr("wb_in", [n_layers, D, IN_TOTAL], BF16)
    wb_branch = dscr("wb_branch", [n_layers, 1536, D], BF16)
    wb_out = dscr("wb_out", [n_layers, D, D], BF16)
    wb_g = dscr("wb_g", [n_layers, D, FFN], BF16)
    wb_u = dscr("wb_u", [n_layers, D, FFN], BF16)
    wb_d = dscr("wb_d", [n_layers, FFN, D], BF16)
    wb_glu = dscr("wb_glu", [n_layers, 512, 512], BF16)
    XBF = dscr("XBF", [D, L], BF16)
    XRES = dscr("XRES", [D, L], F32, kind_dbg)
    X1BF = dscr("X1BF", [D, L], BF16)
    X1RES = dscr("X1RES", [D, L], F32, kind_dbg)
    UT = dscr("UT", [512, L], BF16, kind_dbg)
    XL = dscr("XL", [512, L], F32, kind_dbg)
    GL = dscr("GL", [512, L], F32, kind_dbg)
    QA = dscr("QA", [8, 70, L], BF16, kind_dbg)
    KA = dscr("KA", [8, 70, L], BF16, kind_dbg)
    VA = dscr("VA", [8, 128, 32, 65], BF16, kind_dbg)
    YS = dscr("YS", [1536, L], BF16, kind_dbg)

    B_wb = {}
    for nm in ("in", "branch", "out", "g", "u", "d", "glu"):
        for l in range(n_layers):
            B_wb[(nm, l)] = []
    B_XBF = [Buf(None, f"XBF{t}") for t in range(NB)]
    B_XRES = [Buf(None, f"XRES{t}") for t in range(NB)]
    B_X1BF = [Buf(None, f"X1BF{t}") for t in range(NB)]
    B_X1RES = [Buf(None, f"X1RES{t}") for t in range(NB)]
    B_UT = Buf(None, "UT")
    B_XL = Buf(None, "XL")
    B_GL = Buf(None, "GL")
    B_QA = Buf(None, "QA")
    B_KA = Buf(None, "KA")
    B_VA = Buf(None, "VA")
    B_YS = [Buf(None, f"YS{k}") for k in range(3)]
    B_OUT = Buf(None, "out")

    PS = [Buf(nc.alloc_psum_tensor(f"psb{i}", [128, 512], F32).ap(), f"ps{i}") for i in range(8)]

    cs = Scope(cx)
    identf = cs.buf([128, 128], F32, "identf")
    identb = cs.buf([128, 128], BF16, "identb")
    onesD = cs.buf([128, 128], BF16, "onesD")
    ones1 = cs.buf([128, 64], F32, "ones1")
    mask8 = cs.buf([128, 128], F32, "mask8")
    negtri = cs.buf([128, 128], BF16, "negtri")
    SELD = dscr("SELD", [2, 128, 8, 8, 128], BF16)
    B_SELD = Buf(None, "SELD")
    cs0 = Scope(cx)
    Sel = cs0.buf([128, 8, 8, 128], BF16, "Sel")
    SelT = cs0.buf([128, 8, 8, 128], BF16, "SelT")

    def pool_fill(buf, val):
        cx.op("pool", lambda e: e.memset(buf.ap, val), writes=[buf])

    def pool_sel(buf, ap, pattern, cmp, base, cm, fill=0.0):
        cx.op("pool", lambda e: e.affine_select(out=ap, in_=ap, pattern=pattern, compare_op=cmp, fill=fill,
                                                base=base, channel_multiplier=cm), reads=[buf], writes=[buf])

    pool_fill(identf, 1.0)
    pool_sel(identf, identf.ap, [[1, 128]], ALU.is_equal, 0, -1)
    pool_fill(identb, 1.0)
    pool_sel(identb, identb.ap, [[1, 128]], ALU.is_equal, 0, -1)
    pool_fill(onesD, 1.0 / D)
    pool_fill(ones1, 1.0)
    pool_fill(mask8, 1.0)
    pool_sel(mask8, mask8.ap.rearrange("p (t c) -> p t c", c=16), [[16, 8], [0, 16]], ALU.is_ge, 15, -1)
    pool_fill(negtri, 0.0)
    pool_sel(negtri, negtri.ap, [[1, 128]], ALU.is_ge, 0, -1, fill=-30000.0)
    pool_fill(Sel, 1.0)
    for gg in range(8):
        a4 = Sel.ap[:, gg, :, :].rearrange("p t (u c) -> p t u c", c=16)
        pool_sel(Sel, a4, [[0, 8], [0, 8], [-1, 16]], ALU.is_equal, -16 * gg, 1)
        pool_sel(Sel, a4, [[-1, 8], [1, 8], [0, 16]], ALU.is_equal, 0, 0)
    pool_fill(SelT, 1.0)
    for gg in range(8):
        a3 = SelT.ap[:, gg, :, :]
        pool_sel(SelT, a3, [[16, 8], [1, 128]], ALU.is_equal, -16 * gg, -1)
        pool_sel(SelT, a3, [[-16, 8], [0, 128]], ALU.is_ge, 0, 1)
        pool_sel(SelT, a3, [[16, 8], [0, 128]], ALU.is_ge, 15, -1)

    cx.dma("sp", SELD[0], Sel.ap, reads=[Sel], writes=[B_SELD])
    cx.dma("sp", SELD[1], SelT.ap, reads=[SelT], writes=[B_SELD])
    cx.barrier()
    cs0.close()

    def convert(src, dst, rows, key):
        r = 0
        while r < rows:
            n = min(128, rows - r)
            bch = Buf(None, "wbch")
            B_wb[key].append(bch)
            cx.dma("pool", dst[r:r + n, :], src[r:r + n, :], writes=[bch])
            r += n

    def convert_layer(l):
        convert(w_in[l], wb_in[l], D, ("in", l))
        convert(w_glu[l], wb_glu[l], 512, ("glu", l))
        convert(w_branch[l], wb_branch[l], 1536, ("branch", l))
        convert(w_out[l], wb_out[l], D, ("out", l))
        convert(w_g[l], wb_g[l], D, ("g", l))
        convert(w_u[l], wb_u[l], D, ("u", l))
        convert(w_d[l], wb_d[l], FFN, ("d", l))

    for t in range(NB):
        for kc in range(8):
            cx.dma("pool", XBF[kc * 128:(kc + 1) * 128, t * TB:(t + 1) * TB],
                   xT[kc * 128:(kc + 1) * 128, t * TB:(t + 1) * TB], writes=[B_XBF[t]])

    convert_layer(0)

    def kview(ap2d):
        return ap2d.rearrange("(kc p) n -> p kc n", p=128)

    def blk(t):
        return slice(t * TB, (t + 1) * TB)

    def phase_proj(l):
        sc_fg = Scope(cx)
        fgT = sc_fg.buf([8, L], F32, "fgT")
        sc = Scope(cx)
        xb = [sc.buf([128, 8, TB], BF16, f"xb{t}") for t in range(NB)]
        for t in range(NB):
            cx.dma("sp", xb[t].ap, kview(XBF)[:, :, blk(t)], reads=[B_XBF[t]], writes=[xb[t]])
        wt = [sc.buf([128, 8, 512], BF16, f"wt{i}") for i in range(2)]
        wfg = sc.buf([128, 8, 8], BF16, "wfg")
        cx.dma("sp", wfg.ap, kview(wb_in[l])[:, :, 3072:3080], reads=B_wb[("in", l)], writes=[wfg])
        st32 = [sc.buf([128, 4, TB], F32, f"st32_{i}") for i in range(2)]
        st16 = [sc.buf([128, 4, TB], BF16, f"st16_{i}") for i in range(2)]
        stqk = [sc.buf([64, 8, TB], BF16, f"stqk_{i}") for i in range(2)]
        vst = [sc.buf([128, 8, 65], BF16, f"vst_{i}") for i in range(2)]
        for v in vst:
            cx.op("pool", lambda e, v=v: e.memset(v.ap, 1.0), writes=[v])
        nps = 0
        nst = 0
        for cg in range(6):
            w = wt[cg % 2]
            cx.dma("sp", w.ap, kview(wb_in[l])[:, :, cg * 512:(cg + 1) * 512], reads=B_wb[("in", l)], writes=[w])
            if cg < 3:
                for t in range(NB):
                    stb = (st16 if cg == 0 else st32)[nst % 2]
                    nst += 1
                    for ct in range(4):
                        ps = PS[nps % 4]
                        nps += 1

                        def mm(e, ps=ps, ct=ct, t=t, w=w):
                            for kc in range(8):
                                i = e.matmul(ps.ap, lhsT=w.ap[:, kc, ct * 128:(ct + 1) * 128], rhs=xb[t].ap[:, kc, :],
                                             start=(kc == 0), stop=(kc == 7))
                            return i
                        cx.op("pe", mm, reads=[w, xb[t]], writes=[ps])
                        fn = AF.Gelu_apprx_tanh if cg == 2 else AF.Identity
                        cx.op("act", lambda e, ps=ps, stb=stb, ct=ct, fn=fn: e.activation(out=stb.ap[:, ct, :], in_=ps.ap, func=fn),
                              reads=[ps], writes=[stb])
                    dst, bd = [(UT, B_UT), (XL, B_XL), (GL, B_GL)][cg]
                    cx.dma("sp", dst.rearrange("(c p) t -> p c t", p=128)[:, :, blk(t)], stb.ap, reads=[stb], writes=[bd])
            elif cg < 5:
                for t in range(NB):
                    stb = stqk[nst % 2]
                    nst += 1
                    for h in range(8):
                        ps = PS[nps % 4]
                        nps += 1

                        def mm(e, ps=ps, h=h, t=t, w=w):
                            for kc in range(8):
                                i = e.matmul(ps.ap[0:64, :], lhsT=w.ap[:, kc, h * 64:(h + 1) * 64], rhs=xb[t].ap[:, kc, :],
                                             start=(kc == 0), stop=(kc == 7))
                            return i
                        cx.op("pe", mm, reads=[w, xb[t]], writes=[ps])
                        sc_ = 0.125 if cg == 3 else 1.0
                        cx.op("act", lambda e, ps=ps, stb=stb, h=h, sc_=sc_: e.activation(out=stb.ap[:, h, :], in_=ps.ap[0:64, :],
                                                                                         func=AF.Identity, scale=sc_),
                              reads=[ps], writes=[stb])
                    dst, bd = (QA, B_QA) if cg == 3 else (KA, B_KA)
                    cx.dma("sp", dst[:, 0:64, blk(t)].rearrange("h d t -> d h t"), stb.ap, reads=[stb], writes=[bd])
            else:
                for tt in range(32):
                    ps = PS[nps % 4]
                    nps += 1
                    t = tt // 4
                    vs = vst[tt % 2]

                    def mm(e, ps=ps, tt=tt, t=t, w=w):
                        o = (tt % 4) * 128
                        for kc in range(8):
                            i = e.matmul(ps.ap, lhsT=xb[t].ap[:, kc, o:o + 128], rhs=w.ap[:, kc, :],
                                         start=(kc == 0), stop=(kc == 7))
                        return i
                    cx.op("pe", mm, reads=[w, xb[t]], writes=[ps])
                    cx.op("act", lambda e, ps=ps, vs=vs: e.activation(out=vs.ap[:, :, 0:64], in_=ps.ap.rearrange("p (h d) -> p h d", d=64),
                                                                      func=AF.Identity), reads=[ps], writes=[vs])
                    cx.dma("sp", VA[:, :, tt, :].rearrange("h p e -> p h e"), vs.ap, reads=[vs], writes=[B_VA])
        for t in range(NB):
            ps = PS[nps % 4]
            nps += 1

            def mm(e, ps=ps, t=t):
                for kc in range(8):
                    i = e.matmul(ps.ap[0:8, :], lhsT=wfg.ap[:, kc, :], rhs=xb[t].ap[:, kc, :], start=(kc == 0), stop=(kc == 7))
                return i
            cx.op("pe", mm, reads=[wfg, xb[t]], writes=[ps])
            cx.op("act", lambda e, ps=ps, t=t: e.activation(out=fgT.ap[:, blk(t)], in_=ps.ap[0:8, :], func=AF.Identity),
                  reads=[ps], writes=[fgT])
        cx.barrier(skip_sw=True)
        sc.close()
        sc = Scope(cx)
        bf = sc.buf([8, 1], F32, "bf")
        cx.dma("sp", bf.ap, b_f[l].rearrange("(h o) -> h o", o=1), writes=[bf])
        nbf = sc.buf([8, 1], F32, "nbf")
        cx.op("dve", lambda e: e.tensor_scalar(out=nbf.ap, in0=bf.ap, scalar1=-1.0, scalar2=None, op0=ALU.mult), reads=[bf], writes=[nbf])
        one8 = sc.buf([8, 1], F32, "one8")
        cx.op("dve", lambda e: e.memset(one8.ap, 1.0), writes=[one8])
        ex = sc.buf([8, L], F32, "ex")
        cx.op("act", lambda e: e.activation(out=ex.ap, in_=fgT.ap, func=AF.Exp, bias=nbf.ap, scale=-1.0), reads=[fgT, nbf], writes=[ex])
        cx.op("act", lambda e: e.activation(out=ex.ap, in_=ex.ap, func=AF.Ln, bias=one8.ap, scale=1.0), reads=[ex, one8], writes=[ex])
        csum = sc.buf([8, L], F32, "csum")
        cx.op("dve", lambda e: e.tensor_tensor_scan(out=csum.ap, data0=one8.ap.to_broadcast([8, L]), data1=ex.ap, initial=0.0,
                                                    op0=ALU.mult, op1=ALU.add), reads=[ex, one8], writes=[csum])
        pcs = [sc.buf([8, L], BF16, f"pc{j}") for j in range(3)]
        ncs = [sc.buf([8, L], BF16, f"nc{j}") for j in range(3)]
        res = ex
        cx.op("dve", lambda e: e.tensor_copy(out=pcs[0].ap, in_=csum.ap), reads=[csum], writes=[pcs[0]])
        cx.op("dve", lambda e: e.tensor_tensor(out=res.ap, in0=csum.ap, in1=pcs[0].ap, op=ALU.subtract), reads=[csum, pcs[0]], writes=[res])
        cx.op("dve", lambda e: e.tensor_copy(out=pcs[1].ap, in_=res.ap), reads=[res], writes=[pcs[1]])
        cx.op("dve", lambda e: e.tensor_tensor(out=res.ap, in0=res.ap, in1=pcs[1].ap, op=ALU.subtract), reads=[res, pcs[1]], writes=[res])
        cx.op("dve", lambda e: e.tensor_copy(out=pcs[2].ap, in_=res.ap), reads=[res], writes=[pcs[2]])
        for j in range(3):
            cx.op("dve", lambda e, j=j: e.tensor_scalar(out=ncs[j].ap, in0=pcs[j].ap, scalar1=-1.0, scalar2=None, op0=ALU.mult),
                  reads=[pcs[j]], writes=[ncs[j]])
        onesb = sc.buf([8, L], BF16, "onesb")
        cx.op("pool", lambda e: e.memset(onesb.ap, 1.0), writes=[onesb])
        for j in range(3):
            cx.dma("sp", QA[:, 64 + j, :], ncs[j].ap, reads=[ncs[j]], writes=[B_QA])
            cx.dma("sp", QA[:, 67 + j, :], onesb.ap, reads=[onesb], writes=[B_QA])
            cx.dma("sp", KA[:, 64 + j, :], onesb.ap, reads=[onesb], writes=[B_KA])
            cx.dma("sp", KA[:, 67 + j, :], pcs[j].ap, reads=[pcs[j]], writes=[B_KA])
        cx.barrier(skip_sw=True)
        sc.close()
        sc_fg.close()

    def phase_attn(l):
        sc = Scope(cx)
        qa = [sc.buf([70, L], BF16, f"qa{i}") for i in range(2)]
        ka = [sc.buf([70, L], BF16, f"ka{i}") for i in range(2)]
        va = [sc.buf([128, 32, 65], BF16, f"va{i}") for i in range(2)]
        NPT = 6
        pt = [sc.buf([128, TB], BF16, f"pt{i}") for i in range(NPT)]
        rden = [sc.buf([128, TB], F32, f"rden{i}") for i in range(2)]
        rb = [sc.buf([64, TB], F32, f"rb{i}") for i in range(2)]
        ost = [sc.buf([64, TB], BF16, f"ost{i}") for i in range(2)]

        def load_head(h):
            cx.dma("sp", qa[h % 2].ap, QA[h], reads=[B_QA], writes=[qa[h % 2]])
            cx.dma("sp", ka[h % 2].ap, KA[h], reads=[B_KA], writes=[ka[h % 2]])
            cx.dma("sp", va[h % 2].ap, VA[h], reads=[B_VA], writes=[va[h % 2]])
        items = []
        nb = 0
        for h in range(8):
            for I in range(NB):
                nkb = 4 * I + 4
                for j in range(nkb):
                    items.append((h, I, j, nkb, nb))
                nb += 1
        LA = 3

        def emit_S(i):
            h, I, j, nkb, b_ = items[i]
            c0 = 128 * max(0, j - 4 * I)
            diag = j >= 4 * I
            ps = PS[i % 4]
            k, q = ka[h % 2], qa[h % 2]

            def mm(e):
                i_ = e.matmul(ps.ap[:, c0:TB], lhsT=k.ap[:, j * 128:(j + 1) * 128], rhs=q.ap[:, I * TB + c0:(I + 1) * TB],
                              start=True, stop=not diag)
                if diag:
                    i_ = e.matmul(ps.ap[:, c0:c0 + 128], lhsT=identb.ap, rhs=negtri.ap, start=False, stop=True)
                return i_
            cx.op("pe", mm, reads=[k, q, identb, negtri], writes=[ps])

        def finalize(h, I, b_):
            po = PS[4 + b_ % 2]
            pr = PS[6 + b_ % 2]
            rd, r_, o_ = rden[b_ % 2], rb[b_ % 2], ost[b_ % 2]
            cx.op("dve", lambda e: e.reciprocal(out=rd.ap[64:65, :], in_=po.ap[64:65, :]), reads=[po], writes=[rd])
            cx.op("pe", lambda e: e.matmul(pr.ap[0:64, :], lhsT=ones1.ap[64:65, :], rhs=rd.ap[64:65, :], start=True, stop=True),
                  reads=[ones1, rd], writes=[pr])
            cx.op("act", lambda e: e.activation(out=r_.ap, in_=pr.ap[0:64, :], func=AF.Identity), reads=[pr], writes=[r_])
            cx.op("dve", lambda e: e.tensor_tensor(out=o_.ap, in0=po.ap[0:64, :], in1=r_.ap, op=ALU.mult), reads=[po, r_], writes=[o_])
            cx.dma("sp", YS[1024 + h * 64:1024 + (h + 1) * 64, blk(I)], o_.ap, reads=[o_], writes=[B_YS[2]])
        load_head(0)
        for i in range(min(LA, len(items))):
            emit_S(i)
        pending = []
        for i, (h, I, j, nkb, b_) in enumerate(items):
            if I == 0 and j == 0 and h + 1 < 8:
                load_head(h + 1)
            if i + LA < len(items):
                emit_S(i + LA)
            c0 = 128 * max(0, j - 4 * I)
            ps = PS[i % 4]
            p = pt[i % NPT]
            v = va[h % 2]
            po = PS[4 + b_ % 2]
            cx.op("act", lambda e: e.activation(out=p.ap[:, c0:TB], in_=ps.ap[:, c0:TB], func=AF.Exp), reads=[ps], writes=[p])
            cx.op("pe", lambda e: e.matmul(po.ap[0:65, c0:TB], lhsT=v.ap[:, j, :], rhs=p.ap[:, c0:TB], start=(j == 0), stop=(j == nkb - 1)),
                  reads=[v, p], writes=[po])
            pending = [(cnt - 1, args) for (cnt, args) in pending]
            for cnt, args in [x for x in pending if x[0] <= 0]:
                finalize(*args)
            pending = [x for x in pending if x[0] > 0]
            if j == nkb - 1:
                pending.append((2, (h, I, b_)))
        for cnt, args in pending:
            finalize(*args)
        cx.barrier(skip_sw=True)
        sc.close()

    def phase_lru(l):
        sc = Scope(cx)
        cw = sc.buf([128, 4, 4], F32, "cw")
        cb = sc.buf([128, 4], F32, "cb")
        ba = sc.buf([128, 4], F32, "ba")
        bx = sc.buf([128, 4], F32, "bx")
        lam = sc.buf([128, 4], F32, "lam")
        sneg = sc.buf([128, 4], F32, "sneg")
        one_c = sc.buf([128, 1], F32, "one_c")
        cx.op("dve", lambda e: e.memset(one_c.ap, 1.0), writes=[one_c])
        for k_ in range(4):
            cx.dma("sp", cw.ap[:, :, k_], lru_conv_w[l, k_].rearrange("(c p) -> p c", p=128), writes=[cw], allow_slow_non_contiguous=True)
        for (dst, src) in ((cb, lru_conv_b), (ba, lru_b_a), (bx, lru_b_x), (lam, lru_lambda)):
            cx.dma("sp", dst.ap, src[l].rearrange("(c p) -> p c", p=128), writes=[dst], allow_slow_non_contiguous=True)
        cx.op("act", lambda e: e.activation(out=sneg.ap, in_=lam.ap, func=AF.Exp, scale=-1.0), reads=[lam], writes=[sneg])
        cx.op("act", lambda e: e.activation(out=sneg.ap, in_=sneg.ap, func=AF.Ln, bias=one_c.ap, scale=1.0), reads=[sneg, one_c], writes=[sneg])
        cx.op("dve", lambda e: e.tensor_scalar(out=sneg.ap, in0=sneg.ap, scalar1=-8.0, scalar2=None, op0=ALU.mult), reads=[sneg], writes=[sneg])
        WA = sc.buf([128, 4, 128], BF16, "WA")
        WX = sc.buf([128, 4, 128], BF16, "WX")
        for Wm, src in ((WA, lru_w_a), (WX, lru_w_x)):
            cx.op("pool", lambda e, Wm=Wm: e.memset(Wm.ap, 0.0), writes=[Wm])
            for c in range(4):
                cx.dma("pool", Wm.ap[0:64, c, 0:64], src[l, 2 * c], writes=[Wm])
                cx.dma("pool", Wm.ap[64:128, c, 64:128], src[l, 2 * c + 1], writes=[Wm])
        hba = sc.buf([128, 4], F32, "hba")
        hbx = sc.buf([128, 4], F32, "hbx")
        hsn = sc.buf([128, 4], F32, "hsn")
        for dst, src in ((hba, ba), (hbx, bx), (hsn, sneg)):
            cx.op("dve", lambda e, dst=dst, src=src: e.tensor_scalar(out=dst.ap, in0=src.ap, scalar1=0.5, scalar2=None, op0=ALU.mult),
                  reads=[src], writes=[dst])
        xl = sc.buf([128, L + 3], F32, "xl")
        gl = sc.buf([128, L], F32, "gl")
        xc = sc.buf([128, L], F32, "xc")
        xcb = sc.buf([128, L], BF16, "xcb")
        a_all = sc.buf([128, L], F32, "a_all")
        tr_all = sc.buf([128, L], F32, "tr_all")
        ti_all = sc.buf([128, L], F32, "ti_all")
        h_all = sc.buf([128, L], F32, "h_all")
        yb = sc.buf([128, L], BF16, "yb")
        cx.op("pool", lambda e: e.memset(xl.ap[:, 0:3], 0.0), writes=[xl])
        n = 0
        for c in range(4):
            cx.dma("sp", xl.ap[:, 3:], XL[c * 128:(c + 1) * 128, :], reads=[B_XL], writes=[xl])
            cx.dma("sp", gl.ap, GL[c * 128:(c + 1) * 128, :], reads=[B_GL], writes=[gl])
            cx.op("dve", lambda e, c=c: e.tensor_scalar(out=xc.ap, in0=xl.ap[:, 0:L], scalar1=cw.ap[:, c, 0:1], scalar2=cb.ap[:, c:c + 1],
                                                        op0=ALU.mult, op1=ALU.add), reads=[xl, cw, cb], writes=[xc])
            for k_ in range(1, 4):
                cx.op("dve", lambda e, c=c, k_=k_: e.scalar_tensor_tensor(out=xc.ap, in0=xl.ap[:, k_:k_ + L], scalar=cw.ap[:, c, k_:k_ + 1],
                                                                         in1=xc.ap, op0=ALU.mult, op1=ALU.add), reads=[xl, cw, xc], writes=[xc])
            for hh_ in range(2):
                hs_ = slice(hh_ * (L // 2), (hh_ + 1) * (L // 2))
                cx.op("act", lambda e, hs_=hs_: e.activation(out=xcb.ap[:, hs_], in_=xc.ap[:, hs_], func=AF.Identity), reads=[xc], writes=[xcb])
            for t in range(NB):
                pa, px = PS[(2 * n) % 4], PS[(2 * n + 1) % 4]
                n += 1
                cx.op("pe", lambda e, pa=pa, c=c, t=t: e.matmul(pa.ap, lhsT=WA.ap[:, c, :], rhs=xcb.ap[:, blk(t)], start=True, stop=True),
                      reads=[WA, xcb], writes=[pa])
                cx.op("pe", lambda e, px=px, c=c, t=t: e.matmul(px.ap, lhsT=WX.ap[:, c, :], rhs=xcb.ap[:, blk(t)], start=True, stop=True),
                      reads=[WX, xcb], writes=[px])
                cx.op("act", lambda e, pa=pa, c=c, t=t: e.activation(out=tr_all.ap[:, blk(t)], in_=pa.ap, func=AF.Tanh, bias=hba.ap[:, c:c + 1], scale=0.5),
                      reads=[pa, hba], writes=[tr_all])
                cx.op("act", lambda e, px=px, c=c, t=t: e.activation(out=ti_all.ap[:, blk(t)], in_=px.ap, func=AF.Tanh, bias=hbx.ap[:, c:c + 1], scale=0.5),
                      reads=[px, hbx], writes=[ti_all])
            for hh_ in range(2):
                hs_ = slice(hh_ * (L // 2), (hh_ + 1) * (L // 2))
                cx.op("act", lambda e, c=c, hs_=hs_: e.activation(out=a_all.ap[:, hs_], in_=tr_all.ap[:, hs_], func=AF.Exp, bias=hsn.ap[:, c:c + 1],
                                                               scale=hsn.ap[:, c:c + 1]), reads=[tr_all, hsn], writes=[a_all])
            for hh_ in range(2):
                hs_ = slice(hh_ * (L // 2), (hh_ + 1) * (L // 2))
                cx.op("act", lambda e, hs_=hs_: e.activation(out=tr_all.ap[:, hs_], in_=a_all.ap[:, hs_], func=AF.Square), reads=[a_all], writes=[tr_all])
            cx.op("dve", lambda e: e.tensor_scalar(out=ti_all.ap, in0=ti_all.ap, scalar1=0.5, scalar2=0.5, op0=ALU.mult, op1=ALU.add),
                  reads=[ti_all], writes=[ti_all])
            cx.op("pool", lambda e: e.tensor_tensor(out=ti_all.ap, in0=ti_all.ap, in1=xc.ap, op=ALU.mult), reads=[ti_all, xc], writes=[ti_all])
            for hh_ in range(2):
                hs_ = slice(hh_ * (L // 2), (hh_ + 1) * (L // 2))
                cx.op("act", lambda e, hs_=hs_: e.activation(out=tr_all.ap[:, hs_], in_=tr_all.ap[:, hs_], func=AF.Sqrt, bias=one_c.ap, scale=-1.0),
                      reads=[tr_all, one_c], writes=[tr_all])
            cx.op("dve", lambda e: e.tensor_tensor(out=ti_all.ap, in0=ti_all.ap, in1=tr_all.ap, op=ALU.mult), reads=[ti_all, tr_all], writes=[ti_all])
            cx.op("dve", lambda e: e.tensor_tensor_scan(out=h_all.ap, data0=a_all.ap, data1=ti_all.ap, initial=0.0, op0=ALU.mult, op1=ALU.add),
                  reads=[a_all, ti_all], writes=[h_all])
            cx.op("dve", lambda e: e.tensor_tensor(out=yb.ap, in0=h_all.ap, in1=gl.ap, op=ALU.mult), reads=[h_all, gl], writes=[yb])
            cx.dma("sp", YS[512 + c * 128:512 + (c + 1) * 128, :], yb.ap, reads=[yb], writes=[B_YS[1]])
        cx.barrier(skip_sw=True)
        sc.close()

    def cmul(eng_a, eng_b, sc_t, outr, outi, ar, ai, br, bi, reads, w_r, w_i, negi=False):
        t1, t2 = sc_t
        cx.op(eng_a, lambda e: e.tensor_tensor(out=t1.ap, in0=ar, in1=br, op=ALU.mult), reads=reads, writes=[t1])
        cx.op(eng_b, lambda e: e.tensor_tensor(out=t2.ap, in0=ai, in1=bi, op=ALU.mult), reads=reads, writes=[t2])
        cx.op(eng_a, lambda e: e.tensor_tensor(out=outr, in0=t1.ap, in1=t2.ap, op=ALU.subtract), reads=[t1, t2], writes=[w_r])
        cx.op(eng_a, lambda e: e.tensor_tensor(out=t1.ap, in0=ar, in1=bi, op=ALU.mult), reads=reads + [w_r], writes=[t1])
        cx.op(eng_b, lambda e: e.tensor_tensor(out=t2.ap, in0=ai, in1=br, op=ALU.mult), reads=reads + [w_r], writes=[t2])
        if negi:
            cx.op(eng_a, lambda e: e.scalar_tensor_tensor(out=outi, in0=t1.ap, scalar=-1.0, in1=t2.ap, op0=ALU.mult, op1=ALU.subtract),
                  reads=[t1, t2], writes=[w_i])
        else:
            cx.op(eng_a, lambda e: e.tensor_tensor(out=outi, in0=t1.ap, in1=t2.ap, op=ALU.add), reads=[t1, t2], writes=[w_i])

    def phase_s5(l):
        ws = Scope(cx)
        Mw = ws.buf([128, 32, 128], BF16, "Mw")
        W1r = ws.buf([128, 32, 64], BF16, "W1r")
        W1i = ws.buf([128, 32, 64], BF16, "W1i")
        W2r = ws.buf([128, 16, 128], BF16, "W2r")
        W2i = ws.buf([128, 16, 128], BF16, "W2i")
        rho = ws.buf([128, 16], F32, "rho")
        ph_r = ws.buf([128, 16], F32, "ph_r")
        ph_i = ws.buf([128, 16], F32, "ph_i")
        dcol = ws.buf([128, 32], F32, "dcol")
        bglu = ws.buf([128, 4], F32, "bglu")
        cx.dma("sp", bglu.ap, s5_b_glu[l].rearrange("(c p) -> p c", p=128), writes=[bglu], allow_slow_non_contiguous=True)
        for tau in range(8):
            cx.dma("sp", dcol.ap[16 * tau:16 * tau + 16, :], s5_d[l].rearrange("(g c) -> c g", c=16), writes=[dcol],
                   allow_slow_non_contiguous=True)
        sc = Scope(cx)
        N = 64
        araw = sc.buf([32, 64], F32, "araw")
        airaw = sc.buf([32, 64], F32, "airaw")
        cx.dma("sp", araw.ap, s5_a_re[l], writes=[araw])
        cx.dma("sp", airaw.ap, s5_a_im[l], writes=[airaw])
        are = sc.buf([N, 32], F32, "are")
        aim = sc.buf([N, 32], F32, "aim")
        for src, dst in ((araw, are), (airaw, aim)):
            cx.op("pe", lambda e, src=src: e.matmul(PS[0].ap[0:64, 0:32], lhsT=src.ap, rhs=identf.ap[0:32, 0:32], start=True, stop=True),
                  reads=[src, identf], writes=[PS[0]])
            cx.op("act", lambda e, dst=dst: e.activation(out=dst.ap, in_=PS[0].ap[0:64, 0:32], func=AF.Identity), reads=[PS[0]], writes=[dst])
        dt = sc.buf([N, 32], F32, "dt")
        cx.dma("sp", dt.ap, s5_log_dt[l].partition_broadcast(N), writes=[dt])
        Br = sc.buf([N, 32, 16], F32, "Br")
        Bi = sc.buf([N, 32, 16], F32, "Bi")
        cx.dma("sp", Br.ap, s5_b_re[l].rearrange("g n c -> n g c"), writes=[Br])
        cx.dma("sp", Bi.ap, s5_b_im[l].rearrange("g n c -> n g c"), writes=[Bi])
        Cr = sc.buf([N, 32, 16], F32, "Cr")
        Ci = sc.buf([N, 32, 16], F32, "Ci")
        craw = sc.buf([128, 4, 64], F32, "craw")
        for src, dst in ((s5_c_re, Cr), (s5_c_im, Ci)):
            cx.dma("sp", craw.ap, src[l].rearrange("(j p) n -> p j n", p=128), writes=[craw])

            def mm(e):
                for j in range(4):
                    i = e.matmul(PS[1].ap[0:64, j * 128:(j + 1) * 128], lhsT=craw.ap[:, j, :], rhs=identf.ap, start=True, stop=True)
                return i
            cx.op("pe", mm, reads=[craw, identf], writes=[PS[1]])
            cx.op("act", lambda e, dst=dst: e.activation(out=dst.ap.rearrange("n g c -> n (g c)"), in_=PS[1].ap[0:64, :], func=AF.Identity),
                  reads=[PS[1]], writes=[dst])

        def small(name):
            return sc.buf([N, 32], F32, name)
        ar, ang, mag, lbr, lbi = small("ar"), small("ang"), small("mag"), small("lbr"), small("lbi")
        t1, t2, t3 = small("t1"), small("t2"), small("t3")
        V = "dve"

        def tt(out, a, b, op_, eng=V):
            cx.op(eng, lambda e: e.tensor_tensor(out=out.ap, in0=a.ap, in1=b.ap, op=op_), reads=[a, b], writes=[out])

        def tsc(out, a, s1, op0, s2=None, op1=None, eng=V):
            if op1 is None:
                cx.op(eng, lambda e: e.tensor_scalar(out=out.ap, in0=a.ap, scalar1=s1, scalar2=None, op0=op0), reads=[a], writes=[out])
            else:
                cx.op(eng, lambda e: e.tensor_scalar(out=out.ap, in0=a.ap, scalar1=s1, scalar2=s2, op0=op0, op1=op1), reads=[a], writes=[out])

        def act(out, a, fn, scale=1.0, bias=None):
            if bias is None:
                cx.op("act", lambda e: e.activation(out=out.ap, in_=a.ap, func=fn, scale=scale), reads=[a], writes=[out])
            else:
                cx.op("act", lambda e: e.activation(out=out.ap, in_=a.ap, func=fn, scale=scale, bias=bias.ap), reads=[a, bias], writes=[out])

        zero_c = sc.buf([N, 1], F32, "zero_c")
        cx.op("dve", lambda e: e.memset(zero_c.ap, 0.0), writes=[zero_c])

        def sin_of(out, angle_buf, shift):
            tsc(t1, angle_buf, 1.0 / TWO_PI, ALU.mult, (shift / TWO_PI) + MAGIC, ALU.add)
            tsc(t1, t1, -MAGIC, ALU.add)
            cx.op(V, lambda e: e.scalar_tensor_tensor(out=t2.ap, in0=t1.ap, scalar=-TWO_PI, in1=angle_buf.ap, op0=ALU.mult, op1=ALU.add),
                  reads=[t1, angle_buf], writes=[t2])
            tsc(t2, t2, float(shift), ALU.add, math.pi - 1e-6, ALU.min)
            tsc(t2, t2, -(math.pi - 1e-6), ALU.max)
            act(out, t2, AF.Sin, bias=zero_c)

        em1, xr_, w_, sn, cm1, nn = small("em1"), small("xr_"), small("w_"), small("sn"), small("cm1"), small("nn")

        def nested(out, var, divs, sign):
            cx.op(V, lambda e: e.memset(out.ap, 1.0), writes=[out])
            for dv in divs:
                tt(t3, out, var, ALU.mult)
                tsc(out, t3, sign / dv, ALU.mult, 1.0, ALU.add)
        ld8 = small("ld8")
        tsc(ld8, dt, 0.125, ALU.mult)
        nested(dt, ld8, [float(k) for k in range(12, 0, -1)], 1.0)
        for _ in range(3):
            tt(dt, dt, dt, ALU.mult)
        tt(ar, are, dt, ALU.mult)
        tt(ang, aim, dt, ALU.mult)
        nested(nn, ar, [9.0, 8.0, 7.0, 6.0, 5.0, 4.0, 3.0, 2.0], 1.0)
        tt(em1, nn, ar, ALU.mult)
        tsc(mag, em1, 1.0, ALU.add)
        C1 = 6.28125
        C2 = TWO_PI - C1
        tsc(t1, ang, 1.0 / TWO_PI, ALU.mult, MAGIC, ALU.add)
        tsc(t1, t1, -MAGIC, ALU.add)
        cx.op(V, lambda e: e.scalar_tensor_tensor(out=xr_.ap, in0=t1.ap, scalar=-C1, in1=ang.ap, op0=ALU.mult, op1=ALU.add),
              reads=[t1, ang], writes=[xr_])
        cx.op(V, lambda e: e.scalar_tensor_tensor(out=xr_.ap, in0=t1.ap, scalar=-C2, in1=xr_.ap, op0=ALU.mult, op1=ALU.add),
              reads=[t1, xr_], writes=[xr_])
        tt(w_, xr_, xr_, ALU.mult)
        nested(nn, w_, [float((2 * k) * (2 * k + 1)) for k in range(10, 0, -1)], -1.0)
        tt(sn, nn, xr_, ALU.mult)
        nested(nn, w_, [float((2 * k + 1) * (2 * k + 2)) for k in range(10, 0, -1)], -1.0)
        tt(cm1, nn, w_, ALU.mult)
        tsc(cm1, cm1, -0.5, ALU.mult)
        lm1 = small("lm1")
        tt(t1, em1, cm1, ALU.mult)
        tt(t2, em1, cm1, ALU.add)
        tt(lm1, t1, t2, ALU.add)
        tsc(lbr, lm1, 1.0, ALU.add)
        tt(lbi, sn, mag, ALU.mult)
        den, qr, qi = small("den"), small("qr"), small("qi")
        tt(den, are, are, ALU.mult)
        tt(t1, aim, aim, ALU.mult)
        tt(den, den, t1, ALU.add)
        cx.op(V, lambda e: e.reciprocal(out=den.ap, in_=den.ap), reads=[den], writes=[den])
        tt(t1, lm1, are, ALU.mult)
        tt(t2, lbi, aim, ALU.mult)
        tt(qr, t1, t2, ALU.add)
        tt(qr, qr, den, ALU.mult)
        tt(t1, lbi, are, ALU.mult)
        tt(t2, lm1, aim, ALU.mult)
        tt(qi, t1, t2, ALU.subtract)
        tt(qi, qi, den, ALU.mult)
        Bbr = sc.buf([N, 32, 16], F32, "Bbr")
        Bbi = sc.buf([N, 32, 16], F32, "Bbi")
        tb1 = sc.buf([N, 32, 16], F32, "tb1")
        tb2 = sc.buf([N, 32, 16], F32, "tb2")

        def bc3(b):
            return b.ap.unsqueeze(2).to_broadcast([N, 32, 16])
        cmul("dve", "pool", (tb1, tb2), Bbr.ap, Bbi.ap, bc3(qr), bc3(qi), Br.ap, Bi.ap, [qr, qi, Br, Bi], Bbr, Bbi)
        Pr = sc.buf([N, 32, 9], F32, "Pr")
        Pi = sc.buf([N, 32, 9], F32, "Pi")
        Qr = sc.buf([N, 32, 8], F32, "Qr")
        Qi = sc.buf([N, 32, 8], F32, "Qi")
        ibr, ibi, im2 = small("ibr"), small("ibi"), small("im2")
        tt(im2, mag, mag, ALU.mult)
        cx.op(V, lambda e: e.reciprocal(out=im2.ap, in_=im2.ap), reads=[im2], writes=[im2])
        tt(ibr, lbr, im2, ALU.mult)
        tt(ibi, lbi, im2, ALU.mult)
        tsc(ibi, ibi, -1.0, ALU.mult)
        for (Xr, Xi, br_, bi_, n_) in ((Pr, Pi, lbr, lbi, 9), (Qr, Qi, ibr, ibi, 8)):
            cx.op(V, lambda e, Xr=Xr: e.memset(Xr.ap[:, :, 0:1], 1.0), writes=[Xr])
            cx.op(V, lambda e, Xi=Xi: e.memset(Xi.ap[:, :, 0:1], 0.0), writes=[Xi])
            for tau in range(1, n_):
                cmul("dve", "pool", (t1, t2), Xr.ap[:, :, tau], Xi.ap[:, :, tau], Xr.ap[:, :, tau - 1], Xi.ap[:, :, tau - 1],
                     br_.ap, bi_.ap, [Xr, Xi, br_, bi_], Xr, Xi)
        GH = 16
        big = [sc.buf([N, GH, 8, 16], F32, f"big{i}") for i in range(8)]
        Ar_, Ai_, Cmr, Cmin, T1, T2, A7r, A7i = big
        mtmp = [sc.buf([128, 128], F32, f"mtmp{i}") for i in range(2)]
        for gh in range(2):
            gs = slice(gh * GH, (gh + 1) * GH)

            def bq(b):
                return b.ap[:, gs, 0:8].unsqueeze(3).to_broadcast([N, GH, 8, 16])

            def bb(b):
                return b.ap[:, gs, :].unsqueeze(2).to_broadcast([N, GH, 8, 16])

            def b7(b, idx):
                return b.ap[:, gs, idx:idx + 1].unsqueeze(3).to_broadcast([N, GH, 8, 16])
            cmul("dve", "pool", (T1, T2), Ar_.ap, Ai_.ap, bq(Qr), bq(Qi), bb(Bbr), bb(Bbi), [Qr, Qi, Bbr, Bbi], Ar_, Ai_)
            cmul("dve", "pool", (T1, T2), Cmr.ap, Cmin.ap, bq(Pr), bq(Pi), bb(Cr), bb(Ci), [Pr, Pi, Cr, Ci], Cmr, Cmin, negi=True)
            cmul("dve", "pool", (T1, T2), A7r.ap, A7i.ap, b7(Pr, 7), b7(Pi, 7), Ar_.ap, Ai_.ap, [Pr, Pi, Ar_, Ai_], A7r, A7i)
            for src, dst in ((A7r, W1r), (A7i, W1i)):
                for g8 in range(2):
                    ps = PS[g8 % 4]

                    def mm(e, ps=ps, src=src, g8=g8):
                        for gi in range(8):
                            i = e.matmul(ps.ap[:, gi * 64:(gi + 1) * 64], lhsT=src.ap[:, g8 * 8 + gi].rearrange("n t c -> n (t c)"),
                                         rhs=identf.ap[0:64, 0:64], start=True, stop=True)
                        return i
                    cx.op("pe", mm, reads=[src, identf], writes=[ps])
                    g0 = gh * GH + g8 * 8
                    cx.op("act", lambda e, ps=ps, dst=dst, g0=g0: e.activation(
                        out=dst.ap[:, g0:g0 + 8, :], in_=ps.ap.rearrange("p (g n) -> p g n", n=64), func=AF.Identity),
                        reads=[ps], writes=[dst])
            for g4 in range(4):
                ps = PS[g4 % 4]

                def mm(e, ps=ps, g4=g4):
                    for gi in range(4):
                        gl_ = g4 * 4 + gi
                        e.matmul(ps.ap[:, gi * 128:(gi + 1) * 128], lhsT=Ar_.ap[:, gl_].rearrange("n t c -> n (t c)"),
                                 rhs=Cmr.ap[:, gl_].rearrange("n t c -> n (t c)"), start=True, stop=False)
                        i = e.matmul(ps.ap[:, gi * 128:(gi + 1) * 128], lhsT=Ai_.ap[:, gl_].rearrange("n t c -> n (t c)"),
                                     rhs=Cmin.ap[:, gl_].rearrange("n t c -> n (t c)"), start=False, stop=True)
                    return i
                cx.op("pe", mm, reads=[Ar_, Ai_, Cmr, Cmin], writes=[ps])
                for gi in range(4):
                    g = gh * GH + g4 * 4 + gi
                    mt = mtmp[g % 2]
                    cx.op("dve", lambda e, ps=ps, gi=gi, mt=mt: e.tensor_tensor(out=mt.ap, in0=ps.ap[:, gi * 128:(gi + 1) * 128], in1=mask8.ap,
                                                                                op=ALU.mult), reads=[ps, mask8], writes=[mt])
                    cx.op("dve", lambda e, g=g, mt=mt: e.scalar_tensor_tensor(out=Mw.ap[:, g, :], in0=identf.ap, scalar=dcol.ap[:, g:g + 1],
                                                                              in1=mt.ap, op0=ALU.mult, op1=ALU.add),
                          reads=[identf, dcol, mt], writes=[Mw])
            W2r_f, W2i_f = A7r, A7i
            cx.op("dve", lambda e: e.tensor_tensor(out=T1.ap, in0=b7(Pr, 1), in1=Cmr.ap, op=ALU.mult), reads=[Pr, Cmr], writes=[T1])
            cx.op("pool", lambda e: e.tensor_tensor(out=T2.ap, in0=b7(Pi, 1), in1=Cmin.ap, op=ALU.mult), reads=[Pi, Cmin], writes=[T2])
            cx.op("dve", lambda e: e.tensor_tensor(out=W2r_f.ap, in0=T1.ap, in1=T2.ap, op=ALU.add), reads=[T1, T2], writes=[W2r_f])
            cx.op("dve", lambda e: e.tensor_tensor(out=T1.ap, in0=b7(Pr, 1), in1=Cmin.ap, op=ALU.mult), reads=[Pr, Cmin, W2r_f], writes=[T1])
            cx.op("pool", lambda e: e.tensor_tensor(out=T2.ap, in0=b7(Pi, 1), in1=Cmr.ap, op=ALU.mult), reads=[Pi, Cmr, W2r_f], writes=[T2])
            cx.op("dve", lambda e: e.tensor_tensor(out=W2i_f.ap, in0=T1.ap, in1=T2.ap, op=ALU.subtract), reads=[T1, T2], writes=[W2i_f])
            for src, dst in ((W2r_f, W2r), (W2i_f, W2i)):
                v = src.ap.rearrange("n (q two) t c -> n q two (t c)", two=2)
                q0 = gh * (GH // 2)
                cx.op("act", lambda e, v=v, dst=dst, q0=q0: e.activation(out=dst.ap[0:64, q0:q0 + GH // 2, :], in_=v[:, :, 0, :], func=AF.Identity),
                      reads=[src], writes=[dst])
                cx.op("act", lambda e, v=v, dst=dst, q0=q0: e.activation(out=dst.ap[64:128, q0:q0 + GH // 2, :], in_=v[:, :, 1, :], func=AF.Identity),
                      reads=[src], writes=[dst])
        r8, e8, p8r, p8i = small("r8"), small("e8"), small("p8r"), small("p8i")
        tt(r8, mag, mag, ALU.mult)
        tt(r8, r8, r8, ALU.mult)
        tt(r8, r8, r8, ALU.mult)
        cx.op(V, lambda e: e.reciprocal(out=e8.ap, in_=r8.ap), reads=[r8], writes=[e8])
        cx.op(V, lambda e: e.tensor_tensor(out=p8r.ap, in0=Pr.ap[:, :, 8], in1=e8.ap, op=ALU.mult), reads=[Pr, e8], writes=[p8r])
        cx.op(V, lambda e: e.tensor_tensor(out=p8i.ap, in0=Pi.ap[:, :, 8], in1=e8.ap, op=ALU.mult), reads=[Pi, e8], writes=[p8i])
        for src, dst in ((r8, rho), (p8r, ph_r), (p8i, ph_i)):
            v = src.ap.rearrange("n (q two) -> n q two", two=2)
            cx.op("act", lambda e, v=v, dst=dst: e.activation(out=dst.ap[0:64], in_=v[:, :, 0], func=AF.Identity), reads=[src], writes=[dst])
            cx.op("act", lambda e, v=v, dst=dst: e.activation(out=dst.ap[64:128], in_=v[:, :, 1], func=AF.Identity), reads=[src], writes=[dst])
        cx.barrier(skip_sw=True)
        sc.close()
        if stop_after == ("s5prep", l):
            ws.close()
            return
        sc = Scope(cx)
        gb = sc.buf([128, 4, L], BF16, "gb")
        SelS = sc.buf([128, 8, 8, 128], BF16, "SelS")
        SelTS = sc.buf([128, 8, 8, 128], BF16, "SelTS")
        cx.dma("sp", SelS.ap, SELD[0], reads=[B_SELD], writes=[SelS])
        cx.dma("sp", SelTS.ap, SELD[1], reads=[B_SELD], writes=[SelTS])
        NQ = 4
        s2 = Scope(cx)
        uT_ = [s2.buf([128, L], BF16, f"uT{i}") for i in range(2)]
        U3_ = [s2.buf([128, 8, TB], BF16, f"U3{i}") for i in range(2)]
        E1r = s2.buf([128, NQ, TB], F32, "E1r")
        E1i = s2.buf([128, NQ, TB], F32, "E1i")
        Hr = s2.buf([128, NQ, TB], BF16, "Hr")
        Hi = s2.buf([128, NQ, TB], BF16, "Hi")
        Y3 = s2.buf([128, 8, TB], BF16, "Y3")
        dd = [s2.buf([128, NQ, 256], F32, f"dd{i}") for i in range(4)]
        tmp2 = [[s2.buf([128, TB], F32, f"s5t{k}_{i}") for i in range(8)] for k in range(2)]
        for qt in range(4):
            uT = uT_[qt % 2]
            U3 = U3_[qt % 2]
            cx.dma("sp", uT.ap, UT[qt * 128:(qt + 1) * 128, :], reads=[B_UT], writes=[uT])
            q0 = qt * NQ
            cx.op("dve", lambda e: e.tensor_copy(out=E1r.ap[:, :, 0], in_=ph_r.ap[:, q0:q0 + NQ]), reads=[ph_r], writes=[E1r])
            cx.op("dve", lambda e: e.tensor_copy(out=E1i.ap[:, :, 0], in_=ph_i.ap[:, q0:q0 + NQ]), reads=[ph_i], writes=[E1i])
            m = 1
            while m < TB:
                def bcm(b, m=m):
                    return b.ap[:, :, m - 1:m].to_broadcast([128, NQ, m])
                cx.op("dve", lambda e, m=m: e.tensor_tensor(out=dd[0].ap[:, :, 0:m], in0=E1r.ap[:, :, 0:m], in1=bcm(E1r), op=ALU.mult),
                      reads=[E1r], writes=[dd[0]])
                cx.op("pool", lambda e, m=m: e.tensor_tensor(out=dd[1].ap[:, :, 0:m], in0=E1i.ap[:, :, 0:m], in1=bcm(E1i), op=ALU.mult),
                      reads=[E1i], writes=[dd[1]])
                cx.op("dve", lambda e, m=m: e.tensor_tensor(out=dd[2].ap[:, :, 0:m], in0=E1r.ap[:, :, 0:m], in1=bcm(E1i), op=ALU.mult),
                      reads=[E1r, E1i], writes=[dd[2]])
                cx.op("pool", lambda e, m=m: e.tensor_tensor(out=dd[3].ap[:, :, 0:m], in0=E1i.ap[:, :, 0:m], in1=bcm(E1r), op=ALU.mult),
                      reads=[E1i, E1r], writes=[dd[3]])
                cx.op("dve", lambda e, m=m: e.tensor_tensor(out=E1r.ap[:, :, m:2 * m], in0=dd[0].ap[:, :, 0:m], in1=dd[1].ap[:, :, 0:m], op=ALU.subtract),
                      reads=[dd[0], dd[1]], writes=[E1r])
                cx.op("dve", lambda e, m=m: e.tensor_tensor(out=E1i.ap[:, :, m:2 * m], in0=dd[2].ap[:, :, 0:m], in1=dd[3].ap[:, :, 0:m], op=ALU.add),
                      reads=[dd[2], dd[3]], writes=[E1i])
                m *= 2
            npsu = 0
            for gl_ in range(8):
                ps = PS[npsu % 4]
                npsu += 1

                def mm(e, ps=ps, gl_=gl_):
                    for tau in range(8):
                        i = e.matmul(ps.ap, lhsT=SelS.ap[:, gl_, tau, :], rhs=uT.ap[:, tau::8], start=(tau == 0), stop=(tau == 7))
                    return i
                cx.op("pe", mm, reads=[SelS, uT], writes=[ps])
                cx.op("act", lambda e, ps=ps, gl_=gl_: e.activation(out=U3.ap[:, gl_, :], in_=ps.ap, func=AF.Identity), reads=[ps], writes=[U3])
            for ql in range(NQ):
                q_ = qt * NQ + ql
                ga, gb_ = 2 * ql, 2 * ql + 1
                pre, pim = PS[4 + (ql % 2) * 2], PS[5 + (ql % 2) * 2]

                def mm(e, pre=pre, pim=pim, ga=ga, gb_=gb_, qt=qt):
                    e.matmul(pre.ap[0:64, :], lhsT=W1r.ap[:, qt * 8 + ga, :], rhs=U3.ap[:, ga, :], start=True, stop=True)
                    e.matmul(pre.ap[64:128, :], lhsT=W1r.ap[:, qt * 8 + gb_, :], rhs=U3.ap[:, gb_, :], start=True, stop=True)
                    e.matmul(pim.ap[0:64, :], lhsT=W1i.ap[:, qt * 8 + ga, :], rhs=U3.ap[:, ga, :], start=True, stop=True)
                    return e.matmul(pim.ap[64:128, :], lhsT=W1i.ap[:, qt * 8 + gb_, :], rhs=U3.ap[:, gb_, :], start=True, stop=True)
                cx.op("pe", mm, reads=[W1r, W1i, U3], writes=[pre, pim])
                xr, xi, a1, a2, vr, vi, sr, si = tmp2[ql % 2]
                cx.op("act", lambda e, pre=pre: e.activation(out=xr.ap, in_=pre.ap, func=AF.Identity), reads=[pre], writes=[xr])
                cx.op("act", lambda e, pim=pim: e.activation(out=xi.ap, in_=pim.ap, func=AF.Identity), reads=[pim], writes=[xi])
                er, ei = E1r.ap[:, ql, :], E1i.ap[:, ql, :]
                cx.op("dve", lambda e, er=er: e.tensor_tensor(out=a1.ap, in0=xr.ap, in1=er, op=ALU.mult), reads=[xr, E1r], writes=[a1])
                cx.op("pool", lambda e, ei=ei: e.tensor_tensor(out=a2.ap, in0=xi.ap, in1=ei, op=ALU.mult), reads=[xi, E1i], writes=[a2])
                cx.op("dve", lambda e: e.tensor_tensor(out=vr.ap, in0=a1.ap, in1=a2.ap, op=ALU.add), reads=[a1, a2], writes=[vr])
                cx.op("dve", lambda e, er=er: e.tensor_tensor(out=a1.ap, in0=xi.ap, in1=er, op=ALU.mult), reads=[xi, E1r], writes=[a1])
                cx.op("pool", lambda e, ei=ei: e.tensor_tensor(out=a2.ap, in0=xr.ap, in1=ei, op=ALU.mult), reads=[xr, E1i], writes=[a2])
                cx.op("dve", lambda e: e.tensor_tensor(out=vi.ap, in0=a1.ap, in1=a2.ap, op=ALU.subtract), reads=[a1, a2], writes=[vi])
                rc = rho.ap[:, q_:q_ + 1].to_broadcast([128, TB])
                cx.op("dve", lambda e, rc=rc: e.tensor_tensor_scan(out=sr.ap, data0=rc, data1=vr.ap, initial=0.0, op0=ALU.mult, op1=ALU.add),
                      reads=[rho, vr], writes=[sr])
                cx.op("dve", lambda e, rc=rc: e.tensor_tensor_scan(out=si.ap, data0=rc, data1=vi.ap, initial=0.0, op0=ALU.mult, op1=ALU.add),
                      reads=[rho, vi], writes=[si])
                cx.op("dve", lambda e, er=er: e.tensor_tensor(out=a1.ap, in0=sr.ap, in1=er, op=ALU.mult), reads=[sr, E1r], writes=[a1])
                cx.op("pool", lambda e, ei=ei: e.tensor_tensor(out=a2.ap, in0=si.ap, in1=ei, op=ALU.mult), reads=[si, E1i], writes=[a2])
                cx.op("pool", lambda e, ql=ql: e.memset(Hr.ap[:, ql, 0:1], 0.0), writes=[Hr])
                cx.op("pool", lambda e, ql=ql: e.memset(Hi.ap[:, ql, 0:1], 0.0), writes=[Hi])
                cx.op("dve", lambda e, ql=ql: e.tensor_tensor(out=Hr.ap[:, ql, 1:TB], in0=a1.ap[:, 0:TB - 1], in1=a2.ap[:, 0:TB - 1], op=ALU.subtract),
                      reads=[a1, a2], writes=[Hr])
                cx.op("dve", lambda e, ei=ei: e.tensor_tensor(out=a1.ap, in0=sr.ap, in1=ei, op=ALU.mult), reads=[sr, E1i], writes=[a1])
                cx.op("pool", lambda e, er=er: e.tensor_tensor(out=a2.ap, in0=si.ap, in1=er, op=ALU.mult), reads=[si, E1r], writes=[a2])
                cx.op("dve", lambda e, ql=ql: e.tensor_tensor(out=Hi.ap[:, ql, 1:TB], in0=a1.ap[:, 0:TB - 1], in1=a2.ap[:, 0:TB - 1], op=ALU.add),
                      reads=[a1, a2], writes=[Hi])
            for gl_ in range(8):
                g = qt * 8 + gl_
                ql, half = gl_ // 2, gl_ % 2
                q_ = qt * NQ + ql
                ps = PS[npsu % 4]
                npsu += 1
                lo, hi = half * 64, half * 64 + 64

                def mm(e, ps=ps, g=g, gl_=gl_, ql=ql, q_=q_, lo=lo, hi=hi):
                    e.matmul(ps.ap, lhsT=Mw.ap[:, g, :], rhs=U3.ap[:, gl_, :], start=True, stop=False)
                    e.matmul(ps.ap, lhsT=W2r.ap[lo:hi, q_, :], rhs=Hr.ap[lo:hi, ql, :], start=False, stop=False)
                    return e.matmul(ps.ap, lhsT=W2i.ap[lo:hi, q_, :], rhs=Hi.ap[lo:hi, ql, :], start=False, stop=True)
                cx.op("pe", mm, reads=[Mw, U3, W2r, W2i, Hr, Hi], writes=[ps])
                cx.op("act", lambda e, ps=ps, gl_=gl_: e.activation(out=Y3.ap[:, gl_, :], in_=ps.ap, func=AF.Identity), reads=[ps], writes=[Y3])
            for tau in range(8):
                ps = PS[npsu % 4]
                npsu += 1

                def mm(e, ps=ps, tau=tau):
                    for gg in range(8):
                        i = e.matmul(ps.ap, lhsT=SelTS.ap[:, gg, tau, :], rhs=Y3.ap[:, gg, :], start=(gg == 0), stop=(gg == 7))
                    return i
                cx.op("pe", mm, reads=[SelTS, Y3], writes=[ps])
                cx.op("act", lambda e, ps=ps, qt=qt, tau=tau: e.activation(out=gb.ap[:, qt, tau::8], in_=ps.ap, func=AF.Gelu_apprx_tanh),
                      reads=[ps], writes=[gb])
        cx.barrier(skip_sw=True)
        s2.close()
        wgl = sc.buf([128, 4, 512], BF16, "wgl")
        cx.dma("sp", wgl.ap, wb_glu[l].rearrange("(kc p) n -> p kc n", p=128), reads=B_wb[("glu", l)], writes=[wgl])
        sg = [sc.buf([128, TB], F32, f"sg{i}") for i in range(2)]
        yst = [sc.buf([128, 4, TB], BF16, f"yst{i}") for i in range(2)]
        n = 0
        for t in range(NB):
            ys_ = yst[t % 2]
            for ct in range(4):
                ps = PS[n % 4]
                s_ = sg[n % 2]
                n += 1

                def mm(e, ps=ps, ct=ct, t=t):
                    for kc in range(4):
                        i = e.matmul(ps.ap, lhsT=wgl.ap[:, kc, ct * 128:(ct + 1) * 128], rhs=gb.ap[:, kc, blk(t)], start=(kc == 0), stop=(kc == 3))
                    return i
                cx.op("pe", mm, reads=[wgl, gb], writes=[ps])
                cx.op("act", lambda e, ps=ps, s_=s_, ct=ct: e.activation(out=s_.ap, in_=ps.ap, func=AF.Sigmoid, bias=bglu.ap[:, ct:ct + 1], scale=1.0),
                      reads=[ps, bglu], writes=[s_])
                cx.op("dve", lambda e, s_=s_, ys_=ys_, ct=ct, t=t: e.tensor_tensor(out=ys_.ap[:, ct, :], in0=gb.ap[:, ct, blk(t)], in1=s_.ap, op=ALU.mult),
                      reads=[gb, s_], writes=[ys_])
            cx.dma("sp", YS[0:512, blk(t)].rearrange("(c p) t -> p c t", p=128), ys_.ap, reads=[ys_], writes=[B_YS[0]])
        cx.barrier(skip_sw=True)
        sc.close()
        ws.close()

    def run_interleaved(gens):
        active = [g for g in gens if g is not None]
        while active:
            for g in list(active):
                try:
                    next(g)
                except StopIteration:
                    active.remove(g)

    def layer_norm_gen(y, gcol, bcol, outb, tmp, stat, sbf):
        pm, pq = PS[6], PS[7]
        ybf, ysq = sbf
        for h2 in range(2):
            sl_ = slice(h2 * 4, (h2 + 1) * 4)
            cx.op("act", lambda e: e.activation(out=ysq.ap[:, sl_, :], in_=y.ap[:, sl_, :], func=AF.Square), reads=[y], writes=[ysq])
            yield
            cx.op("act", lambda e: e.activation(out=ybf.ap[:, sl_, :], in_=y.ap[:, sl_, :], func=AF.Identity), reads=[y], writes=[ybf])
            yield

        def mm1(e):
            for kc in range(8):
                i = e.matmul(pm.ap, lhsT=onesD.ap, rhs=ybf.ap[:, kc, :], start=(kc == 0), stop=(kc == 7))
            return i

        def mm2(e):
            for kc in range(8):
                i = e.matmul(pq.ap, lhsT=onesD.ap, rhs=ysq.ap[:, kc, :], start=(kc == 0), stop=(kc == 7))
            return i
        cx.op("pe", mm1, reads=[onesD, ybf], writes=[pm])
        yield
        cx.op("pe", mm2, reads=[onesD, ysq], writes=[pq])
        yield
        mean, rstd = stat
        cx.op("act", lambda e: e.activation(out=mean.ap, in_=pm.ap, func=AF.Identity), reads=[pm], writes=[mean])
        cx.op("act", lambda e: e.activation(out=rstd.ap, in_=pm.ap, func=AF.Square), reads=[pm], writes=[rstd])
        yield
        cx.op("dve", lambda e: e.tensor_tensor(out=rstd.ap, in0=pq.ap, in1=rstd.ap, op=ALU.subtract), reads=[pq, rstd], writes=[rstd])
        cx.op("dve", lambda e: e.tensor_scalar(out=rstd.ap, in0=rstd.ap, scalar1=0.0, scalar2=LN_EPS, op0=ALU.max, op1=ALU.add),
              reads=[rstd], writes=[rstd])
        yield
        cx.op("act", lambda e: e.activation(out=rstd.ap, in_=rstd.ap, func=AF.Sqrt), reads=[rstd], writes=[rstd])
        cx.op("dve", lambda e: e.reciprocal(out=rstd.ap, in_=rstd.ap), reads=[rstd], writes=[rstd])
        yield
        for h2 in range(2):
            sl_ = slice(h2 * 4, (h2 + 1) * 4)
            mb = mean.ap.unsqueeze(1).to_broadcast([128, 4, TB])
            rb_ = rstd.ap.unsqueeze(1).to_broadcast([128, 4, TB])
            cx.op("dve", lambda e: e.tensor_tensor(out=tmp.ap[:, sl_, :], in0=y.ap[:, sl_, :], in1=mb, op=ALU.subtract), reads=[y, mean], writes=[tmp])
            yield
            cx.op("pool", lambda e: e.tensor_tensor(out=tmp.ap[:, sl_, :], in0=tmp.ap[:, sl_, :], in1=rb_, op=ALU.mult), reads=[tmp, rstd], writes=[tmp])
            yield
        for kc in range(8):
            cx.op("dve", lambda e, kc=kc: e.tensor_scalar(out=y.ap[:, kc, :], in0=tmp.ap[:, kc, :], scalar1=gcol.ap[:, kc:kc + 1],
                                                          scalar2=bcol.ap[:, kc:kc + 1], op0=ALU.mult, op1=ALU.add),
                  reads=[tmp, gcol, bcol], writes=[y])
            yield
        for h2 in range(2):
            sl_ = slice(h2 * 4, (h2 + 1) * 4)
            cx.op("act", lambda e: e.activation(out=outb.ap[:, sl_, :], in_=y.ap[:, sl_, :], func=AF.Identity), reads=[y], writes=[outb])
            yield

    def load_cols(sc, src, l, name, n=8):
        b = sc.buf([128, n], F32, name)
        cx.dma("sp", b.ap, src[l].rearrange("(c p) -> p c", p=128), writes=[b], allow_slow_non_contiguous=True)
        return b

    def phase_mix(l):
        sc = Scope(cx)
        wbr = sc.buf([128, 12, D], BF16, "wbr")
        cx.dma("sp", wbr.ap, wb_branch[l].rearrange("(j p) n -> p j n", p=128), reads=B_wb[("branch", l)], writes=[wbr])
        wgd = [sc.buf([128, 8, 3, 128], BF16, f"wgd{i}") for i in range(2)]
        wgsrc = kview(wb_in[l])[:, :, 3080:6152].rearrange("p kc (k3 dc j) -> p kc k3 dc j", k3=3, dc=8)
        wo = sc.buf([128, 8, D], BF16, "wo")
        cx.dma("sp", wo.ap, kview(wb_out[l]), reads=B_wb[("out", l)], writes=[wo])
        bg = load_cols(sc, b_gate, l, "bg", 24)
        g1 = load_cols(sc, ln1_g, l, "g1")
        b1 = load_cols(sc, ln1_b, l, "b1")
        xb = sc.buf([128, 8, TB], BF16, "mxb")
        xr = [sc.buf([128, TB], F32, f"mxr{i}") for i in range(2)]
        ys = sc.buf([128, 12, TB], BF16, "mys")
        mixb = sc.buf([128, 8, TB], BF16, "mixb")
        yvs = [sc.buf([128, 8, TB], F32, f"yv{i}") for i in range(2)]
        o16 = sc.buf([128, 8, TB], BF16, "mo16")
        tmp = sc.buf([128, 8, TB], F32, "lntmp")
        sbf = (sc.buf([128, 8, TB], BF16, "lnybf"), sc.buf([128, 8, TB], BF16, "lnysq"))
        stat = (sc.buf([128, TB], F32, "mean"), sc.buf([128, TB], F32, "rstd"))
        gsb = [sc.buf([128, TB], F32, f"gsb{i}") for i in range(3)]
        acc = [sc.buf([128, TB], F32, f"acc{i}") for i in range(2)]
        xres_src = xT if l == 0 else XRES
        st = {"n": 0, "nr": 0, "nw": 0}

        def genA(t):
            x_, y_ = xb, ys
            yv = yvs[t % 2]
            cx.dma("sp", x_.ap, kview(XBF)[:, :, blk(t)], reads=[B_XBF[t]], writes=[x_])
            cx.dma("sp", y_.ap, YS.rearrange("(j p) t -> p j t", p=128)[:, :, blk(t)], reads=B_YS, writes=[y_])
            for dc in range(8):
                a_ = acc[dc % 2]
                wg_ = wgd[st["nw"] % 2]
                st["nw"] += 1
                for k3_ in range(3):
                    cx.dma("sp", wg_.ap[:, :, k3_, :], wgsrc[:, :, k3_, dc, :], reads=B_wb[("in", l)], writes=[wg_])
                for k3 in range(3):
                    n = st["n"]
                    st["n"] += 1
                    pp, pg = PS[(2 * n) % 6], PS[(2 * n + 1) % 6]
                    g_ = gsb[n % 3]

                    def mmp(e, pp=pp, k3=k3, dc=dc, y_=y_):
                        for kc in range(4):
                            i = e.matmul(pp.ap, lhsT=wbr.ap[:, k3 * 4 + kc, dc * 128:(dc + 1) * 128], rhs=y_.ap[:, k3 * 4 + kc, :],
                                         start=(kc == 0), stop=(kc == 3))
                        return i

                    def mmg(e, pg=pg, k3=k3, x_=x_, wg_=wg_):
                        for kc in range(8):
                            i = e.matmul(pg.ap, lhsT=wg_.ap[:, kc, k3, :], rhs=x_.ap[:, kc, :], start=(kc == 0), stop=(kc == 7))
                        return i
                    cx.op("pe", mmg, reads=[wg_, x_], writes=[pg])
                    cx.op("pe", mmp, reads=[wbr, y_], writes=[pp])
                    cx.op("act", lambda e, pg=pg, g_=g_, k3=k3, dc=dc: e.activation(out=g_.ap, in_=pg.ap, func=AF.Sigmoid,
                                                                                   bias=bg.ap[:, k3 * 8 + dc:k3 * 8 + dc + 1], scale=1.0),
                          reads=[pg, bg], writes=[g_])
                    if k3 == 0:
                        cx.op("dve", lambda e, pp=pp, g_=g_, a_=a_: e.tensor_tensor(out=a_.ap, in0=pp.ap, in1=g_.ap, op=ALU.mult),
                              reads=[pp, g_], writes=[a_])
                    else:
                        cx.op("dve", lambda e, pp=pp, g_=g_: e.tensor_tensor(out=g_.ap, in0=pp.ap, in1=g_.ap, op=ALU.mult),
                              reads=[pp, g_], writes=[g_])
                        if k3 == 1:
                            cx.op("pool", lambda e, g_=g_, a_=a_: e.tensor_tensor(out=a_.ap, in0=a_.ap, in1=g_.ap, op=ALU.add),
                                  reads=[a_, g_], writes=[a_])
                        else:
                            cx.op("pool", lambda e, g_=g_, a_=a_, dc=dc: e.tensor_tensor(out=mixb.ap[:, dc, :], in0=a_.ap, in1=g_.ap, op=ALU.add),
                                  reads=[a_, g_], writes=[mixb])
                    yield
            for dc in range(8):
                po = PS[6 + dc % 2]
                r_ = xr[st["nr"] % 2]
                st["nr"] += 1
                cx.dma("sp", r_.ap, xres_src[dc * 128:(dc + 1) * 128, blk(t)], reads=[B_XRES[t]], writes=[r_])

                def mmo(e, po=po, dc=dc):
                    for kc in range(8):
                        i = e.matmul(po.ap, lhsT=wo.ap[:, kc, dc * 128:(dc + 1) * 128], rhs=mixb.ap[:, kc, :], start=(kc == 0), stop=(kc == 7))
                    return i
                cx.op("pe", mmo, reads=[wo, mixb], writes=[po])
                cx.op("dve", lambda e, po=po, dc=dc, r_=r_: e.scalar_tensor_tensor(out=yv.ap[:, dc, :], in0=r_.ap, scalar=float(ALPHA),
                                                                                 in1=po.ap, op0=ALU.mult, op1=ALU.add),
                      reads=[r_, po], writes=[yv])
                yield

        def genB(t):
            yv = yvs[t % 2]
            yield from layer_norm_gen(yv, g1, b1, o16, tmp, stat, sbf)
            cx.dma("sp", kview(X1RES)[:, :, blk(t)], yv.ap, reads=[yv], writes=[B_X1RES[t]])
            cx.dma("sp", kview(X1BF)[:, :, blk(t)], o16.ap, reads=[o16], writes=[B_X1BF[t]])
            yield
        run_interleaved([genA(0)])
        for t in range(NB):
            run_interleaved([genA(t + 1) if t + 1 < NB else None, genB(t)])
        cx.barrier(skip_sw=True)
        sc.close()

    def phase_ffn(l, last):
        sc = Scope(cx)
        wdn = sc.buf([128, NHC, D], BF16, "wdn")
        cx.dma("sp", wdn.ap, wb_d[l].rearrange("(j p) n -> p j n", p=128), reads=B_wb[("d", l)], writes=[wdn])
        g2 = load_cols(sc, ln2_g, l, "g2")
        b2 = load_cols(sc, ln2_b, l, "b2")
        xb = sc.buf([128, 8, TB], BF16, "fxb")
        hT = sc.buf([128, NHC, TB], BF16, "hT")
        wgu = [sc.buf([128, 2, 8, 256], BF16, f"wgu{i}") for i in range(2)]
        sl = [sc.buf([128, TB], F32, f"sl{i}") for i in range(2)]
        xr = [sc.buf([128, TB], F32, f"fxr{i}") for i in range(2)]
        yvs = [sc.buf([128, 8, TB], F32, f"fyv{i}") for i in range(2)]
        tmp = sc.buf([128, 8, TB], F32, "flntmp")
        sbf = (sc.buf([128, 8, TB], BF16, "flnybf"), sc.buf([128, 8, TB], BF16, "flnysq"))
        o16 = sc.buf([128, 8, TB], BF16, "fo16")
        stat = (sc.buf([128, TB], F32, "fmean"), sc.buf([128, TB], F32, "frstd"))
        st = {"n": 0, "nr": 0, "nw": 0}

        def genA(t):
            yv = yvs[t % 2]
            cx.dma("sp", xb.ap, kview(X1BF)[:, :, blk(t)], reads=[B_X1BF[t]], writes=[xb])
            for hp in range(NHC // 2):
                w = wgu[st["nw"] % 2]
                st["nw"] += 1
                cx.dma("sp", w.ap[:, 0], kview(wb_g[l])[:, :, hp * 256:(hp + 1) * 256], reads=B_wb[("g", l)], writes=[w])
                cx.dma("sp", w.ap[:, 1], kview(wb_u[l])[:, :, hp * 256:(hp + 1) * 256], reads=B_wb[("u", l)], writes=[w])
                for hh in range(2):
                    hc = hp * 2 + hh
                    n = st["n"]
                    st["n"] += 1
                    pg, pu = PS[(2 * n) % 6], PS[(2 * n + 1) % 6]
                    s_ = sl[n % 2]

                    def mmg(e, pg=pg, w=w, hh=hh):
                        for kc in range(8):
                            i = e.matmul(pg.ap, lhsT=w.ap[:, 0, kc, hh * 128:(hh + 1) * 128], rhs=xb.ap[:, kc, :], start=(kc == 0), stop=(kc == 7))
                        return i

                    def mmu(e, pu=pu, w=w, hh=hh):
                        for kc in range(8):
                            i = e.matmul(pu.ap, lhsT=w.ap[:, 1, kc, hh * 128:(hh + 1) * 128], rhs=xb.ap[:, kc, :], start=(kc == 0), stop=(kc == 7))
                        return i
                    cx.op("pe", mmg, reads=[w, xb], writes=[pg])
                    cx.op("pe", mmu, reads=[w, xb], writes=[pu])
                    cx.op("act", lambda e, pg=pg, s_=s_: e.activation(out=s_.ap, in_=pg.ap, func=AF.Silu), reads=[pg], writes=[s_])
                    cx.op("dve", lambda e, pu=pu, s_=s_, hc=hc: e.tensor_tensor(out=hT.ap[:, hc, :], in0=pu.ap, in1=s_.ap, op=ALU.mult),
                          reads=[pu, s_], writes=[hT])
                    yield
            for dc in range(8):
                po = PS[6 + dc % 2]
                r_ = xr[st["nr"] % 2]
                st["nr"] += 1
                cx.dma("sp", r_.ap, X1RES[dc * 128:(dc + 1) * 128, blk(t)], reads=[B_X1RES[t]], writes=[r_])

                def mmo(e, po=po, dc=dc):
                    for hc in range(NHC):
                        i = e.matmul(po.ap, lhsT=wdn.ap[:, hc, dc * 128:(dc + 1) * 128], rhs=hT.ap[:, hc, :], start=(hc == 0), stop=(hc == NHC - 1))
                    return i
                cx.op("pe", mmo, reads=[wdn, hT], writes=[po])
                cx.op("dve", lambda e, po=po, dc=dc, r_=r_: e.scalar_tensor_tensor(out=yv.ap[:, dc, :], in0=r_.ap, scalar=float(ALPHA),
                                                                                 in1=po.ap, op0=ALU.mult, op1=ALU.add),
                      reads=[r_, po], writes=[yv])
                yield

        def genB(t):
            yv = yvs[t % 2]
            yield from layer_norm_gen(yv, g2, b2, o16, tmp, stat, sbf)
            if last:
                cx.dma("sp", kview(outT)[:, :, blk(t)], yv.ap, reads=[yv], writes=[B_OUT])
            else:
                cx.dma("sp", kview(XRES)[:, :, blk(t)], yv.ap, reads=[yv], writes=[B_XRES[t]])
                cx.dma("sp", kview(XBF)[:, :, blk(t)], o16.ap, reads=[o16], writes=[B_XBF[t]])
            yield
        run_interleaved([genA(0)])
        for t in range(NB):
            run_interleaved([genA(t + 1) if t + 1 < NB else None, genB(t)])
        cx.barrier(skip_sw=True)
        sc.close()

    cx.barrier(skip_sw=True)
    for l in range(n_layers):
        if stop_after == ("setup", l):
            break
        phase_proj(l)
        if l + 1 < n_layers:
            convert_layer(l + 1)
        if stop_after == ("proj", l):
            break
        phase_attn(l)
        if stop_after == ("attn", l):
            break
        phase_lru(l)
        if stop_after == ("lru", l):
            break
        phase_s5(l)
        if stop_after in (("s5", l), ("s5prep", l)):
            break
        phase_mix(l)
        if stop_after == ("mix", l):
            break
        phase_ffn(l, last=(l == n_layers - 1))
    cx.barrier()
    return nc


INPUT_ORDER = ["w_in", "w_branch", "w_out", "w_ffn_gate", "w_ffn_up", "w_ffn_down", "s5_w_glu", "lru_w_a", "lru_w_x",
               "b_f", "b_gate", "s5_a_re", "s5_a_im", "s5_log_dt", "s5_b_re", "s5_b_im", "s5_c_re", "s5_c_im", "s5_d",
               "s5_b_glu", "lru_conv_w", "lru_conv_b", "lru_b_a", "lru_b_x", "lru_lambda", "ln1_g", "ln1_b", "ln2_g", "ln2_b"]


def layout_inputs(inputs, n_layers=DEPTH):
    f = lambda a: np.ascontiguousarray(np.asarray(a, dtype=np.float32)[:n_layers])
    shared = {}
    for k in INPUT_ORDER:
        a = f(inputs[k])
        if k == "w_branch":
            a = a.reshape(n_layers, 1536, D)
        elif k in ("s5_c_re", "s5_c_im"):
            a = a.reshape(n_layers, 512, 64)
        elif k in ("lru_b_a", "lru_b_x"):
            a = a.reshape(n_layers, 512)
        shared[k] = np.ascontiguousarray(a)
    return shared


def kernel(**inputs):
    x = np.asarray(inputs["x"], dtype=np.float32)
    shared = layout_inputs(inputs)
    nc = bass.Bass("TRN2", target_bir_lowering=False)
    build(nc)
    in_maps = []
    for c in range(8):
        m = dict(shared)
        m["xT"] = np.ascontiguousarray(x[c % 4].T)
        in_maps.append(m)
    res = run_bass_kernel_spmd(nc, in_maps, core_ids=list(range(8)))
    out = np.stack([np.ascontiguousarray(res.results[b]["outT"].T) for b in range(4)], axis=0)
    return out.astype(np.float32)
```

```python
import math
import numpy as np
import concourse.bass as bass
import concourse.mybir as mybir
from concourse.bass_utils import run_bass_kernel_spmd

F32 = mybir.dt.float32
BF16 = mybir.dt.bfloat16
AF = mybir.ActivationFunctionType
ALU = mybir.AluOpType

D = 1024
L = 4096
DEPTH = 4
NB = 8
TB = 512
IN_TOTAL = 6152
FFN = 2816
NHC = 22
ALPHA = (2.0 * DEPTH) ** 0.25
LN_EPS = 1e-5
MAGIC = 12582912.0
TWO_PI = 2.0 * math.pi


class Buf:
    __slots__ = ("ap", "w", "r", "name")

    def __init__(self, ap, name=""):
        self.ap = ap
        self.w = {}
        self.r = {}
        self.name = name


class Ctx:
    def __init__(self, nc):
        self.nc = nc
        self.E = {"pe": nc.tensor, "act": nc.scalar, "dve": nc.vector, "pool": nc.gpsimd, "sp": nc.sync}
        self.sem = {}
        self.cnt = {}
        self.nsem = 0
        for e in ("pe", "act", "dve", "pool"):
            self._new_sem(e)
        self.seen = {e: {} for e in self.E}
        self.dma_sems = {"sp": [nc.alloc_semaphore(f"dq{i}") for i in range(60)],
                         "pool": [nc.alloc_semaphore(f"dqs{i}") for i in range(16)]}
        self.dma_cnt = {k: [0] * len(v) for k, v in self.dma_sems.items()}
        self.dma_rr = {"sp": 0, "pool": 0}
        self.semobj = {}
        self.uid = 0

    def _new_sem(self, e):
        s = self.nc.alloc_semaphore(f"s_{e}_{self.nsem}")
        self.nsem += 1
        self.sem[e] = s
        self.cnt[e] = 0

    def _key(self, s):
        k = id(s)
        self.semobj[k] = s
        return k

    def _wait(self, e, deps):
        seen = self.seen[e]
        for k, v in deps.items():
            if seen.get(k, 0) >= v:
                continue
            self.E[e].wait_ge(self.semobj[k], v)
            seen[k] = v

    @staticmethod
    def _merge(dst, src):
        for k, v in src.items():
            if dst.get(k, 0) < v:
                dst[k] = v

    def _deps(self, reads, writes):
        deps = {}
        for b in reads:
            self._merge(deps, b.w)
        for b in writes:
            self._merge(deps, b.w)
            self._merge(deps, b.r)
        return deps

    def _commit(self, tok, reads, writes):
        for b in reads:
            self._merge(b.r, tok)
        for b in writes:
            b.w = dict(tok)
            b.r = {}

    def op(self, e, emit, reads=(), writes=()):
        self._wait(e, self._deps(reads, writes))
        ins = emit(self.E[e])
        if self.cnt[e] >= 30000:
            self._new_sem(e)
        s = self.sem[e]
        self.cnt[e] += 1
        ins.then_inc(s, 1)
        tok = {self._key(s): self.cnt[e]}
        self._commit(tok, reads, writes)
        return tok

    def dma(self, e, out, in_, reads=(), writes=(), **kw):
        self._wait(e, self._deps(reads, writes))
        sems, cnts = self.dma_sems[e], self.dma_cnt[e]
        i = self.dma_rr[e]
        self.dma_rr[e] = (i + 1) % len(sems)
        if cnts[i] >= 30000:
            sems[i] = self.nc.alloc_semaphore(f"dqx{self.nsem}")
            self.nsem += 1
            cnts[i] = 0
        s = sems[i]
        if cnts[i] > 0:
            self._wait(e, {self._key(s): cnts[i]})
        cnts[i] += 16
        self.E[e].dma_start(out=out, in_=in_, **kw).then_inc(s, 16)
        tok = {self._key(s): cnts[i]}
        self._commit(tok, reads, writes)
        return tok

    def barrier(self, skip_sw=False):
        allt = {}
        for e in ("pe", "act", "dve", "pool"):
            if self.cnt[e] > 0:
                allt[self._key(self.sem[e])] = self.cnt[e]
        for q in self.dma_sems:
            for i, s in enumerate(self.dma_sems[q]):
                if self.dma_cnt[q][i] > 0 and not (q == "pool" and skip_sw):
                    allt[self._key(s)] = self.dma_cnt[q][i]
        for e in self.E:
            self._wait(e, allt)


class Scope:
    def __init__(self, cx):
        self.cx = cx
        self.guards = []

    def sb(self, shape, dt=F32, name=None):
        self.cx.uid += 1
        g = self.cx.nc.sbuf_tensor(f"{name or 't'}_{self.cx.uid}", list(shape), dt)
        t = g.__enter__()
        self.guards.append(g)
        return t.ap()

    def buf(self, shape, dt=F32, name=None):
        return Buf(self.sb(shape, dt, name), name or "")

    def close(self):
        for g in reversed(self.guards):
            g.__exit__(None, None, None)
        self.guards = []


def build(nc, n_layers=DEPTH, dbg=False, stop_after=None):
    cx = Ctx(nc)
    kind_dbg = "ExternalOutput" if dbg else "Internal"

    def din(name, shape):
        return nc.dram_tensor(name, list(shape), F32, kind="ExternalInput").ap()

    def dscr(name, shape, dt, k="Internal"):
        return nc.dram_tensor(name, list(shape), dt, kind=k).ap()

    xT = din("xT", [D, L])
    w_in = din("w_in", [n_layers, D, IN_TOTAL])
    w_branch = din("w_branch", [n_layers, 1536, D])
    w_out = din("w_out", [n_layers, D, D])
    w_g = din("w_ffn_gate", [n_layers, D, FFN])
    w_u = din("w_ffn_up", [n_layers, D, FFN])
    w_d = din("w_ffn_down", [n_layers, FFN, D])
    w_glu = din("s5_w_glu", [n_layers, 512, 512])
    lru_w_a = din("lru_w_a", [n_layers, 8, 64, 64])
    lru_w_x = din("lru_w_x", [n_layers, 8, 64, 64])
    b_f = din("b_f", [n_layers, 8])
    b_gate = din("b_gate", [n_layers, 3072])
    s5_a_re = din("s5_a_re", [n_layers, 32, 64])
    s5_a_im = din("s5_a_im", [n_layers, 32, 64])
    s5_log_dt = din("s5_log_dt", [n_layers, 32])
    s5_b_re = din("s5_b_re", [n_layers, 32, 64, 16])
    s5_b_im = din("s5_b_im", [n_layers, 32, 64, 16])
    s5_c_re = din("s5_c_re", [n_layers, 512, 64])
    s5_c_im = din("s5_c_im", [n_layers, 512, 64])
    s5_d = din("s5_d", [n_layers, 512])
    s5_b_glu = din("s5_b_glu", [n_layers, 512])
    lru_conv_w = din("lru_conv_w", [n_layers, 4, 512])
    lru_conv_b = din("lru_conv_b", [n_layers, 512])
    lru_b_a = din("lru_b_a", [n_layers, 512])
    lru_b_x = din("lru_b_x", [n_layers, 512])
    lru_lambda = din("lru_lambda", [n_layers, 512])
    ln1_g = din("ln1_g", [n_layers, D])
    ln1_b = din("ln1_b", [n_layers, D])
    ln2_g = din("ln2_g", [n_layers, D])
    ln2_b = din("ln2_b", [n_layers, D])
    outT = nc.dram_tensor("outT", [D, L], F32, kind="ExternalOutput").ap()

    wb_in = dscr("wb_in", [n_layers, D, IN_TOTAL], BF16)
    wb_branch = dscr("wb_branch", [n_layers, 1536, D], BF16)
    wb_out = dscr("wb_out", [n_layers, D, D], BF16)
    wb_g = dscr("wb_g", [n_layers, D, FFN], BF16)
    wb_u = dscr("wb_u", [n_layers, D, FFN], BF16)
    wb_d = dscr("wb_d", [n_layers, FFN, D], BF16)
    wb_glu = dscr("wb_glu", [n_layers, 512, 512], BF16)
    XBF = dscr("XBF", [D, L], BF16)
    XRES = dscr("XRES", [D, L], F32, kind_dbg)
    X1BF = dscr("X1BF", [D, L], BF16)
    X1RES = dscr("X1RES", [D, L], F32, kind_dbg)
    UT = dscr("UT", [512, L], BF16, kind_dbg)
    XL = dscr("XL", [512, L], F32, kind_dbg)
    GL = dscr("GL", [512, L], F32, kind_dbg)
    QA = dscr("QA", [8, 70, L], BF16, kind_dbg)
    KA = dscr("KA", [8, 70, L], BF16, kind_dbg)
    VA = dscr("VA", [8, 128, 32, 128], BF16, kind_dbg)
    YS = dscr("YS", [1536, L], BF16, kind_dbg)

    B_wb = {}
    for nm in ("in", "branch", "out", "g", "u", "d", "glu"):
        for l in range(n_layers):
            B_wb[(nm, l)] = []
    B_XBF = [Buf(None, f"XBF{t}") for t in range(NB)]
    B_XRES = [Buf(None, f"XRES{t}") for t in range(NB)]
    B_X1BF = [Buf(None, f"X1BF{t}") for t in range(NB)]
    B_X1RES = [Buf(None, f"X1RES{t}") for t in range(NB)]
    B_UT = Buf(None, "UT")
    B_XL = Buf(None, "XL")
    B_GL = Buf(None, "GL")
    B_QA = Buf(None, "QA")
    B_KA = Buf(None, "KA")
    B_VA = Buf(None, "VA")
    B_YS = [Buf(None, f"YS{k}") for k in range(3)]
    B_OUT = Buf(None, "out")

    PS = [Buf(nc.alloc_psum_tensor(f"psb{i}", [128, 512], F32).ap(), f"ps{i}") for i in range(8)]

    cs = Scope(cx)
    identf = cs.buf([128, 128], F32, "identf")
    identb = cs.buf([128, 128], BF16, "identb")
    onesD = cs.buf([128, 128], BF16, "onesD")
    ones1 = cs.buf([128, 64], F32, "ones1")
    mask8 = cs.buf([128, 128], F32, "mask8")
    negtri = cs.buf([128, 128], BF16, "negtri")
    SELD = dscr("SELD", [2, 128, 8, 8, 128], BF16)
    B_SELD = Buf(None, "SELD")
    cs0 = Scope(cx)
    Sel = cs0.buf([128, 8, 8, 128], BF16, "Sel")
    SelT = cs0.buf([128, 8, 8, 128], BF16, "SelT")

    def pool_fill(buf, val):
        cx.op("pool", lambda e: e.memset(buf.ap, val), writes=[buf])

    def pool_sel(buf, ap, pattern, cmp, base, cm, fill=0.0):
        cx.op("pool", lambda e: e.affine_select(out=ap, in_=ap, pattern=pattern, compare_op=cmp, fill=fill,
                                                base=base, channel_multiplier=cm), reads=[buf], writes=[buf])

    pool_fill(identf, 1.0)
    pool_sel(identf, identf.ap, [[1, 128]], ALU.is_equal, 0, -1)
    pool_fill(identb, 1.0)
    pool_sel(identb, identb.ap, [[1, 128]], ALU.is_equal, 0, -1)
    pool_fill(onesD, 1.0 / D)
    pool_fill(ones1, 1.0)
    pool_fill(mask8, 1.0)
    pool_sel(mask8, mask8.ap.rearrange("p (t c) -> p t c", c=16), [[16, 8], [0, 16]], ALU.is_ge, 15, -1)
    pool_fill(negtri, 0.0)
    pool_sel(negtri, negtri.ap, [[1, 128]], ALU.is_ge, 0, -1, fill=-30000.0)
    pool_fill(Sel, 1.0)
    for gg in range(8):
        a4 = Sel.ap[:, gg, :, :].rearrange("p t (u c) -> p t u c", c=16)
        pool_sel(Sel, a4, [[0, 8], [0, 8], [-1, 16]], ALU.is_equal, -16 * gg, 1)
        pool_sel(Sel, a4, [[-1, 8], [1, 8], [0, 16]], ALU.is_equal, 0, 0)
    pool_fill(SelT, 1.0)
    for gg in range(8):
        a3 = SelT.ap[:, gg, :, :]
        pool_sel(SelT, a3, [[16, 8], [1, 128]], ALU.is_equal, -16 * gg, -1)
        pool_sel(SelT, a3, [[-16, 8], [0, 128]], ALU.is_ge, 0, 1)
        pool_sel(SelT, a3, [[16, 8], [0, 128]], ALU.is_ge, 15, -1)

    cx.dma("sp", SELD[0], Sel.ap, reads=[Sel], writes=[B_SELD])
    cx.dma("sp", SELD[1], SelT.ap, reads=[SelT], writes=[B_SELD])
    cx.barrier()
    cs0.close()

    def convert(src, dst, rows, key):
        r = 0
        while r < rows:
            n = min(128, rows - r)
            bch = Buf(None, "wbch")
            B_wb[key].append(bch)
            cx.dma("pool", dst[r:r + n, :], src[r:r + n, :], writes=[bch])
            r += n

    def convert_layer(l):
        convert(w_in[l], wb_in[l], D, ("in", l))
        convert(w_glu[l], wb_glu[l], 512, ("glu", l))
        convert(w_branch[l], wb_branch[l], 1536, ("branch", l))
        convert(w_out[l], wb_out[l], D, ("out", l))
        convert(w_g[l], wb_g[l], D, ("g", l))
        convert(w_u[l], wb_u[l], D, ("u", l))
        convert(w_d[l], wb_d[l], FFN, ("d", l))

    for t in range(NB):
        for kc in range(8):
            cx.dma("pool", XBF[kc * 128:(kc + 1) * 128, t * TB:(t + 1) * TB],
                   xT[kc * 128:(kc + 1) * 128, t * TB:(t + 1) * TB], writes=[B_XBF[t]])

    convert_layer(0)

    def kview(ap2d):
        return ap2d.rearrange("(kc p) n -> p kc n", p=128)

    def blk(t):
        return slice(t * TB, (t + 1) * TB)

    def phase_proj(l):
        sc_fg = Scope(cx)
        fgT = sc_fg.buf([8, L], F32, "fgT")
        sc = Scope(cx)
        xb = [sc.buf([128, 8, TB], BF16, f"xb{t}") for t in range(NB)]
        for t in range(NB):
            cx.dma("sp", xb[t].ap, kview(XBF)[:, :, blk(t)], reads=[B_XBF[t]], writes=[xb[t]])
        wt = [sc.buf([128, 8, 512], BF16, f"wt{i}") for i in range(2)]
        wfg = sc.buf([128, 8, 8], BF16, "wfg")
        cx.dma("sp", wfg.ap, kview(wb_in[l])[:, :, 3072:3080], reads=B_wb[("in", l)], writes=[wfg])
        st32 = [sc.buf([128, 4, TB], F32, f"st32_{i}") for i in range(2)]
        st16 = [sc.buf([128, 4, TB], BF16, f"st16_{i}") for i in range(2)]
        stqk = [sc.buf([128, 4, TB], BF16, f"stqk_{i}") for i in range(2)]
        vst = [sc.buf([128, 8, 128], BF16, f"vst_{i}") for i in range(2)]
        for v in vst:
            cx.op("pool", lambda e, v=v: e.memset(v.ap, 1.0), writes=[v])
        nps = 0
        nst = 0
        for cg in range(6):
            w = wt[cg % 2]
            cx.dma("sp", w.ap, kview(wb_in[l])[:, :, cg * 512:(cg + 1) * 512], reads=B_wb[("in", l)], writes=[w])
            if cg < 3:
                for t in range(NB):
                    stb = (st16 if cg == 0 else st32)[nst % 2]
                    nst += 1
                    for ct in range(4):
                        ps = PS[nps % 4]
                        nps += 1

                        def mm(e, ps=ps, ct=ct, t=t, w=w):
                            for kc in range(8):
                                i = e.matmul(ps.ap, lhsT=w.ap[:, kc, ct * 128:(ct + 1) * 128], rhs=xb[t].ap[:, kc, :],
                                             start=(kc == 0), stop=(kc == 7))
                            return i
                        cx.op("pe", mm, reads=[w, xb[t]], writes=[ps])
                        fn = AF.Gelu_apprx_tanh if cg == 2 else AF.Identity
                        cx.op("act", lambda e, ps=ps, stb=stb, ct=ct, fn=fn: e.activation(out=stb.ap[:, ct, :], in_=ps.ap, func=fn),
                              reads=[ps], writes=[stb])
                    dst, bd = [(UT, B_UT), (XL, B_XL), (GL, B_GL)][cg]
                    cx.dma("sp", dst.rearrange("(c p) t -> p c t", p=128)[:, :, blk(t)], stb.ap, reads=[stb], writes=[bd])
            elif cg < 5:
                for t in range(NB):
                    stb = stqk[nst % 2]
                    nst += 1
                    for hp in range(4):
                        ps = PS[nps % 4]
                        nps += 1

                        def mm(e, ps=ps, hp=hp, t=t, w=w):
                            for kc in range(8):
                                i = e.matmul(ps.ap, lhsT=w.ap[:, kc, hp * 128:(hp + 1) * 128], rhs=xb[t].ap[:, kc, :],
                                             start=(kc == 0), stop=(kc == 7))
                            return i
                        cx.op("pe", mm, reads=[w, xb[t]], writes=[ps])
                        sc_ = 0.125 if cg == 3 else 1.0
                        cx.op("act", lambda e, ps=ps, stb=stb, hp=hp, sc_=sc_: e.activation(out=stb.ap[:, hp, :], in_=ps.ap,
                                                                                           func=AF.Identity, scale=sc_),
                              reads=[ps], writes=[stb])
                    dst, bd = (QA, B_QA) if cg == 3 else (KA, B_KA)
                    for two in range(2):
                        cx.dma("sp", dst[two::2, 0:64, blk(t)].rearrange("hp d t -> d hp t"), stb.ap[two * 64:(two + 1) * 64, :, :],
                               reads=[stb], writes=[bd])
            else:
                for tt in range(32):
                    ps = PS[nps % 4]
                    nps += 1
                    t = tt // 4
                    vs = vst[tt % 2]

                    def mm(e, ps=ps, tt=tt, t=t, w=w):
                        o = (tt % 4) * 128
                        for kc in range(8):
                            i = e.matmul(ps.ap, lhsT=xb[t].ap[:, kc, o:o + 128], rhs=w.ap[:, kc, :],
                                         start=(kc == 0), stop=(kc == 7))
                        return i
                    cx.op("pe", mm, reads=[w, xb[t]], writes=[ps])
                    cx.op("act", lambda e, ps=ps, vs=vs: e.activation(out=vs.ap[:, :, 0:64], in_=ps.ap.rearrange("p (h d) -> p h d", d=64),
                                                                      func=AF.Identity), reads=[ps], writes=[vs])
                    cx.dma("sp", VA[:, :, tt, :].rearrange("h p e -> p h e"), vs.ap, reads=[vs], writes=[B_VA])
        for t in range(NB):
            ps = PS[nps % 4]
            nps += 1

            def mm(e, ps=ps, t=t):
                for kc in range(8):
                    i = e.matmul(ps.ap[0:8, :], lhsT=wfg.ap[:, kc, :], rhs=xb[t].ap[:, kc, :], start=(kc == 0), stop=(kc == 7))
                return i
            cx.op("pe", mm, reads=[wfg, xb[t]], writes=[ps])
            cx.op("act", lambda e, ps=ps, t=t: e.activation(out=fgT.ap[:, blk(t)], in_=ps.ap[0:8, :], func=AF.Identity),
                  reads=[ps], writes=[fgT])
        cx.barrier(skip_sw=True)
        sc.close()
        sc = Scope(cx)
        bf = sc.buf([8, 1], F32, "bf")
        cx.dma("sp", bf.ap, b_f[l].rearrange("(h o) -> h o", o=1), writes=[bf])
        nbf = sc.buf([8, 1], F32, "nbf")
        cx.op("dve", lambda e: e.tensor_scalar(out=nbf.ap, in0=bf.ap, scalar1=-1.0, scalar2=None, op0=ALU.mult), reads=[bf], writes=[nbf])
        one8 = sc.buf([8, 1], F32, "one8")
        cx.op("dve", lambda e: e.memset(one8.ap, 1.0), writes=[one8])
        ex = sc.buf([8, L], F32, "ex")
        cx.op("act", lambda e: e.activation(out=ex.ap, in_=fgT.ap, func=AF.Exp, bias=nbf.ap, scale=-1.0), reads=[fgT, nbf], writes=[ex])
        cx.op("act", lambda e: e.activation(out=ex.ap, in_=ex.ap, func=AF.Ln, bias=one8.ap, scale=1.0), reads=[ex, one8], writes=[ex])
        csum = sc.buf([8, L], F32, "csum")
        cx.op("dve", lambda e: e.tensor_tensor_scan(out=csum.ap, data0=one8.ap.to_broadcast([8, L]), data1=ex.ap, initial=0.0,
                                                    op0=ALU.mult, op1=ALU.add), reads=[ex, one8], writes=[csum])
        pcs = [sc.buf([8, L], BF16, f"pc{j}") for j in range(3)]
        ncs = [sc.buf([8, L], BF16, f"nc{j}") for j in range(3)]
        res = ex
        cx.op("dve", lambda e: e.tensor_copy(out=pcs[0].ap, in_=csum.ap), reads=[csum], writes=[pcs[0]])
        cx.op("dve", lambda e: e.tensor_tensor(out=res.ap, in0=csum.ap, in1=pcs[0].ap, op=ALU.subtract), reads=[csum, pcs[0]], writes=[res])
        cx.op("dve", lambda e: e.tensor_copy(out=pcs[1].ap, in_=res.ap), reads=[res], writes=[pcs[1]])
        cx.op("dve", lambda e: e.tensor_tensor(out=res.ap, in0=res.ap, in1=pcs[1].ap, op=ALU.subtract), reads=[res, pcs[1]], writes=[res])
        cx.op("dve", lambda e: e.tensor_copy(out=pcs[2].ap, in_=res.ap), reads=[res], writes=[pcs[2]])
        for j in range(3):
            cx.op("dve", lambda e, j=j: e.tensor_scalar(out=ncs[j].ap, in0=pcs[j].ap, scalar1=-1.0, scalar2=None, op0=ALU.mult),
                  reads=[pcs[j]], writes=[ncs[j]])
        onesb = sc.buf([8, L], BF16, "onesb")
        cx.op("pool", lambda e: e.memset(onesb.ap, 1.0), writes=[onesb])
        for j in range(3):
            cx.dma("sp", QA[:, 64 + j, :], ncs[j].ap, reads=[ncs[j]], writes=[B_QA])
            cx.dma("sp", QA[:, 67 + j, :], onesb.ap, reads=[onesb], writes=[B_QA])
            cx.dma("sp", KA[:, 64 + j, :], onesb.ap, reads=[onesb], writes=[B_KA])
            cx.dma("sp", KA[:, 67 + j, :], pcs[j].ap, reads=[pcs[j]], writes=[B_KA])
        cx.barrier(skip_sw=True)
        sc.close()
        sc_fg.close()

    def phase_attn(l):
        sc = Scope(cx)
        qa = [sc.buf([70, L], BF16, f"qa{i}") for i in range(2)]
        ka = [sc.buf([70, L], BF16, f"ka{i}") for i in range(2)]
        va = [sc.buf([128, 32, 128], BF16, f"va{i}") for i in range(2)]
        NPT = 6
        pt = [sc.buf([128, TB], BF16, f"pt{i}") for i in range(NPT)]
        rden = [sc.buf([128, TB], F32, f"rden{i}") for i in range(2)]
        rb = [sc.buf([64, TB], F32, f"rb{i}") for i in range(2)]
        ost = [sc.buf([64, TB], BF16, f"ost{i}") for i in range(2)]

        def load_head(h):
            cx.dma("sp", qa[h % 2].ap, QA[h], reads=[B_QA], writes=[qa[h % 2]])
            cx.dma("sp", ka[h % 2].ap, KA[h], reads=[B_KA], writes=[ka[h % 2]])
            cx.dma("sp", va[h % 2].ap, VA[h], reads=[B_VA], writes=[va[h % 2]])
        items = []
        nb = 0
        for h in range(8):
            for I in range(NB):
                nkb = 4 * I + 4
                for j in range(nkb):
                    items.append((h, I, j, nkb, nb))
                nb += 1
        LA = 3

        def emit_S(i):
            h, I, j, nkb, b_ = items[i]
            c0 = 128 * max(0, j - 4 * I)
            diag = j >= 4 * I
            ps = PS[i % 4]
            k, q = ka[h % 2], qa[h % 2]

            def mm(e):
                i_ = e.matmul(ps.ap[:, c0:TB], lhsT=k.ap[:, j * 128:(j + 1) * 128], rhs=q.ap[:, I * TB + c0:(I + 1) * TB],
                              start=True, stop=not diag)
                if diag:
                    i_ = e.matmul(ps.ap[:, c0:c0 + 128], lhsT=identb.ap, rhs=negtri.ap, start=False, stop=True)
                return i_
            cx.op("pe", mm, reads=[k, q, identb, negtri], writes=[ps])

        def finalize(h, I, b_):
            po = PS[4 + b_ % 2]
            pr = PS[6 + b_ % 2]
            rd, r_, o_ = rden[b_ % 2], rb[b_ % 2], ost[b_ % 2]
            cx.op("dve", lambda e: e.reciprocal(out=rd.ap[64:128, :], in_=po.ap[64:128, :]), reads=[po], writes=[rd])
            cx.op("act", lambda e: e.activation(out=r_.ap, in_=rd.ap[64:128, :], func=AF.Identity), reads=[rd], writes=[r_])
            cx.op("dve", lambda e: e.tensor_tensor(out=o_.ap, in0=po.ap[0:64, :], in1=r_.ap, op=ALU.mult), reads=[po, r_], writes=[o_])
            cx.dma("sp", YS[1024 + h * 64:1024 + (h + 1) * 64, blk(I)], o_.ap, reads=[o_], writes=[B_YS[2]])
        load_head(0)
        for i in range(min(LA, len(items))):
            emit_S(i)
        pending = []
        for i, (h, I, j, nkb, b_) in enumerate(items):
            if I == 0 and j == 0 and h + 1 < 8:
                load_head(h + 1)
            if i + LA < len(items):
                emit_S(i + LA)
            c0 = 128 * max(0, j - 4 * I)
            ps = PS[i % 4]
            p = pt[i % NPT]
            v = va[h % 2]
            po = PS[4 + b_ % 2]
            cx.op("act", lambda e: e.activation(out=p.ap[:, c0:TB], in_=ps.ap[:, c0:TB], func=AF.Exp), reads=[ps], writes=[p])
            cx.op("pe", lambda e: e.matmul(po.ap[:, c0:TB], lhsT=v.ap[:, j, :], rhs=p.ap[:, c0:TB], start=(j == 0), stop=(j == nkb - 1)),
                  reads=[v, p], writes=[po])
            pending = [(cnt - 1, args) for (cnt, args) in pending]
            for cnt, args in [x for x in pending if x[0] <= 0]:
                finalize(*args)
            pending = [x for x in pending if x[0] > 0]
            if j == nkb - 1:
                pending.append((2, (h, I, b_)))
        for cnt, args in pending:
            finalize(*args)
        cx.barrier(skip_sw=True)
        sc.close()

    def phase_lru(l):
        sc = Scope(cx)
        cw = sc.buf([128, 4, 4], F32, "cw")
        cb = sc.buf([128, 4], F32, "cb")
        ba = sc.buf([128, 4], F32, "ba")
        bx = sc.buf([128, 4], F32, "bx")
        lam = sc.buf([128, 4], F32, "lam")
        sneg = sc.buf([128, 4], F32, "sneg")
        one_c = sc.buf([128, 1], F32, "one_c")
        cx.op("dve", lambda e: e.memset(one_c.ap, 1.0), writes=[one_c])
        for k_ in range(4):
            cx.dma("sp", cw.ap[:, :, k_], lru_conv_w[l, k_].rearrange("(c p) -> p c", p=128), writes=[cw], allow_slow_non_contiguous=True)
        for (dst, src) in ((cb, lru_conv_b), (ba, lru_b_a), (bx, lru_b_x), (lam, lru_lambda)):
            cx.dma("sp", dst.ap, src[l].rearrange("(c p) -> p c", p=128), writes=[dst], allow_slow_non_contiguous=True)
        cx.op("act", lambda e: e.activation(out=sneg.ap, in_=lam.ap, func=AF.Exp, scale=-1.0), reads=[lam], writes=[sneg])
        cx.op("act", lambda e: e.activation(out=sneg.ap, in_=sneg.ap, func=AF.Ln, bias=one_c.ap, scale=1.0), reads=[sneg, one_c], writes=[sneg])
        cx.op("dve", lambda e: e.tensor_scalar(out=sneg.ap, in0=sneg.ap, scalar1=-8.0, scalar2=None, op0=ALU.mult), reads=[sneg], writes=[sneg])
        WA = sc.buf([128, 4, 128], BF16, "WA")
        WX = sc.buf([128, 4, 128], BF16, "WX")
        for Wm, src in ((WA, lru_w_a), (WX, lru_w_x)):
            cx.op("pool", lambda e, Wm=Wm: e.memset(Wm.ap, 0.0), writes=[Wm])
            for c in range(4):
                cx.dma("pool", Wm.ap[0:64, c, 0:64], src[l, 2 * c], writes=[Wm])
                cx.dma("pool", Wm.ap[64:128, c, 64:128], src[l, 2 * c + 1], writes=[Wm])
        hba = sc.buf([128, 4], F32, "hba")
        hbx = sc.buf([128, 4], F32, "hbx")
        hsn = sc.buf([128, 4], F32, "hsn")
        for dst, src in ((hba, ba), (hbx, bx), (hsn, sneg)):
            cx.op("dve", lambda e, dst=dst, src=src: e.tensor_scalar(out=dst.ap, in0=src.ap, scalar1=0.5, scalar2=None, op0=ALU.mult),
                  reads=[src], writes=[dst])
        xl = sc.buf([128, L + 3], F32, "xl")
        gl = sc.buf([128, L], F32, "gl")
        xc = sc.buf([128, L], F32, "xc")
        xcb = sc.buf([128, L], BF16, "xcb")
        a_all = sc.buf([128, L], F32, "a_all")
        tr_all = sc.buf([128, L], F32, "tr_all")
        ti_all = sc.buf([128, L], F32, "ti_all")
        h_all = sc.buf([128, L], F32, "h_all")
        yb = sc.buf([128, L], BF16, "yb")
        cx.op("pool", lambda e: e.memset(xl.ap[:, 0:3], 0.0), writes=[xl])
        n = 0
        for c in range(4):
            cx.dma("sp", xl.ap[:, 3:], XL[c * 128:(c + 1) * 128, :], reads=[B_XL], writes=[xl])
            cx.dma("sp", gl.ap, GL[c * 128:(c + 1) * 128, :], reads=[B_GL], writes=[gl])
            cx.op("dve", lambda e, c=c: e.tensor_scalar(out=xc.ap, in0=xl.ap[:, 0:L], scalar1=cw.ap[:, c, 0:1], scalar2=cb.ap[:, c:c + 1],
                                                        op0=ALU.mult, op1=ALU.add), reads=[xl, cw, cb], writes=[xc])
            for k_ in range(1, 4):
                cx.op("dve", lambda e, c=c, k_=k_: e.scalar_tensor_tensor(out=xc.ap, in0=xl.ap[:, k_:k_ + L], scalar=cw.ap[:, c, k_:k_ + 1],
                                                                         in1=xc.ap, op0=ALU.mult, op1=ALU.add), reads=[xl, cw, xc], writes=[xc])
            for hh_ in range(2):
                hs_ = slice(hh_ * (L // 2), (hh_ + 1) * (L // 2))
                cx.op("act", lambda e, hs_=hs_: e.activation(out=xcb.ap[:, hs_], in_=xc.ap[:, hs_], func=AF.Identity), reads=[xc], writes=[xcb])
            for t in range(NB):
                pa, px = PS[(2 * n) % 4], PS[(2 * n + 1) % 4]
                n += 1
                cx.op("pe", lambda e, pa=pa, c=c, t=t: e.matmul(pa.ap, lhsT=WA.ap[:, c, :], rhs=xcb.ap[:, blk(t)], start=True, stop=True),
                      reads=[WA, xcb], writes=[pa])
                cx.op("pe", lambda e, px=px, c=c, t=t: e.matmul(px.ap, lhsT=WX.ap[:, c, :], rhs=xcb.ap[:, blk(t)], start=True, stop=True),
                      reads=[WX, xcb], writes=[px])
                cx.op("act", lambda e, pa=pa, c=c, t=t: e.activation(out=tr_all.ap[:, blk(t)], in_=pa.ap, func=AF.Tanh, bias=hba.ap[:, c:c + 1], scale=0.5),
                      reads=[pa, hba], writes=[tr_all])
                cx.op("act", lambda e, px=px, c=c, t=t: e.activation(out=ti_all.ap[:, blk(t)], in_=px.ap, func=AF.Tanh, bias=hbx.ap[:, c:c + 1], scale=0.5),
                      reads=[px, hbx], writes=[ti_all])
            for hh_ in range(2):
                hs_ = slice(hh_ * (L // 2), (hh_ + 1) * (L // 2))
                cx.op("act", lambda e, c=c, hs_=hs_: e.activation(out=a_all.ap[:, hs_], in_=tr_all.ap[:, hs_], func=AF.Exp, bias=hsn.ap[:, c:c + 1],
                                                               scale=hsn.ap[:, c:c + 1]), reads=[tr_all, hsn], writes=[a_all])
            for hh_ in range(2):
                hs_ = slice(hh_ * (L // 2), (hh_ + 1) * (L // 2))
                cx.op("act", lambda e, hs_=hs_: e.activation(out=tr_all.ap[:, hs_], in_=a_all.ap[:, hs_], func=AF.Square), reads=[a_all], writes=[tr_all])
            cx.op("dve", lambda e: e.tensor_scalar(out=ti_all.ap, in0=ti_all.ap, scalar1=0.5, scalar2=0.5, op0=ALU.mult, op1=ALU.add),
                  reads=[ti_all], writes=[ti_all])
            cx.op("pool", lambda e: e.tensor_tensor(out=ti_all.ap, in0=ti_all.ap, in1=xc.ap, op=ALU.mult), reads=[ti_all, xc], writes=[ti_all])
            for hh_ in range(2):
                hs_ = slice(hh_ * (L // 2), (hh_ + 1) * (L // 2))
                cx.op("act", lambda e, hs_=hs_: e.activation(out=tr_all.ap[:, hs_], in_=tr_all.ap[:, hs_], func=AF.Sqrt, bias=one_c.ap, scale=-1.0),
                      reads=[tr_all, one_c], writes=[tr_all])
            cx.op("dve", lambda e: e.tensor_tensor(out=ti_all.ap, in0=ti_all.ap, in1=tr_all.ap, op=ALU.mult), reads=[ti_all, tr_all], writes=[ti_all])
            cx.op("dve", lambda e: e.tensor_tensor_scan(out=h_all.ap, data0=a_all.ap, data1=ti_all.ap, initial=0.0, op0=ALU.mult, op1=ALU.add),
                  reads=[a_all, ti_all], writes=[h_all])
            cx.op("dve", lambda e: e.tensor_tensor(out=yb.ap, in0=h_all.ap, in1=gl.ap, op=ALU.mult), reads=[h_all, gl], writes=[yb])
            cx.dma("sp", YS[512 + c * 128:512 + (c + 1) * 128, :], yb.ap, reads=[yb], writes=[B_YS[1]])
        cx.barrier(skip_sw=True)
        sc.close()

    def cmul(eng_a, eng_b, sc_t, outr, outi, ar, ai, br, bi, reads, w_r, w_i, negi=False):
        t1, t2 = sc_t
        cx.op(eng_a, lambda e: e.tensor_tensor(out=t1.ap, in0=ar, in1=br, op=ALU.mult), reads=reads, writes=[t1])
        cx.op(eng_b, lambda e: e.tensor_tensor(out=t2.ap, in0=ai, in1=bi, op=ALU.mult), reads=reads, writes=[t2])
        cx.op(eng_a, lambda e: e.tensor_tensor(out=outr, in0=t1.ap, in1=t2.ap, op=ALU.subtract), reads=[t1, t2], writes=[w_r])
        cx.op(eng_a, lambda e: e.tensor_tensor(out=t1.ap, in0=ar, in1=bi, op=ALU.mult), reads=reads + [w_r], writes=[t1])
        cx.op(eng_b, lambda e: e.tensor_tensor(out=t2.ap, in0=ai, in1=br, op=ALU.mult), reads=reads + [w_r], writes=[t2])
        if negi:
            cx.op(eng_a, lambda e: e.scalar_tensor_tensor(out=outi, in0=t1.ap, scalar=-1.0, in1=t2.ap, op0=ALU.mult, op1=ALU.subtract),
                  reads=[t1, t2], writes=[w_i])
        else:
            cx.op(eng_a, lambda e: e.tensor_tensor(out=outi, in0=t1.ap, in1=t2.ap, op=ALU.add), reads=[t1, t2], writes=[w_i])

    def phase_s5(l):
        ws = Scope(cx)
        Mw = ws.buf([128, 32, 128], BF16, "Mw")
        W1r = ws.buf([128, 32, 64], BF16, "W1r")
        W1i = ws.buf([128, 32, 64], BF16, "W1i")
        W2r = ws.buf([128, 16, 128], BF16, "W2r")
        W2i = ws.buf([128, 16, 128], BF16, "W2i")
        rho = ws.buf([128, 16], F32, "rho")
        ph_r = ws.buf([128, 16], F32, "ph_r")
        ph_i = ws.buf([128, 16], F32, "ph_i")
        dcol = ws.buf([128, 32], F32, "dcol")
        bglu = ws.buf([128, 4], F32, "bglu")
        cx.dma("sp", bglu.ap, s5_b_glu[l].rearrange("(c p) -> p c", p=128), writes=[bglu], allow_slow_non_contiguous=True)
        for tau in range(8):
            cx.dma("sp", dcol.ap[16 * tau:16 * tau + 16, :], s5_d[l].rearrange("(g c) -> c g", c=16), writes=[dcol],
                   allow_slow_non_contiguous=True)
        sc = Scope(cx)
        N = 64
        araw = sc.buf([32, 64], F32, "araw")
        airaw = sc.buf([32, 64], F32, "airaw")
        cx.dma("sp", araw.ap, s5_a_re[l], writes=[araw])
        cx.dma("sp", airaw.ap, s5_a_im[l], writes=[airaw])
        are = sc.buf([N, 32], F32, "are")
        aim = sc.buf([N, 32], F32, "aim")
        for src, dst in ((araw, are), (airaw, aim)):
            cx.op("pe", lambda e, src=src: e.matmul(PS[0].ap[0:64, 0:32], lhsT=src.ap, rhs=identf.ap[0:32, 0:32], start=True, stop=True),
                  reads=[src, identf], writes=[PS[0]])
            cx.op("act", lambda e, dst=dst: e.activation(out=dst.ap, in_=PS[0].ap[0:64, 0:32], func=AF.Identity), reads=[PS[0]], writes=[dst])
        dt = sc.buf([N, 32], F32, "dt")
        cx.dma("sp", dt.ap, s5_log_dt[l].partition_broadcast(N), writes=[dt])
        Br = sc.buf([N, 32, 16], F32, "Br")
        Bi = sc.buf([N, 32, 16], F32, "Bi")
        cx.dma("sp", Br.ap, s5_b_re[l].rearrange("g n c -> n g c"), writes=[Br])
        cx.dma("sp", Bi.ap, s5_b_im[l].rearrange("g n c -> n g c"), writes=[Bi])
        Cr = sc.buf([N, 32, 16], F32, "Cr")
        Ci = sc.buf([N, 32, 16], F32, "Ci")
        craw = sc.buf([128, 4, 64], F32, "craw")
        for src, dst in ((s5_c_re, Cr), (s5_c_im, Ci)):
            cx.dma("sp", craw.ap, src[l].rearrange("(j p) n -> p j n", p=128), writes=[craw])

            def mm(e):
                for j in range(4):
                    i = e.matmul(PS[1].ap[0:64, j * 128:(j + 1) * 128], lhsT=craw.ap[:, j, :], rhs=identf.ap, start=True, stop=True)
                return i
            cx.op("pe", mm, reads=[craw, identf], writes=[PS[1]])
            cx.op("act", lambda e, dst=dst: e.activation(out=dst.ap.rearrange("n g c -> n (g c)"), in_=PS[1].ap[0:64, :], func=AF.Identity),
                  reads=[PS[1]], writes=[dst])

        def small(name):
            return sc.buf([N, 32], F32, name)
        ar, ang, mag, lbr, lbi = small("ar"), small("ang"), small("mag"), small("lbr"), small("lbi")
        t1, t2, t3 = small("t1"), small("t2"), small("t3")
        V = "dve"

        def tt(out, a, b, op_, eng=V):
            cx.op(eng, lambda e: e.tensor_tensor(out=out.ap, in0=a.ap, in1=b.ap, op=op_), reads=[a, b], writes=[out])

        def tsc(out, a, s1, op0, s2=None, op1=None, eng=V):
            if op1 is None:
                cx.op(eng, lambda e: e.tensor_scalar(out=out.ap, in0=a.ap, scalar1=s1, scalar2=None, op0=op0), reads=[a], writes=[out])
            else:
                cx.op(eng, lambda e: e.tensor_scalar(out=out.ap, in0=a.ap, scalar1=s1, scalar2=s2, op0=op0, op1=op1), reads=[a], writes=[out])

        def act(out, a, fn, scale=1.0, bias=None):
            if bias is None:
                cx.op("act", lambda e: e.activation(out=out.ap, in_=a.ap, func=fn, scale=scale), reads=[a], writes=[out])
            else:
                cx.op("act", lambda e: e.activation(out=out.ap, in_=a.ap, func=fn, scale=scale, bias=bias.ap), reads=[a, bias], writes=[out])

        zero_c = sc.buf([N, 1], F32, "zero_c")
        cx.op("dve", lambda e: e.memset(zero_c.ap, 0.0), writes=[zero_c])

        def sin_of(out, angle_buf, shift):
            tsc(t1, angle_buf, 1.0 / TWO_PI, ALU.mult, (shift / TWO_PI) + MAGIC, ALU.add)
            tsc(t1, t1, -MAGIC, ALU.add)
            cx.op(V, lambda e: e.scalar_tensor_tensor(out=t2.ap, in0=t1.ap, scalar=-TWO_PI, in1=angle_buf.ap, op0=ALU.mult, op1=ALU.add),
                  reads=[t1, angle_buf], writes=[t2])
            tsc(t2, t2, float(shift), ALU.add, math.pi - 1e-6, ALU.min)
            tsc(t2, t2, -(math.pi - 1e-6), ALU.max)
            act(out, t2, AF.Sin, bias=zero_c)

        em1, xr_, w_, sn, cm1, nn = small("em1"), small("xr_"), small("w_"), small("sn"), small("cm1"), small("nn")

        def nested(out, var, divs, sign):
            cx.op(V, lambda e: e.memset(out.ap, 1.0), writes=[out])
            for dv in divs:
                tt(t3, out, var, ALU.mult)
                tsc(out, t3, sign / dv, ALU.mult, 1.0, ALU.add)
        ld8 = small("ld8")
        tsc(ld8, dt, 0.125, ALU.mult)
        nested(dt, ld8, [float(k) for k in range(12, 0, -1)], 1.0)
        for _ in range(3):
            tt(dt, dt, dt, ALU.mult)
        tt(ar, are, dt, ALU.mult)
        tt(ang, aim, dt, ALU.mult)
        nested(nn, ar, [9.0, 8.0, 7.0, 6.0, 5.0, 4.0, 3.0, 2.0], 1.0)
        tt(em1, nn, ar, ALU.mult)
        tsc(mag, em1, 1.0, ALU.add)
        C1 = 6.28125
        C2 = TWO_PI - C1
        tsc(t1, ang, 1.0 / TWO_PI, ALU.mult, MAGIC, ALU.add)
        tsc(t1, t1, -MAGIC, ALU.add)
        cx.op(V, lambda e: e.scalar_tensor_tensor(out=xr_.ap, in0=t1.ap, scalar=-C1, in1=ang.ap, op0=ALU.mult, op1=ALU.add),
              reads=[t1, ang], writes=[xr_])
        cx.op(V, lambda e: e.scalar_tensor_tensor(out=xr_.ap, in0=t1.ap, scalar=-C2, in1=xr_.ap, op0=ALU.mult, op1=ALU.add),
              reads=[t1, xr_], writes=[xr_])
        tt(w_, xr_, xr_, ALU.mult)
        nested(nn, w_, [float((2 * k) * (2 * k + 1)) for k in range(10, 0, -1)], -1.0)
        tt(sn, nn, xr_, ALU.mult)
        nested(nn, w_, [float((2 * k + 1) * (2 * k + 2)) for k in range(10, 0, -1)], -1.0)
        tt(cm1, nn, w_, ALU.mult)
        tsc(cm1, cm1, -0.5, ALU.mult)
        lm1 = small("lm1")
        tt(t1, em1, cm1, ALU.mult)
        tt(t2, em1, cm1, ALU.add)
        tt(lm1, t1, t2, ALU.add)
        tsc(lbr, lm1, 1.0, ALU.add)
        tt(lbi, sn, mag, ALU.mult)
        den, qr, qi = small("den"), small("qr"), small("qi")
        tt(den, are, are, ALU.mult)
        tt(t1, aim, aim, ALU.mult)
        tt(den, den, t1, ALU.add)
        cx.op(V, lambda e: e.reciprocal(out=den.ap, in_=den.ap), reads=[den], writes=[den])
        tt(t1, lm1, are, ALU.mult)
        tt(t2, lbi, aim, ALU.mult)
        tt(qr, t1, t2, ALU.add)
        tt(qr, qr, den, ALU.mult)
        tt(t1, lbi, are, ALU.mult)
        tt(t2, lm1, aim, ALU.mult)
        tt(qi, t1, t2, ALU.subtract)
        tt(qi, qi, den, ALU.mult)
        Bbr = sc.buf([N, 32, 16], F32, "Bbr")
        Bbi = sc.buf([N, 32, 16], F32, "Bbi")
        tb1 = sc.buf([N, 32, 16], F32, "tb1")
        tb2 = sc.buf([N, 32, 16], F32, "tb2")

        def bc3(b):
            return b.ap.unsqueeze(2).to_broadcast([N, 32, 16])
        cmul("dve", "pool", (tb1, tb2), Bbr.ap, Bbi.ap, bc3(qr), bc3(qi), Br.ap, Bi.ap, [qr, qi, Br, Bi], Bbr, Bbi)
        Pr = sc.buf([N, 32, 9], F32, "Pr")
        Pi = sc.buf([N, 32, 9], F32, "Pi")
        Qr = sc.buf([N, 32, 8], F32, "Qr")
        Qi = sc.buf([N, 32, 8], F32, "Qi")
        ibr, ibi, im2 = small("ibr"), small("ibi"), small("im2")
        tt(im2, mag, mag, ALU.mult)
        cx.op(V, lambda e: e.reciprocal(out=im2.ap, in_=im2.ap), reads=[im2], writes=[im2])
        tt(ibr, lbr, im2, ALU.mult)
        tt(ibi, lbi, im2, ALU.mult)
        tsc(ibi, ibi, -1.0, ALU.mult)
        for (Xr, Xi, br_, bi_, n_) in ((Pr, Pi, lbr, lbi, 9), (Qr, Qi, ibr, ibi, 8)):
            cx.op(V, lambda e, Xr=Xr: e.memset(Xr.ap[:, :, 0:1], 1.0), writes=[Xr])
            cx.op(V, lambda e, Xi=Xi: e.memset(Xi.ap[:, :, 0:1], 0.0), writes=[Xi])
            for tau in range(1, n_):
                cmul("dve", "pool", (t1, t2), Xr.ap[:, :, tau], Xi.ap[:, :, tau], Xr.ap[:, :, tau - 1], Xi.ap[:, :, tau - 1],
                     br_.ap, bi_.ap, [Xr, Xi, br_, bi_], Xr, Xi)
        GH = 16
        big = [sc.buf([N, GH, 8, 16], F32, f"big{i}") for i in range(8)]
        Ar_, Ai_, Cmr, Cmin, T1, T2, A7r, A7i = big
        mtmp = [sc.buf([128, 128], F32, f"mtmp{i}") for i in range(2)]
        for gh in range(2):
            gs = slice(gh * GH, (gh + 1) * GH)

            def bq(b):
                return b.ap[:, gs, 0:8].unsqueeze(3).to_broadcast([N, GH, 8, 16])

            def bb(b):
                return b.ap[:, gs, :].unsqueeze(2).to_broadcast([N, GH, 8, 16])

            def b7(b, idx):
                return b.ap[:, gs, idx:idx + 1].unsqueeze(3).to_broadcast([N, GH, 8, 16])
            cmul("dve", "pool", (T1, T2), Ar_.ap, Ai_.ap, bq(Qr), bq(Qi), bb(Bbr), bb(Bbi), [Qr, Qi, Bbr, Bbi], Ar_, Ai_)
            cmul("dve", "pool", (T1, T2), Cmr.ap, Cmin.ap, bq(Pr), bq(Pi), bb(Cr), bb(Ci), [Pr, Pi, Cr, Ci], Cmr, Cmin, negi=True)
            cmul("dve", "pool", (T1, T2), A7r.ap, A7i.ap, b7(Pr, 7), b7(Pi, 7), Ar_.ap, Ai_.ap, [Pr, Pi, Ar_, Ai_], A7r, A7i)
            for src, dst in ((A7r, W1r), (A7i, W1i)):
                for g8 in range(2):
                    ps = PS[g8 % 4]

                    def mm(e, ps=ps, src=src, g8=g8):
                        for gi in range(8):
                            i = e.matmul(ps.ap[:, gi * 64:(gi + 1) * 64], lhsT=src.ap[:, g8 * 8 + gi].rearrange("n t c -> n (t c)"),
                                         rhs=identf.ap[0:64, 0:64], start=True, stop=True)
                        return i
                    cx.op("pe", mm, reads=[src, identf], writes=[ps])
                    g0 = gh * GH + g8 * 8
                    cx.op("act", lambda e, ps=ps, dst=dst, g0=g0: e.activation(
                        out=dst.ap[:, g0:g0 + 8, :], in_=ps.ap.rearrange("p (g n) -> p g n", n=64), func=AF.Identity),
                        reads=[ps], writes=[dst])
            for g4 in range(4):
                ps = PS[g4 % 4]

                def mm(e, ps=ps, g4=g4):
                    for gi in range(4):
                        gl_ = g4 * 4 + gi
                        e.matmul(ps.ap[:, gi * 128:(gi + 1) * 128], lhsT=Ar_.ap[:, gl_].rearrange("n t c -> n (t c)"),
                                 rhs=Cmr.ap[:, gl_].rearrange("n t c -> n (t c)"), start=True, stop=False)
                        i = e.matmul(ps.ap[:, gi * 128:(gi + 1) * 128], lhsT=Ai_.ap[:, gl_].rearrange("n t c -> n (t c)"),
                                     rhs=Cmin.ap[:, gl_].rearrange("n t c -> n (t c)"), start=False, stop=True)
                    return i
                cx.op("pe", mm, reads=[Ar_, Ai_, Cmr, Cmin], writes=[ps])
                for gi in range(4):
                    g = gh * GH + g4 * 4 + gi
                    mt = mtmp[g % 2]
                    cx.op("dve", lambda e, ps=ps, gi=gi, mt=mt: e.tensor_tensor(out=mt.ap, in0=ps.ap[:, gi * 128:(gi + 1) * 128], in1=mask8.ap,
                                                                                op=ALU.mult), reads=[ps, mask8], writes=[mt])
                    cx.op("dve", lambda e, g=g, mt=mt: e.scalar_tensor_tensor(out=Mw.ap[:, g, :], in0=identf.ap, scalar=dcol.ap[:, g:g + 1],
                                                                              in1=mt.ap, op0=ALU.mult, op1=ALU.add),
                          reads=[identf, dcol, mt], writes=[Mw])
            W2r_f, W2i_f = A7r, A7i
            cx.op("dve", lambda e: e.tensor_tensor(out=T1.ap, in0=b7(Pr, 1), in1=Cmr.ap, op=ALU.mult), reads=[Pr, Cmr], writes=[T1])
            cx.op("pool", lambda e: e.tensor_tensor(out=T2.ap, in0=b7(Pi, 1), in1=Cmin.ap, op=ALU.mult), reads=[Pi, Cmin], writes=[T2])
            cx.op("dve", lambda e: e.tensor_tensor(out=W2r_f.ap, in0=T1.ap, in1=T2.ap, op=ALU.add), reads=[T1, T2], writes=[W2r_f])
            cx.op("dve", lambda e: e.tensor_tensor(out=T1.ap, in0=b7(Pr, 1), in1=Cmin.ap, op=ALU.mult), reads=[Pr, Cmin, W2r_f], writes=[T1])
            cx.op("pool", lambda e: e.tensor_tensor(out=T2.ap, in0=b7(Pi, 1), in1=Cmr.ap, op=ALU.mult), reads=[Pi, Cmr, W2r_f], writes=[T2])
            cx.op("dve", lambda e: e.tensor_tensor(out=W2i_f.ap, in0=T1.ap, in1=T2.ap, op=ALU.subtract), reads=[T1, T2], writes=[W2i_f])
            for src, dst in ((W2r_f, W2r), (W2i_f, W2i)):
                v = src.ap.rearrange("n (q two) t c -> n q two (t c)", two=2)
                q0 = gh * (GH // 2)
                cx.op("act", lambda e, v=v, dst=dst, q0=q0: e.activation(out=dst.ap[0:64, q0:q0 + GH // 2, :], in_=v[:, :, 0, :], func=AF.Identity),
                      reads=[src], writes=[dst])
                cx.op("act", lambda e, v=v, dst=dst, q0=q0: e.activation(out=dst.ap[64:128, q0:q0 + GH // 2, :], in_=v[:, :, 1, :], func=AF.Identity),
                      reads=[src], writes=[dst])
        r8, e8, p8r, p8i = small("r8"), small("e8"), small("p8r"), small("p8i")
        tt(r8, mag, mag, ALU.mult)
        tt(r8, r8, r8, ALU.mult)
        tt(r8, r8, r8, ALU.mult)
        cx.op(V, lambda e: e.reciprocal(out=e8.ap, in_=r8.ap), reads=[r8], writes=[e8])
        cx.op(V, lambda e: e.tensor_tensor(out=p8r.ap, in0=Pr.ap[:, :, 8], in1=e8.ap, op=ALU.mult), reads=[Pr, e8], writes=[p8r])
        cx.op(V, lambda e: e.tensor_tensor(out=p8i.ap, in0=Pi.ap[:, :, 8], in1=e8.ap, op=ALU.mult), reads=[Pi, e8], writes=[p8i])
        for src, dst in ((r8, rho), (p8r, ph_r), (p8i, ph_i)):
            v = src.ap.rearrange("n (q two) -> n q two", two=2)
            cx.op("act", lambda e, v=v, dst=dst: e.activation(out=dst.ap[0:64], in_=v[:, :, 0], func=AF.Identity), reads=[src], writes=[dst])
            cx.op("act", lambda e, v=v, dst=dst: e.activation(out=dst.ap[64:128], in_=v[:, :, 1], func=AF.Identity), reads=[src], writes=[dst])
        cx.barrier(skip_sw=True)
        sc.close()
        if stop_after == ("s5prep", l):
            ws.close()
            return
        sc = Scope(cx)
        gb = sc.buf([128, 4, L], BF16, "gb")
        SelS = sc.buf([128, 8, 8, 128], BF16, "SelS")
        SelTS = sc.buf([128, 8, 8, 128], BF16, "SelTS")
        cx.dma("sp", SelS.ap, SELD[0], reads=[B_SELD], writes=[SelS])
        cx.dma("sp", SelTS.ap, SELD[1], reads=[B_SELD], writes=[SelTS])
        NQ = 4
        s2 = Scope(cx)
        uT_ = [s2.buf([128, L], BF16, f"uT{i}") for i in range(2)]
        U3_ = [s2.buf([128, 8, TB], BF16, f"U3{i}") for i in range(2)]
        E1r = s2.buf([128, NQ, TB], F32, "E1r")
        E1i = s2.buf([128, NQ, TB], F32, "E1i")
        Hr = s2.buf([128, NQ, TB], BF16, "Hr")
        Hi = s2.buf([128, NQ, TB], BF16, "Hi")
        Y3 = s2.buf([128, 8, TB], BF16, "Y3")
        dd = [s2.buf([128, NQ, 256], F32, f"dd{i}") for i in range(4)]
        tmp2 = [[s2.buf([128, TB], F32, f"s5t{k}_{i}") for i in range(8)] for k in range(2)]
        for qt in range(4):
            uT = uT_[qt % 2]
            U3 = U3_[qt % 2]
            cx.dma("sp", uT.ap, UT[qt * 128:(qt + 1) * 128, :], reads=[B_UT], writes=[uT])
            q0 = qt * NQ
            cx.op("dve", lambda e: e.tensor_copy(out=E1r.ap[:, :, 0], in_=ph_r.ap[:, q0:q0 + NQ]), reads=[ph_r], writes=[E1r])
            cx.op("dve", lambda e: e.tensor_copy(out=E1i.ap[:, :, 0], in_=ph_i.ap[:, q0:q0 + NQ]), reads=[ph_i], writes=[E1i])
            m = 1
            while m < TB:
                def bcm(b, m=m):
                    return b.ap[:, :, m - 1:m].to_broadcast([128, NQ, m])
                cx.op("dve", lambda e, m=m: e.tensor_tensor(out=dd[0].ap[:, :, 0:m], in0=E1r.ap[:, :, 0:m], in1=bcm(E1r), op=ALU.mult),
                      reads=[E1r], writes=[dd[0]])
                cx.op("pool", lambda e, m=m: e.tensor_tensor(out=dd[1].ap[:, :, 0:m], in0=E1i.ap[:, :, 0:m], in1=bcm(E1i), op=ALU.mult),
                      reads=[E1i], writes=[dd[1]])
                cx.op("dve", lambda e, m=m: e.tensor_tensor(out=dd[2].ap[:, :, 0:m], in0=E1r.ap[:, :, 0:m], in1=bcm(E1i), op=ALU.mult),
                      reads=[E1r, E1i], writes=[dd[2]])
                cx.op("pool", lambda e, m=m: e.tensor_tensor(out=dd[3].ap[:, :, 0:m], in0=E1i.ap[:, :, 0:m], in1=bcm(E1r), op=ALU.mult),
                      reads=[E1i, E1r], writes=[dd[3]])
                cx.op("dve", lambda e, m=m: e.tensor_tensor(out=E1r.ap[:, :, m:2 * m], in0=dd[0].ap[:, :, 0:m], in1=dd[1].ap[:, :, 0:m], op=ALU.subtract),
                      reads=[dd[0], dd[1]], writes=[E1r])
                cx.op("dve", lambda e, m=m: e.tensor_tensor(out=E1i.ap[:, :, m:2 * m], in0=dd[2].ap[:, :, 0:m], in1=dd[3].ap[:, :, 0:m], op=ALU.add),
                      reads=[dd[2], dd[3]], writes=[E1i])
                m *= 2
            npsu = 0
            for gl_ in range(8):
                ps = PS[npsu % 4]
                npsu += 1

                def mm(e, ps=ps, gl_=gl_):
                    for tau in range(8):
                        i = e.matmul(ps.ap, lhsT=SelS.ap[:, gl_, tau, :], rhs=uT.ap[:, tau::8], start=(tau == 0), stop=(tau == 7))
                    return i
                cx.op("pe", mm, reads=[SelS, uT], writes=[ps])
                cx.op("act", lambda e, ps=ps, gl_=gl_: e.activation(out=U3.ap[:, gl_, :], in_=ps.ap, func=AF.Identity), reads=[ps], writes=[U3])
            for ql in range(NQ):
                q_ = qt * NQ + ql
                ga, gb_ = 2 * ql, 2 * ql + 1
                pre, pim = PS[4 + (ql % 2) * 2], PS[5 + (ql % 2) * 2]

                def mm(e, pre=pre, pim=pim, ga=ga, gb_=gb_, qt=qt):
                    e.matmul(pre.ap[0:64, :], lhsT=W1r.ap[:, qt * 8 + ga, :], rhs=U3.ap[:, ga, :], start=True, stop=True)
                    e.matmul(pre.ap[64:128, :], lhsT=W1r.ap[:, qt * 8 + gb_, :], rhs=U3.ap[:, gb_, :], start=True, stop=True)
                    e.matmul(pim.ap[0:64, :], lhsT=W1i.ap[:, qt * 8 + ga, :], rhs=U3.ap[:, ga, :], start=True, stop=True)
                    return e.matmul(pim.ap[64:128, :], lhsT=W1i.ap[:, qt * 8 + gb_, :], rhs=U3.ap[:, gb_, :], start=True, stop=True)
                cx.op("pe", mm, reads=[W1r, W1i, U3], writes=[pre, pim])
                xr, xi, a1, a2, vr, vi, sr, si = tmp2[ql % 2]
                cx.op("act", lambda e, pre=pre: e.activation(out=xr.ap, in_=pre.ap, func=AF.Identity), reads=[pre], writes=[xr])
                cx.op("act", lambda e, pim=pim: e.activation(out=xi.ap, in_=pim.ap, func=AF.Identity), reads=[pim], writes=[xi])
                er, ei = E1r.ap[:, ql, :], E1i.ap[:, ql, :]
                cx.op("dve", lambda e, er=er: e.tensor_tensor(out=a1.ap, in0=xr.ap, in1=er, op=ALU.mult), reads=[xr, E1r], writes=[a1])
                cx.op("pool", lambda e, ei=ei: e.tensor_tensor(out=a2.ap, in0=xi.ap, in1=ei, op=ALU.mult), reads=[xi, E1i], writes=[a2])
                cx.op("dve", lambda e: e.tensor_tensor(out=vr.ap, in0=a1.ap, in1=a2.ap, op=ALU.add), reads=[a1, a2], writes=[vr])
                cx.op("dve", lambda e, er=er: e.tensor_tensor(out=a1.ap, in0=xi.ap, in1=er, op=ALU.mult), reads=[xi, E1r], writes=[a1])
                cx.op("pool", lambda e, ei=ei: e.tensor_tensor(out=a2.ap, in0=xr.ap, in1=ei, op=ALU.mult), reads=[xr, E1i], writes=[a2])
                cx.op("dve", lambda e: e.tensor_tensor(out=vi.ap, in0=a1.ap, in1=a2.ap, op=ALU.subtract), reads=[a1, a2], writes=[vi])
                rc = rho.ap[:, q_:q_ + 1].to_broadcast([128, TB])
                cx.op("dve", lambda e, rc=rc: e.tensor_tensor_scan(out=sr.ap, data0=rc, data1=vr.ap, initial=0.0, op0=ALU.mult, op1=ALU.add),
                      reads=[rho, vr], writes=[sr])
                cx.op("dve", lambda e, rc=rc: e.tensor_tensor_scan(out=si.ap, data0=rc, data1=vi.ap, initial=0.0, op0=ALU.mult, op1=ALU.add),
                      reads=[rho, vi], writes=[si])
                cx.op("dve", lambda e, er=er: e.tensor_tensor(out=a1.ap, in0=sr.ap, in1=er, op=ALU.mult), reads=[sr, E1r], writes=[a1])
                cx.op("pool", lambda e, ei=ei: e.tensor_tensor(out=a2.ap, in0=si.ap, in1=ei, op=ALU.mult), reads=[si, E1i], writes=[a2])
                cx.op("pool", lambda e, ql=ql: e.memset(Hr.ap[:, ql, 0:1], 0.0), writes=[Hr])
                cx.op("pool", lambda e, ql=ql: e.memset(Hi.ap[:, ql, 0:1], 0.0), writes=[Hi])
                cx.op("dve", lambda e, ql=ql: e.tensor_tensor(out=Hr.ap[:, ql, 1:TB], in0=a1.ap[:, 0:TB - 1], in1=a2.ap[:, 0:TB - 1], op=ALU.subtract),
                      reads=[a1, a2], writes=[Hr])
                cx.op("dve", lambda e, ei=ei: e.tensor_tensor(out=a1.ap, in0=sr.ap, in1=ei, op=ALU.mult), reads=[sr, E1i], writes=[a1])
                cx.op("pool", lambda e, er=er: e.tensor_tensor(out=a2.ap, in0=si.ap, in1=er, op=ALU.mult), reads=[si, E1r], writes=[a2])
                cx.op("dve", lambda e, ql=ql: e.tensor_tensor(out=Hi.ap[:, ql, 1:TB], in0=a1.ap[:, 0:TB - 1], in1=a2.ap[:, 0:TB - 1], op=ALU.add),
                      reads=[a1, a2], writes=[Hi])
            for gl_ in range(8):
                g = qt * 8 + gl_
                ql, half = gl_ // 2, gl_ % 2
                q_ = qt * NQ + ql
                ps = PS[npsu % 4]
                npsu += 1
                lo, hi = half * 64, half * 64 + 64

                def mm(e, ps=ps, g=g, gl_=gl_, ql=ql, q_=q_, lo=lo, hi=hi):
                    e.matmul(ps.ap, lhsT=Mw.ap[:, g, :], rhs=U3.ap[:, gl_, :], start=True, stop=False)
                    e.matmul(ps.ap, lhsT=W2r.ap[lo:hi, q_, :], rhs=Hr.ap[lo:hi, ql, :], start=False, stop=False)
                    return e.matmul(ps.ap, lhsT=W2i.ap[lo:hi, q_, :], rhs=Hi.ap[lo:hi, ql, :], start=False, stop=True)
                cx.op("pe", mm, reads=[Mw, U3, W2r, W2i, Hr, Hi], writes=[ps])
                cx.op("act", lambda e, ps=ps, gl_=gl_: e.activation(out=Y3.ap[:, gl_, :], in_=ps.ap, func=AF.Identity), reads=[ps], writes=[Y3])
            for tau in range(8):
                ps = PS[npsu % 4]
                npsu += 1

                def mm(e, ps=ps, tau=tau):
                    for gg in range(8):
                        i = e.matmul(ps.ap, lhsT=SelTS.ap[:, gg, tau, :], rhs=Y3.ap[:, gg, :], start=(gg == 0), stop=(gg == 7))
                    return i
                cx.op("pe", mm, reads=[SelTS, Y3], writes=[ps])
                cx.op("act", lambda e, ps=ps, qt=qt, tau=tau: e.activation(out=gb.ap[:, qt, tau::8], in_=ps.ap, func=AF.Gelu_apprx_tanh),
                      reads=[ps], writes=[gb])
        cx.barrier(skip_sw=True)
        s2.close()
        wgl = sc.buf([128, 4, 512], BF16, "wgl")
        cx.dma("sp", wgl.ap, wb_glu[l].rearrange("(kc p) n -> p kc n", p=128), reads=B_wb[("glu", l)], writes=[wgl])
        sg = [sc.buf([128, TB], F32, f"sg{i}") for i in range(2)]
        yst = [sc.buf([128, 4, TB], BF16, f"yst{i}") for i in range(2)]
        n = 0
        for t in range(NB):
            ys_ = yst[t % 2]
            for ct in range(4):
                ps = PS[n % 4]
                s_ = sg[n % 2]
                n += 1

                def mm(e, ps=ps, ct=ct, t=t):
                    for kc in range(4):
                        i = e.matmul(ps.ap, lhsT=wgl.ap[:, kc, ct * 128:(ct + 1) * 128], rhs=gb.ap[:, kc, blk(t)], start=(kc == 0), stop=(kc == 3))
                    return i
                cx.op("pe", mm, reads=[wgl, gb], writes=[ps])
                cx.op("act", lambda e, ps=ps, s_=s_, ct=ct: e.activation(out=s_.ap, in_=ps.ap, func=AF.Sigmoid, bias=bglu.ap[:, ct:ct + 1], scale=1.0),
                      reads=[ps, bglu], writes=[s_])
                cx.op("dve", lambda e, s_=s_, ys_=ys_, ct=ct, t=t: e.tensor_tensor(out=ys_.ap[:, ct, :], in0=gb.ap[:, ct, blk(t)], in1=s_.ap, op=ALU.mult),
                      reads=[gb, s_], writes=[ys_])
            cx.dma("sp", YS[0:512, blk(t)].rearrange("(c p) t -> p c t", p=128), ys_.ap, reads=[ys_], writes=[B_YS[0]])
        cx.barrier(skip_sw=True)
        sc.close()
        ws.close()

    def run_interleaved(gens):
        active = [g for g in gens if g is not None]
        while active:
            for g in list(active):
                try:
                    next(g)
                except StopIteration:
                    active.remove(g)

    def layer_norm_gen(y, gcol, bcol, outb, tmp, stat, sbf):
        pm, pq = PS[6], PS[7]
        ybf, ysq = sbf
        for h2 in range(2):
            sl_ = slice(h2 * 4, (h2 + 1) * 4)
            cx.op("act", lambda e: e.activation(out=ysq.ap[:, sl_, :], in_=y.ap[:, sl_, :], func=AF.Square), reads=[y], writes=[ysq])
            yield
            cx.op("act", lambda e: e.activation(out=ybf.ap[:, sl_, :], in_=y.ap[:, sl_, :], func=AF.Identity), reads=[y], writes=[ybf])
            yield

        def mm1(e):
            for kc in range(8):
                i = e.matmul(pm.ap, lhsT=onesD.ap, rhs=ybf.ap[:, kc, :], start=(kc == 0), stop=(kc == 7))
            return i

        def mm2(e):
            for kc in range(8):
                i = e.matmul(pq.ap, lhsT=onesD.ap, rhs=ysq.ap[:, kc, :], start=(kc == 0), stop=(kc == 7))
            return i
        cx.op("pe", mm1, reads=[onesD, ybf], writes=[pm])
        yield
        cx.op("pe", mm2, reads=[onesD, ysq], writes=[pq])
        yield
        mean, rstd = stat
        cx.op("act", lambda e: e.activation(out=mean.ap, in_=pm.ap, func=AF.Identity), reads=[pm], writes=[mean])
        cx.op("act", lambda e: e.activation(out=rstd.ap, in_=pm.ap, func=AF.Square), reads=[pm], writes=[rstd])
        yield
        cx.op("dve", lambda e: e.tensor_tensor(out=rstd.ap, in0=pq.ap, in1=rstd.ap, op=ALU.subtract), reads=[pq, rstd], writes=[rstd])
        cx.op("dve", lambda e: e.tensor_scalar(out=rstd.ap, in0=rstd.ap, scalar1=0.0, scalar2=LN_EPS, op0=ALU.max, op1=ALU.add),
              reads=[rstd], writes=[rstd])
        yield
        cx.op("act", lambda e: e.activation(out=rstd.ap, in_=rstd.ap, func=AF.Sqrt), reads=[rstd], writes=[rstd])
        cx.op("dve", lambda e: e.reciprocal(out=rstd.ap, in_=rstd.ap), reads=[rstd], writes=[rstd])
        yield
        for h2 in range(2):
            sl_ = slice(h2 * 4, (h2 + 1) * 4)
            mb = mean.ap.unsqueeze(1).to_broadcast([128, 4, TB])
            rb_ = rstd.ap.unsqueeze(1).to_broadcast([128, 4, TB])
            cx.op("dve", lambda e: e.tensor_tensor(out=tmp.ap[:, sl_, :], in0=y.ap[:, sl_, :], in1=mb, op=ALU.subtract), reads=[y, mean], writes=[tmp])
            yield
            cx.op("pool", lambda e: e.tensor_tensor(out=tmp.ap[:, sl_, :], in0=tmp.ap[:, sl_, :], in1=rb_, op=ALU.mult), reads=[tmp, rstd], writes=[tmp])
            yield
        for kc in range(8):
            cx.op("dve", lambda e, kc=kc: e.tensor_scalar(out=y.ap[:, kc, :], in0=tmp.ap[:, kc, :], scalar1=gcol.ap[:, kc:kc + 1],
                                                          scalar2=bcol.ap[:, kc:kc + 1], op0=ALU.mult, op1=ALU.add),
                  reads=[tmp, gcol, bcol], writes=[y])
            yield
        for h2 in range(2):
            sl_ = slice(h2 * 4, (h2 + 1) * 4)
            cx.op("act", lambda e: e.activation(out=outb.ap[:, sl_, :], in_=y.ap[:, sl_, :], func=AF.Identity), reads=[y], writes=[outb])
            yield

    def load_cols(sc, src, l, name, n=8):
        b = sc.buf([128, n], F32, name)
        cx.dma("sp", b.ap, src[l].rearrange("(c p) -> p c", p=128), writes=[b], allow_slow_non_contiguous=True)
        return b

    def phase_mix(l):
        sc = Scope(cx)
        wbr = sc.buf([128, 12, D], BF16, "wbr")
        cx.dma("sp", wbr.ap, wb_branch[l].rearrange("(j p) n -> p j n", p=128), reads=B_wb[("branch", l)], writes=[wbr])
        wgd = [sc.buf([128, 8, 3, 128], BF16, f"wgd{i}") for i in range(2)]
        wgsrc = kview(wb_in[l])[:, :, 3080:6152].rearrange("p kc (k3 dc j) -> p kc k3 dc j", k3=3, dc=8)
        wo = sc.buf([128, 8, D], BF16, "wo")
        cx.dma("sp", wo.ap, kview(wb_out[l]), reads=B_wb[("out", l)], writes=[wo])
        bg = load_cols(sc, b_gate, l, "bg", 24)
        g1 = load_cols(sc, ln1_g, l, "g1")
        b1 = load_cols(sc, ln1_b, l, "b1")
        xb = sc.buf([128, 8, TB], BF16, "mxb")
        xr = [sc.buf([128, TB], F32, f"mxr{i}") for i in range(2)]
        ys = sc.buf([128, 12, TB], BF16, "mys")
        mixb = sc.buf([128, 8, TB], BF16, "mixb")
        yvs = [sc.buf([128, 8, TB], F32, f"yv{i}") for i in range(2)]
        o16 = sc.buf([128, 8, TB], BF16, "mo16")
        tmp = sc.buf([128, 8, TB], F32, "lntmp")
        sbf = (sc.buf([128, 8, TB], BF16, "lnybf"), sc.buf([128, 8, TB], BF16, "lnysq"))
        stat = (sc.buf([128, TB], F32, "mean"), sc.buf([128, TB], F32, "rstd"))
        gsb = [sc.buf([128, TB], F32, f"gsb{i}") for i in range(3)]
        acc = [sc.buf([128, TB], F32, f"acc{i}") for i in range(2)]
        xres_src = xT if l == 0 else XRES
        st = {"n": 0, "nr": 0, "nw": 0}

        def genA(t):
            x_, y_ = xb, ys
            yv = yvs[t % 2]
            cx.dma("sp", x_.ap, kview(XBF)[:, :, blk(t)], reads=[B_XBF[t]], writes=[x_])
            cx.dma("sp", y_.ap, YS.rearrange("(j p) t -> p j t", p=128)[:, :, blk(t)], reads=B_YS, writes=[y_])
            for dc in range(8):
                a_ = acc[dc % 2]
                wg_ = wgd[st["nw"] % 2]
                st["nw"] += 1
                for k3_ in range(3):
                    cx.dma("sp", wg_.ap[:, :, k3_, :], wgsrc[:, :, k3_, dc, :], reads=B_wb[("in", l)], writes=[wg_])
                for k3 in range(3):
                    n = st["n"]
                    st["n"] += 1
                    pp, pg = PS[(2 * n) % 6], PS[(2 * n + 1) % 6]
                    g_ = gsb[n % 3]

                    def mmp(e, pp=pp, k3=k3, dc=dc, y_=y_):
                        for kc in range(4):
                            i = e.matmul(pp.ap, lhsT=wbr.ap[:, k3 * 4 + kc, dc * 128:(dc + 1) * 128], rhs=y_.ap[:, k3 * 4 + kc, :],
                                         start=(kc == 0), stop=(kc == 3))
                        return i

                    def mmg(e, pg=pg, k3=k3, x_=x_, wg_=wg_):
                        for kc in range(8):
                            i = e.matmul(pg.ap, lhsT=wg_.ap[:, kc, k3, :], rhs=x_.ap[:, kc, :], start=(kc == 0), stop=(kc == 7))
                        return i
                    cx.op("pe", mmg, reads=[wg_, x_], writes=[pg])
                    cx.op("pe", mmp, reads=[wbr, y_], writes=[pp])
                    cx.op("act", lambda e, pg=pg, g_=g_, k3=k3, dc=dc: e.activation(out=g_.ap, in_=pg.ap, func=AF.Sigmoid,
                                                                                   bias=bg.ap[:, k3 * 8 + dc:k3 * 8 + dc + 1], scale=1.0),
                          reads=[pg, bg], writes=[g_])
                    if k3 == 0:
                        cx.op("dve", lambda e, pp=pp, g_=g_, a_=a_: e.tensor_tensor(out=a_.ap, in0=pp.ap, in1=g_.ap, op=ALU.mult),
                              reads=[pp, g_], writes=[a_])
                    else:
                        cx.op("dve", lambda e, pp=pp, g_=g_: e.tensor_tensor(out=g_.ap, in0=pp.ap, in1=g_.ap, op=ALU.mult),
                              reads=[pp, g_], writes=[g_])
                        if k3 == 1:
                            cx.op("pool", lambda e, g_=g_, a_=a_: e.tensor_tensor(out=a_.ap, in0=a_.ap, in1=g_.ap, op=ALU.add),
                                  reads=[a_, g_], writes=[a_])
                        else:
                            cx.op("pool", lambda e, g_=g_, a_=a_, dc=dc: e.tensor_tensor(out=mixb.ap[:, dc, :], in0=a_.ap, in1=g_.ap, op=ALU.add),
                                  reads=[a_, g_], writes=[mixb])
                    yield
            for dc in range(8):
                po = PS[6 + dc % 2]
                r_ = xr[st["nr"] % 2]
                st["nr"] += 1
                cx.dma("sp", r_.ap, xres_src[dc * 128:(dc + 1) * 128, blk(t)], reads=[B_XRES[t]], writes=[r_])

                def mmo(e, po=po, dc=dc):
                    for kc in range(8):
                        i = e.matmul(po.ap, lhsT=wo.ap[:, kc, dc * 128:(dc + 1) * 128], rhs=mixb.ap[:, kc, :], start=(kc == 0), stop=(kc == 7))
                    return i
                cx.op("pe", mmo, reads=[wo, mixb], writes=[po])
                cx.op("dve", lambda e, po=po, dc=dc, r_=r_: e.scalar_tensor_tensor(out=yv.ap[:, dc, :], in0=r_.ap, scalar=float(ALPHA),
                                                                                 in1=po.ap, op0=ALU.mult, op1=ALU.add),
                      reads=[r_, po], writes=[yv])
                yield

        def genB(t):
            yv = yvs[t % 2]
            yield from layer_norm_gen(yv, g1, b1, o16, tmp, stat, sbf)
            cx.dma("sp", kview(X1RES)[:, :, blk(t)], yv.ap, reads=[yv], writes=[B_X1RES[t]])
            cx.dma("sp", kview(X1BF)[:, :, blk(t)], o16.ap, reads=[o16], writes=[B_X1BF[t]])
            yield
        run_interleaved([genA(0)])
        for t in range(NB):
            run_interleaved([genA(t + 1) if t + 1 < NB else None, genB(t)])
        cx.barrier(skip_sw=True)
        sc.close()

    def phase_ffn(l, last):
        sc = Scope(cx)
        wdn = sc.buf([128, NHC, D], BF16, "wdn")
        cx.dma("sp", wdn.ap, wb_d[l].rearrange("(j p) n -> p j n", p=128), reads=B_wb[("d", l)], writes=[wdn])
        g2 = load_cols(sc, ln2_g, l, "g2")
        b2 = load_cols(sc, ln2_b, l, "b2")
        xb = sc.buf([128, 8, TB], BF16, "fxb")
        hT = sc.buf([128, NHC, TB], BF16, "hT")
        wgu = [sc.buf([128, 2, 8, 256], BF16, f"wgu{i}") for i in range(2)]
        sl = [sc.buf([128, TB], F32, f"sl{i}") for i in range(2)]
        xr = [sc.buf([128, TB], F32, f"fxr{i}") for i in range(2)]
        yvs = [sc.buf([128, 8, TB], F32, f"fyv{i}") for i in range(2)]
        tmp = sc.buf([128, 8, TB], F32, "flntmp")
        sbf = (sc.buf([128, 8, TB], BF16, "flnybf"), sc.buf([128, 8, TB], BF16, "flnysq"))
        o16 = sc.buf([128, 8, TB], BF16, "fo16")
        stat = (sc.buf([128, TB], F32, "fmean"), sc.buf([128, TB], F32, "frstd"))
        st = {"n": 0, "nr": 0, "nw": 0}

        def genA(t):
            yv = yvs[t % 2]
            cx.dma("sp", xb.ap, kview(X1BF)[:, :, blk(t)], reads=[B_X1BF[t]], writes=[xb])
            for hp in range(NHC // 2):
                w = wgu[st["nw"] % 2]
                st["nw"] += 1
                cx.dma("sp", w.ap[:, 0], kview(wb_g[l])[:, :, hp * 256:(hp + 1) * 256], reads=B_wb[("g", l)], writes=[w])
                cx.dma("sp", w.ap[:, 1], kview(wb_u[l])[:, :, hp * 256:(hp + 1) * 256], reads=B_wb[("u", l)], writes=[w])
                for hh in range(2):
                    hc = hp * 2 + hh
                    n = st["n"]
                    st["n"] += 1
                    pg, pu = PS[(2 * n) % 6], PS[(2 * n + 1) % 6]
                    s_ = sl[n % 2]

                    def mmg(e, pg=pg, w=w, hh=hh):
                        for kc in range(8):
                            i = e.matmul(pg.ap, lhsT=w.ap[:, 0, kc, hh * 128:(hh + 1) * 128], rhs=xb.ap[:, kc, :], start=(kc == 0), stop=(kc == 7))
                        return i

                    def mmu(e, pu=pu, w=w, hh=hh):
                        for kc in range(8):
                            i = e.matmul(pu.ap, lhsT=w.ap[:, 1, kc, hh * 128:(hh + 1) * 128], rhs=xb.ap[:, kc, :], start=(kc == 0), stop=(kc == 7))
                        return i
                    cx.op("pe", mmg, reads=[w, xb], writes=[pg])
                    cx.op("pe", mmu, reads=[w, xb], writes=[pu])
                    cx.op("act", lambda e, pg=pg, s_=s_: e.activation(out=s_.ap, in_=pg.ap, func=AF.Silu), reads=[pg], writes=[s_])
                    cx.op("dve", lambda e, pu=pu, s_=s_, hc=hc: e.tensor_tensor(out=hT.ap[:, hc, :], in0=pu.ap, in1=s_.ap, op=ALU.mult),
                          reads=[pu, s_], writes=[hT])
                    yield
            for dc in range(8):
                po = PS[6 + dc % 2]
                r_ = xr[st["nr"] % 2]
                st["nr"] += 1
                cx.dma("sp", r_.ap, X1RES[dc * 128:(dc + 1) * 128, blk(t)], reads=[B_X1RES[t]], writes=[r_])

                def mmo(e, po=po, dc=dc):
                    for hc in range(NHC):
                        i = e.matmul(po.ap, lhsT=wdn.ap[:, hc, dc * 128:(dc + 1) * 128], rhs=hT.ap[:, hc, :], start=(hc == 0), stop=(hc == NHC - 1))
                    return i
                cx.op("pe", mmo, reads=[wdn, hT], writes=[po])
                cx.op("dve", lambda e, po=po, dc=dc, r_=r_: e.scalar_tensor_tensor(out=yv.ap[:, dc, :], in0=r_.ap, scalar=float(ALPHA),
                                                                                 in1=po.ap, op0=ALU.mult, op1=ALU.add),
                      reads=[r_, po], writes=[yv])
                yield

        def genB(t):
            yv = yvs[t % 2]
            yield from layer_norm_gen(yv, g2, b2, o16, tmp, stat, sbf)
            if last:
                cx.dma("sp", kview(outT)[:, :, blk(t)], yv.ap, reads=[yv], writes=[B_OUT])
            else:
                cx.dma("sp", kview(XRES)[:, :, blk(t)], yv.ap, reads=[yv], writes=[B_XRES[t]])
                cx.dma("sp", kview(XBF)[:, :, blk(t)], o16.ap, reads=[o16], writes=[B_XBF[t]])
            yield
        run_interleaved([genA(0)])
        for t in range(NB):
            run_interleaved([genA(t + 1) if t + 1 < NB else None, genB(t)])
        cx.barrier(skip_sw=True)
        sc.close()

    cx.barrier(skip_sw=True)
    for l in range(n_layers):
        if stop_after == ("setup", l):
            break
        phase_proj(l)
        if l + 1 < n_layers:
            convert_layer(l + 1)
        if stop_after == ("proj", l):
            break
        phase_attn(l)
        if stop_after == ("attn", l):
            break
        phase_lru(l)
        if stop_after == ("lru", l):
            break
        phase_s5(l)
        if stop_after in (("s5", l), ("s5prep", l)):
            break
        phase_mix(l)
        if stop_after == ("mix", l):
            break
        phase_ffn(l, last=(l == n_layers - 1))
    cx.barrier()
    return nc


INPUT_ORDER = ["w_in", "w_branch", "w_out", "w_ffn_gate", "w_ffn_up", "w_ffn_down", "s5_w_glu", "lru_w_a", "lru_w_x",
               "b_f", "b_gate", "s5_a_re", "s5_a_im", "s5_log_dt", "s5_b_re", "s5_b_im", "s5_c_re", "s5_c_im", "s5_d",
               "s5_b_glu", "lru_conv_w", "lru_conv_b", "lru_b_a", "lru_b_x", "lru_lambda", "ln1_g", "ln1_b", "ln2_g", "ln2_b"]


def layout_inputs(inputs, n_layers=DEPTH):
    f = lambda a: np.ascontiguousarray(np.asarray(a, dtype=np.float32)[:n_layers])
    shared = {}
    for k in INPUT_ORDER:
        a = f(inputs[k])
        if k == "w_branch":
            a = a.reshape(n_layers, 1536, D)
        elif k in ("s5_c_re", "s5_c_im"):
            a = a.reshape(n_layers, 512, 64)
        elif k in ("lru_b_a", "lru_b_x"):
            a = a.reshape(n_layers, 512)
        shared[k] = np.ascontiguousarray(a)
    return shared


def kernel(**inputs):
    x = np.asarray(inputs["x"], dtype=np.float32)
    shared = layout_inputs(inputs)
    nc = bass.Bass("TRN2", target_bir_lowering=False)
    build(nc)
    in_maps = []
    for c in range(8):
        m = dict(shared)
        m["xT"] = np.ascontiguousarray(x[c % 4].T)
        in_maps.append(m)
    res = run_bass_kernel_spmd(nc, in_maps, core_ids=list(range(8)))
    out = np.stack([np.ascontiguousarray(res.results[b]["outT"].T) for b in range(4)], axis=0)
    return out.astype(np.float32)
```

```python
import math
import numpy as np
import concourse.bass as bass
import concourse.mybir as mybir
from concourse.bass_utils import run_bass_kernel_spmd

F32 = mybir.dt.float32
BF16 = mybir.dt.bfloat16
AF = mybir.ActivationFunctionType
ALU = mybir.AluOpType

D = 1024
L = 4096
DEPTH = 4
NB = 8
TB = 512
IN_TOTAL = 6152
FFN = 2816
NHC = 22
ALPHA = (2.0 * DEPTH) ** 0.25
LN_EPS = 1e-5
MAGIC = 12582912.0
TWO_PI = 2.0 * math.pi


class Buf:
    __slots__ = ("ap", "w", "r", "name")

    def __init__(self, ap, name=""):
        self.ap = ap
        self.w = {}
        self.r = {}
        self.name = name


class Ctx:
    def __init__(self, nc):
        self.nc = nc
        self.E = {"pe": nc.tensor, "act": nc.scalar, "dve": nc.vector, "pool": nc.gpsimd, "sp": nc.sync}
        self.sem = {}
        self.cnt = {}
        self.nsem = 0
        for e in ("pe", "act", "dve", "pool"):
            self._new_sem(e)
        self.seen = {e: {} for e in self.E}
        self.dma_sems = {"sp": [nc.alloc_semaphore(f"dq{i}") for i in range(60)],
                         "pool": [nc.alloc_semaphore(f"dqs{i}") for i in range(16)]}
        self.dma_cnt = {k: [0] * len(v) for k, v in self.dma_sems.items()}
        self.dma_rr = {"sp": 0, "pool": 0}
        self.uid = 0

    def _new_sem(self, e):
        s = self.nc.alloc_semaphore(f"s_{e}_{self.nsem}")
        self.nsem += 1
        self.sem[e] = s
        self.cnt[e] = 0
        if not hasattr(self, "own"):
            self.own = {}
            self.semobj = {}
        self.own.setdefault(e, set()).add(self._key(s))

    def _key(self, s):
        k = id(s)
        self.semobj[k] = s
        return k

    def _wait(self, e, deps):
        seen = self.seen[e]
        own = self.own.get(e, ())
        for k, v in deps.items():
            if seen.get(k, 0) >= v:
                continue
            if k in own:
                cur = self.sem.get(e)
                if e == "pe" or cur is None or k != id(cur) or v <= self.cnt[e] - 1:
                    continue
            self.E[e].wait_ge(self.semobj[k], v)
            seen[k] = v

    @staticmethod
    def _merge(dst, src):
        for k, v in src.items():
            if dst.get(k, 0) < v:
                dst[k] = v

    def _deps(self, reads, writes):
        deps = {}
        for b in reads:
            self._merge(deps, b.w)
        for b in writes:
            self._merge(deps, b.w)
            self._merge(deps, b.r)
        return deps

    def _commit(self, tok, reads, writes):
        for b in reads:
            self._merge(b.r, tok)
        for b in writes:
            b.w = dict(tok)
            b.r = {}

    def op(self, e, emit, reads=(), writes=()):
        self._wait(e, self._deps(reads, writes))
        ins = emit(self.E[e])
        if self.cnt[e] >= 30000:
            self._new_sem(e)
        s = self.sem[e]
        self.cnt[e] += 1
        ins.then_inc(s, 1)
        tok = {self._key(s): self.cnt[e]}
        self._commit(tok, reads, writes)
        return tok

    def dma(self, e, out, in_, reads=(), writes=(), **kw):
        self._wait(e, self._deps(reads, writes))
        sems, cnts = self.dma_sems[e], self.dma_cnt[e]
        i = self.dma_rr[e]
        self.dma_rr[e] = (i + 1) % len(sems)
        if cnts[i] >= 30000:
            sems[i] = self.nc.alloc_semaphore(f"dqx{self.nsem}")
            self.nsem += 1
            cnts[i] = 0
        s = sems[i]
        if cnts[i] > 0:
            self._wait(e, {self._key(s): cnts[i]})
        cnts[i] += 16
        self.E[e].dma_start(out=out, in_=in_, **kw).then_inc(s, 16)
        tok = {self._key(s): cnts[i]}
        self._commit(tok, reads, writes)
        return tok

    def barrier(self, skip_sw=False):
        allt = {}
        for e in ("pe", "act", "dve", "pool"):
            if self.cnt[e] > 0:
                allt[self._key(self.sem[e])] = self.cnt[e]
        for q in self.dma_sems:
            for i, s in enumerate(self.dma_sems[q]):
                if self.dma_cnt[q][i] > 0 and not (q == "pool" and skip_sw):
                    allt[self._key(s)] = self.dma_cnt[q][i]
        for e in self.E:
            self._wait(e, allt)


class Scope:
    def __init__(self, cx):
        self.cx = cx
        self.guards = []

    def sb(self, shape, dt=F32, name=None):
        self.cx.uid += 1
        g = self.cx.nc.sbuf_tensor(f"{name or 't'}_{self.cx.uid}", list(shape), dt)
        t = g.__enter__()
        self.guards.append(g)
        return t.ap()

    def buf(self, shape, dt=F32, name=None):
        return Buf(self.sb(shape, dt, name), name or "")

    def close(self):
        for g in reversed(self.guards):
            g.__exit__(None, None, None)
        self.guards = []


def build(nc, n_layers=DEPTH, dbg=False, stop_after=None):
    cx = Ctx(nc)
    kind_dbg = "ExternalOutput" if dbg else "Internal"

    def din(name, shape):
        return nc.dram_tensor(name, list(shape), F32, kind="ExternalInput").ap()

    def dscr(name, shape, dt, k="Internal"):
        return nc.dram_tensor(name, list(shape), dt, kind=k).ap()

    xT = din("xT", [D, L])
    w_in = din("w_in", [n_layers, D, IN_TOTAL])
    w_branch = din("w_branch", [n_layers, 1536, D])
    w_out = din("w_out", [n_layers, D, D])
    w_g = din("w_ffn_gate", [n_layers, D, FFN])
    w_u = din("w_ffn_up", [n_layers, D, FFN])
    w_d = din("w_ffn_down", [n_layers, FFN, D])
    w_glu = din("s5_w_glu", [n_layers, 512, 512])
    lru_w_a = din("lru_w_a", [n_layers, 8, 64, 64])
    lru_w_x = din("lru_w_x", [n_layers, 8, 64, 64])
    b_f = din("b_f", [n_layers, 8])
    b_gate = din("b_gate", [n_layers, 3072])
    s5_a_re = din("s5_a_re", [n_layers, 32, 64])
    s5_a_im = din("s5_a_im", [n_layers, 32, 64])
    s5_log_dt = din("s5_log_dt", [n_layers, 32])
    s5_b_re = din("s5_b_re", [n_layers, 32, 64, 16])
    s5_b_im = din("s5_b_im", [n_layers, 32, 64, 16])
    s5_c_re = din("s5_c_re", [n_layers, 512, 64])
    s5_c_im = din("s5_c_im", [n_layers, 512, 64])
    s5_d = din("s5_d", [n_layers, 512])
    s5_b_glu = din("s5_b_glu", [n_layers, 512])
    lru_conv_w = din("lru_conv_w", [n_layers, 4, 512])
    lru_conv_b = din("lru_conv_b", [n_layers, 512])
    lru_b_a = din("lru_b_a", [n_layers, 512])
    lru_b_x = din("lru_b_x", [n_layers, 512])
    lru_lambda = din("lru_lambda", [n_layers, 512])
    ln1_g = din("ln1_g", [n_layers, D])
    ln1_b = din("ln1_b", [n_layers, D])
    ln2_g = din("ln2_g", [n_layers, D])
    ln2_b = din("ln2_b", [n_layers, D])
    outT = nc.dram_tensor("outT", [D, L], F32, kind="ExternalOutput").ap()

    wb_in = dscr("wb_in", [n_layers, D, IN_TOTAL], BF16)
    wb_branch = dscr("wb_branch", [n_layers, 1536, D], BF16)
    wb_out = dscr("wb_out", [n_layers, D, D], BF16)
    wb_g = dscr("wb_g", [n_layers, D, FFN], BF16)
    wb_u = dscr("wb_u", [n_layers, D, FFN], BF16)
    wb_d = dscr("wb_d", [n_layers, FFN, D], BF16)
    wb_glu = dscr("wb_glu", [n_layers, 512, 512], BF16)
    XBF = dscr("XBF", [D, L], BF16)
    XRES = dscr("XRES", [D, L], F32, kind_dbg)
    X1BF = dscr("X1BF", [D, L], BF16)
    X1RES = dscr("X1RES", [D, L], F32, kind_dbg)
    UT = dscr("UT", [512, L], BF16, kind_dbg)
    XL = dscr("XL", [512, L], F32, kind_dbg)
    GL = dscr("GL", [512, L], F32, kind_dbg)
    QA = dscr("QA", [8, 70, L], BF16, kind_dbg)
    KA = dscr("KA", [8, 70, L], BF16, kind_dbg)
    VA = dscr("VA", [8, 128, 32, 128], BF16, kind_dbg)
    YS = dscr("YS", [1536, L], BF16, kind_dbg)

    B_wb = {}
    for nm in ("in", "branch", "out", "g", "u", "d", "glu"):
        for l in range(n_layers):
            B_wb[(nm, l)] = []
    B_XBF = [Buf(None, f"XBF{t}") for t in range(NB)]
    B_XRES = [Buf(None, f"XRES{t}") for t in range(NB)]
    B_X1BF = [Buf(None, f"X1BF{t}") for t in range(NB)]
    B_X1RES = [Buf(None, f"X1RES{t}") for t in range(NB)]
    B_UT = Buf(None, "UT")
    B_XL = Buf(None, "XL")
    B_GL = Buf(None, "GL")
    B_QA = Buf(None, "QA")
    B_KA = Buf(None, "KA")
    B_VA = Buf(None, "VA")
    B_YS = [Buf(None, f"YS{k}") for k in range(3)]
    B_OUT = Buf(None, "out")

    PS = [Buf(nc.alloc_psum_tensor(f"psb{i}", [128, 512], F32).ap(), f"ps{i}") for i in range(8)]

    cs = Scope(cx)
    identf = cs.buf([128, 128], F32, "identf")
    identb = cs.buf([128, 128], BF16, "identb")
    onesD = cs.buf([128, 128], BF16, "onesD")
    ones1 = cs.buf([128, 64], F32, "ones1")
    mask8 = cs.buf([128, 128], F32, "mask8")
    negtri = cs.buf([128, 128], BF16, "negtri")
    SELD = dscr("SELD", [2, 128, 8, 8, 128], BF16)
    B_SELD = Buf(None, "SELD")
    cs0 = Scope(cx)
    Sel = cs0.buf([128, 8, 8, 128], BF16, "Sel")
    SelT = cs0.buf([128, 8, 8, 128], BF16, "SelT")

    def pool_fill(buf, val):
        cx.op("pool", lambda e: e.memset(buf.ap, val), writes=[buf])

    def pool_sel(buf, ap, pattern, cmp, base, cm, fill=0.0):
        cx.op("pool", lambda e: e.affine_select(out=ap, in_=ap, pattern=pattern, compare_op=cmp, fill=fill,
                                                base=base, channel_multiplier=cm), reads=[buf], writes=[buf])

    pool_fill(identf, 1.0)
    pool_sel(identf, identf.ap, [[1, 128]], ALU.is_equal, 0, -1)
    pool_fill(identb, 1.0)
    pool_sel(identb, identb.ap, [[1, 128]], ALU.is_equal, 0, -1)
    pool_fill(onesD, 1.0 / D)
    pool_fill(ones1, 1.0)
    pool_fill(mask8, 1.0)
    pool_sel(mask8, mask8.ap.rearrange("p (t c) -> p t c", c=16), [[16, 8], [0, 16]], ALU.is_ge, 15, -1)
    pool_fill(negtri, 0.0)
    pool_sel(negtri, negtri.ap, [[1, 128]], ALU.is_ge, 0, -1, fill=-30000.0)
    pool_fill(Sel, 1.0)
    for gg in range(8):
        a4 = Sel.ap[:, gg, :, :].rearrange("p t (u c) -> p t u c", c=16)
        pool_sel(Sel, a4, [[0, 8], [0, 8], [-1, 16]], ALU.is_equal, -16 * gg, 1)
        pool_sel(Sel, a4, [[-1, 8], [1, 8], [0, 16]], ALU.is_equal, 0, 0)
    pool_fill(SelT, 1.0)
    for gg in range(8):
        a3 = SelT.ap[:, gg, :, :]
        pool_sel(SelT, a3, [[16, 8], [1, 128]], ALU.is_equal, -16 * gg, -1)
        pool_sel(SelT, a3, [[-16, 8], [0, 128]], ALU.is_ge, 0, 1)
        pool_sel(SelT, a3, [[16, 8], [0, 128]], ALU.is_ge, 15, -1)

    cx.dma("sp", SELD[0], Sel.ap, reads=[Sel], writes=[B_SELD])
    cx.dma("sp", SELD[1], SelT.ap, reads=[SelT], writes=[B_SELD])
    cx.barrier()
    cs0.close()

    def convert(src, dst, rows, key):
        r = 0
        while r < rows:
            n = min(128, rows - r)
            bch = Buf(None, "wbch")
            B_wb[key].append(bch)
            cx.dma("pool", dst[r:r + n, :], src[r:r + n, :], writes=[bch])
            r += n

    def convert_layer(l):
        convert(w_in[l], wb_in[l], D, ("in", l))
        convert(w_glu[l], wb_glu[l], 512, ("glu", l))
        convert(w_branch[l], wb_branch[l], 1536, ("branch", l))
        convert(w_out[l], wb_out[l], D, ("out", l))
        convert(w_g[l], wb_g[l], D, ("g", l))
        convert(w_u[l], wb_u[l], D, ("u", l))
        convert(w_d[l], wb_d[l], FFN, ("d", l))

    for t in range(NB):
        for kc in range(8):
            cx.dma("pool", XBF[kc * 128:(kc + 1) * 128, t * TB:(t + 1) * TB],
                   xT[kc * 128:(kc + 1) * 128, t * TB:(t + 1) * TB], writes=[B_XBF[t]])

    convert_layer(0)

    def kview(ap2d):
        return ap2d.rearrange("(kc p) n -> p kc n", p=128)

    def blk(t):
        return slice(t * TB, (t + 1) * TB)

    def phase_proj(l):
        sc_fg = Scope(cx)
        fgT = sc_fg.buf([8, L], F32, "fgT")
        sc = Scope(cx)
        xb = [sc.buf([128, 8, TB], BF16, f"xb{t}") for t in range(NB)]
        for t in range(NB):
            cx.dma("sp", xb[t].ap, kview(XBF)[:, :, blk(t)], reads=[B_XBF[t]], writes=[xb[t]])
        wt = [sc.buf([128, 8, 512], BF16, f"wt{i}") for i in range(2)]
        wfg = sc.buf([128, 8, 8], BF16, "wfg")
        cx.dma("sp", wfg.ap, kview(wb_in[l])[:, :, 3072:3080], reads=B_wb[("in", l)], writes=[wfg])
        st32 = [sc.buf([128, 4, TB], F32, f"st32_{i}") for i in range(2)]
        st16 = [sc.buf([128, 4, TB], BF16, f"st16_{i}") for i in range(2)]
        stqk = [sc.buf([128, 4, TB], BF16, f"stqk_{i}") for i in range(2)]
        vst = [sc.buf([128, 8, 128], BF16, f"vst_{i}") for i in range(2)]
        for v in vst:
            cx.op("pool", lambda e, v=v: e.memset(v.ap, 1.0), writes=[v])
        nps = 0
        nst = 0
        for cg in range(6):
            w = wt[cg % 2]
            cx.dma("sp", w.ap, kview(wb_in[l])[:, :, cg * 512:(cg + 1) * 512], reads=B_wb[("in", l)], writes=[w])
            if cg < 3:
                for t in range(NB):
                    stb = (st16 if cg == 0 else st32)[nst % 2]
                    nst += 1
                    for ct in range(4):
                        ps = PS[nps % 4]
                        nps += 1

                        def mm(e, ps=ps, ct=ct, t=t, w=w):
                            for kc in range(8):
                                i = e.matmul(ps.ap, lhsT=w.ap[:, kc, ct * 128:(ct + 1) * 128], rhs=xb[t].ap[:, kc, :],
                                             start=(kc == 0), stop=(kc == 7))
                            return i
                        cx.op("pe", mm, reads=[w, xb[t]], writes=[ps])
                        fn = AF.Gelu_apprx_tanh if cg == 2 else AF.Identity
                        cx.op("act", lambda e, ps=ps, stb=stb, ct=ct, fn=fn: e.activation(out=stb.ap[:, ct, :], in_=ps.ap, func=fn),
                              reads=[ps], writes=[stb])
                    dst, bd = [(UT, B_UT), (XL, B_XL), (GL, B_GL)][cg]
                    cx.dma("sp", dst.rearrange("(c p) t -> p c t", p=128)[:, :, blk(t)], stb.ap, reads=[stb], writes=[bd])
            elif cg < 5:
                for t in range(NB):
                    stb = stqk[nst % 2]
                    nst += 1
                    for hp in range(4):
                        ps = PS[nps % 4]
                        nps += 1

                        def mm(e, ps=ps, hp=hp, t=t, w=w):
                            for kc in range(8):
                                i = e.matmul(ps.ap, lhsT=w.ap[:, kc, hp * 128:(hp + 1) * 128], rhs=xb[t].ap[:, kc, :],
                                             start=(kc == 0), stop=(kc == 7))
                            return i
                        cx.op("pe", mm, reads=[w, xb[t]], writes=[ps])
                        sc_ = 0.125 if cg == 3 else 1.0
                        cx.op("act", lambda e, ps=ps, stb=stb, hp=hp, sc_=sc_: e.activation(out=stb.ap[:, hp, :], in_=ps.ap,
                                                                                           func=AF.Identity, scale=sc_),
                              reads=[ps], writes=[stb])
                    dst, bd = (QA, B_QA) if cg == 3 else (KA, B_KA)
                    for two in range(2):
                        cx.dma("sp", dst[two::2, 0:64, blk(t)].rearrange("hp d t -> d hp t"), stb.ap[two * 64:(two + 1) * 64, :, :],
                               reads=[stb], writes=[bd])
            else:
                for tt in range(32):
                    ps = PS[nps % 4]
                    nps += 1
                    t = tt // 4
                    vs = vst[tt % 2]

                    def mm(e, ps=ps, tt=tt, t=t, w=w):
                        o = (tt % 4) * 128
                        for kc in range(8):
                            i = e.matmul(ps.ap, lhsT=xb[t].ap[:, kc, o:o + 128], rhs=w.ap[:, kc, :],
                                         start=(kc == 0), stop=(kc == 7))
                        return i
                    cx.op("pe", mm, reads=[w, xb[t]], writes=[ps])
                    cx.op("act", lambda e, ps=ps, vs=vs: e.activation(out=vs.ap[:, :, 0:64], in_=ps.ap.rearrange("p (h d) -> p h d", d=64),
                                                                      func=AF.Identity), reads=[ps], writes=[vs])
                    cx.dma("sp", VA[:, :, tt, :].rearrange("h p e -> p h e"), vs.ap, reads=[vs], writes=[B_VA])
        for t in range(NB):
            ps = PS[nps % 4]
            nps += 1

            def mm(e, ps=ps, t=t):
                for kc in range(8):
                    i = e.matmul(ps.ap[0:8, :], lhsT=wfg.ap[:, kc, :], rhs=xb[t].ap[:, kc, :], start=(kc == 0), stop=(kc == 7))
                return i
            cx.op("pe", mm, reads=[wfg, xb[t]], writes=[ps])
            cx.op("act", lambda e, ps=ps, t=t: e.activation(out=fgT.ap[:, blk(t)], in_=ps.ap[0:8, :], func=AF.Identity),
                  reads=[ps], writes=[fgT])
        cx.barrier(skip_sw=True)
        sc.close()
        sc = Scope(cx)
        bf = sc.buf([8, 1], F32, "bf")
        cx.dma("sp", bf.ap, b_f[l].rearrange("(h o) -> h o", o=1), writes=[bf])
        nbf = sc.buf([8, 1], F32, "nbf")
        cx.op("dve", lambda e: e.tensor_scalar(out=nbf.ap, in0=bf.ap, scalar1=-1.0, scalar2=None, op0=ALU.mult), reads=[bf], writes=[nbf])
        one8 = sc.buf([8, 1], F32, "one8")
        cx.op("dve", lambda e: e.memset(one8.ap, 1.0), writes=[one8])
        ex = sc.buf([8, L], F32, "ex")
        cx.op("act", lambda e: e.activation(out=ex.ap, in_=fgT.ap, func=AF.Exp, bias=nbf.ap, scale=-1.0), reads=[fgT, nbf], writes=[ex])
        cx.op("act", lambda e: e.activation(out=ex.ap, in_=ex.ap, func=AF.Ln, bias=one8.ap, scale=1.0), reads=[ex, one8], writes=[ex])
        csum = sc.buf([8, L], F32, "csum")
        cx.op("dve", lambda e: e.tensor_tensor_scan(out=csum.ap, data0=one8.ap.to_broadcast([8, L]), data1=ex.ap, initial=0.0,
                                                    op0=ALU.mult, op1=ALU.add), reads=[ex, one8], writes=[csum])
        pcs = [sc.buf([8, L], BF16, f"pc{j}") for j in range(3)]
        ncs = [sc.buf([8, L], BF16, f"nc{j}") for j in range(3)]
        res = ex
        cx.op("dve", lambda e: e.tensor_copy(out=pcs[0].ap, in_=csum.ap), reads=[csum], writes=[pcs[0]])
        cx.op("dve", lambda e: e.tensor_tensor(out=res.ap, in0=csum.ap, in1=pcs[0].ap, op=ALU.subtract), reads=[csum, pcs[0]], writes=[res])
        cx.op("dve", lambda e: e.tensor_copy(out=pcs[1].ap, in_=res.ap), reads=[res], writes=[pcs[1]])
        cx.op("dve", lambda e: e.tensor_tensor(out=res.ap, in0=res.ap, in1=pcs[1].ap, op=ALU.subtract), reads=[res, pcs[1]], writes=[res])
        cx.op("dve", lambda e: e.tensor_copy(out=pcs[2].ap, in_=res.ap), reads=[res], writes=[pcs[2]])
        for j in range(3):
            cx.op("dve", lambda e, j=j: e.tensor_scalar(out=ncs[j].ap, in0=pcs[j].ap, scalar1=-1.0, scalar2=None, op0=ALU.mult),
                  reads=[pcs[j]], writes=[ncs[j]])
        onesb = sc.buf([8, L], BF16, "onesb")
        cx.op("pool", lambda e: e.memset(onesb.ap, 1.0), writes=[onesb])
        for j in range(3):
            cx.dma("sp", QA[:, 64 + j, :], ncs[j].ap, reads=[ncs[j]], writes=[B_QA])
            cx.dma("sp", QA[:, 67 + j, :], onesb.ap, reads=[onesb], writes=[B_QA])
            cx.dma("sp", KA[:, 64 + j, :], onesb.ap, reads=[onesb], writes=[B_KA])
            cx.dma("sp", KA[:, 67 + j, :], pcs[j].ap, reads=[pcs[j]], writes=[B_KA])
        cx.barrier(skip_sw=True)
        sc.close()
        sc_fg.close()

    def phase_attn(l):
        sc = Scope(cx)
        qa = [sc.buf([70, L], BF16, f"qa{i}") for i in range(2)]
        ka = [sc.buf([70, L], BF16, f"ka{i}") for i in range(2)]
        va = [sc.buf([128, 32, 128], BF16, f"va{i}") for i in range(2)]
        NPT = 6
        pt = [sc.buf([128, TB], BF16, f"pt{i}") for i in range(NPT)]
        rden = [sc.buf([128, TB], F32, f"rden{i}") for i in range(2)]
        rb = [sc.buf([64, TB], F32, f"rb{i}") for i in range(2)]
        ost = [sc.buf([64, TB], BF16, f"ost{i}") for i in range(2)]

        def load_head(h):
            cx.dma("sp", qa[h % 2].ap, QA[h], reads=[B_QA], writes=[qa[h % 2]])
            cx.dma("sp", ka[h % 2].ap, KA[h], reads=[B_KA], writes=[ka[h % 2]])
            cx.dma("sp", va[h % 2].ap, VA[h], reads=[B_VA], writes=[va[h % 2]])
        items = []
        nb = 0
        for h in range(8):
            for I in range(NB):
                nkb = 4 * I + 4
                for j in range(nkb):
                    items.append((h, I, j, nkb, nb))
                nb += 1
        LA = 3

        def emit_S(i):
            h, I, j, nkb, b_ = items[i]
            c0 = 128 * max(0, j - 4 * I)
            diag = j >= 4 * I
            ps = PS[i % 4]
            k, q = ka[h % 2], qa[h % 2]

            def mm(e):
                i_ = e.matmul(ps.ap[:, c0:TB], lhsT=k.ap[:, j * 128:(j + 1) * 128], rhs=q.ap[:, I * TB + c0:(I + 1) * TB],
                              start=True, stop=not diag)
                if diag:
                    i_ = e.matmul(ps.ap[:, c0:c0 + 128], lhsT=identb.ap, rhs=negtri.ap, start=False, stop=True)
                return i_
            cx.op("pe", mm, reads=[k, q, identb, negtri], writes=[ps])

        def finalize(h, I, b_):
            po = PS[4 + b_ % 2]
            pr = PS[6 + b_ % 2]
            rd, r_, o_ = rden[b_ % 2], rb[b_ % 2], ost[b_ % 2]
            cx.op("dve", lambda e: e.reciprocal(out=rd.ap[64:128, :], in_=po.ap[64:128, :]), reads=[po], writes=[rd])
            cx.op("pool", lambda e: e.tensor_copy(out=r_.ap, in_=rd.ap[64:128, :]), reads=[rd], writes=[r_])
            cx.op("dve", lambda e: e.tensor_tensor(out=o_.ap, in0=po.ap[0:64, :], in1=r_.ap, op=ALU.mult), reads=[po, r_], writes=[o_])
            cx.dma("sp", YS[1024 + h * 64:1024 + (h + 1) * 64, blk(I)], o_.ap, reads=[o_], writes=[B_YS[2]])
        load_head(0)
        for i in range(min(LA, len(items))):
            emit_S(i)
        pending = []
        for i, (h, I, j, nkb, b_) in enumerate(items):
            if I == 0 and j == 0 and h + 1 < 8:
                load_head(h + 1)
            if i + LA < len(items):
                emit_S(i + LA)
            c0 = 128 * max(0, j - 4 * I)
            ps = PS[i % 4]
            p = pt[i % NPT]
            v = va[h % 2]
            po = PS[4 + b_ % 2]
            cx.op("act", lambda e: e.activation(out=p.ap[:, c0:TB], in_=ps.ap[:, c0:TB], func=AF.Exp), reads=[ps], writes=[p])
            cx.op("pe", lambda e: e.matmul(po.ap[:, c0:TB], lhsT=v.ap[:, j, :], rhs=p.ap[:, c0:TB], start=(j == 0), stop=(j == nkb - 1)),
                  reads=[v, p], writes=[po])
            pending = [(cnt - 1, args) for (cnt, args) in pending]
            for cnt, args in [x for x in pending if x[0] <= 0]:
                finalize(*args)
            pending = [x for x in pending if x[0] > 0]
            if j == nkb - 1:
                pending.append((2, (h, I, b_)))
        for cnt, args in pending:
            finalize(*args)
        cx.barrier(skip_sw=True)
        sc.close()

    def phase_lru(l):
        sc = Scope(cx)
        cw = sc.buf([128, 4, 4], F32, "cw")
        cb = sc.buf([128, 4], F32, "cb")
        ba = sc.buf([128, 4], F32, "ba")
        bx = sc.buf([128, 4], F32, "bx")
        lam = sc.buf([128, 4], F32, "lam")
        sneg = sc.buf([128, 4], F32, "sneg")
        one_c = sc.buf([128, 1], F32, "one_c")
        cx.op("dve", lambda e: e.memset(one_c.ap, 1.0), writes=[one_c])
        for k_ in range(4):
            cx.dma("sp", cw.ap[:, :, k_], lru_conv_w[l, k_].rearrange("(c p) -> p c", p=128), writes=[cw], allow_slow_non_contiguous=True)
        for (dst, src) in ((cb, lru_conv_b), (ba, lru_b_a), (bx, lru_b_x), (lam, lru_lambda)):
            cx.dma("sp", dst.ap, src[l].rearrange("(c p) -> p c", p=128), writes=[dst], allow_slow_non_contiguous=True)
        cx.op("act", lambda e: e.activation(out=sneg.ap, in_=lam.ap, func=AF.Exp, scale=-1.0), reads=[lam], writes=[sneg])
        cx.op("act", lambda e: e.activation(out=sneg.ap, in_=sneg.ap, func=AF.Ln, bias=one_c.ap, scale=1.0), reads=[sneg, one_c], writes=[sneg])
        cx.op("dve", lambda e: e.tensor_scalar(out=sneg.ap, in0=sneg.ap, scalar1=-8.0, scalar2=None, op0=ALU.mult), reads=[sneg], writes=[sneg])
        WA = sc.buf([128, 4, 128], BF16, "WA")
        WX = sc.buf([128, 4, 128], BF16, "WX")
        for Wm, src in ((WA, lru_w_a), (WX, lru_w_x)):
            cx.op("pool", lambda e, Wm=Wm: e.memset(Wm.ap, 0.0), writes=[Wm])
            for c in range(4):
                cx.dma("pool", Wm.ap[0:64, c, 0:64], src[l, 2 * c], writes=[Wm])
                cx.dma("pool", Wm.ap[64:128, c, 64:128], src[l, 2 * c + 1], writes=[Wm])
        hba = sc.buf([128, 4], F32, "hba")
        hbx = sc.buf([128, 4], F32, "hbx")
        hsn = sc.buf([128, 4], F32, "hsn")
        for dst, src in ((hba, ba), (hbx, bx), (hsn, sneg)):
            cx.op("dve", lambda e, dst=dst, src=src: e.tensor_scalar(out=dst.ap, in0=src.ap, scalar1=0.5, scalar2=None, op0=ALU.mult),
                  reads=[src], writes=[dst])
        xl = sc.buf([128, L + 3], F32, "xl")
        gl = sc.buf([128, L], F32, "gl")
        xc = sc.buf([128, L], F32, "xc")
        xcb = sc.buf([128, L], BF16, "xcb")
        a_all = sc.buf([128, L], F32, "a_all")
        tr_all = sc.buf([128, L], F32, "tr_all")
        ti_all = sc.buf([128, L], F32, "ti_all")
        h_all = sc.buf([128, L], F32, "h_all")
        yb = sc.buf([128, L], BF16, "yb")
        cx.op("pool", lambda e: e.memset(xl.ap[:, 0:3], 0.0), writes=[xl])
        n = 0
        for c in range(4):
            cx.dma("sp", xl.ap[:, 3:], XL[c * 128:(c + 1) * 128, :], reads=[B_XL], writes=[xl])
            cx.dma("sp", gl.ap, GL[c * 128:(c + 1) * 128, :], reads=[B_GL], writes=[gl])
            cx.op("dve", lambda e, c=c: e.tensor_scalar(out=xc.ap, in0=xl.ap[:, 0:L], scalar1=cw.ap[:, c, 0:1], scalar2=cb.ap[:, c:c + 1],
                                                        op0=ALU.mult, op1=ALU.add), reads=[xl, cw, cb], writes=[xc])
            for k_ in range(1, 4):
                cx.op("dve", lambda e, c=c, k_=k_: e.scalar_tensor_tensor(out=xc.ap, in0=xl.ap[:, k_:k_ + L], scalar=cw.ap[:, c, k_:k_ + 1],
                                                                         in1=xc.ap, op0=ALU.mult, op1=ALU.add), reads=[xl, cw, xc], writes=[xc])
            for hh_ in range(2):
                hs_ = slice(hh_ * (L // 2), (hh_ + 1) * (L // 2))
                cx.op("act", lambda e, hs_=hs_: e.activation(out=xcb.ap[:, hs_], in_=xc.ap[:, hs_], func=AF.Identity), reads=[xc], writes=[xcb])
            for t in range(NB):
                pa, px = PS[(2 * n) % 4], PS[(2 * n + 1) % 4]
                n += 1
                cx.op("pe", lambda e, pa=pa, c=c, t=t: e.matmul(pa.ap, lhsT=WA.ap[:, c, :], rhs=xcb.ap[:, blk(t)], start=True, stop=True),
                      reads=[WA, xcb], writes=[pa])
                cx.op("pe", lambda e, px=px, c=c, t=t: e.matmul(px.ap, lhsT=WX.ap[:, c, :], rhs=xcb.ap[:, blk(t)], start=True, stop=True),
                      reads=[WX, xcb], writes=[px])
                cx.op("act", lambda e, pa=pa, c=c, t=t: e.activation(out=tr_all.ap[:, blk(t)], in_=pa.ap, func=AF.Tanh, bias=hba.ap[:, c:c + 1], scale=0.5),
                      reads=[pa, hba], writes=[tr_all])
                cx.op("act", lambda e, px=px, c=c, t=t: e.activation(out=ti_all.ap[:, blk(t)], in_=px.ap, func=AF.Tanh, bias=hbx.ap[:, c:c + 1], scale=0.5),
                      reads=[px, hbx], writes=[ti_all])
            for hh_ in range(2):
                hs_ = slice(hh_ * (L // 2), (hh_ + 1) * (L // 2))
                cx.op("act", lambda e, c=c, hs_=hs_: e.activation(out=a_all.ap[:, hs_], in_=tr_all.ap[:, hs_], func=AF.Exp, bias=hsn.ap[:, c:c + 1],
                                                               scale=hsn.ap[:, c:c + 1]), reads=[tr_all, hsn], writes=[a_all])
            for hh_ in range(2):
                hs_ = slice(hh_ * (L // 2), (hh_ + 1) * (L // 2))
                cx.op("act", lambda e, hs_=hs_: e.activation(out=tr_all.ap[:, hs_], in_=a_all.ap[:, hs_], func=AF.Square), reads=[a_all], writes=[tr_all])
            cx.op("dve", lambda e: e.tensor_scalar(out=ti_all.ap, in0=ti_all.ap, scalar1=0.5, scalar2=0.5, op0=ALU.mult, op1=ALU.add),
                  reads=[ti_all], writes=[ti_all])
            cx.op("pool", lambda e: e.tensor_tensor(out=ti_all.ap, in0=ti_all.ap, in1=xc.ap, op=ALU.mult), reads=[ti_all, xc], writes=[ti_all])
            for hh_ in range(2):
                hs_ = slice(hh_ * (L // 2), (hh_ + 1) * (L // 2))
                cx.op("act", lambda e, hs_=hs_: e.activation(out=tr_all.ap[:, hs_], in_=tr_all.ap[:, hs_], func=AF.Sqrt, bias=one_c.ap, scale=-1.0),
                      reads=[tr_all, one_c], writes=[tr_all])
            cx.op("dve", lambda e: e.tensor_tensor(out=ti_all.ap, in0=ti_all.ap, in1=tr_all.ap, op=ALU.mult), reads=[ti_all, tr_all], writes=[ti_all])
            cx.op("dve", lambda e: e.tensor_tensor_scan(out=h_all.ap, data0=a_all.ap, data1=ti_all.ap, initial=0.0, op0=ALU.mult, op1=ALU.add),
                  reads=[a_all, ti_all], writes=[h_all])
            cx.op("dve", lambda e: e.tensor_tensor(out=yb.ap, in0=h_all.ap, in1=gl.ap, op=ALU.mult), reads=[h_all, gl], writes=[yb])
            cx.dma("sp", YS[512 + c * 128:512 + (c + 1) * 128, :], yb.ap, reads=[yb], writes=[B_YS[1]])
        cx.barrier(skip_sw=True)
        sc.close()

    def cmul(eng_a, eng_b, sc_t, outr, outi, ar, ai, br, bi, reads, w_r, w_i, negi=False):
        t1, t2 = sc_t
        cx.op(eng_a, lambda e: e.tensor_tensor(out=t1.ap, in0=ar, in1=br, op=ALU.mult), reads=reads, writes=[t1])
        cx.op(eng_b, lambda e: e.tensor_tensor(out=t2.ap, in0=ai, in1=bi, op=ALU.mult), reads=reads, writes=[t2])
        cx.op(eng_a, lambda e: e.tensor_tensor(out=outr, in0=t1.ap, in1=t2.ap, op=ALU.subtract), reads=[t1, t2], writes=[w_r])
        cx.op(eng_a, lambda e: e.tensor_tensor(out=t1.ap, in0=ar, in1=bi, op=ALU.mult), reads=reads + [w_r], writes=[t1])
        cx.op(eng_b, lambda e: e.tensor_tensor(out=t2.ap, in0=ai, in1=br, op=ALU.mult), reads=reads + [w_r], writes=[t2])
        if negi:
            cx.op(eng_a, lambda e: e.scalar_tensor_tensor(out=outi, in0=t1.ap, scalar=-1.0, in1=t2.ap, op0=ALU.mult, op1=ALU.subtract),
                  reads=[t1, t2], writes=[w_i])
        else:
            cx.op(eng_a, lambda e: e.tensor_tensor(out=outi, in0=t1.ap, in1=t2.ap, op=ALU.add), reads=[t1, t2], writes=[w_i])

    def phase_s5(l):
        ws = Scope(cx)
        Mw = ws.buf([128, 32, 128], BF16, "Mw")
        W1r = ws.buf([128, 32, 64], BF16, "W1r")
        W1i = ws.buf([128, 32, 64], BF16, "W1i")
        W2r = ws.buf([128, 16, 128], BF16, "W2r")
        W2i = ws.buf([128, 16, 128], BF16, "W2i")
        rho = ws.buf([128, 16], F32, "rho")
        ph_r = ws.buf([128, 16], F32, "ph_r")
        ph_i = ws.buf([128, 16], F32, "ph_i")
        dcol = ws.buf([128, 32], F32, "dcol")
        bglu = ws.buf([128, 4], F32, "bglu")
        cx.dma("sp", bglu.ap, s5_b_glu[l].rearrange("(c p) -> p c", p=128), writes=[bglu], allow_slow_non_contiguous=True)
        for tau in range(8):
            cx.dma("sp", dcol.ap[16 * tau:16 * tau + 16, :], s5_d[l].rearrange("(g c) -> c g", c=16), writes=[dcol],
                   allow_slow_non_contiguous=True)
        sc = Scope(cx)
        N = 64
        araw = sc.buf([32, 64], F32, "araw")
        airaw = sc.buf([32, 64], F32, "airaw")
        cx.dma("sp", araw.ap, s5_a_re[l], writes=[araw])
        cx.dma("sp", airaw.ap, s5_a_im[l], writes=[airaw])
        are = sc.buf([N, 32], F32, "are")
        aim = sc.buf([N, 32], F32, "aim")
        for src, dst in ((araw, are), (airaw, aim)):
            cx.op("pe", lambda e, src=src: e.matmul(PS[0].ap[0:64, 0:32], lhsT=src.ap, rhs=identf.ap[0:32, 0:32], start=True, stop=True),
                  reads=[src, identf], writes=[PS[0]])
            cx.op("act", lambda e, dst=dst: e.activation(out=dst.ap, in_=PS[0].ap[0:64, 0:32], func=AF.Identity), reads=[PS[0]], writes=[dst])
        dt = sc.buf([N, 32], F32, "dt")
        cx.dma("sp", dt.ap, s5_log_dt[l].partition_broadcast(N), writes=[dt])
        Br = sc.buf([N, 32, 16], F32, "Br")
        Bi = sc.buf([N, 32, 16], F32, "Bi")
        cx.dma("sp", Br.ap, s5_b_re[l].rearrange("g n c -> n g c"), writes=[Br])
        cx.dma("sp", Bi.ap, s5_b_im[l].rearrange("g n c -> n g c"), writes=[Bi])
        Cr = sc.buf([N, 32, 16], F32, "Cr")
        Ci = sc.buf([N, 32, 16], F32, "Ci")
        craw = sc.buf([128, 4, 64], F32, "craw")
        for src, dst in ((s5_c_re, Cr), (s5_c_im, Ci)):
            cx.dma("sp", craw.ap, src[l].rearrange("(j p) n -> p j n", p=128), writes=[craw])

            def mm(e):
                for j in range(4):
                    i = e.matmul(PS[1].ap[0:64, j * 128:(j + 1) * 128], lhsT=craw.ap[:, j, :], rhs=identf.ap, start=True, stop=True)
                return i
            cx.op("pe", mm, reads=[craw, identf], writes=[PS[1]])
            cx.op("act", lambda e, dst=dst: e.activation(out=dst.ap.rearrange("n g c -> n (g c)"), in_=PS[1].ap[0:64, :], func=AF.Identity),
                  reads=[PS[1]], writes=[dst])

        def small(name):
            return sc.buf([N, 32], F32, name)
        ar, ang, mag, lbr, lbi = small("ar"), small("ang"), small("mag"), small("lbr"), small("lbi")
        t1, t2, t3 = small("t1"), small("t2"), small("t3")
        V = "dve"

        def tt(out, a, b, op_, eng=V):
            cx.op(eng, lambda e: e.tensor_tensor(out=out.ap, in0=a.ap, in1=b.ap, op=op_), reads=[a, b], writes=[out])

        def tsc(out, a, s1, op0, s2=None, op1=None, eng=V):
            if op1 is None:
                cx.op(eng, lambda e: e.tensor_scalar(out=out.ap, in0=a.ap, scalar1=s1, scalar2=None, op0=op0), reads=[a], writes=[out])
            else:
                cx.op(eng, lambda e: e.tensor_scalar(out=out.ap, in0=a.ap, scalar1=s1, scalar2=s2, op0=op0, op1=op1), reads=[a], writes=[out])

        def act(out, a, fn, scale=1.0, bias=None):
            if bias is None:
                cx.op("act", lambda e: e.activation(out=out.ap, in_=a.ap, func=fn, scale=scale), reads=[a], writes=[out])
            else:
                cx.op("act", lambda e: e.activation(out=out.ap, in_=a.ap, func=fn, scale=scale, bias=bias.ap), reads=[a, bias], writes=[out])

        zero_c = sc.buf([N, 1], F32, "zero_c")
        cx.op("dve", lambda e: e.memset(zero_c.ap, 0.0), writes=[zero_c])

        def sin_of(out, angle_buf, shift):
            tsc(t1, angle_buf, 1.0 / TWO_PI, ALU.mult, (shift / TWO_PI) + MAGIC, ALU.add)
            tsc(t1, t1, -MAGIC, ALU.add)
            cx.op(V, lambda e: e.scalar_tensor_tensor(out=t2.ap, in0=t1.ap, scalar=-TWO_PI, in1=angle_buf.ap, op0=ALU.mult, op1=ALU.add),
                  reads=[t1, angle_buf], writes=[t2])
            tsc(t2, t2, float(shift), ALU.add, math.pi - 1e-6, ALU.min)
            tsc(t2, t2, -(math.pi - 1e-6), ALU.max)
            act(out, t2, AF.Sin, bias=zero_c)

        em1, xr_, w_, sn, cm1, nn = small("em1"), small("xr_"), small("w_"), small("sn"), small("cm1"), small("nn")

        def nested(out, var, divs, sign):
            cx.op(V, lambda e: e.memset(out.ap, 1.0), writes=[out])
            for dv in divs:
                tt(t3, out, var, ALU.mult)
                tsc(out, t3, sign / dv, ALU.mult, 1.0, ALU.add)
        ld8 = small("ld8")
        tsc(ld8, dt, 0.125, ALU.mult)
        nested(dt, ld8, [float(k) for k in range(12, 0, -1)], 1.0)
        for _ in range(3):
            tt(dt, dt, dt, ALU.mult)
        tt(ar, are, dt, ALU.mult)
        tt(ang, aim, dt, ALU.mult)
        nested(nn, ar, [9.0, 8.0, 7.0, 6.0, 5.0, 4.0, 3.0, 2.0], 1.0)
        tt(em1, nn, ar, ALU.mult)
        tsc(mag, em1, 1.0, ALU.add)
        C1 = 6.28125
        C2 = TWO_PI - C1
        tsc(t1, ang, 1.0 / TWO_PI, ALU.mult, MAGIC, ALU.add)
        tsc(t1, t1, -MAGIC, ALU.add)
        cx.op(V, lambda e: e.scalar_tensor_tensor(out=xr_.ap, in0=t1.ap, scalar=-C1, in1=ang.ap, op0=ALU.mult, op1=ALU.add),
              reads=[t1, ang], writes=[xr_])
        cx.op(V, lambda e: e.scalar_tensor_tensor(out=xr_.ap, in0=t1.ap, scalar=-C2, in1=xr_.ap, op0=ALU.mult, op1=ALU.add),
              reads=[t1, xr_], writes=[xr_])
        tt(w_, xr_, xr_, ALU.mult)
        nested(nn, w_, [float((2 * k) * (2 * k + 1)) for k in range(10, 0, -1)], -1.0)
        tt(sn, nn, xr_, ALU.mult)
        nested(nn, w_, [float((2 * k + 1) * (2 * k + 2)) for k in range(10, 0, -1)], -1.0)
        tt(cm1, nn, w_, ALU.mult)
        tsc(cm1, cm1, -0.5, ALU.mult)
        lm1 = small("lm1")
        tt(t1, em1, cm1, ALU.mult)
        tt(t2, em1, cm1, ALU.add)
        tt(lm1, t1, t2, ALU.add)
        tsc(lbr, lm1, 1.0, ALU.add)
        tt(lbi, sn, mag, ALU.mult)
        den, qr, qi = small("den"), small("qr"), small("qi")
        tt(den, are, are, ALU.mult)
        tt(t1, aim, aim, ALU.mult)
        tt(den, den, t1, ALU.add)
        cx.op(V, lambda e: e.reciprocal(out=den.ap, in_=den.ap), reads=[den], writes=[den])
        tt(t1, lm1, are, ALU.mult)
        tt(t2, lbi, aim, ALU.mult)
        tt(qr, t1, t2, ALU.add)
        tt(qr, qr, den, ALU.mult)
        tt(t1, lbi, are, ALU.mult)
        tt(t2, lm1, aim, ALU.mult)
        tt(qi, t1, t2, ALU.subtract)
        tt(qi, qi, den, ALU.mult)
        Bbr = sc.buf([N, 32, 16], F32, "Bbr")
        Bbi = sc.buf([N, 32, 16], F32, "Bbi")
        tb1 = sc.buf([N, 32, 16], F32, "tb1")
        tb2 = sc.buf([N, 32, 16], F32, "tb2")

        def bc3(b):
            return b.ap.unsqueeze(2).to_broadcast([N, 32, 16])
        cmul("dve", "pool", (tb1, tb2), Bbr.ap, Bbi.ap, bc3(qr), bc3(qi), Br.ap, Bi.ap, [qr, qi, Br, Bi], Bbr, Bbi)
        Pr = sc.buf([N, 32, 9], F32, "Pr")
        Pi = sc.buf([N, 32, 9], F32, "Pi")
        Qr = sc.buf([N, 32, 8], F32, "Qr")
        Qi = sc.buf([N, 32, 8], F32, "Qi")
        ibr, ibi, im2 = small("ibr"), small("ibi"), small("im2")
        tt(im2, mag, mag, ALU.mult)
        cx.op(V, lambda e: e.reciprocal(out=im2.ap, in_=im2.ap), reads=[im2], writes=[im2])
        tt(ibr, lbr, im2, ALU.mult)
        tt(ibi, lbi, im2, ALU.mult)
        tsc(ibi, ibi, -1.0, ALU.mult)
        for (Xr, Xi, br_, bi_, n_) in ((Pr, Pi, lbr, lbi, 9), (Qr, Qi, ibr, ibi, 8)):
            cx.op(V, lambda e, Xr=Xr: e.memset(Xr.ap[:, :, 0:1], 1.0), writes=[Xr])
            cx.op(V, lambda e, Xi=Xi: e.memset(Xi.ap[:, :, 0:1], 0.0), writes=[Xi])
            for tau in range(1, n_):
                cmul("dve", "pool", (t1, t2), Xr.ap[:, :, tau], Xi.ap[:, :, tau], Xr.ap[:, :, tau - 1], Xi.ap[:, :, tau - 1],
                     br_.ap, bi_.ap, [Xr, Xi, br_, bi_], Xr, Xi)
        GH = 16
        big = [sc.buf([N, GH, 8, 16], F32, f"big{i}") for i in range(8)]
        Ar_, Ai_, Cmr, Cmin, T1, T2, A7r, A7i = big
        mtmp = [sc.buf([128, 128], F32, f"mtmp{i}") for i in range(2)]
        for gh in range(2):
            gs = slice(gh * GH, (gh + 1) * GH)

            def bq(b):
                return b.ap[:, gs, 0:8].unsqueeze(3).to_broadcast([N, GH, 8, 16])

            def bb(b):
                return b.ap[:, gs, :].unsqueeze(2).to_broadcast([N, GH, 8, 16])

            def b7(b, idx):
                return b.ap[:, gs, idx:idx + 1].unsqueeze(3).to_broadcast([N, GH, 8, 16])
            cmul("dve", "pool", (T1, T2), Ar_.ap, Ai_.ap, bq(Qr), bq(Qi), bb(Bbr), bb(Bbi), [Qr, Qi, Bbr, Bbi], Ar_, Ai_)
            cmul("dve", "pool", (T1, T2), Cmr.ap, Cmin.ap, bq(Pr), bq(Pi), bb(Cr), bb(Ci), [Pr, Pi, Cr, Ci], Cmr, Cmin, negi=True)
            cmul("dve", "pool", (T1, T2), A7r.ap, A7i.ap, b7(Pr, 7), b7(Pi, 7), Ar_.ap, Ai_.ap, [Pr, Pi, Ar_, Ai_], A7r, A7i)
            for src, dst in ((A7r, W1r), (A7i, W1i)):
                for g8 in range(2):
                    ps = PS[g8 % 4]

                    def mm(e, ps=ps, src=src, g8=g8):
                        for gi in range(8):
                            i = e.matmul(ps.ap[:, gi * 64:(gi + 1) * 64], lhsT=src.ap[:, g8 * 8 + gi].rearrange("n t c -> n (t c)"),
                                         rhs=identf.ap[0:64, 0:64], start=True, stop=True)
                        return i
                    cx.op("pe", mm, reads=[src, identf], writes=[ps])
                    g0 = gh * GH + g8 * 8
                    cx.op("act", lambda e, ps=ps, dst=dst, g0=g0: e.activation(
                        out=dst.ap[:, g0:g0 + 8, :], in_=ps.ap.rearrange("p (g n) -> p g n", n=64), func=AF.Identity),
                        reads=[ps], writes=[dst])
            for g4 in range(4):
                ps = PS[g4 % 4]

                def mm(e, ps=ps, g4=g4):
                    for gi in range(4):
                        gl_ = g4 * 4 + gi
                        e.matmul(ps.ap[:, gi * 128:(gi + 1) * 128], lhsT=Ar_.ap[:, gl_].rearrange("n t c -> n (t c)"),
                                 rhs=Cmr.ap[:, gl_].rearrange("n t c -> n (t c)"), start=True, stop=False)
                        i = e.matmul(ps.ap[:, gi * 128:(gi + 1) * 128], lhsT=Ai_.ap[:, gl_].rearrange("n t c -> n (t c)"),
                                     rhs=Cmin.ap[:, gl_].rearrange("n t c -> n (t c)"), start=False, stop=True)
                    return i
                cx.op("pe", mm, reads=[Ar_, Ai_, Cmr, Cmin], writes=[ps])
                for gi in range(4):
                    g = gh * GH + g4 * 4 + gi
                    mt = mtmp[g % 2]
                    cx.op("dve", lambda e, ps=ps, gi=gi, mt=mt: e.tensor_tensor(out=mt.ap, in0=ps.ap[:, gi * 128:(gi + 1) * 128], in1=mask8.ap,
                                                                                op=ALU.mult), reads=[ps, mask8], writes=[mt])
                    cx.op("dve", lambda e, g=g, mt=mt: e.scalar_tensor_tensor(out=Mw.ap[:, g, :], in0=identf.ap, scalar=dcol.ap[:, g:g + 1],
                                                                              in1=mt.ap, op0=ALU.mult, op1=ALU.add),
                          reads=[identf, dcol, mt], writes=[Mw])
            W2r_f, W2i_f = A7r, A7i
            cx.op("dve", lambda e: e.tensor_tensor(out=T1.ap, in0=b7(Pr, 1), in1=Cmr.ap, op=ALU.mult), reads=[Pr, Cmr], writes=[T1])
            cx.op("pool", lambda e: e.tensor_tensor(out=T2.ap, in0=b7(Pi, 1), in1=Cmin.ap, op=ALU.mult), reads=[Pi, Cmin], writes=[T2])
            cx.op("dve", lambda e: e.tensor_tensor(out=W2r_f.ap, in0=T1.ap, in1=T2.ap, op=ALU.add), reads=[T1, T2], writes=[W2r_f])
            cx.op("dve", lambda e: e.tensor_tensor(out=T1.ap, in0=b7(Pr, 1), in1=Cmin.ap, op=ALU.mult), reads=[Pr, Cmin, W2r_f], writes=[T1])
            cx.op("pool", lambda e: e.tensor_tensor(out=T2.ap, in0=b7(Pi, 1), in1=Cmr.ap, op=ALU.mult), reads=[Pi, Cmr, W2r_f], writes=[T2])
            cx.op("dve", lambda e: e.tensor_tensor(out=W2i_f.ap, in0=T1.ap, in1=T2.ap, op=ALU.subtract), reads=[T1, T2], writes=[W2i_f])
            for src, dst in ((W2r_f, W2r), (W2i_f, W2i)):
                v = src.ap.rearrange("n (q two) t c -> n q two (t c)", two=2)
                q0 = gh * (GH // 2)
                cx.op("act", lambda e, v=v, dst=dst, q0=q0: e.activation(out=dst.ap[0:64, q0:q0 + GH // 2, :], in_=v[:, :, 0, :], func=AF.Identity),
                      reads=[src], writes=[dst])
                cx.op("act", lambda e, v=v, dst=dst, q0=q0: e.activation(out=dst.ap[64:128, q0:q0 + GH // 2, :], in_=v[:, :, 1, :], func=AF.Identity),
                      reads=[src], writes=[dst])
        r8, e8, p8r, p8i = small("r8"), small("e8"), small("p8r"), small("p8i")
        tt(r8, mag, mag, ALU.mult)
        tt(r8, r8, r8, ALU.mult)
        tt(r8, r8, r8, ALU.mult)
        cx.op(V, lambda e: e.reciprocal(out=e8.ap, in_=r8.ap), reads=[r8], writes=[e8])
        cx.op(V, lambda e: e.tensor_tensor(out=p8r.ap, in0=Pr.ap[:, :, 8], in1=e8.ap, op=ALU.mult), reads=[Pr, e8], writes=[p8r])
        cx.op(V, lambda e: e.tensor_tensor(out=p8i.ap, in0=Pi.ap[:, :, 8], in1=e8.ap, op=ALU.mult), reads=[Pi, e8], writes=[p8i])
        for src, dst in ((r8, rho), (p8r, ph_r), (p8i, ph_i)):
            v = src.ap.rearrange("n (q two) -> n q two", two=2)
            cx.op("act", lambda e, v=v, dst=dst: e.activation(out=dst.ap[0:64], in_=v[:, :, 0], func=AF.Identity), reads=[src], writes=[dst])
            cx.op("act", lambda e, v=v, dst=dst: e.activation(out=dst.ap[64:128], in_=v[:, :, 1], func=AF.Identity), reads=[src], writes=[dst])
        cx.barrier(skip_sw=True)
        sc.close()
        if stop_after == ("s5prep", l):
            ws.close()
            return
        sc = Scope(cx)
        gb = sc.buf([128, 4, L], BF16, "gb")
        SelS = sc.buf([128, 8, 8, 128], BF16, "SelS")
        SelTS = sc.buf([128, 8, 8, 128], BF16, "SelTS")
        cx.dma("sp", SelS.ap, SELD[0], reads=[B_SELD], writes=[SelS])
        cx.dma("sp", SelTS.ap, SELD[1], reads=[B_SELD], writes=[SelTS])
        NQ = 4
        s2 = Scope(cx)
        uT_ = [s2.buf([128, L], BF16, f"uT{i}") for i in range(2)]
        U3_ = [s2.buf([128, 8, TB], BF16, f"U3{i}") for i in range(2)]
        E1r = s2.buf([128, NQ, TB], F32, "E1r")
        E1i = s2.buf([128, NQ, TB], F32, "E1i")
        Hr = s2.buf([128, NQ, TB], BF16, "Hr")
        Hi = s2.buf([128, NQ, TB], BF16, "Hi")
        Y3 = s2.buf([128, 8, TB], BF16, "Y3")
        dd = [s2.buf([128, NQ, 256], F32, f"dd{i}") for i in range(4)]
        tmp2 = [[s2.buf([128, TB], F32, f"s5t{k}_{i}") for i in range(8)] for k in range(2)]
        for qt in range(4):
            uT = uT_[qt % 2]
            U3 = U3_[qt % 2]
            cx.dma("sp", uT.ap, UT[qt * 128:(qt + 1) * 128, :], reads=[B_UT], writes=[uT])
            q0 = qt * NQ
            cx.op("dve", lambda e: e.tensor_copy(out=E1r.ap[:, :, 0], in_=ph_r.ap[:, q0:q0 + NQ]), reads=[ph_r], writes=[E1r])
            cx.op("dve", lambda e: e.tensor_copy(out=E1i.ap[:, :, 0], in_=ph_i.ap[:, q0:q0 + NQ]), reads=[ph_i], writes=[E1i])
            m = 1
            while m < TB:
                def bcm(b, m=m):
                    return b.ap[:, :, m - 1:m].to_broadcast([128, NQ, m])
                cx.op("dve", lambda e, m=m: e.tensor_tensor(out=dd[0].ap[:, :, 0:m], in0=E1r.ap[:, :, 0:m], in1=bcm(E1r), op=ALU.mult),
                      reads=[E1r], writes=[dd[0]])
                cx.op("pool", lambda e, m=m: e.tensor_tensor(out=dd[1].ap[:, :, 0:m], in0=E1i.ap[:, :, 0:m], in1=bcm(E1i), op=ALU.mult),
                      reads=[E1i], writes=[dd[1]])
                cx.op("dve", lambda e, m=m: e.tensor_tensor(out=dd[2].ap[:, :, 0:m], in0=E1r.ap[:, :, 0:m], in1=bcm(E1i), op=ALU.mult),
                      reads=[E1r, E1i], writes=[dd[2]])
                cx.op("pool", lambda e, m=m: e.tensor_tensor(out=dd[3].ap[:, :, 0:m], in0=E1i.ap[:, :, 0:m], in1=bcm(E1r), op=ALU.mult),
                      reads=[E1i, E1r], writes=[dd[3]])
                cx.op("dve", lambda e, m=m: e.tensor_tensor(out=E1r.ap[:, :, m:2 * m], in0=dd[0].ap[:, :, 0:m], in1=dd[1].ap[:, :, 0:m], op=ALU.subtract),
                      reads=[dd[0], dd[1]], writes=[E1r])
                cx.op("dve", lambda e, m=m: e.tensor_tensor(out=E1i.ap[:, :, m:2 * m], in0=dd[2].ap[:, :, 0:m], in1=dd[3].ap[:, :, 0:m], op=ALU.add),
                      reads=[dd[2], dd[3]], writes=[E1i])
                m *= 2
            npsu = 0
            for gl_ in range(8):
                ps = PS[npsu % 4]
                npsu += 1

                def mm(e, ps=ps, gl_=gl_):
                    for tau in range(8):
                        i = e.matmul(ps.ap, lhsT=SelS.ap[:, gl_, tau, :], rhs=uT.ap[:, tau::8], start=(tau == 0), stop=(tau == 7))
                    return i
                cx.op("pe", mm, reads=[SelS, uT], writes=[ps])
                cx.op("act", lambda e, ps=ps, gl_=gl_: e.activation(out=U3.ap[:, gl_, :], in_=ps.ap, func=AF.Identity), reads=[ps], writes=[U3])
            for ql in range(NQ):
                q_ = qt * NQ + ql
                ga, gb_ = 2 * ql, 2 * ql + 1
                pre, pim = PS[4 + (ql % 2) * 2], PS[5 + (ql % 2) * 2]

                def mm(e, pre=pre, pim=pim, ga=ga, gb_=gb_, qt=qt):
                    e.matmul(pre.ap[0:64, :], lhsT=W1r.ap[:, qt * 8 + ga, :], rhs=U3.ap[:, ga, :], start=True, stop=True)
                    e.matmul(pre.ap[64:128, :], lhsT=W1r.ap[:, qt * 8 + gb_, :], rhs=U3.ap[:, gb_, :], start=True, stop=True)
                    e.matmul(pim.ap[0:64, :], lhsT=W1i.ap[:, qt * 8 + ga, :], rhs=U3.ap[:, ga, :], start=True, stop=True)
                    return e.matmul(pim.ap[64:128, :], lhsT=W1i.ap[:, qt * 8 + gb_, :], rhs=U3.ap[:, gb_, :], start=True, stop=True)
                cx.op("pe", mm, reads=[W1r, W1i, U3], writes=[pre, pim])
                xr, xi, a1, a2, vr, vi, sr, si = tmp2[ql % 2]
                cx.op("act", lambda e, pre=pre: e.activation(out=xr.ap, in_=pre.ap, func=AF.Identity), reads=[pre], writes=[xr])
                cx.op("act", lambda e, pim=pim: e.activation(out=xi.ap, in_=pim.ap, func=AF.Identity), reads=[pim], writes=[xi])
                er, ei = E1r.ap[:, ql, :], E1i.ap[:, ql, :]
                cx.op("dve", lambda e, er=er: e.tensor_tensor(out=a1.ap, in0=xr.ap, in1=er, op=ALU.mult), reads=[xr, E1r], writes=[a1])
                cx.op("pool", lambda e, ei=ei: e.tensor_tensor(out=a2.ap, in0=xi.ap, in1=ei, op=ALU.mult), reads=[xi, E1i], writes=[a2])
                cx.op("dve", lambda e: e.tensor_tensor(out=vr.ap, in0=a1.ap, in1=a2.ap, op=ALU.add), reads=[a1, a2], writes=[vr])
                cx.op("dve", lambda e, er=er: e.tensor_tensor(out=a1.ap, in0=xi.ap, in1=er, op=ALU.mult), reads=[xi, E1r], writes=[a1])
                cx.op("pool", lambda e, ei=ei: e.tensor_tensor(out=a2.ap, in0=xr.ap, in1=ei, op=ALU.mult), reads=[xr, E1i], writes=[a2])
                cx.op("dve", lambda e: e.tensor_tensor(out=vi.ap, in0=a1.ap, in1=a2.ap, op=ALU.subtract), reads=[a1, a2], writes=[vi])
                rc = rho.ap[:, q_:q_ + 1].to_broadcast([128, TB])
                cx.op("dve", lambda e, rc=rc: e.tensor_tensor_scan(out=sr.ap, data0=rc, data1=vr.ap, initial=0.0, op0=ALU.mult, op1=ALU.add),
                      reads=[rho, vr], writes=[sr])
                cx.op("dve", lambda e, rc=rc: e.tensor_tensor_scan(out=si.ap, data0=rc, data1=vi.ap, initial=0.0, op0=ALU.mult, op1=ALU.add),
                      reads=[rho, vi], writes=[si])
                cx.op("dve", lambda e, er=er: e.tensor_tensor(out=a1.ap, in0=sr.ap, in1=er, op=ALU.mult), reads=[sr, E1r], writes=[a1])
                cx.op("pool", lambda e, ei=ei: e.tensor_tensor(out=a2.ap, in0=si.ap, in1=ei, op=ALU.mult), reads=[si, E1i], writes=[a2])
                cx.op("pool", lambda e, ql=ql: e.memset(Hr.ap[:, ql, 0:1], 0.0), writes=[Hr])
                cx.op("pool", lambda e, ql=ql: e.memset(Hi.ap[:, ql, 0:1], 0.0), writes=[Hi])
                cx.op("dve", lambda e, ql=ql: e.tensor_tensor(out=Hr.ap[:, ql, 1:TB], in0=a1.ap[:, 0:TB - 1], in1=a2.ap[:, 0:TB - 1], op=ALU.subtract),
                      reads=[a1, a2], writes=[Hr])
                cx.op("dve", lambda e, ei=ei: e.tensor_tensor(out=a1.ap, in0=sr.ap, in1=ei, op=ALU.mult), reads=[sr, E1i], writes=[a1])
                cx.op("pool", lambda e, er=er: e.tensor_tensor(out=a2.ap, in0=si.ap, in1=er, op=ALU.mult), reads=[si, E1r], writes=[a2])
                cx.op("dve", lambda e, ql=ql: e.tensor_tensor(out=Hi.ap[:, ql, 1:TB], in0=a1.ap[:, 0:TB - 1], in1=a2.ap[:, 0:TB - 1], op=ALU.add),
                      reads=[a1, a2], writes=[Hi])
            for gl_ in range(8):
                g = qt * 8 + gl_
                ql, half = gl_ // 2, gl_ % 2
                q_ = qt * NQ + ql
                ps = PS[npsu % 4]
                npsu += 1
                lo, hi = half * 64, half * 64 + 64

                def mm(e, ps=ps, g=g, gl_=gl_, ql=ql, q_=q_, lo=lo, hi=hi):
                    e.matmul(ps.ap, lhsT=Mw.ap[:, g, :], rhs=U3.ap[:, gl_, :], start=True, stop=False)
                    e.matmul(ps.ap, lhsT=W2r.ap[lo:hi, q_, :], rhs=Hr.ap[lo:hi, ql, :], start=False, stop=False)
                    return e.matmul(ps.ap, lhsT=W2i.ap[lo:hi, q_, :], rhs=Hi.ap[lo:hi, ql, :], start=False, stop=True)
                cx.op("pe", mm, reads=[Mw, U3, W2r, W2i, Hr, Hi], writes=[ps])
                cx.op("act", lambda e, ps=ps, gl_=gl_: e.activation(out=Y3.ap[:, gl_, :], in_=ps.ap, func=AF.Identity), reads=[ps], writes=[Y3])
            for tau in range(8):
                ps = PS[npsu % 4]
                npsu += 1

                def mm(e, ps=ps, tau=tau):
                    for gg in range(8):
                        i = e.matmul(ps.ap, lhsT=SelTS.ap[:, gg, tau, :], rhs=Y3.ap[:, gg, :], start=(gg == 0), stop=(gg == 7))
                    return i
                cx.op("pe", mm, reads=[SelTS, Y3], writes=[ps])
                cx.op("act", lambda e, ps=ps, qt=qt, tau=tau: e.activation(out=gb.ap[:, qt, tau::8], in_=ps.ap, func=AF.Gelu_apprx_tanh),
                      reads=[ps], writes=[gb])
        cx.barrier(skip_sw=True)
        s2.close()
        wgl = sc.buf([128, 4, 512], BF16, "wgl")
        cx.dma("sp", wgl.ap, wb_glu[l].rearrange("(kc p) n -> p kc n", p=128), reads=B_wb[("glu", l)], writes=[wgl])
        sg = [sc.buf([128, TB], F32, f"sg{i}") for i in range(2)]
        yst = [sc.buf([128, 4, TB], BF16, f"yst{i}") for i in range(2)]
        n = 0
        for t in range(NB):
            ys_ = yst[t % 2]
            for ct in range(4):
                ps = PS[n % 4]
                s_ = sg[n % 2]
                n += 1

                def mm(e, ps=ps, ct=ct, t=t):
                    for kc in range(4):
                        i = e.matmul(ps.ap, lhsT=wgl.ap[:, kc, ct * 128:(ct + 1) * 128], rhs=gb.ap[:, kc, blk(t)], start=(kc == 0), stop=(kc == 3))
                    return i
                cx.op("pe", mm, reads=[wgl, gb], writes=[ps])
                cx.op("act", lambda e, ps=ps, s_=s_, ct=ct: e.activation(out=s_.ap, in_=ps.ap, func=AF.Sigmoid, bias=bglu.ap[:, ct:ct + 1], scale=1.0),
                      reads=[ps, bglu], writes=[s_])
                cx.op("dve", lambda e, s_=s_, ys_=ys_, ct=ct, t=t: e.tensor_tensor(out=ys_.ap[:, ct, :], in0=gb.ap[:, ct, blk(t)], in1=s_.ap, op=ALU.mult),
                      reads=[gb, s_], writes=[ys_])
            cx.dma("sp", YS[0:512, blk(t)].rearrange("(c p) t -> p c t", p=128), ys_.ap, reads=[ys_], writes=[B_YS[0]])
        cx.barrier(skip_sw=True)
        sc.close()
        ws.close()

    def run_interleaved(gens):
        active = [g for g in gens if g is not None]
        while active:
            for g in list(active):
                try:
                    next(g)
                except StopIteration:
                    active.remove(g)

    def layer_norm_gen(y, gcol, bcol, outb, tmp, stat, sbf):
        pm, pq = PS[6], PS[7]
        ybf, ysq = sbf
        for h2 in range(2):
            sl_ = slice(h2 * 4, (h2 + 1) * 4)
            cx.op("act", lambda e: e.activation(out=ysq.ap[:, sl_, :], in_=y.ap[:, sl_, :], func=AF.Square), reads=[y], writes=[ysq])
            yield
            cx.op("act", lambda e: e.activation(out=ybf.ap[:, sl_, :], in_=y.ap[:, sl_, :], func=AF.Identity), reads=[y], writes=[ybf])
            yield

        def mm1(e):
            for kc in range(8):
                i = e.matmul(pm.ap, lhsT=onesD.ap, rhs=ybf.ap[:, kc, :], start=(kc == 0), stop=(kc == 7))
            return i

        def mm2(e):
            for kc in range(8):
                i = e.matmul(pq.ap, lhsT=onesD.ap, rhs=ysq.ap[:, kc, :], start=(kc == 0), stop=(kc == 7))
            return i
        cx.op("pe", mm1, reads=[onesD, ybf], writes=[pm])
        yield
        cx.op("pe", mm2, reads=[onesD, ysq], writes=[pq])
        yield
        mean, rstd = stat
        cx.op("act", lambda e: e.activation(out=mean.ap, in_=pm.ap, func=AF.Identity), reads=[pm], writes=[mean])
        cx.op("act", lambda e: e.activation(out=rstd.ap, in_=pm.ap, func=AF.Square), reads=[pm], writes=[rstd])
        yield
        cx.op("dve", lambda e: e.tensor_tensor(out=rstd.ap, in0=pq.ap, in1=rstd.ap, op=ALU.subtract), reads=[pq, rstd], writes=[rstd])
        cx.op("dve", lambda e: e.tensor_scalar(out=rstd.ap, in0=rstd.ap, scalar1=0.0, scalar2=LN_EPS, op0=ALU.max, op1=ALU.add),
              reads=[rstd], writes=[rstd])
        yield
        cx.op("act", lambda e: e.activation(out=rstd.ap, in_=rstd.ap, func=AF.Sqrt), reads=[rstd], writes=[rstd])
        cx.op("dve", lambda e: e.reciprocal(out=rstd.ap, in_=rstd.ap), reads=[rstd], writes=[rstd])
        yield
        for h2 in range(2):
            sl_ = slice(h2 * 4, (h2 + 1) * 4)
            mb = mean.ap.unsqueeze(1).to_broadcast([128, 4, TB])
            rb_ = rstd.ap.unsqueeze(1).to_broadcast([128, 4, TB])
            cx.op("dve", lambda e: e.tensor_tensor(out=tmp.ap[:, sl_, :], in0=y.ap[:, sl_, :], in1=mb, op=ALU.subtract), reads=[y, mean], writes=[tmp])
            yield
            cx.op("pool", lambda e: e.tensor_tensor(out=tmp.ap[:, sl_, :], in0=tmp.ap[:, sl_, :], in1=rb_, op=ALU.mult), reads=[tmp, rstd], writes=[tmp])
            yield
        for kc in range(8):
            cx.op("dve", lambda e, kc=kc: e.tensor_scalar(out=y.ap[:, kc, :], in0=tmp.ap[:, kc, :], scalar1=gcol.ap[:, kc:kc + 1],
                                                          scalar2=bcol.ap[:, kc:kc + 1], op0=ALU.mult, op1=ALU.add),
                  reads=[tmp, gcol, bcol], writes=[y])
            yield
        for h2 in range(2):
            sl_ = slice(h2 * 4, (h2 + 1) * 4)
            cx.op("act", lambda e: e.activation(out=outb.ap[:, sl_, :], in_=y.ap[:, sl_, :], func=AF.Identity), reads=[y], writes=[outb])
            yield

    def load_cols(sc, src, l, name, n=8):
        b = sc.buf([128, n], F32, name)
        cx.dma("sp", b.ap, src[l].rearrange("(c p) -> p c", p=128), writes=[b], allow_slow_non_contiguous=True)
        return b

    def phase_mix(l):
        sc = Scope(cx)
        wbr = sc.buf([128, 12, D], BF16, "wbr")
        cx.dma("sp", wbr.ap, wb_branch[l].rearrange("(j p) n -> p j n", p=128), reads=B_wb[("branch", l)], writes=[wbr])
        wgd = [sc.buf([128, 8, 3, 128], BF16, f"wgd{i}") for i in range(2)]
        wgsrc = kview(wb_in[l])[:, :, 3080:6152].rearrange("p kc (k3 dc j) -> p kc k3 dc j", k3=3, dc=8)
        wo = sc.buf([128, 8, D], BF16, "wo")
        cx.dma("sp", wo.ap, kview(wb_out[l]), reads=B_wb[("out", l)], writes=[wo])
        bg = load_cols(sc, b_gate, l, "bg", 24)
        g1 = load_cols(sc, ln1_g, l, "g1")
        b1 = load_cols(sc, ln1_b, l, "b1")
        xb = sc.buf([128, 8, TB], BF16, "mxb")
        xr = [sc.buf([128, TB], F32, f"mxr{i}") for i in range(2)]
        ys = sc.buf([128, 12, TB], BF16, "mys")
        mixb = sc.buf([128, 8, TB], BF16, "mixb")
        yvs = [sc.buf([128, 8, TB], F32, f"yv{i}") for i in range(2)]
        o16 = sc.buf([128, 8, TB], BF16, "mo16")
        tmp = sc.buf([128, 8, TB], F32, "lntmp")
        sbf = (sc.buf([128, 8, TB], BF16, "lnybf"), sc.buf([128, 8, TB], BF16, "lnysq"))
        stat = (sc.buf([128, TB], F32, "mean"), sc.buf([128, TB], F32, "rstd"))
        gsb = [sc.buf([128, TB], F32, f"gsb{i}") for i in range(3)]
        acc = [sc.buf([128, TB], F32, f"acc{i}") for i in range(2)]
        xres_src = xT if l == 0 else XRES
        st = {"n": 0, "nr": 0, "nw": 0}

        def genA(t):
            x_, y_ = xb, ys
            yv = yvs[t % 2]
            cx.dma("sp", x_.ap, kview(XBF)[:, :, blk(t)], reads=[B_XBF[t]], writes=[x_])
            cx.dma("sp", y_.ap, YS.rearrange("(j p) t -> p j t", p=128)[:, :, blk(t)], reads=B_YS, writes=[y_])
            for dc in range(8):
                a_ = acc[dc % 2]
                wg_ = wgd[st["nw"] % 2]
                st["nw"] += 1
                for k3_ in range(3):
                    cx.dma("sp", wg_.ap[:, :, k3_, :], wgsrc[:, :, k3_, dc, :], reads=B_wb[("in", l)], writes=[wg_])
                for k3 in range(3):
                    n = st["n"]
                    st["n"] += 1
                    pp, pg = PS[(2 * n) % 6], PS[(2 * n + 1) % 6]
                    g_ = gsb[n % 3]

                    def mmp(e, pp=pp, k3=k3, dc=dc, y_=y_):
                        for kc in range(4):
                            i = e.matmul(pp.ap, lhsT=wbr.ap[:, k3 * 4 + kc, dc * 128:(dc + 1) * 128], rhs=y_.ap[:, k3 * 4 + kc, :],
                                         start=(kc == 0), stop=(kc == 3))
                        return i

                    def mmg(e, pg=pg, k3=k3, x_=x_, wg_=wg_):
                        for kc in range(8):
                            i = e.matmul(pg.ap, lhsT=wg_.ap[:, kc, k3, :], rhs=x_.ap[:, kc, :], start=(kc == 0), stop=(kc == 7))
                        return i
                    cx.op("pe", mmg, reads=[wg_, x_], writes=[pg])
                    cx.op("pe", mmp, reads=[wbr, y_], writes=[pp])
                    cx.op("act", lambda e, pg=pg, g_=g_, k3=k3, dc=dc: e.activation(out=g_.ap, in_=pg.ap, func=AF.Sigmoid,
                                                                                   bias=bg.ap[:, k3 * 8 + dc:k3 * 8 + dc + 1], scale=1.0),
                          reads=[pg, bg], writes=[g_])
                    if k3 == 0:
                        cx.op("dve", lambda e, pp=pp, g_=g_, a_=a_: e.tensor_tensor(out=a_.ap, in0=pp.ap, in1=g_.ap, op=ALU.mult),
                              reads=[pp, g_], writes=[a_])
                    else:
                        cx.op("dve", lambda e, pp=pp, g_=g_: e.tensor_tensor(out=g_.ap, in0=pp.ap, in1=g_.ap, op=ALU.mult),
                              reads=[pp, g_], writes=[g_])
                        if k3 == 1:
                            cx.op("pool", lambda e, g_=g_, a_=a_: e.tensor_tensor(out=a_.ap, in0=a_.ap, in1=g_.ap, op=ALU.add),
                                  reads=[a_, g_], writes=[a_])
                        else:
                            cx.op("pool", lambda e, g_=g_, a_=a_, dc=dc: e.tensor_tensor(out=mixb.ap[:, dc, :], in0=a_.ap, in1=g_.ap, op=ALU.add),
                                  reads=[a_, g_], writes=[mixb])
                    yield
            for dc in range(8):
                po = PS[6 + dc % 2]
                r_ = xr[st["nr"] % 2]
                st["nr"] += 1
                cx.dma("sp", r_.ap, xres_src[dc * 128:(dc + 1) * 128, blk(t)], reads=[B_XRES[t]], writes=[r_])

                def mmo(e, po=po, dc=dc):
                    for kc in range(8):
                        i = e.matmul(po.ap, lhsT=wo.ap[:, kc, dc * 128:(dc + 1) * 128], rhs=mixb.ap[:, kc, :], start=(kc == 0), stop=(kc == 7))
                    return i
                cx.op("pe", mmo, reads=[wo, mixb], writes=[po])
                cx.op("dve", lambda e, po=po, dc=dc, r_=r_: e.scalar_tensor_tensor(out=yv.ap[:, dc, :], in0=r_.ap, scalar=float(ALPHA),
                                                                                 in1=po.ap, op0=ALU.mult, op1=ALU.add),
                      reads=[r_, po], writes=[yv])
                yield

        def genB(t):
            yv = yvs[t % 2]
            yield from layer_norm_gen(yv, g1, b1, o16, tmp, stat, sbf)
            cx.dma("sp", kview(X1RES)[:, :, blk(t)], yv.ap, reads=[yv], writes=[B_X1RES[t]])
            cx.dma("sp", kview(X1BF)[:, :, blk(t)], o16.ap, reads=[o16], writes=[B_X1BF[t]])
            yield
        run_interleaved([genA(0)])
        for t in range(NB):
            run_interleaved([genA(t + 1) if t + 1 < NB else None, genB(t)])
        cx.barrier(skip_sw=True)
        sc.close()

    def phase_ffn(l, last):
        sc = Scope(cx)
        wdn = sc.buf([128, NHC, D], BF16, "wdn")
        cx.dma("sp", wdn.ap, wb_d[l].rearrange("(j p) n -> p j n", p=128), reads=B_wb[("d", l)], writes=[wdn])
        g2 = load_cols(sc, ln2_g, l, "g2")
        b2 = load_cols(sc, ln2_b, l, "b2")
        xb = sc.buf([128, 8, TB], BF16, "fxb")
        hT = sc.buf([128, NHC, TB], BF16, "hT")
        wgu = [sc.buf([128, 2, 8, 256], BF16, f"wgu{i}") for i in range(2)]
        sl = [sc.buf([128, TB], F32, f"sl{i}") for i in range(2)]
        xr = [sc.buf([128, TB], F32, f"fxr{i}") for i in range(2)]
        yvs = [sc.buf([128, 8, TB], F32, f"fyv{i}") for i in range(2)]
        tmp = sc.buf([128, 8, TB], F32, "flntmp")
        sbf = (sc.buf([128, 8, TB], BF16, "flnybf"), sc.buf([128, 8, TB], BF16, "flnysq"))
        o16 = sc.buf([128, 8, TB], BF16, "fo16")
        stat = (sc.buf([128, TB], F32, "fmean"), sc.buf([128, TB], F32, "frstd"))
        st = {"n": 0, "nr": 0, "nw": 0}

        def genA(t):
            yv = yvs[t % 2]
            cx.dma("sp", xb.ap, kview(X1BF)[:, :, blk(t)], reads=[B_X1BF[t]], writes=[xb])
            for hp in range(NHC // 2):
                w = wgu[st["nw"] % 2]
                st["nw"] += 1
                cx.dma("sp", w.ap[:, 0], kview(wb_g[l])[:, :, hp * 256:(hp + 1) * 256], reads=B_wb[("g", l)], writes=[w])
                cx.dma("sp", w.ap[:, 1], kview(wb_u[l])[:, :, hp * 256:(hp + 1) * 256], reads=B_wb[("u", l)], writes=[w])
                for hh in range(2):
                    hc = hp * 2 + hh
                    n = st["n"]
                    st["n"] += 1
                    pg, pu = PS[(2 * n) % 6], PS[(2 * n + 1) % 6]
                    s_ = sl[n % 2]

                    def mmg(e, pg=pg, w=w, hh=hh):
                        for kc in range(8):
                            i = e.matmul(pg.ap, lhsT=w.ap[:, 0, kc, hh * 128:(hh + 1) * 128], rhs=xb.ap[:, kc, :], start=(kc == 0), stop=(kc == 7))
                        return i

                    def mmu(e, pu=pu, w=w, hh=hh):
                        for kc in range(8):
                            i = e.matmul(pu.ap, lhsT=w.ap[:, 1, kc, hh * 128:(hh + 1) * 128], rhs=xb.ap[:, kc, :], start=(kc == 0), stop=(kc == 7))
                        return i
                    cx.op("pe", mmg, reads=[w, xb], writes=[pg])
                    cx.op("pe", mmu, reads=[w, xb], writes=[pu])
                    cx.op("act", lambda e, pg=pg, s_=s_: e.activation(out=s_.ap, in_=pg.ap, func=AF.Silu), reads=[pg], writes=[s_])
                    cx.op("dve", lambda e, pu=pu, s_=s_, hc=hc: e.tensor_tensor(out=hT.ap[:, hc, :], in0=pu.ap, in1=s_.ap, op=ALU.mult),
                          reads=[pu, s_], writes=[hT])
                    yield
            for dc in range(8):
                po = PS[6 + dc % 2]
                r_ = xr[st["nr"] % 2]
                st["nr"] += 1
                cx.dma("sp", r_.ap, X1RES[dc * 128:(dc + 1) * 128, blk(t)], reads=[B_X1RES[t]], writes=[r_])

                def mmo(e, po=po, dc=dc):
                    for hc in range(NHC):
                        i = e.matmul(po.ap, lhsT=wdn.ap[:, hc, dc * 128:(dc + 1) * 128], rhs=hT.ap[:, hc, :], start=(hc == 0), stop=(hc == NHC - 1))
                    return i
                cx.op("pe", mmo, reads=[wdn, hT], writes=[po])
                cx.op("dve", lambda e, po=po, dc=dc, r_=r_: e.scalar_tensor_tensor(out=yv.ap[:, dc, :], in0=r_.ap, scalar=float(ALPHA),
                                                                                 in1=po.ap, op0=ALU.mult, op1=ALU.add),
                      reads=[r_, po], writes=[yv])
                yield

        def genB(t):
            yv = yvs[t % 2]
            yield from layer_norm_gen(yv, g2, b2, o16, tmp, stat, sbf)
            if last:
                cx.dma("sp", kview(outT)[:, :, blk(t)], yv.ap, reads=[yv], writes=[B_OUT])
            else:
                cx.dma("sp", kview(XRES)[:, :, blk(t)], yv.ap, reads=[yv], writes=[B_XRES[t]])
                cx.dma("sp", kview(XBF)[:, :, blk(t)], o16.ap, reads=[o16], writes=[B_XBF[t]])
            yield
        run_interleaved([genA(0)])
        for t in range(NB):
            run_interleaved([genA(t + 1) if t + 1 < NB else None, genB(t)])
        cx.barrier(skip_sw=True)
        sc.close()

    cx.barrier(skip_sw=True)
    for l in range(n_layers):
        if stop_after == ("setup", l):
            break
        phase_proj(l)
        if l + 1 < n_layers:
            convert_layer(l + 1)
        if stop_after == ("proj", l):
            break
        phase_attn(l)
        if stop_after == ("attn", l):
            break
        phase_lru(l)
        if stop_after == ("lru", l):
            break
        phase_s5(l)
        if stop_after in (("s5", l), ("s5prep", l)):
            break
        phase_mix(l)
        if stop_after == ("mix", l):
            break
        phase_ffn(l, last=(l == n_layers - 1))
    cx.barrier()
    return nc


INPUT_ORDER = ["w_in", "w_branch", "w_out", "w_ffn_gate", "w_ffn_up", "w_ffn_down", "s5_w_glu", "lru_w_a", "lru_w_x",
               "b_f", "b_gate", "s5_a_re", "s5_a_im", "s5_log_dt", "s5_b_re", "s5_b_im", "s5_c_re", "s5_c_im", "s5_d",
               "s5_b_glu", "lru_conv_w", "lru_conv_b", "lru_b_a", "lru_b_x", "lru_lambda", "ln1_g", "ln1_b", "ln2_g", "ln2_b"]


def layout_inputs(inputs, n_layers=DEPTH):
    f = lambda a: np.ascontiguousarray(np.asarray(a, dtype=np.float32)[:n_layers])
    shared = {}
    for k in INPUT_ORDER:
        a = f(inputs[k])
        if k == "w_branch":
            a = a.reshape(n_layers, 1536, D)
        elif k in ("s5_c_re", "s5_c_im"):
            a = a.reshape(n_layers, 512, 64)
        elif k in ("lru_b_a", "lru_b_x"):
            a = a.reshape(n_layers, 512)
        shared[k] = np.ascontiguousarray(a)
    return shared


def kernel(**inputs):
    x = np.asarray(inputs["x"], dtype=np.float32)
    shared = layout_inputs(inputs)
    nc = bass.Bass("TRN2", target_bir_lowering=False)
    build(nc)
    in_maps = []
    for c in range(8):
        m = dict(shared)
        m["xT"] = np.ascontiguousarray(x[c % 4].T)
        in_maps.append(m)
    res = run_bass_kernel_spmd(nc, in_maps, core_ids=list(range(8)))
    out = np.stack([np.ascontiguousarray(res.results[b]["outT"].T) for b in range(4)], axis=0)
    return out.astype(np.float32)
```

```python
import math
import numpy as np
import concourse.bass as bass
import concourse.mybir as mybir
from concourse.bass_utils import run_bass_kernel_spmd

F32 = mybir.dt.float32
BF16 = mybir.dt.bfloat16
AF = mybir.ActivationFunctionType
ALU = mybir.AluOpType

D = 1024
L = 4096
DEPTH = 4
NB = 8
TB = 512
IN_TOTAL = 6152
FFN = 2816
NHC = 22
ALPHA = (2.0 * DEPTH) ** 0.25
LN_EPS = 1e-5
MAGIC = 12582912.0
TWO_PI = 2.0 * math.pi


class Buf:
    __slots__ = ("ap", "w", "r", "name")

    def __init__(self, ap, name=""):
        self.ap = ap
        self.w = {}
        self.r = {}
        self.name = name


class Ctx:
    def __init__(self, nc):
        self.nc = nc
        self.E = {"pe": nc.tensor, "act": nc.scalar, "dve": nc.vector, "pool": nc.gpsimd, "sp": nc.sync}
        self.sem = {}
        self.cnt = {}
        self.nsem = 0
        for e in ("pe", "act", "dve", "pool"):
            self._new_sem(e)
        self.seen = {e: {} for e in self.E}
        self.dma_sems = {"sp": [nc.alloc_semaphore(f"dq{i}") for i in range(60)],
                         "pool": [nc.alloc_semaphore(f"dqs{i}") for i in range(16)]}
        self.dma_cnt = {k: [0] * len(v) for k, v in self.dma_sems.items()}
        self.dma_rr = {"sp": 0, "pool": 0}
        self.uid = 0

    def _new_sem(self, e):
        s = self.nc.alloc_semaphore(f"s_{e}_{self.nsem}")
        self.nsem += 1
        self.sem[e] = s
        self.cnt[e] = 0
        if not hasattr(self, "own"):
            self.own = {}
            self.semobj = {}
        self.own.setdefault(e, set()).add(self._key(s))

    def _key(self, s):
        k = id(s)
        self.semobj[k] = s
        return k

    def _wait(self, e, deps):
        seen = self.seen[e]
        own = self.own.get(e, ())
        for k, v in deps.items():
            if seen.get(k, 0) >= v:
                continue
            if k in own:
                cur = self.sem.get(e)
                if e == "pe" or cur is None or k != id(cur) or v <= self.cnt[e] - 1:
                    continue
            self.E[e].wait_ge(self.semobj[k], v)
            seen[k] = v

    @staticmethod
    def _merge(dst, src):
        for k, v in src.items():
            if dst.get(k, 0) < v:
                dst[k] = v

    def _deps(self, reads, writes):
        deps = {}
        for b in reads:
            self._merge(deps, b.w)
        for b in writes:
            self._merge(deps, b.w)
            self._merge(deps, b.r)
        return deps

    def _commit(self, tok, reads, writes):
        for b in reads:
            self._merge(b.r, tok)
        for b in writes:
            b.w = dict(tok)
            b.r = {}

    def op(self, e, emit, reads=(), writes=()):
        self._wait(e, self._deps(reads, writes))
        ins = emit(self.E[e])
        if self.cnt[e] >= 30000:
            self._new_sem(e)
        s = self.sem[e]
        self.cnt[e] += 1
        ins.then_inc(s, 1)
        tok = {self._key(s): self.cnt[e]}
        self._commit(tok, reads, writes)
        return tok

    def dma(self, e, out, in_, reads=(), writes=(), **kw):
        self._wait(e, self._deps(reads, writes))
        sems, cnts = self.dma_sems[e], self.dma_cnt[e]
        i = self.dma_rr[e]
        self.dma_rr[e] = (i + 1) % len(sems)
        if cnts[i] >= 30000:
            sems[i] = self.nc.alloc_semaphore(f"dqx{self.nsem}")
            self.nsem += 1
            cnts[i] = 0
        s = sems[i]
        if cnts[i] > 0:
            self._wait(e, {self._key(s): cnts[i]})
        cnts[i] += 16
        self.E[e].dma_start(out=out, in_=in_, **kw).then_inc(s, 16)
        tok = {self._key(s): cnts[i]}
        self._commit(tok, reads, writes)
        return tok

    def barrier(self, skip_sw=False):
        allt = {}
        for e in ("pe", "act", "dve", "pool"):
            if self.cnt[e] > 0:
                allt[self._key(self.sem[e])] = self.cnt[e]
        for q in self.dma_sems:
            for i, s in enumerate(self.dma_sems[q]):
                if self.dma_cnt[q][i] > 0 and not (q == "pool" and skip_sw):
                    allt[self._key(s)] = self.dma_cnt[q][i]
        for e in self.E:
            self._wait(e, allt)


class Scope:
    def __init__(self, cx):
        self.cx = cx
        self.guards = []

    def sb(self, shape, dt=F32, name=None):
        self.cx.uid += 1
        g = self.cx.nc.sbuf_tensor(f"{name or 't'}_{self.cx.uid}", list(shape), dt)
        t = g.__enter__()
        self.guards.append(g)
        return t.ap()

    def buf(self, shape, dt=F32, name=None):
        return Buf(self.sb(shape, dt, name), name or "")

    def close(self):
        for g in reversed(self.guards):
            g.__exit__(None, None, None)
        self.guards = []


def build(nc, n_layers=DEPTH, dbg=False, stop_after=None):
    cx = Ctx(nc)
    kind_dbg = "ExternalOutput" if dbg else "Internal"

    def din(name, shape):
        return nc.dram_tensor(name, list(shape), F32, kind="ExternalInput").ap()

    def dscr(name, shape, dt, k="Internal"):
        return nc.dram_tensor(name, list(shape), dt, kind=k).ap()

    xT = din("xT", [D, L])
    w_in = din("w_in", [n_layers, D, IN_TOTAL])
    w_branch = din("w_branch", [n_layers, 1536, D])
    w_out = din("w_out", [n_layers, D, D])
    w_g = din("w_ffn_gate", [n_layers, D, FFN])
    w_u = din("w_ffn_up", [n_layers, D, FFN])
    w_d = din("w_ffn_down", [n_layers, FFN, D])
    w_glu = din("s5_w_glu", [n_layers, 512, 512])
    lru_w_a = din("lru_w_a", [n_layers, 8, 64, 64])
    lru_w_x = din("lru_w_x", [n_layers, 8, 64, 64])
    b_f = din("b_f", [n_layers, 8])
    b_gate = din("b_gate", [n_layers, 3072])
    s5_a_re = din("s5_a_re", [n_layers, 32, 64])
    s5_a_im = din("s5_a_im", [n_layers, 32, 64])
    s5_log_dt = din("s5_log_dt", [n_layers, 32])
    s5_b_re = din("s5_b_re", [n_layers, 32, 64, 16])
    s5_b_im = din("s5_b_im", [n_layers, 32, 64, 16])
    s5_c_re = din("s5_c_re", [n_layers, 512, 64])
    s5_c_im = din("s5_c_im", [n_layers, 512, 64])
    s5_d = din("s5_d", [n_layers, 512])
    s5_b_glu = din("s5_b_glu", [n_layers, 512])
    lru_conv_w = din("lru_conv_w", [n_layers, 4, 512])
    lru_conv_b = din("lru_conv_b", [n_layers, 512])
    lru_b_a = din("lru_b_a", [n_layers, 512])
    lru_b_x = din("lru_b_x", [n_layers, 512])
    lru_lambda = din("lru_lambda", [n_layers, 512])
    ln1_g = din("ln1_g", [n_layers, D])
    ln1_b = din("ln1_b", [n_layers, D])
    ln2_g = din("ln2_g", [n_layers, D])
    ln2_b = din("ln2_b", [n_layers, D])
    outT = nc.dram_tensor("outT", [D, L], F32, kind="ExternalOutput").ap()

    wb_in = dscr("wb_in", [n_layers, D, IN_TOTAL], BF16)
    wb_branch = dscr("wb_branch", [n_layers, 1536, D], BF16)
    wb_out = dscr("wb_out", [n_layers, D, D], BF16)
    wb_g = dscr("wb_g", [n_layers, D, FFN], BF16)
    wb_u = dscr("wb_u", [n_layers, D, FFN], BF16)
    wb_d = dscr("wb_d", [n_layers, FFN, D], BF16)
    wb_glu = dscr("wb_glu", [n_layers, 512, 512], BF16)
    XBF = dscr("XBF", [D, L], BF16)
    XRES = dscr("XRES", [D, L], F32, kind_dbg)
    X1BF = dscr("X1BF", [D, L], BF16)
    X1RES = dscr("X1RES", [D, L], F32, kind_dbg)
    UT = dscr("UT", [512, L], BF16, kind_dbg)
    XL = dscr("XL", [512, L], F32, kind_dbg)
    GL = dscr("GL", [512, L], F32, kind_dbg)
    QA = dscr("QA", [8, 70, L], BF16, kind_dbg)
    KA = dscr("KA", [8, 70, L], BF16, kind_dbg)
    VA = dscr("VA", [8, 128, 32, 128], BF16, kind_dbg)
    YS = dscr("YS", [1536, L], BF16, kind_dbg)

    B_wb = {}
    for nm in ("in", "branch", "out", "g", "u", "d", "glu"):
        for l in range(n_layers):
            B_wb[(nm, l)] = []
    B_XBF = [Buf(None, f"XBF{t}") for t in range(NB)]
    B_XRES = [Buf(None, f"XRES{t}") for t in range(NB)]
    B_X1BF = [Buf(None, f"X1BF{t}") for t in range(NB)]
    B_X1RES = [Buf(None, f"X1RES{t}") for t in range(NB)]
    B_UT = Buf(None, "UT")
    B_XL = Buf(None, "XL")
    B_GL = Buf(None, "GL")
    B_QA = Buf(None, "QA")
    B_KA = Buf(None, "KA")
    B_VA = Buf(None, "VA")
    B_YS = [Buf(None, f"YS{k}") for k in range(3)]
    B_OUT = Buf(None, "out")

    PS = [Buf(nc.alloc_psum_tensor(f"psb{i}", [128, 512], F32).ap(), f"ps{i}") for i in range(8)]

    cs = Scope(cx)
    identf = cs.buf([128, 128], F32, "identf")
    identb = cs.buf([128, 128], BF16, "identb")
    onesD = cs.buf([128, 128], BF16, "onesD")
    ones1 = cs.buf([128, 64], F32, "ones1")
    mask8 = cs.buf([128, 128], F32, "mask8")
    negtri = cs.buf([128, 128], BF16, "negtri")
    SELD = dscr("SELD", [2, 128, 8, 8, 128], BF16)
    B_SELD = Buf(None, "SELD")
    cs0 = Scope(cx)
    Sel = cs0.buf([128, 8, 8, 128], BF16, "Sel")
    SelT = cs0.buf([128, 8, 8, 128], BF16, "SelT")

    def pool_fill(buf, val):
        cx.op("pool", lambda e: e.memset(buf.ap, val), writes=[buf])

    def pool_sel(buf, ap, pattern, cmp, base, cm, fill=0.0):
        cx.op("pool", lambda e: e.affine_select(out=ap, in_=ap, pattern=pattern, compare_op=cmp, fill=fill,
                                                base=base, channel_multiplier=cm), reads=[buf], writes=[buf])

    pool_fill(identf, 1.0)
    pool_sel(identf, identf.ap, [[1, 128]], ALU.is_equal, 0, -1)
    pool_fill(identb, 1.0)
    pool_sel(identb, identb.ap, [[1, 128]], ALU.is_equal, 0, -1)
    pool_fill(onesD, 1.0 / D)
    pool_fill(ones1, 1.0)
    pool_fill(mask8, 1.0)
    pool_sel(mask8, mask8.ap.rearrange("p (t c) -> p t c", c=16), [[16, 8], [0, 16]], ALU.is_ge, 15, -1)
    pool_fill(negtri, 0.0)
    pool_sel(negtri, negtri.ap, [[1, 128]], ALU.is_ge, 0, -1, fill=-30000.0)
    pool_fill(Sel, 1.0)
    for gg in range(8):
        a4 = Sel.ap[:, gg, :, :].rearrange("p t (u c) -> p t u c", c=16)
        pool_sel(Sel, a4, [[0, 8], [0, 8], [-1, 16]], ALU.is_equal, -16 * gg, 1)
        pool_sel(Sel, a4, [[-1, 8], [1, 8], [0, 16]], ALU.is_equal, 0, 0)
    pool_fill(SelT, 1.0)
    for gg in range(8):
        a3 = SelT.ap[:, gg, :, :]
        pool_sel(SelT, a3, [[16, 8], [1, 128]], ALU.is_equal, -16 * gg, -1)
        pool_sel(SelT, a3, [[-16, 8], [0, 128]], ALU.is_ge, 0, 1)
        pool_sel(SelT, a3, [[16, 8], [0, 128]], ALU.is_ge, 15, -1)

    cx.dma("sp", SELD[0], Sel.ap, reads=[Sel], writes=[B_SELD])
    cx.dma("sp", SELD[1], SelT.ap, reads=[SelT], writes=[B_SELD])
    cx.barrier()
    cs0.close()

    def convert(src, dst, rows, key):
        r = 0
        while r < rows:
            n = min(128, rows - r)
            bch = Buf(None, "wbch")
            B_wb[key].append(bch)
            cx.dma("pool", dst[r:r + n, :], src[r:r + n, :], writes=[bch])
            r += n

    def convert_layer(l):
        convert(w_in[l], wb_in[l], D, ("in", l))
        convert(w_glu[l], wb_glu[l], 512, ("glu", l))
        convert(w_branch[l], wb_branch[l], 1536, ("branch", l))
        convert(w_out[l], wb_out[l], D, ("out", l))
        convert(w_g[l], wb_g[l], D, ("g", l))
        convert(w_u[l], wb_u[l], D, ("u", l))
        convert(w_d[l], wb_d[l], FFN, ("d", l))

    for t in range(NB):
        for kc in range(8):
            cx.dma("pool", XBF[kc * 128:(kc + 1) * 128, t * TB:(t + 1) * TB],
                   xT[kc * 128:(kc + 1) * 128, t * TB:(t + 1) * TB], writes=[B_XBF[t]])

    convert_layer(0)

    def kview(ap2d):
        return ap2d.rearrange("(kc p) n -> p kc n", p=128)

    def blk(t):
        return slice(t * TB, (t + 1) * TB)

    def phase_proj(l):
        sc_fg = Scope(cx)
        fgT = sc_fg.buf([8, L], F32, "fgT")
        sc = Scope(cx)
        xb = [sc.buf([128, 8, TB], BF16, f"xb{t}") for t in range(NB)]
        for t in range(NB):
            cx.dma("sp", xb[t].ap, kview(XBF)[:, :, blk(t)], reads=[B_XBF[t]], writes=[xb[t]])
        wt = [sc.buf([128, 8, 512], BF16, f"wt{i}") for i in range(2)]
        wfg = sc.buf([128, 8, 8], BF16, "wfg")
        cx.dma("sp", wfg.ap, kview(wb_in[l])[:, :, 3072:3080], reads=B_wb[("in", l)], writes=[wfg])
        st32 = [sc.buf([128, 4, TB], F32, f"st32_{i}") for i in range(2)]
        st16 = [sc.buf([128, 4, TB], BF16, f"st16_{i}") for i in range(2)]
        stqk = [sc.buf([128, 4, TB], BF16, f"stqk_{i}") for i in range(2)]
        vst = [sc.buf([128, 8, 128], BF16, f"vst_{i}") for i in range(2)]
        for v in vst:
            cx.op("pool", lambda e, v=v: e.memset(v.ap, 1.0), writes=[v])
        nps = 0
        nst = 0
        for cg in range(6):
            w = wt[cg % 2]
            cx.dma("sp", w.ap, kview(wb_in[l])[:, :, cg * 512:(cg + 1) * 512], reads=B_wb[("in", l)], writes=[w])
            if cg < 3:
                for t in range(NB):
                    stb = (st16 if cg == 0 else st32)[nst % 2]
                    nst += 1
                    for ct in range(4):
                        ps = PS[nps % 4]
                        nps += 1

                        def mm(e, ps=ps, ct=ct, t=t, w=w):
                            for kc in range(8):
                                i = e.matmul(ps.ap, lhsT=w.ap[:, kc, ct * 128:(ct + 1) * 128], rhs=xb[t].ap[:, kc, :],
                                             start=(kc == 0), stop=(kc == 7))
                            return i
                        cx.op("pe", mm, reads=[w, xb[t]], writes=[ps])
                        fn = AF.Gelu_apprx_tanh if cg == 2 else AF.Identity
                        cx.op("act", lambda e, ps=ps, stb=stb, ct=ct, fn=fn: e.activation(out=stb.ap[:, ct, :], in_=ps.ap, func=fn),
                              reads=[ps], writes=[stb])
                    dst, bd = [(UT, B_UT), (XL, B_XL), (GL, B_GL)][cg]
                    cx.dma("sp", dst.rearrange("(c p) t -> p c t", p=128)[:, :, blk(t)], stb.ap, reads=[stb], writes=[bd])
            elif cg < 5:
                for t in range(NB):
                    stb = stqk[nst % 2]
                    nst += 1
                    for hp in range(4):
                        ps = PS[nps % 4]
                        nps += 1

                        def mm(e, ps=ps, hp=hp, t=t, w=w):
                            for kc in range(8):
                                i = e.matmul(ps.ap, lhsT=w.ap[:, kc, hp * 128:(hp + 1) * 128], rhs=xb[t].ap[:, kc, :],
                                             start=(kc == 0), stop=(kc == 7))
                            return i
                        cx.op("pe", mm, reads=[w, xb[t]], writes=[ps])
                        sc_ = 0.125 if cg == 3 else 1.0
                        cx.op("act", lambda e, ps=ps, stb=stb, hp=hp, sc_=sc_: e.activation(out=stb.ap[:, hp, :], in_=ps.ap,
                                                                                           func=AF.Identity, scale=sc_),
                              reads=[ps], writes=[stb])
                    dst, bd = (QA, B_QA) if cg == 3 else (KA, B_KA)
                    for two in range(2):
                        cx.dma("sp", dst[two::2, 0:64, blk(t)].rearrange("hp d t -> d hp t"), stb.ap[two * 64:(two + 1) * 64, :, :],
                               reads=[stb], writes=[bd])
            else:
                for tt in range(32):
                    ps = PS[nps % 4]
                    nps += 1
                    t = tt // 4
                    vs = vst[tt % 2]

                    def mm(e, ps=ps, tt=tt, t=t, w=w):
                        o = (tt % 4) * 128
                        for kc in range(8):
                            i = e.matmul(ps.ap, lhsT=xb[t].ap[:, kc, o:o + 128], rhs=w.ap[:, kc, :],
                                         start=(kc == 0), stop=(kc == 7))
                        return i
                    cx.op("pe", mm, reads=[w, xb[t]], writes=[ps])
                    cx.op("act", lambda e, ps=ps, vs=vs: e.activation(out=vs.ap[:, :, 0:64], in_=ps.ap.rearrange("p (h d) -> p h d", d=64),
                                                                      func=AF.Identity), reads=[ps], writes=[vs])
                    cx.dma("sp", VA[:, :, tt, :].rearrange("h p e -> p h e"), vs.ap, reads=[vs], writes=[B_VA])
        for t in range(NB):
            ps = PS[nps % 4]
            nps += 1

            def mm(e, ps=ps, t=t):
                for kc in range(8):
                    i = e.matmul(ps.ap[0:8, :], lhsT=wfg.ap[:, kc, :], rhs=xb[t].ap[:, kc, :], start=(kc == 0), stop=(kc == 7))
                return i
            cx.op("pe", mm, reads=[wfg, xb[t]], writes=[ps])
            cx.op("act", lambda e, ps=ps, t=t: e.activation(out=fgT.ap[:, blk(t)], in_=ps.ap[0:8, :], func=AF.Identity),
                  reads=[ps], writes=[fgT])
        cx.barrier(skip_sw=True)
        sc.close()
        sc = Scope(cx)
        bf = sc.buf([8, 1], F32, "bf")
        cx.dma("sp", bf.ap, b_f[l].rearrange("(h o) -> h o", o=1), writes=[bf])
        nbf = sc.buf([8, 1], F32, "nbf")
        cx.op("dve", lambda e: e.tensor_scalar(out=nbf.ap, in0=bf.ap, scalar1=-1.0, scalar2=None, op0=ALU.mult), reads=[bf], writes=[nbf])
        one8 = sc.buf([8, 1], F32, "one8")
        cx.op("dve", lambda e: e.memset(one8.ap, 1.0), writes=[one8])
        ex = sc.buf([8, L], F32, "ex")
        cx.op("act", lambda e: e.activation(out=ex.ap, in_=fgT.ap, func=AF.Exp, bias=nbf.ap, scale=-1.0), reads=[fgT, nbf], writes=[ex])
        cx.op("act", lambda e: e.activation(out=ex.ap, in_=ex.ap, func=AF.Ln, bias=one8.ap, scale=1.0), reads=[ex, one8], writes=[ex])
        csum = sc.buf([8, L], F32, "csum")
        cx.op("dve", lambda e: e.tensor_tensor_scan(out=csum.ap, data0=one8.ap.to_broadcast([8, L]), data1=ex.ap, initial=0.0,
                                                    op0=ALU.mult, op1=ALU.add), reads=[ex, one8], writes=[csum])
        pcs = [sc.buf([8, L], BF16, f"pc{j}") for j in range(3)]
        ncs = [sc.buf([8, L], BF16, f"nc{j}") for j in range(3)]
        res = ex
        cx.op("dve", lambda e: e.tensor_copy(out=pcs[0].ap, in_=csum.ap), reads=[csum], writes=[pcs[0]])
        cx.op("dve", lambda e: e.tensor_tensor(out=res.ap, in0=csum.ap, in1=pcs[0].ap, op=ALU.subtract), reads=[csum, pcs[0]], writes=[res])
        cx.op("dve", lambda e: e.tensor_copy(out=pcs[1].ap, in_=res.ap), reads=[res], writes=[pcs[1]])
        cx.op("dve", lambda e: e.tensor_tensor(out=res.ap, in0=res.ap, in1=pcs[1].ap, op=ALU.subtract), reads=[res, pcs[1]], writes=[res])
        cx.op("dve", lambda e: e.tensor_copy(out=pcs[2].ap, in_=res.ap), reads=[res], writes=[pcs[2]])
        for j in range(3):
            cx.op("dve", lambda e, j=j: e.tensor_scalar(out=ncs[j].ap, in0=pcs[j].ap, scalar1=-1.0, scalar2=None, op0=ALU.mult),
                  reads=[pcs[j]], writes=[ncs[j]])
        onesb = sc.buf([8, L], BF16, "onesb")
        cx.op("pool", lambda e: e.memset(onesb.ap, 1.0), writes=[onesb])
        for j in range(3):
            cx.dma("sp", QA[:, 64 + j, :], ncs[j].ap, reads=[ncs[j]], writes=[B_QA])
            cx.dma("sp", QA[:, 67 + j, :], onesb.ap, reads=[onesb], writes=[B_QA])
            cx.dma("sp", KA[:, 64 + j, :], onesb.ap, reads=[onesb], writes=[B_KA])
            cx.dma("sp", KA[:, 67 + j, :], pcs[j].ap, reads=[pcs[j]], writes=[B_KA])
        cx.barrier(skip_sw=True)
        sc.close()
        sc_fg.close()

    def phase_attn(l):
        sc = Scope(cx)
        qa = [sc.buf([70, L], BF16, f"qa{i}") for i in range(2)]
        ka = [sc.buf([70, L], BF16, f"ka{i}") for i in range(2)]
        va = [sc.buf([128, 32, 128], BF16, f"va{i}") for i in range(2)]
        NPT = 6
        pt = [sc.buf([128, TB], BF16, f"pt{i}") for i in range(NPT)]
        rden = [sc.buf([128, TB], F32, f"rden{i}") for i in range(4)]
        rb = [sc.buf([64, TB], F32, f"rb{i}") for i in range(4)]
        ost = [sc.buf([64, TB], BF16, f"ost{i}") for i in range(4)]

        def load_head(h):
            cx.dma("sp", qa[h % 2].ap, QA[h], reads=[B_QA], writes=[qa[h % 2]])
            cx.dma("sp", ka[h % 2].ap, KA[h], reads=[B_KA], writes=[ka[h % 2]])
            cx.dma("sp", va[h % 2].ap, VA[h], reads=[B_VA], writes=[va[h % 2]])
        items = []
        nb = 0
        for h in range(8):
            for I in range(NB):
                nkb = 4 * I + 4
                for j in range(nkb):
                    items.append((h, I, j, nkb, nb))
                nb += 1
        LA = 3

        def emit_S(i):
            h, I, j, nkb, b_ = items[i]
            c0 = 128 * max(0, j - 4 * I)
            diag = j >= 4 * I
            ps = PS[i % 4]
            k, q = ka[h % 2], qa[h % 2]

            def mm(e):
                i_ = e.matmul(ps.ap[:, c0:TB], lhsT=k.ap[:, j * 128:(j + 1) * 128], rhs=q.ap[:, I * TB + c0:(I + 1) * TB],
                              start=True, stop=not diag)
                if diag:
                    i_ = e.matmul(ps.ap[:, c0:c0 + 128], lhsT=identb.ap, rhs=negtri.ap, start=False, stop=True)
                return i_
            cx.op("pe", mm, reads=[k, q, identb, negtri], writes=[ps])

        def finalize(h, I, b_):
            po = PS[4 + b_ % 4]
            rd, r_, o_ = rden[b_ % 4], rb[b_ % 4], ost[b_ % 4]
            cx.op("dve", lambda e: e.reciprocal(out=rd.ap[64:128, :], in_=po.ap[64:128, :]), reads=[po], writes=[rd])
            cx.op("pool", lambda e: e.tensor_copy(out=r_.ap, in_=rd.ap[64:128, :]), reads=[rd], writes=[r_])
            cx.op("dve", lambda e: e.tensor_tensor(out=o_.ap, in0=po.ap[0:64, :], in1=r_.ap, op=ALU.mult), reads=[po, r_], writes=[o_])
            cx.dma("sp", YS[1024 + h * 64:1024 + (h + 1) * 64, blk(I)], o_.ap, reads=[o_], writes=[B_YS[2]])
        load_head(0)
        for i in range(min(LA, len(items))):
            emit_S(i)
        pending = []
        for i, (h, I, j, nkb, b_) in enumerate(items):
            if I == 0 and j == 0 and h + 1 < 8:
                load_head(h + 1)
            if i + LA < len(items):
                emit_S(i + LA)
            c0 = 128 * max(0, j - 4 * I)
            ps = PS[i % 4]
            p = pt[i % NPT]
            v = va[h % 2]
            po = PS[4 + b_ % 4]
            cx.op("act", lambda e: e.activation(out=p.ap[:, c0:TB], in_=ps.ap[:, c0:TB], func=AF.Exp), reads=[ps], writes=[p])
            cx.op("pe", lambda e: e.matmul(po.ap[:, c0:TB], lhsT=v.ap[:, j, :], rhs=p.ap[:, c0:TB], start=(j == 0), stop=(j == nkb - 1)),
                  reads=[v, p], writes=[po])
            pending = [(cnt - 1, args) for (cnt, args) in pending]
            for cnt, args in [x for x in pending if x[0] <= 0]:
                finalize(*args)
            pending = [x for x in pending if x[0] > 0]
            if j == nkb - 1:
                pending.append((2, (h, I, b_)))
        for cnt, args in pending:
            finalize(*args)
        cx.barrier(skip_sw=True)
        sc.close()

    def phase_lru(l):
        sc = Scope(cx)
        cw = sc.buf([128, 4, 4], F32, "cw")
        cb = sc.buf([128, 4], F32, "cb")
        ba = sc.buf([128, 4], F32, "ba")
        bx = sc.buf([128, 4], F32, "bx")
        lam = sc.buf([128, 4], F32, "lam")
        sneg = sc.buf([128, 4], F32, "sneg")
        one_c = sc.buf([128, 1], F32, "one_c")
        cx.op("dve", lambda e: e.memset(one_c.ap, 1.0), writes=[one_c])
        for k_ in range(4):
            cx.dma("sp", cw.ap[:, :, k_], lru_conv_w[l, k_].rearrange("(c p) -> p c", p=128), writes=[cw], allow_slow_non_contiguous=True)
        for (dst, src) in ((cb, lru_conv_b), (ba, lru_b_a), (bx, lru_b_x), (lam, lru_lambda)):
            cx.dma("sp", dst.ap, src[l].rearrange("(c p) -> p c", p=128), writes=[dst], allow_slow_non_contiguous=True)
        cx.op("act", lambda e: e.activation(out=sneg.ap, in_=lam.ap, func=AF.Exp, scale=-1.0), reads=[lam], writes=[sneg])
        cx.op("act", lambda e: e.activation(out=sneg.ap, in_=sneg.ap, func=AF.Ln, bias=one_c.ap, scale=1.0), reads=[sneg, one_c], writes=[sneg])
        cx.op("dve", lambda e: e.tensor_scalar(out=sneg.ap, in0=sneg.ap, scalar1=-8.0, scalar2=None, op0=ALU.mult), reads=[sneg], writes=[sneg])
        WA = sc.buf([128, 4, 128], BF16, "WA")
        WX = sc.buf([128, 4, 128], BF16, "WX")
        for Wm, src in ((WA, lru_w_a), (WX, lru_w_x)):
            cx.op("pool", lambda e, Wm=Wm: e.memset(Wm.ap, 0.0), writes=[Wm])
            for c in range(4):
                cx.dma("pool", Wm.ap[0:64, c, 0:64], src[l, 2 * c], writes=[Wm])
                cx.dma("pool", Wm.ap[64:128, c, 64:128], src[l, 2 * c + 1], writes=[Wm])
        hba = sc.buf([128, 4], F32, "hba")
        hbx = sc.buf([128, 4], F32, "hbx")
        hsn = sc.buf([128, 4], F32, "hsn")
        for dst, src in ((hba, ba), (hbx, bx), (hsn, sneg)):
            cx.op("dve", lambda e, dst=dst, src=src: e.tensor_scalar(out=dst.ap, in0=src.ap, scalar1=0.5, scalar2=None, op0=ALU.mult),
                  reads=[src], writes=[dst])
        xl = sc.buf([128, L + 3], F32, "xl")
        gl = sc.buf([128, L], F32, "gl")
        xc = sc.buf([128, L], F32, "xc")
        xcb = sc.buf([128, L], BF16, "xcb")
        a_all = sc.buf([128, L], F32, "a_all")
        tr_all = sc.buf([128, L], F32, "tr_all")
        ti_all = sc.buf([128, L], F32, "ti_all")
        h_all = sc.buf([128, L], F32, "h_all")
        yb = sc.buf([128, L], BF16, "yb")
        cx.op("pool", lambda e: e.memset(xl.ap[:, 0:3], 0.0), writes=[xl])
        n = 0
        for c in range(4):
            cx.dma("sp", xl.ap[:, 3:], XL[c * 128:(c + 1) * 128, :], reads=[B_XL], writes=[xl])
            cx.dma("sp", gl.ap, GL[c * 128:(c + 1) * 128, :], reads=[B_GL], writes=[gl])
            cx.op("dve", lambda e, c=c: e.tensor_scalar(out=xc.ap, in0=xl.ap[:, 0:L], scalar1=cw.ap[:, c, 0:1], scalar2=cb.ap[:, c:c + 1],
                                                        op0=ALU.mult, op1=ALU.add), reads=[xl, cw, cb], writes=[xc])
            for k_ in range(1, 4):
                cx.op("dve", lambda e, c=c, k_=k_: e.scalar_tensor_tensor(out=xc.ap, in0=xl.ap[:, k_:k_ + L], scalar=cw.ap[:, c, k_:k_ + 1],
                                                                         in1=xc.ap, op0=ALU.mult, op1=ALU.add), reads=[xl, cw, xc], writes=[xc])
            for hh_ in range(2):
                hs_ = slice(hh_ * (L // 2), (hh_ + 1) * (L // 2))
                cx.op("act", lambda e, hs_=hs_: e.activation(out=xcb.ap[:, hs_], in_=xc.ap[:, hs_], func=AF.Identity), reads=[xc], writes=[xcb])
            for t in range(NB):
                pa, px = PS[(2 * n) % 4], PS[(2 * n + 1) % 4]
                n += 1
                cx.op("pe", lambda e, pa=pa, c=c, t=t: e.matmul(pa.ap, lhsT=WA.ap[:, c, :], rhs=xcb.ap[:, blk(t)], start=True, stop=True),
                      reads=[WA, xcb], writes=[pa])
                cx.op("pe", lambda e, px=px, c=c, t=t: e.matmul(px.ap, lhsT=WX.ap[:, c, :], rhs=xcb.ap[:, blk(t)], start=True, stop=True),
                      reads=[WX, xcb], writes=[px])
                cx.op("act", lambda e, pa=pa, c=c, t=t: e.activation(out=tr_all.ap[:, blk(t)], in_=pa.ap, func=AF.Tanh, bias=hba.ap[:, c:c + 1], scale=0.5),
                      reads=[pa, hba], writes=[tr_all])
                cx.op("act", lambda e, px=px, c=c, t=t: e.activation(out=ti_all.ap[:, blk(t)], in_=px.ap, func=AF.Tanh, bias=hbx.ap[:, c:c + 1], scale=0.5),
                      reads=[px, hbx], writes=[ti_all])
            for hh_ in range(2):
                hs_ = slice(hh_ * (L // 2), (hh_ + 1) * (L // 2))
                cx.op("act", lambda e, c=c, hs_=hs_: e.activation(out=a_all.ap[:, hs_], in_=tr_all.ap[:, hs_], func=AF.Exp, bias=hsn.ap[:, c:c + 1],
                                                               scale=hsn.ap[:, c:c + 1]), reads=[tr_all, hsn], writes=[a_all])
            for hh_ in range(2):
                hs_ = slice(hh_ * (L // 2), (hh_ + 1) * (L // 2))
                cx.op("act", lambda e, hs_=hs_: e.activation(out=tr_all.ap[:, hs_], in_=a_all.ap[:, hs_], func=AF.Square), reads=[a_all], writes=[tr_all])
            cx.op("dve", lambda e: e.tensor_scalar(out=ti_all.ap, in0=ti_all.ap, scalar1=0.5, scalar2=0.5, op0=ALU.mult, op1=ALU.add),
                  reads=[ti_all], writes=[ti_all])
            cx.op("pool", lambda e: e.tensor_tensor(out=ti_all.ap, in0=ti_all.ap, in1=xc.ap, op=ALU.mult), reads=[ti_all, xc], writes=[ti_all])
            for hh_ in range(2):
                hs_ = slice(hh_ * (L // 2), (hh_ + 1) * (L // 2))
                cx.op("act", lambda e, hs_=hs_: e.activation(out=tr_all.ap[:, hs_], in_=tr_all.ap[:, hs_], func=AF.Sqrt, bias=one_c.ap, scale=-1.0),
                      reads=[tr_all, one_c], writes=[tr_all])
            cx.op("dve", lambda e: e.tensor_tensor(out=ti_all.ap, in0=ti_all.ap, in1=tr_all.ap, op=ALU.mult), reads=[ti_all, tr_all], writes=[ti_all])
            cx.op("dve", lambda e: e.tensor_tensor_scan(out=h_all.ap, data0=a_all.ap, data1=ti_all.ap, initial=0.0, op0=ALU.mult, op1=ALU.add),
                  reads=[a_all, ti_all], writes=[h_all])
            cx.op("dve", lambda e: e.tensor_tensor(out=yb.ap, in0=h_all.ap, in1=gl.ap, op=ALU.mult), reads=[h_all, gl], writes=[yb])
            cx.dma("sp", YS[512 + c * 128:512 + (c + 1) * 128, :], yb.ap, reads=[yb], writes=[B_YS[1]])
        cx.barrier(skip_sw=True)
        sc.close()

    def cmul(eng_a, eng_b, sc_t, outr, outi, ar, ai, br, bi, reads, w_r, w_i, negi=False):
        t1, t2 = sc_t
        cx.op(eng_a, lambda e: e.tensor_tensor(out=t1.ap, in0=ar, in1=br, op=ALU.mult), reads=reads, writes=[t1])
        cx.op(eng_b, lambda e: e.tensor_tensor(out=t2.ap, in0=ai, in1=bi, op=ALU.mult), reads=reads, writes=[t2])
        cx.op(eng_a, lambda e: e.tensor_tensor(out=outr, in0=t1.ap, in1=t2.ap, op=ALU.subtract), reads=[t1, t2], writes=[w_r])
        cx.op(eng_a, lambda e: e.tensor_tensor(out=t1.ap, in0=ar, in1=bi, op=ALU.mult), reads=reads + [w_r], writes=[t1])
        cx.op(eng_b, lambda e: e.tensor_tensor(out=t2.ap, in0=ai, in1=br, op=ALU.mult), reads=reads + [w_r], writes=[t2])
        if negi:
            cx.op(eng_a, lambda e: e.scalar_tensor_tensor(out=outi, in0=t1.ap, scalar=-1.0, in1=t2.ap, op0=ALU.mult, op1=ALU.subtract),
                  reads=[t1, t2], writes=[w_i])
        else:
            cx.op(eng_a, lambda e: e.tensor_tensor(out=outi, in0=t1.ap, in1=t2.ap, op=ALU.add), reads=[t1, t2], writes=[w_i])

    def phase_s5(l):
        ws = Scope(cx)
        Mw = ws.buf([128, 32, 128], BF16, "Mw")
        W1r = ws.buf([128, 32, 64], BF16, "W1r")
        W1i = ws.buf([128, 32, 64], BF16, "W1i")
        W2r = ws.buf([128, 16, 128], BF16, "W2r")
        W2i = ws.buf([128, 16, 128], BF16, "W2i")
        rho = ws.buf([128, 16], F32, "rho")
        ph_r = ws.buf([128, 16], F32, "ph_r")
        ph_i = ws.buf([128, 16], F32, "ph_i")
        dcol = ws.buf([128, 32], F32, "dcol")
        bglu = ws.buf([128, 4], F32, "bglu")
        cx.dma("sp", bglu.ap, s5_b_glu[l].rearrange("(c p) -> p c", p=128), writes=[bglu], allow_slow_non_contiguous=True)
        for tau in range(8):
            cx.dma("sp", dcol.ap[16 * tau:16 * tau + 16, :], s5_d[l].rearrange("(g c) -> c g", c=16), writes=[dcol],
                   allow_slow_non_contiguous=True)
        sc = Scope(cx)
        N = 64
        araw = sc.buf([32, 64], F32, "araw")
        airaw = sc.buf([32, 64], F32, "airaw")
        cx.dma("sp", araw.ap, s5_a_re[l], writes=[araw])
        cx.dma("sp", airaw.ap, s5_a_im[l], writes=[airaw])
        are = sc.buf([N, 32], F32, "are")
        aim = sc.buf([N, 32], F32, "aim")
        for src, dst in ((araw, are), (airaw, aim)):
            cx.op("pe", lambda e, src=src: e.matmul(PS[0].ap[0:64, 0:32], lhsT=src.ap, rhs=identf.ap[0:32, 0:32], start=True, stop=True),
                  reads=[src, identf], writes=[PS[0]])
            cx.op("act", lambda e, dst=dst: e.activation(out=dst.ap, in_=PS[0].ap[0:64, 0:32], func=AF.Identity), reads=[PS[0]], writes=[dst])
        dt = sc.buf([N, 32], F32, "dt")
        cx.dma("sp", dt.ap, s5_log_dt[l].partition_broadcast(N), writes=[dt])
        Br = sc.buf([N, 32, 16], F32, "Br")
        Bi = sc.buf([N, 32, 16], F32, "Bi")
        cx.dma("sp", Br.ap, s5_b_re[l].rearrange("g n c -> n g c"), writes=[Br])
        cx.dma("sp", Bi.ap, s5_b_im[l].rearrange("g n c -> n g c"), writes=[Bi])
        Cr = sc.buf([N, 32, 16], F32, "Cr")
        Ci = sc.buf([N, 32, 16], F32, "Ci")
        craw = sc.buf([128, 4, 64], F32, "craw")
        for src, dst in ((s5_c_re, Cr), (s5_c_im, Ci)):
            cx.dma("sp", craw.ap, src[l].rearrange("(j p) n -> p j n", p=128), writes=[craw])

            def mm(e):
                for j in range(4):
                    i = e.matmul(PS[1].ap[0:64, j * 128:(j + 1) * 128], lhsT=craw.ap[:, j, :], rhs=identf.ap, start=True, stop=True)
                return i
            cx.op("pe", mm, reads=[craw, identf], writes=[PS[1]])
            cx.op("act", lambda e, dst=dst: e.activation(out=dst.ap.rearrange("n g c -> n (g c)"), in_=PS[1].ap[0:64, :], func=AF.Identity),
                  reads=[PS[1]], writes=[dst])

        def small(name):
            return sc.buf([N, 32], F32, name)
        ar, ang, mag, lbr, lbi = small("ar"), small("ang"), small("mag"), small("lbr"), small("lbi")
        t1, t2, t3 = small("t1"), small("t2"), small("t3")
        V = "dve"

        def tt(out, a, b, op_, eng=V):
            cx.op(eng, lambda e: e.tensor_tensor(out=out.ap, in0=a.ap, in1=b.ap, op=op_), reads=[a, b], writes=[out])

        def tsc(out, a, s1, op0, s2=None, op1=None, eng=V):
            if op1 is None:
                cx.op(eng, lambda e: e.tensor_scalar(out=out.ap, in0=a.ap, scalar1=s1, scalar2=None, op0=op0), reads=[a], writes=[out])
            else:
                cx.op(eng, lambda e: e.tensor_scalar(out=out.ap, in0=a.ap, scalar1=s1, scalar2=s2, op0=op0, op1=op1), reads=[a], writes=[out])

        def act(out, a, fn, scale=1.0, bias=None):
            if bias is None:
                cx.op("act", lambda e: e.activation(out=out.ap, in_=a.ap, func=fn, scale=scale), reads=[a], writes=[out])
            else:
                cx.op("act", lambda e: e.activation(out=out.ap, in_=a.ap, func=fn, scale=scale, bias=bias.ap), reads=[a, bias], writes=[out])

        zero_c = sc.buf([N, 1], F32, "zero_c")
        cx.op("dve", lambda e: e.memset(zero_c.ap, 0.0), writes=[zero_c])

        def sin_of(out, angle_buf, shift):
            tsc(t1, angle_buf, 1.0 / TWO_PI, ALU.mult, (shift / TWO_PI) + MAGIC, ALU.add)
            tsc(t1, t1, -MAGIC, ALU.add)
            cx.op(V, lambda e: e.scalar_tensor_tensor(out=t2.ap, in0=t1.ap, scalar=-TWO_PI, in1=angle_buf.ap, op0=ALU.mult, op1=ALU.add),
                  reads=[t1, angle_buf], writes=[t2])
            tsc(t2, t2, float(shift), ALU.add, math.pi - 1e-6, ALU.min)
            tsc(t2, t2, -(math.pi - 1e-6), ALU.max)
            act(out, t2, AF.Sin, bias=zero_c)

        em1, xr_, w_, sn, cm1, nn = small("em1"), small("xr_"), small("w_"), small("sn"), small("cm1"), small("nn")

        def nested(out, var, divs, sign):
            cx.op(V, lambda e: e.memset(out.ap, 1.0), writes=[out])
            for dv in divs:
                tt(t3, out, var, ALU.mult)
                tsc(out, t3, sign / dv, ALU.mult, 1.0, ALU.add)
        ld8 = small("ld8")
        tsc(ld8, dt, 0.125, ALU.mult)
        nested(dt, ld8, [float(k) for k in range(12, 0, -1)], 1.0)
        for _ in range(3):
            tt(dt, dt, dt, ALU.mult)
        tt(ar, are, dt, ALU.mult)
        tt(ang, aim, dt, ALU.mult)
        nested(nn, ar, [9.0, 8.0, 7.0, 6.0, 5.0, 4.0, 3.0, 2.0], 1.0)
        tt(em1, nn, ar, ALU.mult)
        tsc(mag, em1, 1.0, ALU.add)
        C1 = 6.28125
        C2 = TWO_PI - C1
        tsc(t1, ang, 1.0 / TWO_PI, ALU.mult, MAGIC, ALU.add)
        tsc(t1, t1, -MAGIC, ALU.add)
        cx.op(V, lambda e: e.scalar_tensor_tensor(out=xr_.ap, in0=t1.ap, scalar=-C1, in1=ang.ap, op0=ALU.mult, op1=ALU.add),
              reads=[t1, ang], writes=[xr_])
        cx.op(V, lambda e: e.scalar_tensor_tensor(out=xr_.ap, in0=t1.ap, scalar=-C2, in1=xr_.ap, op0=ALU.mult, op1=ALU.add),
              reads=[t1, xr_], writes=[xr_])
        tt(w_, xr_, xr_, ALU.mult)
        nested(nn, w_, [float((2 * k) * (2 * k + 1)) for k in range(10, 0, -1)], -1.0)
        tt(sn, nn, xr_, ALU.mult)
        nested(nn, w_, [float((2 * k + 1) * (2 * k + 2)) for k in range(10, 0, -1)], -1.0)
        tt(cm1, nn, w_, ALU.mult)
        tsc(cm1, cm1, -0.5, ALU.mult)
        lm1 = small("lm1")
        tt(t1, em1, cm1, ALU.mult)
        tt(t2, em1, cm1, ALU.add)
        tt(lm1, t1, t2, ALU.add)
        tsc(lbr, lm1, 1.0, ALU.add)
        tt(lbi, sn, mag, ALU.mult)
        den, qr, qi = small("den"), small("qr"), small("qi")
        tt(den, are, are, ALU.mult)
        tt(t1, aim, aim, ALU.mult)
        tt(den, den, t1, ALU.add)
        cx.op(V, lambda e: e.reciprocal(out=den.ap, in_=den.ap), reads=[den], writes=[den])
        tt(t1, lm1, are, ALU.mult)
        tt(t2, lbi, aim, ALU.mult)
        tt(qr, t1, t2, ALU.add)
        tt(qr, qr, den, ALU.mult)
        tt(t1, lbi, are, ALU.mult)
        tt(t2, lm1, aim, ALU.mult)
        tt(qi, t1, t2, ALU.subtract)
        tt(qi, qi, den, ALU.mult)
        Bbr = sc.buf([N, 32, 16], F32, "Bbr")
        Bbi = sc.buf([N, 32, 16], F32, "Bbi")
        tb1 = sc.buf([N, 32, 16], F32, "tb1")
        tb2 = sc.buf([N, 32, 16], F32, "tb2")

        def bc3(b):
            return b.ap.unsqueeze(2).to_broadcast([N, 32, 16])
        cmul("dve", "pool", (tb1, tb2), Bbr.ap, Bbi.ap, bc3(qr), bc3(qi), Br.ap, Bi.ap, [qr, qi, Br, Bi], Bbr, Bbi)
        Pr = sc.buf([N, 32, 9], F32, "Pr")
        Pi = sc.buf([N, 32, 9], F32, "Pi")
        Qr = sc.buf([N, 32, 8], F32, "Qr")
        Qi = sc.buf([N, 32, 8], F32, "Qi")
        ibr, ibi, im2 = small("ibr"), small("ibi"), small("im2")
        tt(im2, mag, mag, ALU.mult)
        cx.op(V, lambda e: e.reciprocal(out=im2.ap, in_=im2.ap), reads=[im2], writes=[im2])
        tt(ibr, lbr, im2, ALU.mult)
        tt(ibi, lbi, im2, ALU.mult)
        tsc(ibi, ibi, -1.0, ALU.mult)
        for (Xr, Xi, br_, bi_, n_) in ((Pr, Pi, lbr, lbi, 9), (Qr, Qi, ibr, ibi, 8)):
            cx.op(V, lambda e, Xr=Xr: e.memset(Xr.ap[:, :, 0:1], 1.0), writes=[Xr])
            cx.op(V, lambda e, Xi=Xi: e.memset(Xi.ap[:, :, 0:1], 0.0), writes=[Xi])
            for tau in range(1, n_):
                cmul("dve", "pool", (t1, t2), Xr.ap[:, :, tau], Xi.ap[:, :, tau], Xr.ap[:, :, tau - 1], Xi.ap[:, :, tau - 1],
                     br_.ap, bi_.ap, [Xr, Xi, br_, bi_], Xr, Xi)
        GH = 16
        big = [sc.buf([N, GH, 8, 16], F32, f"big{i}") for i in range(8)]
        Ar_, Ai_, Cmr, Cmin, T1, T2, A7r, A7i = big
        mtmp = [sc.buf([128, 128], F32, f"mtmp{i}") for i in range(2)]
        for gh in range(2):
            gs = slice(gh * GH, (gh + 1) * GH)

            def bq(b):
                return b.ap[:, gs, 0:8].unsqueeze(3).to_broadcast([N, GH, 8, 16])

            def bb(b):
                return b.ap[:, gs, :].unsqueeze(2).to_broadcast([N, GH, 8, 16])

            def b7(b, idx):
                return b.ap[:, gs, idx:idx + 1].unsqueeze(3).to_broadcast([N, GH, 8, 16])
            cmul("dve", "pool", (T1, T2), Ar_.ap, Ai_.ap, bq(Qr), bq(Qi), bb(Bbr), bb(Bbi), [Qr, Qi, Bbr, Bbi], Ar_, Ai_)
            cmul("dve", "pool", (T1, T2), Cmr.ap, Cmin.ap, bq(Pr), bq(Pi), bb(Cr), bb(Ci), [Pr, Pi, Cr, Ci], Cmr, Cmin, negi=True)
            cmul("dve", "pool", (T1, T2), A7r.ap, A7i.ap, b7(Pr, 7), b7(Pi, 7), Ar_.ap, Ai_.ap, [Pr, Pi, Ar_, Ai_], A7r, A7i)
            for src, dst in ((A7r, W1r), (A7i, W1i)):
                for g8 in range(2):
                    ps = PS[g8 % 4]

                    def mm(e, ps=ps, src=src, g8=g8):
                        for gi in range(8):
                            i = e.matmul(ps.ap[:, gi * 64:(gi + 1) * 64], lhsT=src.ap[:, g8 * 8 + gi].rearrange("n t c -> n (t c)"),
                                         rhs=identf.ap[0:64, 0:64], start=True, stop=True)
                        return i
                    cx.op("pe", mm, reads=[src, identf], writes=[ps])
                    g0 = gh * GH + g8 * 8
                    cx.op("act", lambda e, ps=ps, dst=dst, g0=g0: e.activation(
                        out=dst.ap[:, g0:g0 + 8, :], in_=ps.ap.rearrange("p (g n) -> p g n", n=64), func=AF.Identity),
                        reads=[ps], writes=[dst])
            for g4 in range(4):
                ps = PS[g4 % 4]

                def mm(e, ps=ps, g4=g4):
                    for gi in range(4):
                        gl_ = g4 * 4 + gi
                        e.matmul(ps.ap[:, gi * 128:(gi + 1) * 128], lhsT=Ar_.ap[:, gl_].rearrange("n t c -> n (t c)"),
                                 rhs=Cmr.ap[:, gl_].rearrange("n t c -> n (t c)"), start=True, stop=False)
                        i = e.matmul(ps.ap[:, gi * 128:(gi + 1) * 128], lhsT=Ai_.ap[:, gl_].rearrange("n t c -> n (t c)"),
                                     rhs=Cmin.ap[:, gl_].rearrange("n t c -> n (t c)"), start=False, stop=True)
                    return i
                cx.op("pe", mm, reads=[Ar_, Ai_, Cmr, Cmin], writes=[ps])
                for gi in range(4):
                    g = gh * GH + g4 * 4 + gi
                    mt = mtmp[g % 2]
                    cx.op("dve", lambda e, ps=ps, gi=gi, mt=mt: e.tensor_tensor(out=mt.ap, in0=ps.ap[:, gi * 128:(gi + 1) * 128], in1=mask8.ap,
                                                                                op=ALU.mult), reads=[ps, mask8], writes=[mt])
                    cx.op("dve", lambda e, g=g, mt=mt: e.scalar_tensor_tensor(out=Mw.ap[:, g, :], in0=identf.ap, scalar=dcol.ap[:, g:g + 1],
                                                                              in1=mt.ap, op0=ALU.mult, op1=ALU.add),
                          reads=[identf, dcol, mt], writes=[Mw])
            W2r_f, W2i_f = A7r, A7i
            cx.op("dve", lambda e: e.tensor_tensor(out=T1.ap, in0=b7(Pr, 1), in1=Cmr.ap, op=ALU.mult), reads=[Pr, Cmr], writes=[T1])
            cx.op("pool", lambda e: e.tensor_tensor(out=T2.ap, in0=b7(Pi, 1), in1=Cmin.ap, op=ALU.mult), reads=[Pi, Cmin], writes=[T2])
            cx.op("dve", lambda e: e.tensor_tensor(out=W2r_f.ap, in0=T1.ap, in1=T2.ap, op=ALU.add), reads=[T1, T2], writes=[W2r_f])
            cx.op("dve", lambda e: e.tensor_tensor(out=T1.ap, in0=b7(Pr, 1), in1=Cmin.ap, op=ALU.mult), reads=[Pr, Cmin, W2r_f], writes=[T1])
            cx.op("pool", lambda e: e.tensor_tensor(out=T2.ap, in0=b7(Pi, 1), in1=Cmr.ap, op=ALU.mult), reads=[Pi, Cmr, W2r_f], writes=[T2])
            cx.op("dve", lambda e: e.tensor_tensor(out=W2i_f.ap, in0=T1.ap, in1=T2.ap, op=ALU.subtract), reads=[T1, T2], writes=[W2i_f])
            for src, dst in ((W2r_f, W2r), (W2i_f, W2i)):
                v = src.ap.rearrange("n (q two) t c -> n q two (t c)", two=2)
                q0 = gh * (GH // 2)
                cx.op("act", lambda e, v=v, dst=dst, q0=q0: e.activation(out=dst.ap[0:64, q0:q0 + GH // 2, :], in_=v[:, :, 0, :], func=AF.Identity),
                      reads=[src], writes=[dst])
                cx.op("act", lambda e, v=v, dst=dst, q0=q0: e.activation(out=dst.ap[64:128, q0:q0 + GH // 2, :], in_=v[:, :, 1, :], func=AF.Identity),
                      reads=[src], writes=[dst])
        r8, e8, p8r, p8i = small("r8"), small("e8"), small("p8r"), small("p8i")
        tt(r8, mag, mag, ALU.mult)
        tt(r8, r8, r8, ALU.mult)
        tt(r8, r8, r8, ALU.mult)
        cx.op(V, lambda e: e.reciprocal(out=e8.ap, in_=r8.ap), reads=[r8], writes=[e8])
        cx.op(V, lambda e: e.tensor_tensor(out=p8r.ap, in0=Pr.ap[:, :, 8], in1=e8.ap, op=ALU.mult), reads=[Pr, e8], writes=[p8r])
        cx.op(V, lambda e: e.tensor_tensor(out=p8i.ap, in0=Pi.ap[:, :, 8], in1=e8.ap, op=ALU.mult), reads=[Pi, e8], writes=[p8i])
        for src, dst in ((r8, rho), (p8r, ph_r), (p8i, ph_i)):
            v = src.ap.rearrange("n (q two) -> n q two", two=2)
            cx.op("act", lambda e, v=v, dst=dst: e.activation(out=dst.ap[0:64], in_=v[:, :, 0], func=AF.Identity), reads=[src], writes=[dst])
            cx.op("act", lambda e, v=v, dst=dst: e.activation(out=dst.ap[64:128], in_=v[:, :, 1], func=AF.Identity), reads=[src], writes=[dst])
        cx.barrier(skip_sw=True)
        sc.close()
        if stop_after == ("s5prep", l):
            ws.close()
            return
        sc = Scope(cx)
        gb = sc.buf([128, 4, L], BF16, "gb")
        SelS = sc.buf([128, 8, 8, 128], BF16, "SelS")
        SelTS = sc.buf([128, 8, 8, 128], BF16, "SelTS")
        cx.dma("sp", SelS.ap, SELD[0], reads=[B_SELD], writes=[SelS])
        cx.dma("sp", SelTS.ap, SELD[1], reads=[B_SELD], writes=[SelTS])
        NQ = 4
        s2 = Scope(cx)
        uT_ = [s2.buf([128, L], BF16, f"uT{i}") for i in range(2)]
        U3_ = [s2.buf([128, 8, TB], BF16, f"U3{i}") for i in range(2)]
        E1r = s2.buf([128, NQ, TB], F32, "E1r")
        E1i = s2.buf([128, NQ, TB], F32, "E1i")
        Hr = s2.buf([128, NQ, TB], BF16, "Hr")
        Hi = s2.buf([128, NQ, TB], BF16, "Hi")
        Y3 = s2.buf([128, 8, TB], BF16, "Y3")
        dd = [s2.buf([128, NQ, 256], F32, f"dd{i}") for i in range(4)]
        tmp2 = [[s2.buf([128, TB], F32, f"s5t{k}_{i}") for i in range(8)] for k in range(2)]
        for qt in range(4):
            uT = uT_[qt % 2]
            U3 = U3_[qt % 2]
            cx.dma("sp", uT.ap, UT[qt * 128:(qt + 1) * 128, :], reads=[B_UT], writes=[uT])
            q0 = qt * NQ
            cx.op("dve", lambda e: e.tensor_copy(out=E1r.ap[:, :, 0], in_=ph_r.ap[:, q0:q0 + NQ]), reads=[ph_r], writes=[E1r])
            cx.op("dve", lambda e: e.tensor_copy(out=E1i.ap[:, :, 0], in_=ph_i.ap[:, q0:q0 + NQ]), reads=[ph_i], writes=[E1i])
            m = 1
            while m < TB:
                def bcm(b, m=m):
                    return b.ap[:, :, m - 1:m].to_broadcast([128, NQ, m])
                cx.op("dve", lambda e, m=m: e.tensor_tensor(out=dd[0].ap[:, :, 0:m], in0=E1r.ap[:, :, 0:m], in1=bcm(E1r), op=ALU.mult),
                      reads=[E1r], writes=[dd[0]])
                cx.op("pool", lambda e, m=m: e.tensor_tensor(out=dd[1].ap[:, :, 0:m], in0=E1i.ap[:, :, 0:m], in1=bcm(E1i), op=ALU.mult),
                      reads=[E1i], writes=[dd[1]])
                cx.op("dve", lambda e, m=m: e.tensor_tensor(out=dd[2].ap[:, :, 0:m], in0=E1r.ap[:, :, 0:m], in1=bcm(E1i), op=ALU.mult),
                      reads=[E1r, E1i], writes=[dd[2]])
                cx.op("pool", lambda e, m=m: e.tensor_tensor(out=dd[3].ap[:, :, 0:m], in0=E1i.ap[:, :, 0:m], in1=bcm(E1r), op=ALU.mult),
                      reads=[E1i, E1r], writes=[dd[3]])
                cx.op("dve", lambda e, m=m: e.tensor_tensor(out=E1r.ap[:, :, m:2 * m], in0=dd[0].ap[:, :, 0:m], in1=dd[1].ap[:, :, 0:m], op=ALU.subtract),
                      reads=[dd[0], dd[1]], writes=[E1r])
                cx.op("dve", lambda e, m=m: e.tensor_tensor(out=E1i.ap[:, :, m:2 * m], in0=dd[2].ap[:, :, 0:m], in1=dd[3].ap[:, :, 0:m], op=ALU.add),
                      reads=[dd[2], dd[3]], writes=[E1i])
                m *= 2
            npsu = 0
            for gl_ in range(8):
                ps = PS[npsu % 4]
                npsu += 1

                def mm(e, ps=ps, gl_=gl_):
                    for tau in range(8):
                        i = e.matmul(ps.ap, lhsT=SelS.ap[:, gl_, tau, :], rhs=uT.ap[:, tau::8], start=(tau == 0), stop=(tau == 7))
                    return i
                cx.op("pe", mm, reads=[SelS, uT], writes=[ps])
                cx.op("act", lambda e, ps=ps, gl_=gl_: e.activation(out=U3.ap[:, gl_, :], in_=ps.ap, func=AF.Identity), reads=[ps], writes=[U3])
            for ql in range(NQ):
                q_ = qt * NQ + ql
                ga, gb_ = 2 * ql, 2 * ql + 1
                pre, pim = PS[4 + (ql % 2) * 2], PS[5 + (ql % 2) * 2]

                def mm(e, pre=pre, pim=pim, ga=ga, gb_=gb_, qt=qt):
                    e.matmul(pre.ap[0:64, :], lhsT=W1r.ap[:, qt * 8 + ga, :], rhs=U3.ap[:, ga, :], start=True, stop=True)
                    e.matmul(pre.ap[64:128, :], lhsT=W1r.ap[:, qt * 8 + gb_, :], rhs=U3.ap[:, gb_, :], start=True, stop=True)
                    e.matmul(pim.ap[0:64, :], lhsT=W1i.ap[:, qt * 8 + ga, :], rhs=U3.ap[:, ga, :], start=True, stop=True)
                    return e.matmul(pim.ap[64:128, :], lhsT=W1i.ap[:, qt * 8 + gb_, :], rhs=U3.ap[:, gb_, :], start=True, stop=True)
                cx.op("pe", mm, reads=[W1r, W1i, U3], writes=[pre, pim])
                xr, xi, a1, a2, vr, vi, sr, si = tmp2[ql % 2]
                cx.op("act", lambda e, pre=pre: e.activation(out=xr.ap, in_=pre.ap, func=AF.Identity), reads=[pre], writes=[xr])
                cx.op("act", lambda e, pim=pim: e.activation(out=xi.ap, in_=pim.ap, func=AF.Identity), reads=[pim], writes=[xi])
                er, ei = E1r.ap[:, ql, :], E1i.ap[:, ql, :]
                cx.op("dve", lambda e, er=er: e.tensor_tensor(out=a1.ap, in0=xr.ap, in1=er, op=ALU.mult), reads=[xr, E1r], writes=[a1])
                cx.op("pool", lambda e, ei=ei: e.tensor_tensor(out=a2.ap, in0=xi.ap, in1=ei, op=ALU.mult), reads=[xi, E1i], writes=[a2])
                cx.op("dve", lambda e: e.tensor_tensor(out=vr.ap, in0=a1.ap, in1=a2.ap, op=ALU.add), reads=[a1, a2], writes=[vr])
                cx.op("dve", lambda e, er=er: e.tensor_tensor(out=a1.ap, in0=xi.ap, in1=er, op=ALU.mult), reads=[xi, E1r], writes=[a1])
                cx.op("pool", lambda e, ei=ei: e.tensor_tensor(out=a2.ap, in0=xr.ap, in1=ei, op=ALU.mult), reads=[xr, E1i], writes=[a2])
                cx.op("dve", lambda e: e.tensor_tensor(out=vi.ap, in0=a1.ap, in1=a2.ap, op=ALU.subtract), reads=[a1, a2], writes=[vi])
                rc = rho.ap[:, q_:q_ + 1].to_broadcast([128, TB])
                cx.op("dve", lambda e, rc=rc: e.tensor_tensor_scan(out=sr.ap, data0=rc, data1=vr.ap, initial=0.0, op0=ALU.mult, op1=ALU.add),
                      reads=[rho, vr], writes=[sr])
                cx.op("dve", lambda e, rc=rc: e.tensor_tensor_scan(out=si.ap, data0=rc, data1=vi.ap, initial=0.0, op0=ALU.mult, op1=ALU.add),
                      reads=[rho, vi], writes=[si])
                cx.op("dve", lambda e, er=er: e.tensor_tensor(out=a1.ap, in0=sr.ap, in1=er, op=ALU.mult), reads=[sr, E1r], writes=[a1])
                cx.op("pool", lambda e, ei=ei: e.tensor_tensor(out=a2.ap, in0=si.ap, in1=ei, op=ALU.mult), reads=[si, E1i], writes=[a2])
                cx.op("pool", lambda e, ql=ql: e.memset(Hr.ap[:, ql, 0:1], 0.0), writes=[Hr])
                cx.op("pool", lambda e, ql=ql: e.memset(Hi.ap[:, ql, 0:1], 0.0), writes=[Hi])
                cx.op("dve", lambda e, ql=ql: e.tensor_tensor(out=Hr.ap[:, ql, 1:TB], in0=a1.ap[:, 0:TB - 1], in1=a2.ap[:, 0:TB - 1], op=ALU.subtract),
                      reads=[a1, a2], writes=[Hr])
                cx.op("dve", lambda e, ei=ei: e.tensor_tensor(out=a1.ap, in0=sr.ap, in1=ei, op=ALU.mult), reads=[sr, E1i], writes=[a1])
                cx.op("pool", lambda e, er=er: e.tensor_tensor(out=a2.ap, in0=si.ap, in1=er, op=ALU.mult), reads=[si, E1r], writes=[a2])
                cx.op("dve", lambda e, ql=ql: e.tensor_tensor(out=Hi.ap[:, ql, 1:TB], in0=a1.ap[:, 0:TB - 1], in1=a2.ap[:, 0:TB - 1], op=ALU.add),
                      reads=[a1, a2], writes=[Hi])
            for gl_ in range(8):
                g = qt * 8 + gl_
                ql, half = gl_ // 2, gl_ % 2
                q_ = qt * NQ + ql
                ps = PS[npsu % 4]
                npsu += 1
                lo, hi = half * 64, half * 64 + 64

                def mm(e, ps=ps, g=g, gl_=gl_, ql=ql, q_=q_, lo=lo, hi=hi):
                    e.matmul(ps.ap, lhsT=Mw.ap[:, g, :], rhs=U3.ap[:, gl_, :], start=True, stop=False)
                    e.matmul(ps.ap, lhsT=W2r.ap[lo:hi, q_, :], rhs=Hr.ap[lo:hi, ql, :], start=False, stop=False)
                    return e.matmul(ps.ap, lhsT=W2i.ap[lo:hi, q_, :], rhs=Hi.ap[lo:hi, ql, :], start=False, stop=True)
                cx.op("pe", mm, reads=[Mw, U3, W2r, W2i, Hr, Hi], writes=[ps])
                cx.op("act", lambda e, ps=ps, gl_=gl_: e.activation(out=Y3.ap[:, gl_, :], in_=ps.ap, func=AF.Identity), reads=[ps], writes=[Y3])
            for tau in range(8):
                ps = PS[npsu % 4]
                npsu += 1

                def mm(e, ps=ps, tau=tau):
                    for gg in range(8):
                        i = e.matmul(ps.ap, lhsT=SelTS.ap[:, gg, tau, :], rhs=Y3.ap[:, gg, :], start=(gg == 0), stop=(gg == 7))
                    return i
                cx.op("pe", mm, reads=[SelTS, Y3], writes=[ps])
                cx.op("act", lambda e, ps=ps, qt=qt, tau=tau: e.activation(out=gb.ap[:, qt, tau::8], in_=ps.ap, func=AF.Gelu_apprx_tanh),
                      reads=[ps], writes=[gb])
        cx.barrier(skip_sw=True)
        s2.close()
        wgl = sc.buf([128, 4, 512], BF16, "wgl")
        cx.dma("sp", wgl.ap, wb_glu[l].rearrange("(kc p) n -> p kc n", p=128), reads=B_wb[("glu", l)], writes=[wgl])
        sg = [sc.buf([128, TB], F32, f"sg{i}") for i in range(2)]
        yst = [sc.buf([128, 4, TB], BF16, f"yst{i}") for i in range(2)]
        n = 0
        for t in range(NB):
            ys_ = yst[t % 2]
            for ct in range(4):
                ps = PS[n % 4]
                s_ = sg[n % 2]
                n += 1

                def mm(e, ps=ps, ct=ct, t=t):
                    for kc in range(4):
                        i = e.matmul(ps.ap, lhsT=wgl.ap[:, kc, ct * 128:(ct + 1) * 128], rhs=gb.ap[:, kc, blk(t)], start=(kc == 0), stop=(kc == 3))
                    return i
                cx.op("pe", mm, reads=[wgl, gb], writes=[ps])
                cx.op("act", lambda e, ps=ps, s_=s_, ct=ct: e.activation(out=s_.ap, in_=ps.ap, func=AF.Sigmoid, bias=bglu.ap[:, ct:ct + 1], scale=1.0),
                      reads=[ps, bglu], writes=[s_])
                cx.op("dve", lambda e, s_=s_, ys_=ys_, ct=ct, t=t: e.tensor_tensor(out=ys_.ap[:, ct, :], in0=gb.ap[:, ct, blk(t)], in1=s_.ap, op=ALU.mult),
                      reads=[gb, s_], writes=[ys_])
            cx.dma("sp", YS[0:512, blk(t)].rearrange("(c p) t -> p c t", p=128), ys_.ap, reads=[ys_], writes=[B_YS[0]])
        cx.barrier(skip_sw=True)
        sc.close()
        ws.close()

    def run_interleaved(gens):
        active = [g for g in gens if g is not None]
        while active:
            for g in list(active):
                try:
                    next(g)
                except StopIteration:
                    active.remove(g)

    def layer_norm_gen(y, gcol, bcol, outb, tmp, stat, sbf):
        pm, pq = PS[6], PS[7]
        ybf, ysq = sbf
        for h2 in range(2):
            sl_ = slice(h2 * 4, (h2 + 1) * 4)
            cx.op("act", lambda e: e.activation(out=ysq.ap[:, sl_, :], in_=y.ap[:, sl_, :], func=AF.Square), reads=[y], writes=[ysq])
            yield
            cx.op("act", lambda e: e.activation(out=ybf.ap[:, sl_, :], in_=y.ap[:, sl_, :], func=AF.Identity), reads=[y], writes=[ybf])
            yield

        def mm1(e):
            for kc in range(8):
                i = e.matmul(pm.ap, lhsT=onesD.ap, rhs=ybf.ap[:, kc, :], start=(kc == 0), stop=(kc == 7))
            return i

        def mm2(e):
            for kc in range(8):
                i = e.matmul(pq.ap, lhsT=onesD.ap, rhs=ysq.ap[:, kc, :], start=(kc == 0), stop=(kc == 7))
            return i
        cx.op("pe", mm1, reads=[onesD, ybf], writes=[pm])
        yield
        cx.op("pe", mm2, reads=[onesD, ysq], writes=[pq])
        yield
        mean, rstd = stat
        cx.op("act", lambda e: e.activation(out=mean.ap, in_=pm.ap, func=AF.Identity), reads=[pm], writes=[mean])
        cx.op("act", lambda e: e.activation(out=rstd.ap, in_=pm.ap, func=AF.Square), reads=[pm], writes=[rstd])
        yield
        cx.op("dve", lambda e: e.tensor_tensor(out=rstd.ap, in0=pq.ap, in1=rstd.ap, op=ALU.subtract), reads=[pq, rstd], writes=[rstd])
        cx.op("dve", lambda e: e.tensor_scalar(out=rstd.ap, in0=rstd.ap, scalar1=0.0, scalar2=LN_EPS, op0=ALU.max, op1=ALU.add),
              reads=[rstd], writes=[rstd])
        yield
        cx.op("act", lambda e: e.activation(out=rstd.ap, in_=rstd.ap, func=AF.Sqrt), reads=[rstd], writes=[rstd])
        cx.op("dve", lambda e: e.reciprocal(out=rstd.ap, in_=rstd.ap), reads=[rstd], writes=[rstd])
        yield
        for h2 in range(2):
            sl_ = slice(h2 * 4, (h2 + 1) * 4)
            mb = mean.ap.unsqueeze(1).to_broadcast([128, 4, TB])
            rb_ = rstd.ap.unsqueeze(1).to_broadcast([128, 4, TB])
            cx.op("dve", lambda e: e.tensor_tensor(out=tmp.ap[:, sl_, :], in0=y.ap[:, sl_, :], in1=mb, op=ALU.subtract), reads=[y, mean], writes=[tmp])
            yield
            cx.op("pool", lambda e: e.tensor_tensor(out=tmp.ap[:, sl_, :], in0=tmp.ap[:, sl_, :], in1=rb_, op=ALU.mult), reads=[tmp, rstd], writes=[tmp])
            yield
        for kc in range(8):
            cx.op("dve", lambda e, kc=kc: e.tensor_scalar(out=y.ap[:, kc, :], in0=tmp.ap[:, kc, :], scalar1=gcol.ap[:, kc:kc + 1],
                                                          scalar2=bcol.ap[:, kc:kc + 1], op0=ALU.mult, op1=ALU.add),
                  reads=[tmp, gcol, bcol], writes=[y])
            yield
        for h2 in range(2):
            sl_ = slice(h2 * 4, (h2 + 1) * 4)
            cx.op("act", lambda e: e.activation(out=outb.ap[:, sl_, :], in_=y.ap[:, sl_, :], func=AF.Identity), reads=[y], writes=[outb])
            yield

    def load_cols(sc, src, l, name, n=8):
        b = sc.buf([128, n], F32, name)
        cx.dma("sp", b.ap, src[l].rearrange("(c p) -> p c", p=128), writes=[b], allow_slow_non_contiguous=True)
        return b

    def phase_mix(l):
        sc = Scope(cx)
        wbr = sc.buf([128, 12, D], BF16, "wbr")
        cx.dma("sp", wbr.ap, wb_branch[l].rearrange("(j p) n -> p j n", p=128), reads=B_wb[("branch", l)], writes=[wbr])
        wgd = [sc.buf([128, 8, 3, 128], BF16, f"wgd{i}") for i in range(2)]
        wgsrc = kview(wb_in[l])[:, :, 3080:6152].rearrange("p kc (k3 dc j) -> p kc k3 dc j", k3=3, dc=8)
        wo = sc.buf([128, 8, D], BF16, "wo")
        cx.dma("sp", wo.ap, kview(wb_out[l]), reads=B_wb[("out", l)], writes=[wo])
        bg = load_cols(sc, b_gate, l, "bg", 24)
        g1 = load_cols(sc, ln1_g, l, "g1")
        b1 = load_cols(sc, ln1_b, l, "b1")
        xb = sc.buf([128, 8, TB], BF16, "mxb")
        xr = [sc.buf([128, TB], F32, f"mxr{i}") for i in range(2)]
        ys = sc.buf([128, 12, TB], BF16, "mys")
        mixb = sc.buf([128, 8, TB], BF16, "mixb")
        yvs = [sc.buf([128, 8, TB], F32, f"yv{i}") for i in range(2)]
        o16 = sc.buf([128, 8, TB], BF16, "mo16")
        tmp = sc.buf([128, 8, TB], F32, "lntmp")
        sbf = (sc.buf([128, 8, TB], BF16, "lnybf"), sc.buf([128, 8, TB], BF16, "lnysq"))
        stat = (sc.buf([128, TB], F32, "mean"), sc.buf([128, TB], F32, "rstd"))
        gsb = [sc.buf([128, TB], F32, f"gsb{i}") for i in range(3)]
        acc = [sc.buf([128, TB], F32, f"acc{i}") for i in range(2)]
        xres_src = xT if l == 0 else XRES
        st = {"n": 0, "nr": 0, "nw": 0}

        def genA(t):
            x_, y_ = xb, ys
            yv = yvs[t % 2]
            cx.dma("sp", x_.ap, kview(XBF)[:, :, blk(t)], reads=[B_XBF[t]], writes=[x_])
            cx.dma("sp", y_.ap, YS.rearrange("(j p) t -> p j t", p=128)[:, :, blk(t)], reads=B_YS, writes=[y_])
            for dc in range(8):
                a_ = acc[dc % 2]
                wg_ = wgd[st["nw"] % 2]
                st["nw"] += 1
                for k3_ in range(3):
                    cx.dma("sp", wg_.ap[:, :, k3_, :], wgsrc[:, :, k3_, dc, :], reads=B_wb[("in", l)], writes=[wg_])
                for k3 in range(3):
                    n = st["n"]
                    st["n"] += 1
                    pp, pg = PS[(2 * n) % 6], PS[(2 * n + 1) % 6]
                    g_ = gsb[n % 3]

                    def mmp(e, pp=pp, k3=k3, dc=dc, y_=y_):
                        for kc in range(4):
                            i = e.matmul(pp.ap, lhsT=wbr.ap[:, k3 * 4 + kc, dc * 128:(dc + 1) * 128], rhs=y_.ap[:, k3 * 4 + kc, :],
                                         start=(kc == 0), stop=(kc == 3))
                        return i

                    def mmg(e, pg=pg, k3=k3, x_=x_, wg_=wg_):
                        for kc in range(8):
                            i = e.matmul(pg.ap, lhsT=wg_.ap[:, kc, k3, :], rhs=x_.ap[:, kc, :], start=(kc == 0), stop=(kc == 7))
                        return i
                    cx.op("pe", mmg, reads=[wg_, x_], writes=[pg])
                    cx.op("pe", mmp, reads=[wbr, y_], writes=[pp])
                    cx.op("act", lambda e, pg=pg, g_=g_, k3=k3, dc=dc: e.activation(out=g_.ap, in_=pg.ap, func=AF.Sigmoid,
                                                                                   bias=bg.ap[:, k3 * 8 + dc:k3 * 8 + dc + 1], scale=1.0),
                          reads=[pg, bg], writes=[g_])
                    if k3 == 0:
                        cx.op("dve", lambda e, pp=pp, g_=g_, a_=a_: e.tensor_tensor(out=a_.ap, in0=pp.ap, in1=g_.ap, op=ALU.mult),
                              reads=[pp, g_], writes=[a_])
                    else:
                        cx.op("dve", lambda e, pp=pp, g_=g_: e.tensor_tensor(out=g_.ap, in0=pp.ap, in1=g_.ap, op=ALU.mult),
                              reads=[pp, g_], writes=[g_])
                        if k3 == 1:
                            cx.op("pool", lambda e, g_=g_, a_=a_: e.tensor_tensor(out=a_.ap, in0=a_.ap, in1=g_.ap, op=ALU.add),
                                  reads=[a_, g_], writes=[a_])
                        else:
                            cx.op("pool", lambda e, g_=g_, a_=a_, dc=dc: e.tensor_tensor(out=mixb.ap[:, dc, :], in0=a_.ap, in1=g_.ap, op=ALU.add),
                                  reads=[a_, g_], writes=[mixb])
                    yield
            for dc in range(8):
                po = PS[6 + dc % 2]
                r_ = xr[st["nr"] % 2]
                st["nr"] += 1
                cx.dma("sp", r_.ap, xres_src[dc * 128:(dc + 1) * 128, blk(t)], reads=[B_XRES[t]], writes=[r_])

                def mmo(e, po=po, dc=dc):
                    for kc in range(8):
                        i = e.matmul(po.ap, lhsT=wo.ap[:, kc, dc * 128:(dc + 1) * 128], rhs=mixb.ap[:, kc, :], start=(kc == 0), stop=(kc == 7))
                    return i
                cx.op("pe", mmo, reads=[wo, mixb], writes=[po])
                cx.op("dve", lambda e, po=po, dc=dc, r_=r_: e.scalar_tensor_tensor(out=yv.ap[:, dc, :], in0=r_.ap, scalar=float(ALPHA),
                                                                                 in1=po.ap, op0=ALU.mult, op1=ALU.add),
                      reads=[r_, po], writes=[yv])
                yield

        def genB(t):
            yv = yvs[t % 2]
            yield from layer_norm_gen(yv, g1, b1, o16, tmp, stat, sbf)
            cx.dma("sp", kview(X1RES)[:, :, blk(t)], yv.ap, reads=[yv], writes=[B_X1RES[t]])
            cx.dma("sp", kview(X1BF)[:, :, blk(t)], o16.ap, reads=[o16], writes=[B_X1BF[t]])
            yield
        run_interleaved([genA(0)])
        for t in range(NB):
            run_interleaved([genA(t + 1) if t + 1 < NB else None, genB(t)])
        cx.barrier(skip_sw=True)
        sc.close()

    def phase_ffn(l, last):
        sc = Scope(cx)
        wdn = sc.buf([128, NHC, D], BF16, "wdn")
        cx.dma("sp", wdn.ap, wb_d[l].rearrange("(j p) n -> p j n", p=128), reads=B_wb[("d", l)], writes=[wdn])
        g2 = load_cols(sc, ln2_g, l, "g2")
        b2 = load_cols(sc, ln2_b, l, "b2")
        xb = sc.buf([128, 8, TB], BF16, "fxb")
        hT = sc.buf([128, NHC, TB], BF16, "hT")
        wgu = [sc.buf([128, 2, 8, 256], BF16, f"wgu{i}") for i in range(2)]
        sl = [sc.buf([128, TB], F32, f"sl{i}") for i in range(2)]
        xr = [sc.buf([128, TB], F32, f"fxr{i}") for i in range(2)]
        yvs = [sc.buf([128, 8, TB], F32, f"fyv{i}") for i in range(2)]
        tmp = sc.buf([128, 8, TB], F32, "flntmp")
        sbf = (sc.buf([128, 8, TB], BF16, "flnybf"), sc.buf([128, 8, TB], BF16, "flnysq"))
        o16 = sc.buf([128, 8, TB], BF16, "fo16")
        stat = (sc.buf([128, TB], F32, "fmean"), sc.buf([128, TB], F32, "frstd"))
        st = {"n": 0, "nr": 0, "nw": 0}

        def genA(t):
            yv = yvs[t % 2]
            cx.dma("sp", xb.ap, kview(X1BF)[:, :, blk(t)], reads=[B_X1BF[t]], writes=[xb])
            for hp in range(NHC // 2):
                w = wgu[st["nw"] % 2]
                st["nw"] += 1
                cx.dma("sp", w.ap[:, 0], kview(wb_g[l])[:, :, hp * 256:(hp + 1) * 256], reads=B_wb[("g", l)], writes=[w])
                cx.dma("sp", w.ap[:, 1], kview(wb_u[l])[:, :, hp * 256:(hp + 1) * 256], reads=B_wb[("u", l)], writes=[w])
                for hh in range(2):
                    hc = hp * 2 + hh
                    n = st["n"]
                    st["n"] += 1
                    pg, pu = PS[(2 * n) % 6], PS[(2 * n + 1) % 6]
                    s_ = sl[n % 2]

                    def mmg(e, pg=pg, w=w, hh=hh):
                        for kc in range(8):
                            i = e.matmul(pg.ap, lhsT=w.ap[:, 0, kc, hh * 128:(hh + 1) * 128], rhs=xb.ap[:, kc, :], start=(kc == 0), stop=(kc == 7))
                        return i

                    def mmu(e, pu=pu, w=w, hh=hh):
                        for kc in range(8):
                            i = e.matmul(pu.ap, lhsT=w.ap[:, 1, kc, hh * 128:(hh + 1) * 128], rhs=xb.ap[:, kc, :], start=(kc == 0), stop=(kc == 7))
                        return i
                    cx.op("pe", mmg, reads=[w, xb], writes=[pg])
                    cx.op("pe", mmu, reads=[w, xb], writes=[pu])
                    cx.op("act", lambda e, pg=pg, s_=s_: e.activation(out=s_.ap, in_=pg.ap, func=AF.Silu), reads=[pg], writes=[s_])
                    cx.op("dve", lambda e, pu=pu, s_=s_, hc=hc: e.tensor_tensor(out=hT.ap[:, hc, :], in0=pu.ap, in1=s_.ap, op=ALU.mult),
                          reads=[pu, s_], writes=[hT])
                    yield
            for dc in range(8):
                po = PS[6 + dc % 2]
                r_ = xr[st["nr"] % 2]
                st["nr"] += 1
                cx.dma("sp", r_.ap, X1RES[dc * 128:(dc + 1) * 128, blk(t)], reads=[B_X1RES[t]], writes=[r_])

                def mmo(e, po=po, dc=dc):
                    for hc in range(NHC):
                        i = e.matmul(po.ap, lhsT=wdn.ap[:, hc, dc * 128:(dc + 1) * 128], rhs=hT.ap[:, hc, :], start=(hc == 0), stop=(hc == NHC - 1))
                    return i
                cx.op("pe", mmo, reads=[wdn, hT], writes=[po])
                cx.op("dve", lambda e, po=po, dc=dc, r_=r_: e.scalar_tensor_tensor(out=yv.ap[:, dc, :], in0=r_.ap, scalar=float(ALPHA),
                                                                                 in1=po.ap, op0=ALU.mult, op1=ALU.add),
                      reads=[r_, po], writes=[yv])
                yield

        def genB(t):
            yv = yvs[t % 2]
            yield from layer_norm_gen(yv, g2, b2, o16, tmp, stat, sbf)
            if last:
                cx.dma("sp", kview(outT)[:, :, blk(t)], yv.ap, reads=[yv], writes=[B_OUT])
            else:
                cx.dma("sp", kview(XRES)[:, :, blk(t)], yv.ap, reads=[yv], writes=[B_XRES[t]])
                cx.dma("sp", kview(XBF)[:, :, blk(t)], o16.ap, reads=[o16], writes=[B_XBF[t]])
            yield
        run_interleaved([genA(0)])
        for t in range(NB):
            run_interleaved([genA(t + 1) if t + 1 < NB else None, genB(t)])
        cx.barrier(skip_sw=True)
        sc.close()

    cx.barrier(skip_sw=True)
    for l in range(n_layers):
        if stop_after == ("setup", l):
            break
        phase_proj(l)
        if l + 1 < n_layers:
            convert_layer(l + 1)
        if stop_after == ("proj", l):
            break
        phase_attn(l)
        if stop_after == ("attn", l):
            break
        phase_lru(l)
        if stop_after == ("lru", l):
            break
        phase_s5(l)
        if stop_after in (("s5", l), ("s5prep", l)):
            break
        phase_mix(l)
        if stop_after == ("mix", l):
            break
        phase_ffn(l, last=(l == n_layers - 1))
    cx.barrier()
    return nc


INPUT_ORDER = ["w_in", "w_branch", "w_out", "w_ffn_gate", "w_ffn_up", "w_ffn_down", "s5_w_glu", "lru_w_a", "lru_w_x",
               "b_f", "b_gate", "s5_a_re", "s5_a_im", "s5_log_dt", "s5_b_re", "s5_b_im", "s5_c_re", "s5_c_im", "s5_d",
               "s5_b_glu", "lru_conv_w", "lru_conv_b", "lru_b_a", "lru_b_x", "lru_lambda", "ln1_g", "ln1_b", "ln2_g", "ln2_b"]


def layout_inputs(inputs, n_layers=DEPTH):
    f = lambda a: np.ascontiguousarray(np.asarray(a, dtype=np.float32)[:n_layers])
    shared = {}
    for k in INPUT_ORDER:
        a = f(inputs[k])
        if k == "w_branch":
            a = a.reshape(n_layers, 1536, D)
        elif k in ("s5_c_re", "s5_c_im"):
            a = a.reshape(n_layers, 512, 64)
        elif k in ("lru_b_a", "lru_b_x"):
            a = a.reshape(n_layers, 512)
        shared[k] = np.ascontiguousarray(a)
    return shared


def kernel(**inputs):
    x = np.asarray(inputs["x"], dtype=np.float32)
    shared = layout_inputs(inputs)
    nc = bass.Bass("TRN2", target_bir_lowering=False)
    build(nc)
    in_maps = []
    for c in range(8):
        m = dict(shared)
        m["xT"] = np.ascontiguousarray(x[c % 4].T)
        in_maps.append(m)
    res = run_bass_kernel_spmd(nc, in_maps, core_ids=list(range(8)))
    out = np.stack([np.ascontiguousarray(res.results[b]["outT"].T) for b in range(4)], axis=0)
    return out.astype(np.float32)
```
